# Optimizing a Trainium2 kernel written in Bass

```python
import math
import jax, jax.numpy as jnp
from jax import lax
import numpy as np

D_MODEL = 1024
BATCH = 4
SEQ = 4096
DEPTH = 4

CTX_LEN = 256
GRID_W = 64
NORM_EPS = 1e-6

ML_HEADS = 4
ML_DK = 256
ML_DV = 256
ML_W = ML_HEADS * ML_DV
ML_CHUNK = 128

DA_HEADS = 8
DA_DH = 64
DA_DV = 2 * DA_DH
DA_QK_W = DA_HEADS * 2 * DA_DH
DA_V_W = DA_HEADS * DA_DV
Q_BLOCK = 128
ROPE_BASE = 10000.0

FN_GROUPS = 4
FN_GC = 256
FN_W = FN_GROUPS * FN_GC

N_BRANCH = 3
D_FF = -(-8 * D_MODEL // (3 * 256)) * 256

IN_SPLITS = (ML_DK * ML_HEADS, ML_DK * ML_HEADS, ML_W, ML_W, 4 * ML_HEADS,
             DA_QK_W, DA_QK_W, DA_V_W, FN_W, N_BRANCH * D_MODEL)
D_IN = sum(IN_SPLITS)

kernel_name = 'hybrid_mlstm_diffattn_fnet_dit'


def _rmsnorm(x, g):
    xf = x.astype(jnp.float32)
    y = xf * lax.rsqrt(jnp.mean(xf * xf, axis=-1, keepdims=True) + NORM_EPS)
    return (y * g.astype(jnp.float32)).astype(x.dtype)


def _modulate(h, shift, scale):
    return h * (1 + scale) + shift


def _split_cols(u):
    idx, acc = [], 0
    for s in IN_SPLITS[:-1]:
        acc += s
        idx.append(acc)
    return jnp.split(u, idx, axis=-1)


def _axial_rope(rows):
    n_freq = DA_DH // 4
    inv = ROPE_BASE ** (-jnp.arange(n_freq, dtype=jnp.float32) / n_freq)
    r = jnp.repeat(jnp.arange(rows, dtype=jnp.float32), GRID_W)
    col = jnp.tile(jnp.arange(GRID_W, dtype=jnp.float32), rows)
    ang = jnp.concatenate([r[:, None] * inv, col[:, None] * inv], axis=-1)
    return jnp.cos(ang), jnp.sin(ang)


def _rope(x, cos, sin):
    shp = (cos.shape[0],) + (1,) * (x.ndim - 3) + (cos.shape[1],)
    c = cos.reshape(shp).astype(x.dtype)
    s = sin.reshape(shp).astype(x.dtype)
    x1, x2 = jnp.split(x, 2, axis=-1)
    return jnp.concatenate([x1 * c - x2 * s, x1 * s + x2 * c], axis=-1)


def _mlstm_scan(q, k, v, i_pre, f_pre, state):
    B, T, H, DK = q.shape
    DV = v.shape[-1]
    L = ML_CHUNK
    NC = T // L

    def chunks(a):
        a = a.astype(jnp.float32).reshape((B, NC, L, H) + a.shape[3:])
        return jnp.moveaxis(jnp.moveaxis(a, 1, 0), 3, 2)

    tri = jnp.tril(jnp.ones((L, L), dtype=bool))

    def step(carry, xs):
        C, n, m = carry
        qc, kc, vc, ic, fc = xs
        b = jnp.cumsum(jax.nn.log_sigmoid(fc), axis=-1)
        dmat = jnp.where(tri, b[..., :, None] - b[..., None, :] + ic[..., None, :], -jnp.inf)
        inter = b + m[..., None]
        m_t = jnp.maximum(inter, jnp.max(dmat, axis=-1))
        a = jnp.exp(dmat - m_t[..., None]) * jnp.einsum('bhtd,bhsd->bhts', qc, kc)
        sc = jnp.exp(inter - m_t)
        num = sc[..., None] * jnp.einsum('bhtd,bhde->bhte', qc, C) + jnp.einsum('bhts,bhse->bhte', a, vc)
        den = sc * jnp.einsum('bhtd,bhd->bht', qc, n) + jnp.sum(a, axis=-1)
        h = num / jnp.maximum(jnp.abs(den), jnp.exp(-m_t))[..., None]
        b_end = b[..., -1]
        g_s = b_end[..., None] - b + ic
        m_new = jnp.maximum(b_end + m, jnp.max(g_s, axis=-1))
        decay = jnp.exp(b_end + m - m_new)
        kw = kc * jnp.exp(g_s - m_new[..., None])[..., None]
        C_new = decay[..., None, None] * C + jnp.einsum('bhsd,bhse->bhde', kw, vc)
        n_new = decay[..., None] * n + jnp.sum(kw, axis=2)
        return (C_new, n_new, m_new), h

    final, hs = lax.scan(step, state, (chunks(q), chunks(k), chunks(v), chunks(i_pre), chunks(f_pre)))
    h = jnp.transpose(hs, (1, 0, 3, 2, 4)).reshape(B, T, H, DV)
    return h, final


def _mlstm_branch(ul, uc, gate_b, head_g):
    def prep(u):
        B, T, _ = u[0].shape
        q = u[0].reshape(B, T, ML_HEADS, ML_DK)
        k = u[1].reshape(B, T, ML_HEADS, ML_DK) * (ML_DK ** -0.5)
        v = u[2].reshape(B, T, ML_HEADS, ML_DV)
        i_f, f_f, i_b, f_b = jnp.split(u[4] + gate_b, 4, axis=-1)
        return q, k, v, i_f, f_f, i_b, f_b

    flip = lambda a: jnp.flip(a, axis=1)
    ql, kl, vl, ifl, ffl, ibl, fbl = prep(ul)
    qc, kc, vc, ifc, ffc, ibc, fbc = prep(uc)
    B = ql.shape[0]
    zero = (jnp.zeros((B, ML_HEADS, ML_DK, ML_DV), jnp.float32),
            jnp.zeros((B, ML_HEADS, ML_DK), jnp.float32),
            jnp.zeros((B, ML_HEADS), jnp.float32))
    hcf, st_f = _mlstm_scan(qc, kc, vc, ifc, ffc, zero)
    hlf, _ = _mlstm_scan(ql, kl, vl, ifl, ffl, st_f)
    hcb, st_b = _mlstm_scan(flip(qc), flip(kc), flip(vc), flip(ibc), flip(fbc), zero)
    hlb, _ = _mlstm_scan(flip(ql), flip(kl), flip(vl), flip(ibl), flip(fbl), st_b)

    def out(hf, hb, u):
        B, T = hf.shape[:2]
        h = _rmsnorm(hf + hb, head_g.reshape(ML_HEADS, ML_DV)).reshape(B, T, ML_W)
        return h.astype(u[3].dtype) * jax.nn.sigmoid(u[3])

    return out(hlf, flip(hlb), ul), out(hcf, flip(hcb), uc)


def _diff_attn_core(q, k, v, lam):
    s = jnp.einsum('bqhmd,bkhmd->bhmqk', q, k).astype(jnp.float32) * (DA_DH ** -0.5)
    p = jax.nn.softmax(s, axis=-1)
    a = p[:, :, 0] - lam * p[:, :, 1]
    return jnp.einsum('bhqk,bkhe->bqhe', a.astype(v.dtype), v)


def _diff_attn_branch(ul, uc, lam_qk, head_g, cos, sin, lam_init, need_ctx):
    def qkv(u):
        B, T, _ = u[5].shape
        return (u[5].reshape(B, T, DA_HEADS, 2, DA_DH),
                u[6].reshape(B, T, DA_HEADS, 2, DA_DH),
                u[7].reshape(B, T, DA_HEADS, DA_DV))

    lq = lam_qk.astype(jnp.float32)
    lam = jnp.exp(jnp.sum(lq[0] * lq[1])) - jnp.exp(jnp.sum(lq[2] * lq[3])) + lam_init
    ql, kl, vl = qkv(ul)
    qc, kc, vc = qkv(uc)
    ql, kl = _rope(ql, cos, sin), _rope(kl, cos, sin)
    k_all = jnp.concatenate([kl, kc], axis=1)
    v_all = jnp.concatenate([vl, vc], axis=1)
    B, T = ql.shape[:2]
    nb = T // Q_BLOCK
    qb = jnp.moveaxis(ql.reshape(B, nb, Q_BLOCK, DA_HEADS, 2, DA_DH), 1, 0)
    ob = lax.map(lambda qq: _diff_attn_core(qq, k_all, v_all, lam), qb)
    ol = jnp.moveaxis(ob, 0, 1).reshape(B, T, DA_HEADS, DA_DV)

    def post(o):
        Bo, To = o.shape[:2]
        return (_rmsnorm(o, head_g.reshape(DA_HEADS, DA_DV)) * (1.0 - lam_init)).reshape(Bo, To, DA_V_W)

    oc = post(_diff_attn_core(qc, kc, vc, lam)) if need_ctx else None
    return post(ol), oc


def _fourier(u):
    B, T, _ = u.shape
    z = u.astype(jnp.float32).reshape(B, T, FN_GROUPS, FN_GC)
    y = jnp.fft.fftn(z, axes=(1, 3), norm='ortho').real
    return y.reshape(B, T, FN_W).astype(u.dtype)


def _merge(ym, yd, yf, g_pre, w_br_ml, w_br_da, w_br_fn, w_out):
    gm, gd, gf = jnp.split(jax.nn.sigmoid(g_pre), N_BRANCH, axis=-1)
    y = gm * (ym @ w_br_ml) + gd * (yd @ w_br_da) + gf * (yf @ w_br_fn)
    return y @ w_out


def _mixer(hl, hc, w_in, ml_gate_b, ml_head_g, da_lam, da_head_g, w_br_ml, w_br_da, w_br_fn, w_out,
           cos, sin, lam_init, need_ctx):
    ul = _split_cols(hl @ w_in)
    uc = _split_cols(hc @ w_in)
    ml_l, ml_c = _mlstm_branch(ul, uc, ml_gate_b, ml_head_g)
    da_l, da_c = _diff_attn_branch(ul, uc, da_lam, da_head_g, cos, sin, lam_init, need_ctx)
    yl = _merge(ml_l, da_l, _fourier(ul[8]), ul[9], w_br_ml, w_br_da, w_br_fn, w_out)
    yc = _merge(ml_c, da_c, _fourier(uc[8]), uc[9], w_br_ml, w_br_da, w_br_fn, w_out) if need_ctx else None
    return yl, yc


def _swiglu(h, w_in, w_out):
    a, b = jnp.split(h @ w_in, 2, axis=-1)
    return (jax.nn.silu(a) * b) @ w_out


def setup_inputs(seed: int = 0) -> dict:
    key = jax.random.key(seed)
    ks = jax.random.split(key, 20)
    f32 = jnp.float32
    D = D_MODEL

    def nrm(k, shape, s):
        return jax.random.normal(k, shape, f32) * s

    f_mask = jnp.array([0.0, 1.0, 0.0, 1.0], f32)[None, :, None]
    f_bias = jnp.linspace(3.0, 6.0, ML_HEADS, dtype=f32)[None, None, :]
    ml_gate_b = (nrm(ks[8], (DEPTH, 4, ML_HEADS), 0.1) + f_mask * f_bias).reshape(DEPTH, 4 * ML_HEADS)
    return {
        'x': nrm(ks[0], (BATCH, SEQ, D), 1.0),
        'c': nrm(ks[1], (BATCH, D), 1.0),
        'ctx': nrm(ks[2], (BATCH, CTX_LEN, D), 1.0),
        'c_ctx': nrm(ks[3], (D,), 1.0),
        'w_ada': nrm(ks[4], (DEPTH, D, 6 * D), 0.5 * D ** -0.5),
        'b_ada': nrm(ks[5], (DEPTH, 6 * D), 0.02),
        'norm_g': 1.0 + nrm(ks[6], (DEPTH, 2, D), 0.02),
        'w_in': nrm(ks[7], (DEPTH, D, D_IN), D ** -0.5),
        'ml_gate_b': ml_gate_b,
        'ml_head_g': 1.0 + nrm(ks[9], (DEPTH, ML_W), 0.02),
        'da_lam': nrm(ks[10], (DEPTH, 4, DA_DH), 0.1),
        'da_head_g': 1.0 + nrm(ks[11], (DEPTH, DA_V_W), 0.02),
        'w_br_ml': nrm(ks[12], (DEPTH, ML_W, D), ML_W ** -0.5),
        'w_br_da': nrm(ks[13], (DEPTH, DA_V_W, D), DA_V_W ** -0.5),
        'w_br_fn': nrm(ks[14], (DEPTH, FN_W, D), FN_W ** -0.5),
        'w_out': nrm(ks[15], (DEPTH, D, D), D ** -0.5),
        'w_ffn_in': nrm(ks[16], (DEPTH, D, 2 * D_FF), D ** -0.5),
        'w_ffn_out': nrm(ks[17], (DEPTH, D_FF, D), D_FF ** -0.5),
        'final_g': 1.0 + nrm(ks[18], (D,), 0.02),
    }


def reference(x, c, ctx, c_ctx, w_ada, b_ada, norm_g, w_in, ml_gate_b, ml_head_g, da_lam, da_head_g,
              w_br_ml, w_br_da, w_br_fn, w_out, w_ffn_in, w_ffn_out, final_g):
    T = x.shape[1]
    ROWS = T // GRID_W
    cos, sin = _axial_rope(ROWS)
    s_lat = jax.nn.silu(c)
    s_ctx = jax.nn.silu(c_ctx)
    xl, xc = x, ctx
    for l in range(DEPTH):
        need_ctx = l < DEPTH - 1
        lam_init = 0.8 - 0.6 * math.exp(-0.3 * l)
        ml = jnp.split((s_lat @ w_ada[l] + b_ada[l])[:, None, :], 6, axis=-1)
        mc = jnp.split((s_ctx @ w_ada[l] + b_ada[l])[None, None, :], 6, axis=-1)
        hl = _modulate(_rmsnorm(xl, norm_g[l, 0]), ml[0], ml[1])
        hc = _modulate(_rmsnorm(xc, norm_g[l, 0]), mc[0], mc[1])
        yl, yc = _mixer(hl, hc, w_in[l], ml_gate_b[l], ml_head_g[l], da_lam[l], da_head_g[l],
                        w_br_ml[l], w_br_da[l], w_br_fn[l], w_out[l], cos, sin, lam_init, need_ctx)
        xl = xl + ml[2] * yl
        xl = xl + ml[5] * _swiglu(_modulate(_rmsnorm(xl, norm_g[l, 1]), ml[3], ml[4]), w_ffn_in[l], w_ffn_out[l])
        if need_ctx:
            xc = xc + mc[2] * yc
            xc = xc + mc[5] * _swiglu(_modulate(_rmsnorm(xc, norm_g[l, 1]), mc[3], mc[4]), w_ffn_in[l], w_ffn_out[l])
    return _rmsnorm(xl, final_g)
```

```python
import math
import numpy as np
import concourse.bass as bass
import concourse.mybir as mybir
from concourse.bass_utils import run_bass_kernel_spmd

F32 = mybir.dt.float32
BF16 = mybir.dt.bfloat16
AF = mybir.ActivationFunctionType
ALU = mybir.AluOpType
AX = mybir.AxisListType

D = 1024
KC = 8
DEPTH = 4
NL = 4096
NCX = 256
NT = NL + NCX
NTILE = NT // 128
D_IN = 11280
D_FF = 2816
FC = D_FF // 128
FN_STOP = 99
EPS = 1e-6
SBUF_BASE = 16640
SBUF_BYTES = 229376


class Buf:
    __slots__ = ("name", "w", "rs", "rd", "excl")

    def __init__(self, name=""):
        self.name = name
        self.excl = False
        self.w = None
        self.rs = {}
        self.rd = []


class Ins:
    __slots__ = ("eng", "fn", "deps", "sig", "sigval", "dma", "dsem", "dval", "dprev")

    def __init__(self, eng, fn, dma):
        self.eng = eng
        self.fn = fn
        self.dma = dma
        self.deps = set()
        self.sig = False
        self.sigval = 0
        self.dsem = None
        self.dval = 0
        self.dprev = 0


class Tl:
    def __init__(self, t, name):
        self.t = t
        self.name = name
        self.buf = Buf(name)
        self.subs = {}

    def b(self, i):
        s = self.subs.get(i)
        if s is None:
            s = Buf("%s.%s" % (self.name, i))
            self.subs[i] = s
        return s

    def allb(self):
        return list(self.subs.values())

    def ap(self):
        return self.t.ap()

    def __getitem__(self, k):
        return self.t[k]


class PsBank(Tl):
    def __init__(self, t, i):
        Tl.__init__(self, t, "ps%d" % i)
        self.i = i
        self.buf.excl = True

    def __getitem__(self, key):
        if isinstance(key, tuple):
            return self.t[key[0], self.i, key[1]]
        return self.t[key, self.i, :]


def _bufs(lst):
    out = []
    for x in lst:
        if x is None:
            continue
        if isinstance(x, Tl):
            out.append(x.buf)
        elif isinstance(x, Buf):
            out.append(x)
        else:
            out.extend(_bufs(x))
    return out


class Prog:
    ENGS = ("pe", "act", "dve", "pool", "sp")
    NDMA = {"sp": 40, "act": 16, "pool": 16}

    def __init__(self, nc):
        self.nc = nc
        self.ins = []
        self.eng = {"pe": nc.tensor, "act": nc.scalar, "dve": nc.vector, "pool": nc.gpsimd, "sp": nc.sync}
        self.last = {e: None for e in self.ENGS}
        self.dmas_open = []
        self.sb_off = SBUF_BASE
        self.sb_mark = 0
        self.nid = 0

    def sb(self, name, shape, dtype, persist=False):
        nbytes = int(np.prod(shape[1:])) * (4 if dtype == F32 else 2)
        nbytes = (nbytes + 63) // 64 * 64
        off = self.sb_off
        assert off + nbytes <= SBUF_BYTES, "SBUF overflow %s %d" % (name, off + nbytes)
        self.nid += 1
        t = self.nc.alloc_sbuf_tensor_at("%s_%d" % (name, self.nid), list(shape), dtype, offset=off)
        self.sb_off = off + nbytes
        return Tl(t, name)

    def phase_begin(self):
        self.sb_mark_stack = getattr(self, "sb_mark_stack", [])
        self.sb_mark_stack.append(self.sb_off)

    def phase_end(self):
        self.barrier()
        self.sb_off = self.sb_mark_stack.pop()

    def dram(self, name, shape, dtype, kind="Internal"):
        t = self.nc.dram_tensor(name, list(shape), dtype, kind=kind)
        return Tl(t, name)

    def op(self, eng, fn, r=(), w=(), dma=False):
        i = Ins(eng, fn, dma)
        rb = _bufs(r)
        wb = _bufs(w)
        ex = [b for b in rb if b.excl]
        if ex:
            rb = [b for b in rb if not b.excl]
            wb = wb + [b for b in ex if b not in wb]
        deps = i.deps
        for b in rb:
            if b.w is not None:
                if not (eng == "pe" and b.w.eng == "pe" and not b.w.dma):
                    deps.add(b.w)
        for b in wb:
            x = b.w
            if x is not None and (x.eng != eng or x.dma or dma):
                deps.add(x)
            for x in b.rs.values():
                if x.eng != eng or dma:
                    deps.add(x)
            for x in b.rd:
                deps.add(x)
        for b in rb:
            if dma:
                b.rd.append(i)
            else:
                b.rs[eng] = i
        for b in wb:
            b.w = i
            b.rs = {}
            b.rd = []
        self.ins.append(i)
        if dma:
            self.dmas_open.append(i)
        else:
            self.last[eng] = i
        return i

    def pe(self, fn, r=(), w=()):
        return self.op("pe", fn, r, w)

    def act(self, fn, r=(), w=()):
        return self.op("act", fn, r, w)

    def dve(self, fn, r=(), w=()):
        return self.op("dve", fn, r, w)

    def pool(self, fn, r=(), w=()):
        return self.op("pool", fn, r, w)

    def dma(self, q, out, in_, r=(), w=()):
        e = self.eng[q]
        return self.op(q, lambda: e.dma_start(out=out, in_=in_), r, w, dma=True)

    def barrier(self):
        prev = [x for x in self.last.values() if x is not None] + list(self.dmas_open)
        self.dmas_open = []
        for e in self.ENGS:
            eh = self.eng[e]
            i = Ins(e, (lambda eh=eh: eh.nop()), False)
            i.deps = set(prev)
            self.ins.append(i)
            self.last[e] = i

    def finalize(self):
        nc = self.nc
        self.barrier()
        esem = {e: nc.alloc_semaphore("es_" + e) for e in self.ENGS}
        dsems = {q: [nc.alloc_semaphore("ds_%s%d" % (q, k)) for k in range(n)] for q, n in self.NDMA.items()}
        duse = {q: [0] * n for q, n in self.NDMA.items()}
        drr = {q: 0 for q in self.NDMA}
        for i in self.ins:
            for d in i.deps:
                if not d.dma:
                    d.sig = True
        cnt = {e: 0 for e in self.ENGS}
        for i in self.ins:
            if i.dma:
                q = i.eng
                k = drr[q]
                drr[q] = (k + 1) % self.NDMA[q]
                i.dsem = dsems[q][k]
                i.dprev = duse[q][k]
                duse[q][k] += 16
                i.dval = duse[q][k]
            elif i.sig:
                cnt[i.eng] += 1
                i.sigval = cnt[i.eng]
        seen = {e: {} for e in self.ENGS}
        nwait = 0
        self.trace = {e: [] for e in self.ENGS}
        for i in self.ins:
            e = i.eng
            eh = self.eng[e]
            need = {}
            for d in i.deps:
                if d.dma:
                    s, v = d.dsem, d.dval
                else:
                    s, v = esem[d.eng], d.sigval
                if need.get(s, 0) < v:
                    need[s] = v
            if i.dma and i.dprev > 0:
                if need.get(i.dsem, 0) < i.dprev:
                    need[i.dsem] = i.dprev
            sn = seen[e]
            wl = []
            for s, v in need.items():
                if sn.get(s, 0) < v:
                    eh.wait_ge(s, v)
                    sn[s] = v
                    nwait += 1
                    wl.append((id(s), v))
            ins = i.fn()
            if i.dma:
                ins.then_inc(i.dsem, 16)
                self.trace[e].append((wl, (id(i.dsem), 16)))
            elif i.sig:
                ins.then_inc(esem[e], 1)
                self.trace[e].append((wl, (id(esem[e]), 1)))
            else:
                self.trace[e].append((wl, None))
        self.simulate()
        return dict(n_ins=len(self.ins), n_wait=nwait, sig=dict(cnt))

    def simulate(self):
        sem = {}
        pc = {e: 0 for e in self.ENGS}
        tr = self.trace
        while True:
            prog = False
            done = True
            for e in self.ENGS:
                t = tr[e]
                while pc[e] < len(t):
                    wl, inc = t[pc[e]]
                    if any(sem.get(s, 0) < v for s, v in wl):
                        break
                    if inc is not None:
                        sem[inc[0]] = sem.get(inc[0], 0) + inc[1]
                    pc[e] += 1
                    prog = True
                if pc[e] < len(t):
                    done = False
            if done:
                return
            if not prog:
                raise RuntimeError("sync deadlock at %s" % {e: (pc[e], len(tr[e])) for e in self.ENGS})


BLKS = [(i * 512, 512, 0) for i in range(NL // 512)] + [(NL, NCX, 1)]
TB1 = 256
BLKS1 = [(i * TB1, TB1, 0) for i in range(NL // TB1)] + [(NL, NCX, 1)]
LAT_TILES = list(range(NL // 128))
CTX_TILES = [NL // 128 + i for i in range(NCX // 128)]

C_MLQ, C_MLK, C_MLV, C_MLO, C_MLG, C_DAQ, C_DAK, C_DAV, C_FN, C_GP = 0, 1024, 2048, 3072, 4096, 4112, 5136, 6160, 7184, 8208
F_MLQ, F_MLK, F_DAQ, F_DAK, F_FN = 0, 1024, 2048, 3072, 4096


def host_consts():
    c = {}
    c["ident_f"] = np.eye(128, dtype=np.float32)
    perm = np.array([(m // 64) * 64 + ((m % 64) + 32) % 64 for m in range(128)])
    pw = np.zeros((128, 128), np.float32)
    pw[perm, np.arange(128)] = 1.0
    c["pswap"] = pw
    n_freq = 16
    inv = (10000.0 ** (-np.arange(n_freq, dtype=np.float32) / n_freq)).astype(np.float32)
    rows = NL // 64
    r = np.repeat(np.arange(rows, dtype=np.float32), 64)
    col = np.tile(np.arange(64, dtype=np.float32), rows)
    ang = np.concatenate([r[:, None] * inv, col[:, None] * inv], axis=-1).astype(np.float32)
    cos = np.cos(ang).astype(np.float32).T
    sin = np.sin(ang).astype(np.float32).T
    ct = np.zeros((128, NL), np.float32)
    st = np.zeros((128, NL), np.float32)
    for p in range(128):
        d = p % 64
        f = d % 32
        ct[p] = cos[f]
        st[p] = -sin[f] if d < 32 else sin[f]
    c["rope_c"] = ct
    c["rope_s"] = st
    c.update(host_consts_fn())
    c.update(host_consts_ml())
    return c


class K:
    pass


def Dl(fn, *a, **kw):
    return lambda: fn(*a, **kw)


def build(debug=None, nlayers=DEPTH, stop_after=None, skip=(), br_input=False, uf_input=False, fn_stop=99):
    global FN_STOP
    FN_STOP = fn_stop
    nc = bass.Bass("TRN2", target_bir_lowering=False)
    P = Prog(nc)
    k = K()
    k.nc, k.P = nc, P
    k.da_heads = 8

    def ein(name, shape, dt=F32):
        return Tl(nc.dram_tensor(name, list(shape), dt, kind="ExternalInput"), name)

    dbg_kind = {}

    def scratch(name, shape, dt):
        kind = "ExternalOutput" if (debug and name in debug) else "Internal"
        return Tl(nc.dram_tensor(name, list(shape), dt, kind=kind), name)

    k.xT_in = ein("xT", [128, KC, NT])
    k.cT = ein("cT", [128, KC, 2])
    k.w_ada = ein("w_ada", [DEPTH, D, 6 * D])
    k.b_adaT = ein("b_adaT", [DEPTH, 128, 48])
    k.norm_gT = ein("norm_gT", [DEPTH, 128, 2, KC])
    k.w_in = ein("w_in", [DEPTH, D, D_IN])
    k.ident_f_d = ein("ident_f", [128, 128])
    k.pswap_d = ein("pswap", [128, 128])
    k.rope_c_d = ein("rope_c", [128, NL])
    k.rope_s_d = ein("rope_s", [128, NL])
    k.da_lam = ein("da_lam", [DEPTH, 4, 64])
    k.da_head_gT = ein("da_head_gT", [DEPTH, 128, 8])
    k.w_br_ml = ein("w_br_ml", [DEPTH, D, D])
    k.w_br_da = ein("w_br_da", [DEPTH, D, D])
    k.w_br_fn = ein("w_br_fn", [DEPTH, D, D])
    k.w_out = ein("w_out", [DEPTH, D, D])
    k.w_ffn_in = ein("w_ffn_in", [DEPTH, D, 2 * D_FF])
    k.w_ffn_out = ein("w_ffn_out", [DEPTH, D_FF, D])
    k.final_gT = ein("final_gT", [128, KC])
    k.ml_triF = ein("ml_triF", [128, 128])
    k.ml_triB = ein("ml_triB", [128, 128])
    k.ml_maskF = ein("ml_maskF", [128, 128])
    k.ml_maskB = ein("ml_maskB", [128, 128])
    k.ml_gate_bR = ein("ml_gate_bR", [DEPTH, NTILE * 16])
    k.ml_head_g = ein("ml_head_g", [DEPTH, 1024])
    k.fn_cs = ein("fn_cs", [128, 2, 512], BF16)
    k.fn_w1 = ein("fn_w1", [128, 128], BF16)
    k.fn_twr = ein("fn_twr", [128, 256])
    k.fn_twi = ein("fn_twi", [128, 256])
    k.fn_w3c = ein("fn_w3c", [128, 128], BF16)
    k.fn_w3s = ein("fn_w3s", [128, 128], BF16)
    k.fn_csx = ein("fn_csx", [128, 2, 2, 256], BF16)

    k.hl_d = scratch("hl_d", [128, KC, NT], BF16)
    k.uF_d = ein("uF_d", [5120, NT], BF16) if uf_input else scratch("uF_d", [5120, NT], BF16)
    k.uT_mlk = scratch("uT_mlk", [NT, 1024], BF16)
    k.uT_mlv = scratch("uT_mlv", [NT, 1024], BF16)
    k.uT_mlo = scratch("uT_mlo", [NT, 1024], BF16)
    k.uT_g = scratch("uT_g", [NT, 16], F32)
    k.uT_dav = scratch("uT_dav", [NT, 1024], BF16)
    k.G_d = scratch("G_d", [3072, NT], BF16)
    k.mods_d = scratch("mods_d", [128, DEPTH * 96], F32)
    k.br_d = ein("br_d", [3, 1024, NT], BF16) if br_input else scratch("br_d", [3, 1024, NT], BF16)
    k.x1_d = scratch("x1_d", [128, KC, NT], F32)
    k.AB_d = scratch("AB_d", [2, NL, 512], BF16)
    k.h_d = scratch("h_d", [2, NT, 1024], F32)
    k.h2_d = scratch("h2_d", [128, KC, NT], BF16)
    k.gT_d = scratch("gT_d", [128, FC, NT], BF16)
    k.xT_d = scratch("xT_d", [128, KC, NT], F32)
    k.out = Tl(nc.dram_tensor("out", [NL, D], F32, kind="ExternalOutput"), "out")
    k.dbg = scratch("dbg", [6, 128, 512], F32) if (debug and "dbg" in debug) else None

    k.ident_f = P.sb("ident_f", [128, 128], F32)
    k.ident_bf = P.sb("ident_bf", [128, 128], BF16)
    k.ones_bf = P.sb("ones_bf", [128, 128], BF16)
    k.mods = P.sb("mods", [128, DEPTH * 96], F32)
    k.Gt = P.sb("Gt", [128, DEPTH * 2 * KC * 2], F32)
    k.psall = nc.alloc_psum_tensor("psall", [128, 8, 512], F32)
    k.ps = [PsBank(k.psall, i) for i in range(8)]

    P.dma("sp", k.ident_f[:], k.ident_f_d.ap()[:, :], w=[k.ident_f])
    P.dve(lambda: nc.vector.tensor_copy(k.ident_bf[:], k.ident_f[:]), r=[k.ident_f], w=[k.ident_bf])
    P.dve(lambda: nc.vector.memset(k.ones_bf[:], 1.0), w=[k.ones_bf])

    if "adaln" not in skip:
        phase_adaln(k)
    if debug and "mods_d" in debug:
        P.dma("sp", k.mods_d.ap()[:, :], k.mods[:], r=[k.mods], w=[k.mods_d])
    for l in range(nlayers):
        if l == 0 and "m1" not in skip:
            phase_norm(k, l, 0, k.xT_in, k.hl_d)
        if stop_after == "norm":
            break
        if "m1" not in skip:
            phase_m1(k, l)
        if stop_after == "m1":
            break
        if "da" not in skip:
            phase_da(k, l)
        if stop_after == "da":
            break
        if "fn" not in skip:
            phase_fn(k, l)
        if stop_after == "fn":
            break
        if "ml" not in skip:
            phase_ml(k, l)
        if stop_after == "ml":
            break
        phase_t1(k, l, k.xT_in if l == 0 else k.xT_d)
        phase_t2a(k, l)
        phase_t2b(k, l, last=(l == nlayers - 1))
    info = P.finalize()
    return nc, info


def mods_view(k, l, chunk0, sel):
    base = l * 96 + chunk0 * 2 + sel
    return k.mods[:, base:base + 15:2]


def mod_col(k, l, chunk, sel):
    base = l * 96 + chunk * 2 + sel
    return k.mods[:, base:base + 1]


def g_col(k, l, which, kc, sel):
    base = ((l * 2 + which) * KC + kc) * 2 + sel
    return k.Gt[:, base:base + 1]


def phase_adaln(k):
    nc, P = k.nc, k.P
    P.phase_begin()
    s_c = P.sb("s_c", [128, KC * 2], F32)
    P.dma("sp", s_c[:], k.cT.ap().rearrange("p k s -> p (k s)"), w=[s_c])
    P.act(lambda: nc.scalar.activation(out=s_c[:], in_=s_c[:], func=AF.Silu), r=[s_c], w=[s_c])
    wst = [P.sb("wa%d" % i, [128, KC, 512], F32) for i in range(2)]
    bad = P.sb("bad", [128, DEPTH * 48], F32)
    ng = P.sb("ng", [128, DEPTH * 2 * KC], F32)
    P.dma("sp", bad[:].rearrange("p (l c) -> p l c", l=DEPTH), k.b_adaT.ap().rearrange("l p c -> p l c"), w=[bad])
    P.dma("sp", ng[:].rearrange("p (l c) -> p l c", l=DEPTH), k.norm_gT.ap().rearrange("l p w c -> p l (w c)"), w=[ng])
    ps0 = k.ps[0]
    gi = 0
    for l in range(DEPTH):
        wv = k.w_ada.ap()[l].rearrange("(k p) n -> p k n", p=128)
        for g in range(12):
            w_ = wst[gi % 2]
            gi += 1
            P.dma("sp", w_[:], wv[:, :, g * 512:(g + 1) * 512], w=[w_])
            for j in range(4):
                col = g * 4 + j
                for kc in range(KC):
                    P.pe(lambda w_=w_, j=j, kc=kc, col=col: nc.tensor.matmul(
                        ps0[:, 2 * col:2 * col + 2], w_[:, kc, j * 128:(j + 1) * 128], s_c[:, 2 * kc:2 * kc + 2],
                        start=(kc == 0), stop=(kc == KC - 1)), r=[w_, s_c], w=[ps0])
        for s in range(2):
            P.dve(lambda l=l, s=s: nc.vector.tensor_tensor(
                out=k.mods[:, l * 96 + s:(l + 1) * 96:2], in0=ps0[:, s:96:2], in1=bad[:, l * 48:(l + 1) * 48], op=ALU.add),
                r=[ps0, bad], w=[k.mods])
        for which in range(2):
            for s in range(2):
                sc = mods_view(k, l, 8 + 24 * which, s)
                gb = ((l * 2 + which) * KC) * 2 + s
                P.dve(lambda sc=sc, gb=gb, l=l, which=which: nc.vector.scalar_tensor_tensor(
                    out=k.Gt[:, gb:gb + 15:2], in0=sc, scalar=1.0, in1=ng[:, (l * 2 + which) * KC:(l * 2 + which + 1) * KC],
                    op0=ALU.add, op1=ALU.mult), r=[k.mods, ng], w=[k.Gt])
    P.phase_end()


def phase_norm(k, l, which, xsrc, hdst):
    nc, P = k.nc, k.P
    P.phase_begin()
    xb = [P.sb("xb%d" % i, [128, KC, 512], F32) for i in range(2)]
    sq = P.sb("sq", [128, KC, 512], BF16)
    rs = P.sb("rs", [128, 512], F32)
    tmp = [P.sb("tmp%d" % i, [128, 512], F32) for i in range(2)]
    hb = [P.sb("hb%d" % i, [128, KC, 512], BF16) for i in range(2)]
    eps = P.sb("eps", [128, 1], F32)
    P.dve(lambda: nc.vector.memset(eps[:], EPS), w=[eps])
    for bi, (t0, n, sel) in enumerate(BLKS):
        x_, h_ = xb[bi % 2], hb[bi % 2]
        P.dma("sp", x_[:, :, :n], xsrc.ap()[:, :, t0:t0 + n], w=[x_])
        norm_block(k, l, which, sel, x_, n, sq, rs, tmp, h_, eps, k.ps[1])
        P.dma("sp", hdst.ap()[:, :, t0:t0 + n], h_[:, :, :n], r=[h_], w=[hdst])
    P.phase_end()


def norm_block(k, l, which, sel, x_, n, sq, rs, tmp, h_, eps, ps):
    nc, P = k.nc, k.P
    P.pool(lambda: nc.gpsimd.tensor_tensor(out=sq[:, :, :n], in0=x_[:, :, :n], in1=x_[:, :, :n], op=ALU.mult), r=[x_], w=[sq])
    for kc in range(KC):
        P.pe(lambda kc=kc: nc.tensor.matmul(ps[:, :n], k.ones_bf[:], sq[:, kc, :n], start=(kc == 0), stop=(kc == KC - 1)),
             r=[k.ones_bf, sq], w=[ps])
    P.act(lambda: nc.scalar.activation(out=rs[:, :n], in_=ps[:, :n], func=AF.Sqrt, scale=1.0 / D, bias=eps[:, 0:1]), r=[ps, eps], w=[rs])
    P.dve(lambda: nc.vector.reciprocal(out=rs[:, :n], in_=rs[:, :n]), r=[rs], w=[rs])
    for kc in range(KC):
        t_ = tmp[kc % 2]
        P.dve(lambda kc=kc, t_=t_: nc.vector.tensor_tensor(out=t_[:, :n], in0=x_[:, kc, :n], in1=rs[:, :n], op=ALU.mult), r=[x_, rs], w=[t_])
        P.act(lambda kc=kc, t_=t_: nc.scalar.activation(out=h_[:, kc, :n], in_=t_[:, :n], func=AF.Identity,
                                                       scale=g_col(k, l, which, kc, sel), bias=mod_col(k, l, 24 * which + kc, sel)),
              r=[t_, k.Gt, k.mods], w=[h_])


def load_w_group(k, wv, c0, ncol, wst, wbf):
    nc, P = k.nc, k.P
    P.dma("sp", wst[:, :, :ncol], wv[:, :, c0:c0 + ncol], w=[wst])
    P.pool(lambda: nc.gpsimd.tensor_copy(wbf[:, :, :ncol], wst[:, :, :ncol]), r=[wst], w=[wbf])


def phase_m1(k, l):
    nc, P = k.nc, k.P
    P.phase_begin()
    hl = P.sb("hl", [128, KC, NT], BF16)
    for kc in range(KC):
        P.dma("sp", hl[:, kc, :], k.hl_d.ap()[:, kc, :], w=[hl.b(kc)])
    hlr = hl.allb()
    wst = [P.sb("wst%d" % i, [128, KC, 512], F32) for i in range(2)]
    wbf = [P.sb("wbf%d" % i, [128, KC, 512], BF16) for i in range(2)]
    fst = [P.sb("fst%d" % i, [128, NT], BF16) for i in range(2)]
    tst = [P.sb("tst%d" % i, [128, 512], BF16) for i in range(3)]
    tstf = [P.sb("tstf%d" % i, [128, 16], F32) for i in range(2)]
    ropc = P.sb("ropc", [128, NL], F32)
    rops = P.sb("rops", [128, NL], F32)
    pswap = P.sb("pswap", [128, 128], F32)
    q32 = [P.sb("q32_%d" % i, [128, 512], F32) for i in range(2)]
    t1 = [P.sb("t1_%d" % i, [128, 512], F32) for i in range(2)]
    t2 = [P.sb("t2_%d" % i, [128, 512], F32) for i in range(2)]
    P.dma("act", ropc[:], k.rope_c_d.ap()[:, :], w=[ropc])
    P.dma("act", rops[:], k.rope_s_d.ap()[:, :], w=[rops])
    P.dma("act", pswap[:], k.pswap_d.ap()[:, :], w=[pswap])
    wv = k.w_in.ap()[l].rearrange("(k p) n -> p k n", p=128)
    st = dict(g=0, ps=0, f=0, t=0, ev=0, r=0)

    def nextps():
        st["ps"] = (st["ps"] + 1) % 6
        return k.ps[2 + st["ps"]]

    def feat_group(c0, dst, row0, kind, tok_blks=BLKS):
        w_, wb_ = wst[st["g"] % 2], wbf[st["g"] % 2]
        st["g"] += 1
        load_w_group(k, wv, c0, 512, w_, wb_)
        for j in range(4):
            stg = fst[st["f"] % 2]
            st["f"] += 1
            for bi, (t0, n, sel) in enumerate(tok_blks):
                ps = nextps()
                sb_ = stg.b(bi)
                for kc in range(KC):
                    P.pe(lambda ps=ps, wb_=wb_, j=j, kc=kc, t0=t0, n=n: nc.tensor.matmul(
                        ps[:, :n], wb_[:, kc, j * 128:(j + 1) * 128], hl[:, kc, t0:t0 + n], start=(kc == 0), stop=(kc == KC - 1)),
                        r=[wb_, hlr], w=[ps])
                o = stg[:, t0:t0 + n]
                if kind == "rope" and sel == 0:
                    q_, a_, b_ = q32[st["r"] % 2], t1[st["r"] % 2], t2[st["r"] % 2]
                    st["r"] += 1
                    psr = k.ps[0 + st["r"] % 2]
                    P.act(lambda ps=ps, q_=q_, n=n: nc.scalar.copy(q_[:, :n], ps[:, :n]), r=[ps], w=[q_])
                    P.pe(lambda psr=psr, q_=q_, n=n: nc.tensor.matmul(psr[:, :n], pswap[:], q_[:, :n], start=True, stop=True),
                         r=[pswap, q_], w=[psr])
                    P.pool(lambda a_=a_, q_=q_, t0=t0, n=n: nc.gpsimd.tensor_tensor(out=a_[:, :n], in0=q_[:, :n], in1=ropc[:, t0:t0 + n], op=ALU.mult),
                           r=[q_, ropc], w=[a_])
                    P.dve(lambda b_=b_, psr=psr, t0=t0, n=n: nc.vector.tensor_tensor(out=b_[:, :n], in0=psr[:, :n], in1=rops[:, t0:t0 + n], op=ALU.mult),
                          r=[psr, rops], w=[b_])
                    P.dve(lambda o=o, a_=a_, b_=b_, n=n: nc.vector.tensor_tensor(out=o, in0=a_[:, :n], in1=b_[:, :n], op=ALU.add),
                          r=[a_, b_], w=[sb_])
                elif kind == "sigm":
                    P.act(lambda o=o, ps=ps, n=n: nc.scalar.activation(out=o, in_=ps[:, :n], func=AF.Sigmoid), r=[ps], w=[sb_])
                elif kind == "s16":
                    P.act(lambda o=o, ps=ps, n=n: nc.scalar.mul(o, ps[:, :n], 1.0 / 16.0), r=[ps], w=[sb_])
                else:
                    st["ev"] += 1
                    if st["ev"] % 2:
                        P.act(lambda o=o, ps=ps, n=n: nc.scalar.copy(o, ps[:, :n]), r=[ps], w=[sb_])
                    else:
                        P.dve(lambda o=o, ps=ps, n=n: nc.vector.tensor_copy(o, ps[:, :n]), r=[ps], w=[sb_])
            ta, tb = tok_blks[0][0], tok_blks[-1][0] + tok_blks[-1][1]
            P.dma("sp", dst.ap()[row0 + j * 128:row0 + (j + 1) * 128, ta:tb], stg[:, ta:tb], r=stg.allb(), w=[dst])

    def tok_group(c0, ncol, dst, dcol0, kind):
        w_, wb_ = wst[st["g"] % 2], wbf[st["g"] % 2]
        st["g"] += 1
        load_w_group(k, wv, c0, ncol, w_, wb_)
        for ti in range(NTILE):
            ps = nextps()
            for kc in range(KC):
                P.pe(lambda ps=ps, wb_=wb_, kc=kc, ti=ti: nc.tensor.matmul(
                    ps[:, :ncol], hl[:, kc, ti * 128:(ti + 1) * 128], wb_[:, kc, :ncol], start=(kc == 0), stop=(kc == KC - 1)),
                    r=[wb_, hlr], w=[ps])
            if kind == "f32":
                stg = tstf[st["t"] % 2]
            else:
                stg = tst[st["t"] % 3]
            st["t"] += 1
            o = stg[:, :ncol]
            if kind == "sigm":
                P.act(lambda o=o, ps=ps: nc.scalar.activation(out=o, in_=ps[:, :ncol], func=AF.Sigmoid), r=[ps], w=[stg])
            elif kind == "s16":
                P.act(lambda o=o, ps=ps: nc.scalar.mul(o, ps[:, :ncol], 1.0 / 16.0), r=[ps], w=[stg])
            else:
                st["ev"] += 1
                if st["ev"] % 2:
                    P.act(lambda o=o, ps=ps: nc.scalar.copy(o, ps[:, :ncol]), r=[ps], w=[stg])
                else:
                    P.dve(lambda o=o, ps=ps: nc.vector.tensor_copy(o, ps[:, :ncol]), r=[ps], w=[stg])
            P.dma("sp", dst.ap()[ti * 128:(ti + 1) * 128, dcol0:dcol0 + ncol], o, r=[stg], w=[dst])

    for g in range(2):
        feat_group(C_MLQ + g * 512, k.uF_d, F_MLQ + g * 512, "copy")
        feat_group(C_MLK + g * 512, k.uF_d, F_MLK + g * 512, "s16")
        feat_group(C_DAQ + g * 512, k.uF_d, F_DAQ + g * 512, "rope")
        feat_group(C_DAK + g * 512, k.uF_d, F_DAK + g * 512, "rope")
        feat_group(C_FN + g * 512, k.uF_d, F_FN + g * 512, "copy")
        tok_group(C_MLK + g * 512, 512, k.uT_mlk, g * 512, "s16")
        tok_group(C_MLV + g * 512, 512, k.uT_mlv, g * 512, "copy")
        tok_group(C_MLO + g * 512, 512, k.uT_mlo, g * 512, "sigm")
        tok_group(C_DAV + g * 512, 512, k.uT_dav, g * 512, "copy")
    tok_group(C_MLG, 16, k.uT_g, 0, "f32")
    for g in range(6):
        feat_group(C_GP + g * 512, k.G_d, g * 512, "sigm")
    P.phase_end()


def fm(a):
    return np.ascontiguousarray(a.T.reshape(KC, 128, a.shape[0]).transpose(1, 0, 2))


def prep_core_inputs(inp, b, consts):
    m = {}
    xall = np.concatenate([inp["x"][b], inp["ctx"][b]], axis=0)
    m["xT"] = fm(xall).astype(np.float32)
    cc = np.stack([inp["c"][b], inp["c_ctx"]], axis=0)
    m["cT"] = np.ascontiguousarray(cc.T.reshape(KC, 128, 2).transpose(1, 0, 2)).astype(np.float32)
    m["w_ada"] = inp["w_ada"]
    m["b_adaT"] = np.ascontiguousarray(inp["b_ada"].reshape(DEPTH, 48, 128).transpose(0, 2, 1))
    m["norm_gT"] = np.ascontiguousarray(inp["norm_g"].reshape(DEPTH, 2, KC, 128).transpose(0, 3, 1, 2))
    m["w_in"] = inp["w_in"]
    m["da_lam"] = inp["da_lam"]
    m["ml_gate_bR"] = np.ascontiguousarray(np.tile(inp["ml_gate_b"][:, None, :], (1, NTILE, 1)).reshape(DEPTH, NTILE * 16))
    m["ml_head_g"] = inp["ml_head_g"]
    for nm in ("w_br_ml", "w_br_da", "w_br_fn", "w_out", "w_ffn_in", "w_ffn_out"):
        m[nm] = inp[nm]
    m["final_gT"] = np.ascontiguousarray(inp["final_g"].reshape(KC, 128).T)
    m["da_head_gT"] = np.ascontiguousarray(inp["da_head_g"].reshape(DEPTH, 8, 128).transpose(0, 2, 1))
    m.update(consts)
    return m


def lam_init_of(l):
    return 0.8 - 0.6 * math.exp(-0.3 * l)


def phase_da(k, l):
    nc, P = k.nc, k.P
    V_, A_, T_, G_ = nc.vector, nc.scalar, nc.tensor, nc.gpsimd
    P.phase_begin()
    li = lam_init_of(l)
    lq = P.sb("lq", [128, 256], F32)
    P.dma("sp", lq[:], k.da_lam.ap()[l:l + 1].rearrange("o a d -> o (a d)").partition_broadcast(128), w=[lq])
    pr = P.sb("pr", [128, 128], F32)
    sc = P.sb("sc", [128, 8], F32)
    eps = P.sb("eps", [128, 1], F32)
    P.dve(Dl(V_.memset, eps[:], EPS), w=[eps])
    P.dve(Dl(V_.tensor_tensor, out=pr[:, 0:64], in0=lq[:, 0:64], in1=lq[:, 64:128], op=ALU.mult), r=[lq], w=[pr])
    P.dve(Dl(V_.tensor_tensor, out=pr[:, 64:128], in0=lq[:, 128:192], in1=lq[:, 192:256], op=ALU.mult), r=[lq], w=[pr])
    P.dve(Dl(V_.reduce_sum, out=sc[:, 0:1], in_=pr[:, 0:64], axis=AX.X), r=[pr], w=[sc])
    P.dve(Dl(V_.reduce_sum, out=sc[:, 1:2], in_=pr[:, 64:128], axis=AX.X), r=[pr], w=[sc])
    P.act(Dl(A_.activation, out=sc[:, 2:4], in_=sc[:, 0:2], func=AF.Exp), r=[sc], w=[sc])
    P.dve(Dl(V_.scalar_tensor_tensor, out=sc[:, 4:5], in0=sc[:, 3:4], scalar=-li, in1=sc[:, 2:3], op0=ALU.add, op1=ALU.subtract),
          r=[sc], w=[sc])
    neglam = sc[:, 4:5]
    hg = P.sb("hg", [128, 8], F32)
    P.dma("sp", hg[:], k.da_head_gT.ap()[l], w=[hg])
    P.dve(Dl(V_.tensor_scalar, out=hg[:], in0=hg[:], scalar1=(1.0 - li), scalar2=None, op0=ALU.mult), r=[hg], w=[hg])

    qz = [[P.sb("qz%d%d" % (i, m), [128, NT], BF16) for m in range(2)] for i in range(2)]
    for i in range(2):
        for m in range(2):
            P.pool(Dl(G_.memset, qz[i][m][:], 0.0), w=[qz[i][m]])
    kT = [P.sb("kT%d" % i, [128, NT], BF16) for i in range(2)]
    V = [P.sb("V%d" % i, [128, NTILE, 128], BF16) for i in range(2)]
    pt = [P.sb("pt%d" % i, [128, 2, 512], BF16) for i in range(3)]
    ystg = [P.sb("ystg%d" % i, [128, NT], BF16) for i in range(2)]
    rd = [P.sb("rd%d" % i, [128, 512], F32) for i in range(2)]
    o_ = [P.sb("o%d" % i, [128, 512], F32) for i in range(2)]
    acc = [P.sb("acc%d" % i, [128, 2, 512], F32) for i in range(2)]
    ones_f = P.sb("ones_fd", [128, 128], F32)
    P.dve(Dl(V_.memset, ones_f[:], 1.0), w=[ones_f])
    ofs = [P.sb("of%d" % i, [128, 512], F32) for i in range(2)]
    sqs = [P.sb("sqd%d" % i, [128, 512], BF16) for i in range(2)]
    rs = P.sb("rsd", [128, 512], F32)
    deferred = []
    cO = [[P.sb("cO%d%d" % (i, m), [128, 512], F32) for m in range(2)] for i in range(2)]
    cD = [[P.sb("cD%d%d" % (i, m), [128, 512], F32) for m in range(2)] for i in range(2)]
    psO = [k.ps[4], k.ps[5]]
    psD = [k.ps[6], k.ps[7]]
    psE = k.ps[0]
    cnt = dict(s=0, p=0)

    for h in range(k.da_heads):
        qz_, k_, v_, y_ = qz[h % 2], kT[h % 2], V[h % 2], ystg[h % 2]
        for m in range(2):
            r0 = F_DAQ + h * 128 + 64 * m
            P.dma("sp", qz_[m][64 * m:64 * m + 64, :], k.uF_d.ap()[r0:r0 + 64, :], w=[qz_[m]])
        P.dma("sp", k_[:], k.uF_d.ap()[F_DAK + h * 128:F_DAK + (h + 1) * 128, :], w=[k_])
        P.dma("sp", v_[:], k.uT_dav.ap()[:, h * 128:(h + 1) * 128].rearrange("(n p) d -> p n d", p=128), w=[v_])
        for bi, (t0, n, sel) in enumerate(BLKS):
            ktiles = (LAT_TILES + CTX_TILES) if sel == 0 else CTX_TILES
            steps = [(m, ktiles[j], ktiles[j + 1], j) for m in range(2) for j in range(0, len(ktiles), 2)]
            nk = len(ktiles)

            def emit_s(i):
                m, ka, kb, j = steps[i]
                pi = cnt["s"] % 2
                cnt["s"] += 1
                banks = [k.ps[2 * pi], k.ps[2 * pi + 1]]
                for x, kt in enumerate((ka, kb)):
                    P.pe(Dl(T_.matmul, banks[x][:, :n], k_[:, kt * 128:(kt + 1) * 128], qz_[m][:, t0:t0 + n], start=True, stop=True),
                         r=[k_, qz_[m]], w=[banks[x]])
                p_ = pt[cnt["p"] % 3]
                cnt["p"] += 1
                P.act(Dl(A_.activation, out=p_[:, :, :n], in_=k.psall[:, 2 * pi:2 * pi + 2, :n], func=AF.Exp, scale=0.125), r=banks, w=[p_])
                return p_

            def emit_o(i, p_):
                m, ka, kb, j = steps[i]
                for x, kt in enumerate((ka, kb)):
                    jj = j + x
                    P.pe(Dl(T_.matmul, psO[m][:, :n], v_[:, kt, :], p_[:, x, :n], start=(jj == 0), stop=(jj == nk - 1)),
                         r=[v_, p_], w=[psO[m]])
                    P.pe(Dl(T_.matmul, psD[m][:, :n], k.ones_bf[:], p_[:, x, :n], start=(jj == 0), stop=(jj == nk - 1)),
                         r=[k.ones_bf, p_], w=[psD[m]])

            pend = []
            for i in range(len(steps)):
                pend.append((i, emit_s(i)))
                if len(pend) > 1:
                    emit_o(*pend.pop(0))
                if i == 8 or i == len(steps) - 1:
                    while deferred:
                        deferred.pop(0)()
            while pend:
                emit_o(*pend.pop(0))
            of_, sq_ = ofs[bi % 2], sqs[bi % 2]
            cO_, cD_ = cO[bi % 2], cD[bi % 2]
            for m in range(2):
                P.act(Dl(A_.copy, cD_[m][:, :n], psD[m][:, :n]), r=[psD[m]], w=[cD_[m]])
                P.act(Dl(A_.copy, cO_[m][:, :n], psO[m][:, :n]), r=[psO[m]], w=[cO_[m]])
            for m in range(2):
                P.dve(Dl(V_.reciprocal, out=rd[m][:, :n], in_=cD_[m][:, :n]), r=[cD_[m]], w=[rd[m]])
                P.dve(Dl(V_.tensor_tensor, out=o_[m][:, :n], in0=cO_[m][:, :n], in1=rd[m][:, :n], op=ALU.mult),
                      r=[cO_[m], rd[m]], w=[o_[m]])
            P.dve(Dl(V_.scalar_tensor_tensor, out=of_[:, :n], in0=o_[1][:, :n], scalar=neglam, in1=o_[0][:, :n], op0=ALU.mult, op1=ALU.add),
                  r=[o_[0], o_[1], sc], w=[of_])
            P.pool(Dl(G_.tensor_tensor, out=sq_[:, :n], in0=of_[:, :n], in1=of_[:, :n], op=ALU.mult), r=[of_], w=[sq_])
            def part_c(n=n, t0=t0, bi=bi, h=h, y_=y_, sq_=sq_, of_=of_):
                P.pe(Dl(T_.matmul, psE[:, :n], k.ones_bf[:], sq_[:, :n], start=True, stop=True), r=[k.ones_bf, sq_], w=[psE])
                P.act(Dl(A_.activation, out=rs[:, :n], in_=psE[:, :n], func=AF.Sqrt, scale=1.0 / 128, bias=eps[:, 0:1]), r=[psE, eps], w=[rs])
                P.dve(Dl(V_.reciprocal, out=rs[:, :n], in_=rs[:, :n]), r=[rs], w=[rs])
                P.dve(Dl(V_.tensor_tensor, out=of_[:, :n], in0=of_[:, :n], in1=rs[:, :n], op=ALU.mult), r=[of_, rs], w=[of_])
                P.act(Dl(A_.activation, out=y_[:, t0:t0 + n], in_=of_[:, :n], func=AF.Copy, scale=hg[:, h:h + 1]), r=[of_, hg], w=[y_.b(bi)])
            deferred.append(part_c)
        while deferred:
            deferred.pop(0)()
        P.dma("sp", k.br_d.ap()[1, h * 128:(h + 1) * 128, :], y_[:], r=y_.allb(), w=[k.br_d.b(("da", h))])
    P.phase_end()


def load_w_full(k, dst_bf, wv, ncols, kcn, wst, q="sp"):
    nc, P = k.nc, k.P
    for gi, c0 in enumerate(range(0, ncols, 512)):
        w_ = wst[gi % 2]
        P.dma(q, w_[:, :kcn, :], wv[:, :, c0:c0 + 512], w=[w_])
        P.pool(Dl(nc.gpsimd.tensor_copy, dst_bf[:, :, c0:c0 + 512], w_[:, :kcn, :]), r=[w_], w=[dst_bf])


def phase_t1(k, l, xsrc):
    nc, P = k.nc, k.P
    V_, A_, T_, G_ = nc.vector, nc.scalar, nc.tensor, nc.gpsimd
    P.phase_begin()
    wst = [P.sb("wst%d" % i, [128, KC, 512], F32) for i in range(2)]
    wbr = [P.sb("wbr%d" % i, [128, KC, 1024], BF16) for i in range(3)]
    wo = P.sb("wo", [128, KC, 1024], BF16)
    for x, wt in enumerate((k.w_br_ml, k.w_br_da, k.w_br_fn)):
        load_w_full(k, wbr[x], wt.ap()[l].rearrange("(k p) n -> p k n", p=128), 1024, KC, wst)
    load_w_full(k, wo, k.w_out.ap()[l].rearrange("(k p) n -> p k n", p=128), 1024, KC, wst)
    brb = [P.sb("brb%d" % i, [128, 24, TB1], BF16) for i in range(2)]
    gb = [P.sb("gb%d" % i, [128, 24, TB1], BF16) for i in range(2)]
    xb = [P.sb("xb%d" % i, [128, KC, TB1], F32) for i in range(2)]
    yb = P.sb("yb", [128, KC, TB1], BF16)
    ta = [P.sb("ta%d" % i, [128, TB1], F32) for i in range(2)]
    tb = [P.sb("tb%d" % i, [128, TB1], F32) for i in range(2)]
    tc = [P.sb("tc%d" % i, [128, TB1], F32) for i in range(2)]
    sq = P.sb("sq", [128, KC, TB1], BF16)
    rs = P.sb("rs", [128, TB1], F32)
    tmp = [P.sb("tmp%d" % i, [128, TB1], F32) for i in range(2)]
    hb = [P.sb("hb%d" % i, [128, KC, TB1], BF16) for i in range(2)]
    eps = P.sb("eps", [128, 1], F32)
    P.dve(Dl(V_.memset, eps[:], EPS), w=[eps])
    gv = k.G_d.ap().rearrange("(c p) t -> p c t", p=128)
    it = 0
    for bi, (t0, n, sel) in enumerate(BLKS1):
        b_, g_, x_, h_ = brb[bi % 2], gb[bi % 2], xb[bi % 2], hb[bi % 2]
        for x in range(3):
            P.dma("sp", b_[:, x * 8:(x + 1) * 8, :n], k.br_d.ap()[x].rearrange("(c p) t -> p c t", p=128)[:, :, t0:t0 + n], w=[b_.b(x)])
        P.dma("act", g_[:, :, :n], gv[:, :, t0:t0 + n], w=[g_])
        P.dma("act", x_[:, :, :n], xsrc.ap()[:, :, t0:t0 + n], w=[x_])
        for oc in range(KC):
            pss = [k.ps[(it % 2) * 3 + x] for x in range(3)]
            a_, b2_, c_ = ta[it % 2], tb[it % 2], tc[it % 2]
            it += 1
            for x in range(3):
                for kc in range(KC):
                    P.pe(Dl(T_.matmul, pss[x][:, :n], wbr[x][:, kc, oc * 128:(oc + 1) * 128], b_[:, x * 8 + kc, :n],
                            start=(kc == 0), stop=(kc == KC - 1)), r=[wbr[x], b_.b(x)], w=[pss[x]])
            P.dve(Dl(V_.tensor_tensor, out=a_[:, :n], in0=pss[0][:, :n], in1=g_[:, oc, :n], op=ALU.mult), r=[pss[0], g_], w=[a_])
            P.dve(Dl(V_.tensor_tensor, out=b2_[:, :n], in0=pss[1][:, :n], in1=g_[:, 8 + oc, :n], op=ALU.mult), r=[pss[1], g_], w=[b2_])
            P.dve(Dl(V_.tensor_tensor, out=c_[:, :n], in0=pss[2][:, :n], in1=g_[:, 16 + oc, :n], op=ALU.mult), r=[pss[2], g_], w=[c_])
            P.pool(Dl(G_.tensor_tensor, out=a_[:, :n], in0=a_[:, :n], in1=b2_[:, :n], op=ALU.add), r=[a_, b2_], w=[a_])
            P.pool(Dl(G_.tensor_tensor, out=yb[:, oc, :n], in0=a_[:, :n], in1=c_[:, :n], op=ALU.add), r=[a_, c_], w=[yb.b(oc)])
        for oc in range(KC):
            ps = k.ps[6]
            for kc in range(KC):
                P.pe(Dl(T_.matmul, ps[:, :n], wo[:, kc, oc * 128:(oc + 1) * 128], yb[:, kc, :n], start=(kc == 0), stop=(kc == KC - 1)),
                     r=[wo, yb.allb()], w=[ps])
            P.dve(Dl(V_.scalar_tensor_tensor, out=x_[:, oc, :n], in0=ps[:, :n], scalar=mod_col(k, l, 16 + oc, sel), in1=x_[:, oc, :n],
                     op0=ALU.mult, op1=ALU.add), r=[ps, x_, k.mods], w=[x_])
        P.dma("sp", k.x1_d.ap()[:, :, t0:t0 + n], x_[:, :, :n], r=[x_], w=[k.x1_d])
        norm_block(k, l, 1, sel, x_, n, sq, rs, tmp, h_, eps, k.ps[7])
        P.dma("sp", k.h2_d.ap()[:, :, t0:t0 + n], h_[:, :, :n], r=[h_], w=[k.h2_d])
    P.phase_end()


def phase_t2a(k, l):
    nc, P = k.nc, k.P
    V_, A_, T_, G_ = nc.vector, nc.scalar, nc.tensor, nc.gpsimd
    P.phase_begin()
    wst = [P.sb("wst%d" % i, [128, KC, 512], F32) for i in range(2)]
    wf = P.sb("wf", [128, KC, 2 * D_FF], BF16)
    load_w_full(k, wf, k.w_ffn_in.ap()[l].rearrange("(k p) n -> p k n", p=128), 2 * D_FF, KC, wst)
    hb = [P.sb("hb%d" % i, [128, KC, 512], BF16) for i in range(2)]
    gblk = [P.sb("gblk%d" % i, [128, FC, 512], BF16) for i in range(2)]
    sa = [P.sb("sa%d" % i, [128, 512], F32) for i in range(2)]
    it = 0
    for bi, (t0, n, sel) in enumerate(BLKS):
        h_, g_ = hb[bi % 2], gblk[bi % 2]
        P.dma("sp", h_[:, :, :n], k.h2_d.ap()[:, :, t0:t0 + n], w=[h_])
        for j in range(FC):
            pa, pb = k.ps[(it % 4) * 2], k.ps[(it % 4) * 2 + 1]
            s_ = sa[it % 2]
            it += 1
            for kc in range(KC):
                P.pe(Dl(T_.matmul, pa[:, :n], wf[:, kc, j * 128:(j + 1) * 128], h_[:, kc, :n], start=(kc == 0), stop=(kc == KC - 1)),
                     r=[wf, h_], w=[pa])
            for kc in range(KC):
                P.pe(Dl(T_.matmul, pb[:, :n], wf[:, kc, D_FF + j * 128:D_FF + (j + 1) * 128], h_[:, kc, :n], start=(kc == 0), stop=(kc == KC - 1)),
                     r=[wf, h_], w=[pb])
            P.act(Dl(A_.activation, out=s_[:, :n], in_=pa[:, :n], func=AF.Silu), r=[pa], w=[s_])
            P.dve(Dl(V_.tensor_tensor, out=g_[:, j, :n], in0=pb[:, :n], in1=s_[:, :n], op=ALU.mult), r=[pb, s_], w=[g_])
        P.dma("sp", k.gT_d.ap()[:, :, t0:t0 + n], g_[:, :, :n], r=[g_], w=[k.gT_d])
    P.phase_end()


def phase_t2b(k, l, last):
    nc, P = k.nc, k.P
    V_, A_, T_, G_ = nc.vector, nc.scalar, nc.tensor, nc.gpsimd
    P.phase_begin()
    wst = [P.sb("wst%d" % i, [128, 11, 512], F32) for i in range(2)]
    wo = P.sb("wo", [128, FC, 1024], BF16)
    wv = k.w_ffn_out.ap()[l].rearrange("(k p) n -> p k n", p=128)
    gi = 0
    for c0 in (0, 512):
        for k0 in (0, 11):
            w_ = wst[gi % 2]
            gi += 1
            P.dma("sp", w_[:], wv[:, k0:k0 + 11, c0:c0 + 512], w=[w_])
            P.pool(Dl(G_.tensor_copy, wo[:, k0:k0 + 11, c0:c0 + 512], w_[:]), r=[w_], w=[wo])
    gblk = [P.sb("gblk%d" % i, [128, FC, 512], BF16) for i in range(2)]
    xb = [P.sb("xb%d" % i, [128, KC, 512], F32) for i in range(2)]
    sq = P.sb("sq", [128, KC, 512], BF16)
    rs = P.sb("rs", [128, 512], F32)
    tmp = [P.sb("tmp%d" % i, [128, 512], F32) for i in range(2)]
    hb = [P.sb("hb%d" % i, [128, KC, 512], BF16) for i in range(2)] if not last else [None, None]
    eps = P.sb("eps", [128, 1], F32)
    P.dve(Dl(V_.memset, eps[:], EPS), w=[eps])
    if last:
        fg = P.sb("fg", [128, KC], F32)
        P.dma("sp", fg[:], k.final_gT.ap()[:, :], w=[fg])
        hf = P.sb("hf", [128, KC, 512], F32)
        ot = [P.sb("ot%d" % i, [128, 1024], F32) for i in range(2)]
    it = 0
    blks = BLKS[:-1] if last else BLKS
    for bi, (t0, n, sel) in enumerate(blks):
        g_, x_, h_ = gblk[bi % 2], xb[bi % 2], hb[bi % 2]
        P.dma("sp", g_[:, :, :n], k.gT_d.ap()[:, :, t0:t0 + n], w=[g_])
        P.dma("act", x_[:, :, :n], k.x1_d.ap()[:, :, t0:t0 + n], w=[x_])
        for oc in range(KC):
            ps = k.ps[it % 4]
            it += 1
            for j in range(FC):
                P.pe(Dl(T_.matmul, ps[:, :n], wo[:, j, oc * 128:(oc + 1) * 128], g_[:, j, :n], start=(j == 0), stop=(j == FC - 1)),
                     r=[wo, g_], w=[ps])
            P.dve(Dl(V_.scalar_tensor_tensor, out=x_[:, oc, :n], in0=ps[:, :n], scalar=mod_col(k, l, 40 + oc, sel), in1=x_[:, oc, :n],
                     op0=ALU.mult, op1=ALU.add), r=[ps, x_, k.mods], w=[x_])
        if not last:
            P.dma("sp", k.xT_d.ap()[:, :, t0:t0 + n], x_[:, :, :n], r=[x_], w=[k.xT_d])
            norm_block(k, l + 1, 0, sel, x_, n, sq, rs, tmp, h_, eps, k.ps[7])
            P.dma("sp", k.hl_d.ap()[:, :, t0:t0 + n], h_[:, :, :n], r=[h_], w=[k.hl_d])
        else:
            ps7 = k.ps[7]
            P.pool(Dl(G_.tensor_tensor, out=sq[:, :, :n], in0=x_[:, :, :n], in1=x_[:, :, :n], op=ALU.mult), r=[x_], w=[sq])
            for kc in range(KC):
                P.pe(Dl(T_.matmul, ps7[:, :n], k.ones_bf[:], sq[:, kc, :n], start=(kc == 0), stop=(kc == KC - 1)), r=[k.ones_bf, sq], w=[ps7])
            P.act(Dl(A_.activation, out=rs[:, :n], in_=ps7[:, :n], func=AF.Sqrt, scale=1.0 / D, bias=eps[:, 0:1]), r=[ps7, eps], w=[rs])
            P.dve(Dl(V_.reciprocal, out=rs[:, :n], in_=rs[:, :n]), r=[rs], w=[rs])
            for kc in range(KC):
                P.dve(Dl(V_.scalar_tensor_tensor, out=hf[:, kc, :n], in0=x_[:, kc, :n], scalar=fg[:, kc:kc + 1], in1=rs[:, :n],
                         op0=ALU.mult, op1=ALU.mult), r=[x_, fg, rs], w=[hf])
            for tt in range(n // 128):
                o_ = ot[tt % 2]
                for half in range(2):
                    pt_ = k.ps[4 + half]
                    for q4 in range(4):
                        kc = half * 4 + q4
                        P.pe(Dl(T_.transpose, pt_[:, q4 * 128:(q4 + 1) * 128], hf[:, kc, tt * 128:(tt + 1) * 128], k.ident_f[:]),
                             r=[hf, k.ident_f], w=[pt_])
                    if half == 0:
                        P.act(Dl(A_.copy, o_[:, 0:512], pt_[:, :]), r=[pt_], w=[o_])
                    else:
                        P.dve(Dl(V_.tensor_copy, o_[:, 512:1024], pt_[:, :]), r=[pt_], w=[o_])
                P.dma("sp", k.out.ap()[t0 + tt * 128:t0 + (tt + 1) * 128, :], o_[:], r=[o_], w=[k.out.b((t0, tt))])
    P.phase_end()


def host_consts_fn():
    import ml_dtypes
    bf = ml_dtypes.bfloat16
    c = {}
    cc = np.arange(256)
    ang = 2 * np.pi * np.outer(cc, cc) / 256.0
    cs = np.concatenate([np.cos(ang), np.sin(ang)], axis=1)
    c["fn_cs"] = np.ascontiguousarray(cs.reshape(2, 128, 512).transpose(1, 0, 2)).astype(bf)
    t = np.arange(64)
    a64 = 2 * np.pi * np.outer(t, t) / 64.0
    w1 = np.zeros((128, 128))
    w1[0:64, 0:64] = np.cos(a64)
    w1[0:64, 64:128] = -np.sin(a64)
    w1[64:128, 0:64] = -np.sin(a64)
    w1[64:128, 64:128] = -np.cos(a64)
    c["fn_w1"] = w1.astype(bf)
    t1 = np.repeat(np.arange(64), 2)
    atw = 2 * np.pi * np.outer(t1, np.arange(64)) / 4096.0
    c["fn_twr"] = np.ascontiguousarray(np.tile(np.cos(atw)[:, None, :], (1, 4, 1)).reshape(128, 256)).astype(np.float32)
    c["fn_twi"] = np.ascontiguousarray(np.tile(-np.sin(atw)[:, None, :], (1, 4, 1)).reshape(128, 256)).astype(np.float32)
    w3c = np.zeros((128, 128))
    w3s = np.zeros((128, 128))
    for gi in range(2):
        w3c[gi::2, gi * 64:(gi + 1) * 64] = np.cos(a64)
        w3s[gi::2, gi * 64:(gi + 1) * 64] = np.sin(a64)
    c["fn_w3c"] = w3c.astype(bf)
    c["fn_w3s"] = w3s.astype(bf)
    kk = np.arange(256)
    a256 = 2 * np.pi * np.outer(kk, kk) / 256.0
    csx = np.stack([np.cos(a256), -np.sin(a256)], axis=1)
    c["fn_csx"] = np.ascontiguousarray(csx.reshape(2, 128, 2, 256).transpose(1, 0, 2, 3)).astype(bf)
    return c


def phase_fn(k, l):
    nc, P = k.nc, k.P
    V_, A_, T_, G_ = nc.vector, nc.scalar, nc.tensor, nc.gpsimd
    P.phase_begin()
    cs = P.sb("cs", [128, 2, 512], BF16)
    w1 = P.sb("w1", [128, 128], BF16)
    twr = P.sb("twr", [128, 256], F32)
    twi = P.sb("twi", [128, 256], F32)
    w3c = P.sb("w3c", [128, 128], BF16)
    w3s = P.sb("w3s", [128, 128], BF16)
    csx = P.sb("csx", [128, 2, 2, 256], BF16)
    for t_, d_ in ((cs, k.fn_cs), (w1, k.fn_w1), (twr, k.fn_twr), (twi, k.fn_twi), (w3c, k.fn_w3c), (w3s, k.fn_w3s), (csx, k.fn_csx)):
        P.dma("act", t_[:], d_.ap(), w=[t_])
    zc = [P.sb("zc%d" % i, [128, 4, 1024], BF16) for i in range(2)]
    stg = [P.sb("stg%d" % i, [128, 2, 512], BF16) for i in range(3)]
    abc = [P.sb("abc%d" % i, [128, 2, 512], BF16) for i in range(2)]
    yf = [[P.sb("yf%d%d" % (gi, ch), [128, NT], BF16) for ch in range(2)] for gi in range(2)]
    D1 = P.sb("D1", [128, 64, 512], BF16)
    H = P.sb("H", [128, 2, 64, 256], BF16)
    wa = [P.sb("fwa%d" % i, [128, 256], F32) for i in range(2)]
    wb = [P.sb("fwb%d" % i, [128, 256], F32) for i in range(2)]
    wc = [P.sb("fwc%d" % i, [128, 256], F32) for i in range(2)]
    wd = [P.sb("fwd%d" % i, [128, 256], F32) for i in range(2)]
    ev = 0
    psi = 0
    for gp in range(2):
        chunks = [(i * 1024, 1024) for i in range(NL // 1024)] + [(NL, NCX)]
        si = 0
        for ci, (c0, cn) in enumerate(chunks):
            z_ = zc[ci % 2]
            for gi in range(2):
                for cc in range(2):
                    r0 = F_FN + (2 * gp + gi) * 256 + cc * 128
                    P.dma("sp", z_[:, gi * 2 + cc, :cn], k.uF_d.ap()[r0:r0 + 128, c0:c0 + cn], w=[z_.b(gi * 2 + cc)])
            for tl in range(cn // 128):
                tok0 = c0 + tl * 128
                isctx = tok0 >= NL
                s_ = abc[(tok0 - NL) // 128] if isctx else stg[si % 3]
                si += 1
                for gi in range(2):
                    ps = k.ps[psi % 4]
                    psi += 1
                    for cc in range(2):
                        P.pe(Dl(T_.matmul, ps[:, :], z_[:, gi * 2 + cc, tl * 128:(tl + 1) * 128], cs[:, cc, :], start=(cc == 0), stop=(cc == 1)),
                             r=[z_.b(gi * 2 + cc), cs], w=[ps])
                    ev += 1
                    if ev % 2:
                        P.act(Dl(A_.copy, s_[:, gi, :], ps[:, :]), r=[ps], w=[s_.b(gi)])
                    else:
                        P.dve(Dl(V_.tensor_copy, s_[:, gi, :], ps[:, :]), r=[ps], w=[s_.b(gi)])
                if not isctx:
                    for ab in range(2):
                        P.dma("sp", k.AB_d.ap()[ab, tok0:tok0 + 128, :].rearrange("t (g c) -> t g c", g=2), s_[:, :, ab * 256:(ab + 1) * 256],
                              r=s_.allb(), w=[k.AB_d.b((ab, tok0 // 2048))])
        if FN_STOP < 1:
            continue
        for ab in range(2):
            for hh in range(2):
                P.dma("sp", D1[ab * 64 + hh * 32:ab * 64 + hh * 32 + 32, :, :],
                      k.AB_d.ap()[ab, hh * 2048:(hh + 1) * 2048, :].rearrange("(t2 t1) c -> t2 t1 c", t1=64),
                      r=[k.AB_d.b((ab, hh))], w=[D1.b((ab, hh))])
        d1r = D1.allb()
        for cb in range(64):
            ps = k.ps[psi % 4]
            psi += 1
            for ci in range(4):
                cp = cb * 4 + ci
                P.pe(Dl(T_.matmul, ps[:, ci * 128:(ci + 1) * 128], D1[:, :, cp:512:256], w1[:, :], start=True, stop=True), r=[d1r, w1], w=[ps])
            psv = ps[:, :].rearrange("p (c r j) -> p c r j", c=4, r=2)
            gr, gim = psv[:, :, 0, :], psv[:, :, 1, :]
            a_, b_, c_, d_ = wa[cb % 2], wb[cb % 2], wc[cb % 2], wd[cb % 2]
            v4 = lambda t: t[:, :].rearrange("p (c j) -> p c j", c=4)
            P.dve(Dl(V_.tensor_tensor, out=v4(a_), in0=gr, in1=v4(twr), op=ALU.mult), r=[ps, twr], w=[a_])
            P.dve(Dl(V_.tensor_tensor, out=v4(b_), in0=gim, in1=v4(twi), op=ALU.mult), r=[ps, twi], w=[b_])
            P.dve(Dl(V_.tensor_tensor, out=v4(c_), in0=gr, in1=v4(twi), op=ALU.mult), r=[ps, twi], w=[c_])
            P.dve(Dl(V_.tensor_tensor, out=v4(d_), in0=gim, in1=v4(twr), op=ALU.mult), r=[ps, twr], w=[d_])
            P.pool(Dl(G_.tensor_tensor, out=H[:, 0, :, cb * 4:(cb + 1) * 4].rearrange("p j c -> p c j"), in0=v4(a_), in1=v4(b_), op=ALU.subtract), r=[a_, b_], w=[H.b(cb)])
            P.pool(Dl(G_.tensor_tensor, out=H[:, 1, :, cb * 4:(cb + 1) * 4].rearrange("p j c -> p c j"), in0=v4(c_), in1=v4(d_), op=ALU.add), r=[c_, d_], w=[H.b(cb)])
        if FN_STOP < 1.5:
            continue
        hr = H.allb()
        for ch in range(2):
            for jb in range(16):
                ps = k.ps[psi % 4]
                psi += 1
                for ji in range(4):
                    j2 = jb * 4 + ji
                    P.pe(Dl(T_.matmul, ps[:, ji * 128:(ji + 1) * 128], H[:, 0, j2, ch * 128:(ch + 1) * 128], w3c[:, :], start=True, stop=False),
                         r=[hr, w3c], w=[ps])
                    P.pe(Dl(T_.matmul, ps[:, ji * 128:(ji + 1) * 128], H[:, 1, j2, ch * 128:(ch + 1) * 128], w3s[:, :], start=False, stop=True),
                         r=[hr, w3s], w=[ps])
                for ji in range(4):
                    if FN_STOP == 1.5:
                        break
                    j2 = jb * 4 + ji
                    for gi in range(2):
                        o = yf[gi][ch][:, j2:NL:64]
                        i_ = ps[:, ji * 128 + gi * 64:ji * 128 + (gi + 1) * 64]
                        ev += 1
                        if (ev % 2 or FN_STOP == 2.1) and FN_STOP != 2.2:
                            P.act(Dl(A_.mul, o, i_, 1.0 / 1024.0), r=[ps], w=[yf[gi][ch].b(jb)])
                        else:
                            P.dve(Dl(V_.tensor_scalar, out=o, in0=i_, scalar1=1.0 / 1024.0, scalar2=None, op0=ALU.mult), r=[ps], w=[yf[gi][ch].b(jb)])
        if FN_STOP < 3:
            continue
        for gi in range(2):
            for ch in range(2):
                ps = k.ps[psi % 4]
                psi += 1
                idx = 0
                for kt in range(2):
                    for ab in range(2):
                        P.pe(Dl(T_.matmul, ps[:, 0:256], abc[kt][:, gi, ab * 256 + ch * 128:ab * 256 + (ch + 1) * 128], csx[:, kt, ab, :],
                                start=(idx == 0), stop=(idx == 3)), r=[abc[kt].allb(), csx], w=[ps])
                        idx += 1
                P.act(Dl(A_.mul, yf[gi][ch][:, NL:NT], ps[:, 0:256], 1.0 / 256.0), r=[ps], w=[yf[gi][ch].b("ctx")])
                r0 = (2 * gp + gi) * 256 + ch * 128
                P.dma("sp", k.br_d.ap()[2, r0:r0 + 128, :], yf[gi][ch][:], r=yf[gi][ch].allb(), w=[k.br_d.b(("fn", r0))])
    P.phase_end()


NEG = -30000.0


def host_consts_ml():
    c = {}
    r = np.arange(128)
    c["ml_triF"] = (r[:, None] <= r[None, :]).astype(np.float32)
    c["ml_triB"] = (r[:, None] >= r[None, :]).astype(np.float32)
    c["ml_maskF"] = np.where(r[None, :] <= r[:, None], 0.0, NEG).astype(np.float32)
    c["ml_maskB"] = np.where(r[None, :] >= r[:, None], 0.0, NEG).astype(np.float32)
    return c


def phase_ml(k, l):
    nc, P = k.nc, k.P
    V_, A_, T_, G_ = nc.vector, nc.scalar, nc.tensor, nc.gpsimd
    P.phase_begin()
    ones_f = P.sb("ones_f", [128, 128], F32)
    tri = [P.sb("triF", [128, 128], F32), P.sb("triB", [128, 128], F32)]
    msk = [P.sb("maskF", [128, 128], F32), P.sb("maskB", [128, 128], F32)]
    P.dve(Dl(V_.memset, ones_f[:], 1.0), w=[ones_f])
    for t_, d_ in ((tri[0], k.ml_triF), (tri[1], k.ml_triB), (msk[0], k.ml_maskF), (msk[1], k.ml_maskB)):
        P.dma("act", t_[:], d_.ap(), w=[t_])
    GL = P.sb("GL", [128, NTILE, 16], F32)
    gbias = P.sb("gbias", [128, NTILE, 16], F32)
    P.dma("sp", GL[:], k.uT_g.ap().rearrange("(n p) c -> p n c", p=128), w=[GL])
    P.dma("sp", gbias[:].rearrange("p n c -> p (n c)"), k.ml_gate_bR.ap()[l:l + 1, :].partition_broadcast(128), w=[gbias])
    P.dve(Dl(V_.tensor_tensor, out=GL[:], in0=GL[:], in1=gbias[:], op=ALU.add), r=[GL, gbias], w=[GL])
    gax = P.sb("gax", [128, NTILE, 2, 4], F32)
    gmn = P.sb("gmn", [128, NTILE, 2, 4], F32)
    GLf = GL[:].rearrange("p n (a c) -> p n a c", a=2)[:, :, :, 4:8]
    P.dve(Dl(V_.scalar_tensor_tensor, out=gax[:], in0=GLf, scalar=-1.0, in1=GLf, op0=ALU.mult, op1=ALU.max), r=[GL], w=[gax])
    P.act(Dl(A_.activation, out=gax[:], in_=gax[:], func=AF.Exp, scale=-1.0), r=[gax], w=[gax])
    P.act(Dl(A_.activation, out=gax[:], in_=gax[:], func=AF.Ln, bias=ones_f[:, 0:1], scale=1.0), r=[gax, ones_f], w=[gax])
    P.dve(Dl(V_.tensor_single_scalar, out=gmn[:], in_=GLf, scalar=0.0, op=ALU.min), r=[GL], w=[gmn])
    P.dve(Dl(V_.tensor_tensor, out=GLf, in0=gmn[:], in1=gax[:], op=ALU.subtract), r=[gmn, gax], w=[GL])

    NH = 2
    qT = [[P.sb("mq%d%d" % (i, c), [128, NT], BF16) for c in range(2)] for i in range(NH)]
    kT = [[P.sb("mk%d%d" % (i, c), [128, NT], BF16) for c in range(2)] for i in range(NH)]
    ktok = [P.sb("mkt%d" % i, [128, NTILE, 256], BF16) for i in range(NH)]
    vaug = [P.sb("mv%d" % i, [128, NTILE, 260], BF16) for i in range(NH)]
    C32 = [[P.sb("C32_%d%d" % (i, c), [128, 257], F32) for c in range(2)] for i in range(2 * NH)]
    Cb = [[P.sb("Cb_%d%d" % (i, c), [128, 260], BF16) for c in range(2)] for i in range(2 * NH)]
    mst = P.sb("mst", [128, 2, 2 * NH], F32)
    NB = 2
    IB = [P.sb("IB%d" % i, [128, 128], F32) for i in range(NB)]
    NLB = [P.sb("NLB%d" % i, [128, 128], F32) for i in range(NB)]
    wm = [P.sb("wm%d" % i, [128, 128], F32) for i in range(NB)]
    Dm = [P.sb("Dm%d" % i, [128, 128], F32) for i in range(NB)]
    st = [P.sb("st%d" % i, [128, 16], F32) for i in range(NB)]
    ex = [P.sb("ex%d" % i, [128, 4], F32) for i in range(NB)]
    a_ = [P.sb("a%d" % i, [128, 128], BF16) for i in range(NB)]
    aTs = [P.sb("aTs%d" % i, [128, 128], BF16) for i in range(NB)]
    Xs = [P.sb("Xs%d" % i, [128, 257], F32) for i in range(NB)]
    Z = [P.sb("Z%d" % i, [128, 257], F32) for i in range(NB)]
    dmr = [P.sb("dmr%d" % i, [128, 2], F32) for i in range(NB)]
    ho = [P.sb("ho%d" % i, [128, 256], F32) for i in range(NB)]
    kw = [P.sb("kw%d" % i, [128, 256], BF16) for i in range(NB)]
    psG, psS, psA, psY, psX, psC = k.ps[0], k.ps[1], k.ps[2], k.ps[3], k.ps[4], [k.ps[5], k.ps[6]]
    psA_bf = psA[:, :].bitcast(BF16)
    for i in range(NH):
        P.dve(Dl(V_.memset, vaug[i][:, :, 256:260], 1.0), w=[vaug[i].b("ones")])

    fwd_tiles = CTX_TILES + LAT_TILES
    bwd_tiles = CTX_TILES[::-1] + LAT_TILES[::-1]
    it = 0
    for hp in range(4 // NH):
        for i in range(NH):
            h = hp * NH + i
            for c in range(2):
                P.dma("sp", qT[i][c][:], k.uF_d.ap()[F_MLQ + h * 256 + c * 128:F_MLQ + h * 256 + (c + 1) * 128, :], w=[qT[i][c]])
                P.dma("sp", kT[i][c][:], k.uF_d.ap()[F_MLK + h * 256 + c * 128:F_MLK + h * 256 + (c + 1) * 128, :], w=[kT[i][c]])
            P.dma("sp", ktok[i][:], k.uT_mlk.ap()[:, h * 256:(h + 1) * 256].rearrange("(n p) d -> p n d", p=128), w=[ktok[i]])
            P.dma("sp", vaug[i][:, :, 0:256], k.uT_mlv.ap()[:, h * 256:(h + 1) * 256].rearrange("(n p) d -> p n d", p=128), w=[vaug[i].b("v")])
        for ch in range(2 * NH):
            for c in range(2):
                P.dve(Dl(V_.memset, C32[ch][c][:], 0.0), w=[C32[ch][c]])
                P.pool(Dl(G_.memset, Cb[ch][c][:], 0.0), w=[Cb[ch][c]])
        P.dve(Dl(V_.memset, mst[:], 0.0), w=[mst])
        for step in range(NTILE):
            for ch in range(2 * NH):
                i, d = ch // 2, ch % 2
                h = hp * NH + i
                ti = (fwd_tiles if d == 0 else bwd_tiles)[step]
                tsl = slice(ti * 128, (ti + 1) * 128)
                b = it % NB
                it += 1
                icol = GL[:, ti, 8 * d + h:8 * d + h + 1]
                fcol = GL[:, ti, 8 * d + 4 + h:8 * d + 4 + h + 1]
                mcur = mst[:, step % 2, ch:ch + 1]
                mnxt = mst[:, (step + 1) % 2, ch:ch + 1]
                s_, e_ = st[b], ex[b]
                vr = [vaug[i].b("v"), vaug[i].b("ones")]
                P.dve(Dl(V_.tensor_scalar, out=IB[b][:], in0=ones_f[:], scalar1=icol, scalar2=None, op0=ALU.mult), r=[ones_f, GL], w=[IB[b]])
                P.dve(Dl(V_.tensor_scalar, out=NLB[b][:], in0=ones_f[:], scalar1=fcol, scalar2=-1.0, op0=ALU.mult, op1=ALU.mult), r=[ones_f, GL], w=[NLB[b]])
                P.pe(Dl(T_.matmul, psG[:, 0:128], IB[b][:], k.ident_f[:], start=True, stop=False), r=[IB[b], k.ident_f], w=[psG])
                P.pe(Dl(T_.matmul, psG[:, 0:128], NLB[b][:], tri[d][:], start=False, stop=True), r=[NLB[b], tri[d]], w=[psG])
                P.pe(Dl(T_.matmul, psG[:, 128:129], tri[d][:], fcol, start=True, stop=True), r=[tri[d], GL], w=[psG])
                P.pe(Dl(T_.matmul, psG[:, 136:137], ones_f[:], fcol, start=True, stop=True), r=[ones_f, GL], w=[psG])
                P.dve(Dl(V_.tensor_tensor, out=wm[b][:], in0=psG[:, 0:128], in1=msk[d][:], op=ALU.add), r=[psG, msk[d]], w=[wm[b]])
                P.dve(Dl(V_.reduce_max, out=s_[:, 1:2], in_=psG[:, 0:128], axis=AX.X), r=[psG], w=[s_])
                P.dve(Dl(V_.tensor_copy, s_[:, 2:4], psG[:, 128:137:8]), r=[psG], w=[s_])
                P.dve(Dl(V_.reduce_max, out=s_[:, 0:1], in_=wm[b][:], axis=AX.X), r=[wm[b]], w=[s_])
                P.dve(Dl(V_.tensor_scalar, out=s_[:, 4:6], in0=s_[:, 0:2], scalar1=mcur, scalar2=None, op0=ALU.max), r=[s_, mst], w=[s_])
                P.dve(Dl(V_.tensor_scalar, out=s_[:, 6:8], in0=s_[:, 4:6], scalar1=-1.0, scalar2=None, op0=ALU.mult), r=[s_], w=[s_])
                P.dve(Dl(V_.tensor_tensor, out=mnxt, in0=s_[:, 3:4], in1=s_[:, 5:6], op=ALU.add), r=[s_], w=[mst])
                P.dve(Dl(V_.tensor_tensor, out=s_[:, 8:9], in0=icol, in1=s_[:, 2:3], op=ALU.subtract), r=[s_, GL], w=[s_])
                P.act(Dl(A_.activation, out=Dm[b][:], in_=wm[b][:], func=AF.Exp, bias=s_[:, 6:7], scale=1.0), r=[wm[b], s_], w=[Dm[b]])
                P.act(Dl(A_.activation, out=e_[:, 0:1], in_=mcur, func=AF.Exp, bias=s_[:, 6:7], scale=1.0), r=[mst, s_], w=[e_])
                P.act(Dl(A_.activation, out=e_[:, 1:2], in_=mcur, func=AF.Exp, bias=s_[:, 7:8], scale=1.0), r=[mst, s_], w=[e_])
                P.act(Dl(A_.activation, out=e_[:, 2:3], in_=s_[:, 8:9], func=AF.Exp, bias=s_[:, 7:8], scale=1.0), r=[s_], w=[e_])
                P.act(Dl(A_.activation, out=e_[:, 3:4], in_=s_[:, 2:3], func=AF.Exp, bias=s_[:, 6:7], scale=-1.0), r=[s_], w=[e_])
                for c in range(2):
                    P.pe(Dl(T_.matmul, psS[:, 0:128], qT[i][c][:, tsl], kT[i][c][:, tsl], start=(c == 0), stop=(c == 1)),
                         r=[qT[i][c], kT[i][c]], w=[psS])
                P.dve(Dl(V_.tensor_tensor, out=a_[b][:], in0=psS[:, 0:128], in1=Dm[b][:], op=ALU.mult), r=[psS, Dm[b]], w=[a_[b]])
                P.pe(Dl(T_.transpose, psA_bf[:, 0:128], a_[b][:], k.ident_bf[:]), r=[a_[b], k.ident_bf], w=[psA])
                P.act(Dl(A_.copy, aTs[b][:], psA_bf[:, 0:128]), r=[psA], w=[aTs[b]])
                P.pe(Dl(T_.matmul, psY[:, 0:257], aTs[b][:], vaug[i][:, ti, 0:257], start=True, stop=True), r=[aTs[b], vr], w=[psY])
                for c in range(2):
                    P.pe(Dl(T_.matmul, psX[:, 0:257], qT[i][c][:, tsl], Cb[ch][c][:, 0:257], start=(c == 0), stop=(c == 1)),
                         r=[qT[i][c], Cb[ch][c]], w=[psX])
                P.act(Dl(A_.activation, out=Xs[b][:], in_=psX[:, 0:257], func=AF.Copy, scale=e_[:, 0:1]), r=[psX, e_], w=[Xs[b]])
                P.dve(Dl(V_.tensor_tensor, out=Z[b][:], in0=psY[:, 0:257], in1=Xs[b][:], op=ALU.add), r=[psY, Xs[b]], w=[Z[b]])
                P.dve(Dl(V_.scalar_tensor_tensor, out=dmr[b][:, 0:1], in0=Z[b][:, 256:257], scalar=-1.0, in1=Z[b][:, 256:257], op0=ALU.mult, op1=ALU.max),
                      r=[Z[b]], w=[dmr[b]])
                P.dve(Dl(V_.tensor_tensor, out=dmr[b][:, 0:1], in0=dmr[b][:, 0:1], in1=e_[:, 3:4], op=ALU.max), r=[dmr[b], e_], w=[dmr[b]])
                P.dve(Dl(V_.reciprocal, out=dmr[b][:, 1:2], in_=dmr[b][:, 0:1]), r=[dmr[b]], w=[dmr[b]])
                P.act(Dl(A_.activation, out=ho[b][:], in_=Z[b][:, 0:256], func=AF.Copy, scale=dmr[b][:, 1:2]), r=[Z[b], dmr[b]], w=[ho[b]])
                P.dma("sp", k.h_d.ap()[d, ti * 128:(ti + 1) * 128, h * 256:(h + 1) * 256], ho[b][:], r=[ho[b]], w=[k.h_d.b((d, ti, h))])
                P.pool(Dl(G_.tensor_scalar, out=kw[b][:], in0=ktok[i][:, ti, :], scalar1=e_[:, 2:3], scalar2=None, op0=ALU.mult), r=[ktok[i], e_], w=[kw[b]])
                for c in range(2):
                    P.pe(Dl(T_.matmul, psC[c][:, 0:257], kw[b][:, c * 128:(c + 1) * 128], vaug[i][:, ti, 0:257], start=True, stop=True),
                         r=[kw[b], vr], w=[psC[c]])
                    P.dve(Dl(V_.scalar_tensor_tensor, out=C32[ch][c][:], in0=C32[ch][c][:], scalar=e_[:, 1:2], in1=psC[c][:, 0:257],
                             op0=ALU.mult, op1=ALU.add), r=[C32[ch][c], e_, psC[c]], w=[C32[ch][c]])
                    P.pool(Dl(G_.tensor_copy, Cb[ch][c][:, 0:257], C32[ch][c][:]), r=[C32[ch][c]], w=[Cb[ch][c]])
    P.phase_end()

    P.phase_begin()
    hgb = P.sb("hgb", [128, 1024], F32)
    P.dma("sp", hgb[:], k.ml_head_g.ap()[l:l + 1, :].partition_broadcast(128), w=[hgb])
    eps = P.sb("eps", [128, 1], F32)
    P.dve(Dl(V_.memset, eps[:], EPS), w=[eps])
    hf = [P.sb("hf%d" % i, [128, 1024], F32) for i in range(2)]
    hb_ = [P.sb("hbk%d" % i, [128, 1024], F32) for i in range(2)]
    og = [P.sb("og%d" % i, [128, 1024], BF16) for i in range(2)]
    sqm = P.sb("sqm", [128, 1024], F32)
    ssm = [P.sb("ssm%d" % i, [128, 4], F32) for i in range(2)]
    ym = [P.sb("ym%d" % i, [128, 1024], BF16) for i in range(2)]
    fst = [P.sb("fstm%d" % i, [128, KC, 512], BF16) for i in range(2)]
    pst = [k.ps[0][:, :].bitcast(BF16), k.ps[1][:, :].bitcast(BF16)]
    for ti in range(NTILE):
        b = ti % 2
        f_, b_, o_, y_, s_ = hf[b], hb_[b], og[b], ym[b], ssm[b]
        P.dma("sp", f_[:], k.h_d.ap()[0, ti * 128:(ti + 1) * 128, :], r=[k.h_d.b((0, ti, h)) for h in range(4)], w=[f_])
        P.dma("sp", b_[:], k.h_d.ap()[1, ti * 128:(ti + 1) * 128, :], r=[k.h_d.b((1, ti, h)) for h in range(4)], w=[b_])
        P.dma("act", o_[:], k.uT_mlo.ap()[ti * 128:(ti + 1) * 128, :], w=[o_])
        P.pool(Dl(G_.tensor_tensor, out=f_[:], in0=f_[:], in1=b_[:], op=ALU.add), r=[f_, b_], w=[f_])
        P.pool(Dl(G_.tensor_tensor, out=sqm[:], in0=f_[:], in1=f_[:], op=ALU.mult), r=[f_], w=[sqm])
        P.dve(Dl(V_.reduce_sum, out=s_[:], in_=sqm[:].rearrange("p (h d) -> p h d", h=4), axis=AX.X), r=[sqm], w=[s_])
        P.act(Dl(A_.activation, out=s_[:], in_=s_[:], func=AF.Sqrt, scale=1.0 / 256, bias=eps[:, 0:1]), r=[s_, eps], w=[s_])
        P.dve(Dl(V_.reciprocal, out=s_[:], in_=s_[:]), r=[s_], w=[s_])
        for h in range(4):
            P.act(Dl(A_.activation, out=f_[:, h * 256:(h + 1) * 256], in_=f_[:, h * 256:(h + 1) * 256], func=AF.Copy, scale=s_[:, h:h + 1]),
                  r=[f_, s_], w=[f_])
        P.dve(Dl(V_.tensor_tensor, out=f_[:], in0=f_[:], in1=hgb[:], op=ALU.mult), r=[f_, hgb], w=[f_])
        P.dve(Dl(V_.tensor_tensor, out=y_[:], in0=f_[:], in1=o_[:], op=ALU.mult), r=[f_, o_], w=[y_])
        ps = k.ps[b]
        for c in range(KC):
            P.pe(Dl(T_.transpose, pst[b][:, c * 128:(c + 1) * 128], y_[:, c * 128:(c + 1) * 128], k.ident_bf[:]), r=[y_, k.ident_bf], w=[ps])
        g4, t4 = ti // 4, ti % 4
        fs = fst[g4 % 2]
        o_ap = fs[:, :, t4 * 128:(t4 + 1) * 128]
        i_ap = pst[b][:, :].rearrange("p (c t) -> p c t", c=KC)
        if ti % 2:
            P.act(Dl(A_.copy, o_ap, i_ap), r=[ps], w=[fs.b(t4)])
        else:
            P.dve(Dl(V_.tensor_copy, o_ap, i_ap), r=[ps], w=[fs.b(t4)])
        if t4 == 3 or ti == NTILE - 1:
            nt_ = (t4 + 1) * 128
            P.dma("sp", k.br_d.ap()[0].rearrange("(c p) t -> p c t", p=128)[:, :, g4 * 512:g4 * 512 + nt_], fs[:, :, :nt_], r=fs.allb(),
                  w=[k.br_d.b(("ml", g4))])
    P.phase_end()


_CACHE = {}


def kernel(**inputs):
    inp = {k_: np.asarray(v) for k_, v in inputs.items()}
    if "nc" not in _CACHE:
        _CACHE["nc"] = build()[0]
        _CACHE["consts"] = host_consts()
    nc = _CACHE["nc"]
    consts = _CACHE["consts"]
    B = inp["x"].shape[0]
    in_maps = [prep_core_inputs(inp, c % B, consts) for c in range(8)]
    res = run_bass_kernel_spmd(nc, in_maps, core_ids=list(range(8)))
    out = np.stack([np.asarray(res.results[b]["out"]) for b in range(B)], axis=0)
    return out.astype(np.float32)
```

```python
import math
import numpy as np
import concourse.bass as bass
import concourse.mybir as mybir
from concourse.bass_utils import run_bass_kernel_spmd

F32 = mybir.dt.float32
BF16 = mybir.dt.bfloat16
AF = mybir.ActivationFunctionType
ALU = mybir.AluOpType
AX = mybir.AxisListType

D = 1024
KC = 8
DEPTH = 4
NL = 4096
NCX = 256
NT = NL + NCX
NTILE = NT // 128
D_IN = 11280
D_FF = 2816
FC = D_FF // 128
FN_STOP = 99
EPS = 1e-6
SBUF_BASE = 16640
SBUF_BYTES = 229376


class Buf:
    __slots__ = ("name", "w", "rs", "rd", "excl")

    def __init__(self, name=""):
        self.name = name
        self.excl = False
        self.w = None
        self.rs = {}
        self.rd = []


class Ins:
    __slots__ = ("eng", "fn", "deps", "sig", "sigval", "dma", "dsem", "dval", "dprev")

    def __init__(self, eng, fn, dma):
        self.eng = eng
        self.fn = fn
        self.dma = dma
        self.deps = set()
        self.sig = False
        self.sigval = 0
        self.dsem = None
        self.dval = 0
        self.dprev = 0


class Tl:
    def __init__(self, t, name):
        self.t = t
        self.name = name
        self.buf = Buf(name)
        self.subs = {}

    def b(self, i):
        s = self.subs.get(i)
        if s is None:
            s = Buf("%s.%s" % (self.name, i))
            self.subs[i] = s
        return s

    def allb(self):
        return list(self.subs.values())

    def ap(self):
        return self.t.ap()

    def __getitem__(self, k):
        return self.t[k]


class PsBank(Tl):
    def __init__(self, t, i):
        Tl.__init__(self, t, "ps%d" % i)
        self.i = i
        self.buf.excl = True

    def __getitem__(self, key):
        if isinstance(key, tuple):
            return self.t[key[0], self.i, key[1]]
        return self.t[key, self.i, :]


def _bufs(lst):
    out = []
    for x in lst:
        if x is None:
            continue
        if isinstance(x, Tl):
            out.append(x.buf)
        elif isinstance(x, Buf):
            out.append(x)
        else:
            out.extend(_bufs(x))
    return out


class Prog:
    ENGS = ("pe", "act", "dve", "pool", "sp")
    NDMA = {"sp": 40, "act": 16, "pool": 16}

    def __init__(self, nc):
        self.nc = nc
        self.ins = []
        self.eng = {"pe": nc.tensor, "act": nc.scalar, "dve": nc.vector, "pool": nc.gpsimd, "sp": nc.sync}
        self.last = {e: None for e in self.ENGS}
        self.dmas_open = []
        self.sb_off = SBUF_BASE
        self.sb_mark = 0
        self.nid = 0

    def sb(self, name, shape, dtype, persist=False):
        nbytes = int(np.prod(shape[1:])) * (4 if dtype == F32 else 2)
        nbytes = (nbytes + 63) // 64 * 64
        off = self.sb_off
        assert off + nbytes <= SBUF_BYTES, "SBUF overflow %s %d" % (name, off + nbytes)
        self.nid += 1
        t = self.nc.alloc_sbuf_tensor_at("%s_%d" % (name, self.nid), list(shape), dtype, offset=off)
        self.sb_off = off + nbytes
        return Tl(t, name)

    def phase_begin(self):
        self.sb_mark_stack = getattr(self, "sb_mark_stack", [])
        self.sb_mark_stack.append(self.sb_off)

    def phase_end(self):
        self.barrier()
        self.sb_off = self.sb_mark_stack.pop()

    def dram(self, name, shape, dtype, kind="Internal"):
        t = self.nc.dram_tensor(name, list(shape), dtype, kind=kind)
        return Tl(t, name)

    def op(self, eng, fn, r=(), w=(), dma=False):
        i = Ins(eng, fn, dma)
        rb = _bufs(r)
        wb = _bufs(w)
        ex = [b for b in rb if b.excl]
        if ex:
            rb = [b for b in rb if not b.excl]
            wb = wb + [b for b in ex if b not in wb]
        deps = i.deps
        for b in rb:
            if b.w is not None:
                if not (eng == "pe" and b.w.eng == "pe" and not b.w.dma):
                    deps.add(b.w)
        for b in wb:
            x = b.w
            if x is not None and (x.eng != eng or x.dma or dma):
                deps.add(x)
            for x in b.rs.values():
                if x.eng != eng or dma:
                    deps.add(x)
            for x in b.rd:
                deps.add(x)
        for b in rb:
            if dma:
                b.rd.append(i)
            else:
                b.rs[eng] = i
        for b in wb:
            b.w = i
            b.rs = {}
            b.rd = []
        self.ins.append(i)
        if dma:
            self.dmas_open.append(i)
        else:
            self.last[eng] = i
        return i

    def pe(self, fn, r=(), w=()):
        return self.op("pe", fn, r, w)

    def act(self, fn, r=(), w=()):
        return self.op("act", fn, r, w)

    def dve(self, fn, r=(), w=()):
        return self.op("dve", fn, r, w)

    def pool(self, fn, r=(), w=()):
        return self.op("pool", fn, r, w)

    def dma(self, q, out, in_, r=(), w=()):
        e = self.eng[q]
        return self.op(q, lambda: e.dma_start(out=out, in_=in_), r, w, dma=True)

    def barrier(self):
        prev = [x for x in self.last.values() if x is not None] + list(self.dmas_open)
        self.dmas_open = []
        for e in self.ENGS:
            eh = self.eng[e]
            i = Ins(e, (lambda eh=eh: eh.nop()), False)
            i.deps = set(prev)
            self.ins.append(i)
            self.last[e] = i

    def finalize(self):
        nc = self.nc
        self.barrier()
        esem = {e: nc.alloc_semaphore("es_" + e) for e in self.ENGS}
        dsems = {q: [nc.alloc_semaphore("ds_%s%d" % (q, k)) for k in range(n)] for q, n in self.NDMA.items()}
        duse = {q: [0] * n for q, n in self.NDMA.items()}
        drr = {q: 0 for q in self.NDMA}
        for i in self.ins:
            for d in i.deps:
                if not d.dma:
                    d.sig = True
        cnt = {e: 0 for e in self.ENGS}
        for i in self.ins:
            if i.dma:
                q = i.eng
                k = drr[q]
                drr[q] = (k + 1) % self.NDMA[q]
                i.dsem = dsems[q][k]
                i.dprev = duse[q][k]
                duse[q][k] += 16
                i.dval = duse[q][k]
            elif i.sig:
                cnt[i.eng] += 1
                i.sigval = cnt[i.eng]
        seen = {e: {} for e in self.ENGS}
        nwait = 0
        self.trace = {e: [] for e in self.ENGS}
        for i in self.ins:
            e = i.eng
            eh = self.eng[e]
            need = {}
            for d in i.deps:
                if d.dma:
                    s, v = d.dsem, d.dval
                else:
                    s, v = esem[d.eng], d.sigval
                if need.get(s, 0) < v:
                    need[s] = v
            if i.dma and i.dprev > 0:
                if need.get(i.dsem, 0) < i.dprev:
                    need[i.dsem] = i.dprev
            sn = seen[e]
            wl = []
            for s, v in need.items():
                if sn.get(s, 0) < v:
                    eh.wait_ge(s, v)
                    sn[s] = v
                    nwait += 1
                    wl.append((id(s), v))
            ins = i.fn()
            if i.dma:
                ins.then_inc(i.dsem, 16)
                self.trace[e].append((wl, (id(i.dsem), 16)))
            elif i.sig:
                ins.then_inc(esem[e], 1)
                self.trace[e].append((wl, (id(esem[e]), 1)))
            else:
                self.trace[e].append((wl, None))
        self.simulate()
        return dict(n_ins=len(self.ins), n_wait=nwait, sig=dict(cnt))

    def simulate(self):
        sem = {}
        pc = {e: 0 for e in self.ENGS}
        tr = self.trace
        while True:
            prog = False
            done = True
            for e in self.ENGS:
                t = tr[e]
                while pc[e] < len(t):
                    wl, inc = t[pc[e]]
                    if any(sem.get(s, 0) < v for s, v in wl):
                        break
                    if inc is not None:
                        sem[inc[0]] = sem.get(inc[0], 0) + inc[1]
                    pc[e] += 1
                    prog = True
                if pc[e] < len(t):
                    done = False
            if done:
                return
            if not prog:
                raise RuntimeError("sync deadlock at %s" % {e: (pc[e], len(tr[e])) for e in self.ENGS})


BLKS = [(i * 512, 512, 0) for i in range(NL // 512)] + [(NL, NCX, 1)]
TB1 = 256
BLKS1 = [(i * TB1, TB1, 0) for i in range(NL // TB1)] + [(NL, NCX, 1)]
LAT_TILES = list(range(NL // 128))
CTX_TILES = [NL // 128 + i for i in range(NCX // 128)]

C_MLQ, C_MLK, C_MLV, C_MLO, C_MLG, C_DAQ, C_DAK, C_DAV, C_FN, C_GP = 0, 1024, 2048, 3072, 4096, 4112, 5136, 6160, 7184, 8208
F_MLQ, F_MLK, F_DAQ, F_DAK, F_FN = 0, 1024, 2048, 3072, 4096


def host_consts():
    c = {}
    c["ident_f"] = np.eye(128, dtype=np.float32)
    perm = np.array([(m // 64) * 64 + ((m % 64) + 32) % 64 for m in range(128)])
    pw = np.zeros((128, 128), np.float32)
    pw[perm, np.arange(128)] = 1.0
    c["pswap"] = pw
    n_freq = 16
    inv = (10000.0 ** (-np.arange(n_freq, dtype=np.float32) / n_freq)).astype(np.float32)
    rows = NL // 64
    r = np.repeat(np.arange(rows, dtype=np.float32), 64)
    col = np.tile(np.arange(64, dtype=np.float32), rows)
    ang = np.concatenate([r[:, None] * inv, col[:, None] * inv], axis=-1).astype(np.float32)
    cos = np.cos(ang).astype(np.float32).T
    sin = np.sin(ang).astype(np.float32).T
    ct = np.zeros((128, NL), np.float32)
    st = np.zeros((128, NL), np.float32)
    for p in range(128):
        d = p % 64
        f = d % 32
        ct[p] = cos[f]
        st[p] = -sin[f] if d < 32 else sin[f]
    c["rope_c"] = ct
    c["rope_s"] = st
    c.update(host_consts_fn())
    c.update(host_consts_ml())
    return c


class K:
    pass


def Dl(fn, *a, **kw):
    return lambda: fn(*a, **kw)


def build(debug=None, nlayers=DEPTH, stop_after=None, skip=(), br_input=False, uf_input=False, fn_stop=99):
    global FN_STOP
    FN_STOP = fn_stop
    nc = bass.Bass("TRN2", target_bir_lowering=False)
    P = Prog(nc)
    k = K()
    k.nc, k.P = nc, P
    k.da_heads = 8

    def ein(name, shape, dt=F32):
        return Tl(nc.dram_tensor(name, list(shape), dt, kind="ExternalInput"), name)

    dbg_kind = {}

    def scratch(name, shape, dt):
        kind = "ExternalOutput" if (debug and name in debug) else "Internal"
        return Tl(nc.dram_tensor(name, list(shape), dt, kind=kind), name)

    k.xT_in = ein("xT", [128, KC, NT])
    k.cT = ein("cT", [128, KC, 2])
    k.w_ada = ein("w_ada", [DEPTH, D, 6 * D])
    k.b_adaT = ein("b_adaT", [DEPTH, 128, 48])
    k.norm_gT = ein("norm_gT", [DEPTH, 128, 2, KC])
    k.w_in = ein("w_in", [DEPTH, D, D_IN])
    k.ident_f_d = ein("ident_f", [128, 128])
    k.pswap_d = ein("pswap", [128, 128])
    k.rope_c_d = ein("rope_c", [128, NL])
    k.rope_s_d = ein("rope_s", [128, NL])
    k.da_lam = ein("da_lam", [DEPTH, 4, 64])
    k.da_head_gT = ein("da_head_gT", [DEPTH, 128, 8])
    k.w_br_ml = ein("w_br_ml", [DEPTH, D, D])
    k.w_br_da = ein("w_br_da", [DEPTH, D, D])
    k.w_br_fn = ein("w_br_fn", [DEPTH, D, D])
    k.w_out = ein("w_out", [DEPTH, D, D])
    k.w_ffn_in = ein("w_ffn_in", [DEPTH, D, 2 * D_FF])
    k.w_ffn_out = ein("w_ffn_out", [DEPTH, D_FF, D])
    k.final_gT = ein("final_gT", [128, KC])
    k.ml_triF = ein("ml_triF", [128, 128])
    k.ml_triB = ein("ml_triB", [128, 128])
    k.ml_maskF = ein("ml_maskF", [128, 128])
    k.ml_maskB = ein("ml_maskB", [128, 128])
    k.ml_gate_bR = ein("ml_gate_bR", [DEPTH, NTILE * 16])
    k.ml_head_g = ein("ml_head_g", [DEPTH, 1024])
    k.fn_cs = ein("fn_cs", [128, 2, 512], BF16)
    k.fn_w1 = ein("fn_w1", [128, 128], BF16)
    k.fn_twr = ein("fn_twr", [128, 256])
    k.fn_twi = ein("fn_twi", [128, 256])
    k.fn_w3c = ein("fn_w3c", [128, 128], BF16)
    k.fn_w3s = ein("fn_w3s", [128, 128], BF16)
    k.fn_csx = ein("fn_csx", [128, 2, 2, 256], BF16)

    k.hl_d = scratch("hl_d", [128, KC, NT], BF16)
    k.uF_d = ein("uF_d", [5120, NT], BF16) if uf_input else scratch("uF_d", [5120, NT], BF16)
    k.uT_mlk = scratch("uT_mlk", [NT, 1024], BF16)
    k.uT_mlv = scratch("uT_mlv", [NT, 1024], BF16)
    k.uT_mlo = scratch("uT_mlo", [NT, 1024], BF16)
    k.uT_g = scratch("uT_g", [NT, 16], F32)
    k.uT_dav = scratch("uT_dav", [NT, 1024], BF16)
    k.G_d = scratch("G_d", [3072, NT], BF16)
    k.mods_d = scratch("mods_d", [128, DEPTH * 96], F32)
    k.br_d = ein("br_d", [3, 1024, NT], BF16) if br_input else scratch("br_d", [3, 1024, NT], BF16)
    k.x1_d = scratch("x1_d", [128, KC, NT], F32)
    k.AB_d = scratch("AB_d", [2, NL, 512], BF16)
    k.h_d = scratch("h_d", [2, NT, 1024], F32)
    k.h2_d = scratch("h2_d", [128, KC, NT], BF16)
    k.gT_d = scratch("gT_d", [128, FC, NT], BF16)
    k.xT_d = scratch("xT_d", [128, KC, NT], F32)
    k.out = Tl(nc.dram_tensor("out", [NL, D], F32, kind="ExternalOutput"), "out")
    k.dbg = scratch("dbg", [6, 128, 512], F32) if (debug and "dbg" in debug) else None

    k.ident_f = P.sb("ident_f", [128, 128], F32)
    k.ident_bf = P.sb("ident_bf", [128, 128], BF16)
    k.ones_bf = P.sb("ones_bf", [128, 128], BF16)
    k.mods = P.sb("mods", [128, DEPTH * 96], F32)
    k.Gt = P.sb("Gt", [128, DEPTH * 2 * KC * 2], F32)
    k.psall = nc.alloc_psum_tensor("psall", [128, 8, 512], F32)
    k.ps = [PsBank(k.psall, i) for i in range(8)]

    P.dma("sp", k.ident_f[:], k.ident_f_d.ap()[:, :], w=[k.ident_f])
    P.dve(lambda: nc.vector.tensor_copy(k.ident_bf[:], k.ident_f[:]), r=[k.ident_f], w=[k.ident_bf])
    P.dve(lambda: nc.vector.memset(k.ones_bf[:], 1.0), w=[k.ones_bf])

    if "adaln" not in skip:
        phase_adaln(k)
    if debug and "mods_d" in debug:
        P.dma("sp", k.mods_d.ap()[:, :], k.mods[:], r=[k.mods], w=[k.mods_d])
    for l in range(nlayers):
        if l == 0 and "m1" not in skip:
            phase_norm(k, l, 0, k.xT_in, k.hl_d)
        if stop_after == "norm":
            break
        if "m1" not in skip:
            phase_m1(k, l)
        if stop_after == "m1":
            break
        if "da" not in skip:
            phase_da(k, l)
        if stop_after == "da":
            break
        if "fn" not in skip:
            phase_fn(k, l)
        if stop_after == "fn":
            break
        if "ml" not in skip:
            phase_ml(k, l)
        if stop_after == "ml":
            break
        phase_t1(k, l, k.xT_in if l == 0 else k.xT_d)
        phase_t2a(k, l)
        phase_t2b(k, l, last=(l == nlayers - 1))
    info = P.finalize()
    return nc, info


def mods_view(k, l, chunk0, sel):
    base = l * 96 + chunk0 * 2 + sel
    return k.mods[:, base:base + 15:2]


def mod_col(k, l, chunk, sel):
    base = l * 96 + chunk * 2 + sel
    return k.mods[:, base:base + 1]


def g_col(k, l, which, kc, sel):
    base = ((l * 2 + which) * KC + kc) * 2 + sel
    return k.Gt[:, base:base + 1]


def phase_adaln(k):
    nc, P = k.nc, k.P
    P.phase_begin()
    s_c = P.sb("s_c", [128, KC * 2], F32)
    P.dma("sp", s_c[:], k.cT.ap().rearrange("p k s -> p (k s)"), w=[s_c])
    P.act(lambda: nc.scalar.activation(out=s_c[:], in_=s_c[:], func=AF.Silu), r=[s_c], w=[s_c])
    wst = [P.sb("wa%d" % i, [128, KC, 512], F32) for i in range(2)]
    bad = P.sb("bad", [128, DEPTH * 48], F32)
    ng = P.sb("ng", [128, DEPTH * 2 * KC], F32)
    P.dma("sp", bad[:].rearrange("p (l c) -> p l c", l=DEPTH), k.b_adaT.ap().rearrange("l p c -> p l c"), w=[bad])
    P.dma("sp", ng[:].rearrange("p (l c) -> p l c", l=DEPTH), k.norm_gT.ap().rearrange("l p w c -> p l (w c)"), w=[ng])
    ps0 = k.ps[0]
    gi = 0
    for l in range(DEPTH):
        wv = k.w_ada.ap()[l].rearrange("(k p) n -> p k n", p=128)
        for g in range(12):
            w_ = wst[gi % 2]
            gi += 1
            P.dma("sp", w_[:], wv[:, :, g * 512:(g + 1) * 512], w=[w_])
            for j in range(4):
                col = g * 4 + j
                for kc in range(KC):
                    P.pe(lambda w_=w_, j=j, kc=kc, col=col: nc.tensor.matmul(
                        ps0[:, 2 * col:2 * col + 2], w_[:, kc, j * 128:(j + 1) * 128], s_c[:, 2 * kc:2 * kc + 2],
                        start=(kc == 0), stop=(kc == KC - 1)), r=[w_, s_c], w=[ps0])
        for s in range(2):
            P.dve(lambda l=l, s=s: nc.vector.tensor_tensor(
                out=k.mods[:, l * 96 + s:(l + 1) * 96:2], in0=ps0[:, s:96:2], in1=bad[:, l * 48:(l + 1) * 48], op=ALU.add),
                r=[ps0, bad], w=[k.mods])
        for which in range(2):
            for s in range(2):
                sc = mods_view(k, l, 8 + 24 * which, s)
                gb = ((l * 2 + which) * KC) * 2 + s
                P.dve(lambda sc=sc, gb=gb, l=l, which=which: nc.vector.scalar_tensor_tensor(
                    out=k.Gt[:, gb:gb + 15:2], in0=sc, scalar=1.0, in1=ng[:, (l * 2 + which) * KC:(l * 2 + which + 1) * KC],
                    op0=ALU.add, op1=ALU.mult), r=[k.mods, ng], w=[k.Gt])
    P.phase_end()


def phase_norm(k, l, which, xsrc, hdst):
    nc, P = k.nc, k.P
    P.phase_begin()
    xb = [P.sb("xb%d" % i, [128, KC, 512], F32) for i in range(2)]
    sq = P.sb("sq", [128, KC, 512], BF16)
    rs = P.sb("rs", [128, 512], F32)
    tmp = [P.sb("tmp%d" % i, [128, 512], F32) for i in range(2)]
    hb = [P.sb("hb%d" % i, [128, KC, 512], BF16) for i in range(2)]
    eps = P.sb("eps", [128, 1], F32)
    P.dve(lambda: nc.vector.memset(eps[:], EPS), w=[eps])
    for bi, (t0, n, sel) in enumerate(BLKS):
        x_, h_ = xb[bi % 2], hb[bi % 2]
        P.dma("sp", x_[:, :, :n], xsrc.ap()[:, :, t0:t0 + n], w=[x_])
        norm_block(k, l, which, sel, x_, n, sq, rs, tmp, h_, eps, k.ps[1])
        P.dma("sp", hdst.ap()[:, :, t0:t0 + n], h_[:, :, :n], r=[h_], w=[hdst])
    P.phase_end()


def norm_block(k, l, which, sel, x_, n, sq, rs, tmp, h_, eps, ps):
    nc, P = k.nc, k.P
    P.pool(lambda: nc.gpsimd.tensor_tensor(out=sq[:, :, :n], in0=x_[:, :, :n], in1=x_[:, :, :n], op=ALU.mult), r=[x_], w=[sq])
    for kc in range(KC):
        P.pe(lambda kc=kc: nc.tensor.matmul(ps[:, :n], k.ones_bf[:], sq[:, kc, :n], start=(kc == 0), stop=(kc == KC - 1)),
             r=[k.ones_bf, sq], w=[ps])
    P.act(lambda: nc.scalar.activation(out=rs[:, :n], in_=ps[:, :n], func=AF.Sqrt, scale=1.0 / D, bias=eps[:, 0:1]), r=[ps, eps], w=[rs])
    P.dve(lambda: nc.vector.reciprocal(out=rs[:, :n], in_=rs[:, :n]), r=[rs], w=[rs])
    for kc in range(KC):
        t_ = tmp[kc % 2]
        P.dve(lambda kc=kc, t_=t_: nc.vector.tensor_tensor(out=t_[:, :n], in0=x_[:, kc, :n], in1=rs[:, :n], op=ALU.mult), r=[x_, rs], w=[t_])
        P.act(lambda kc=kc, t_=t_: nc.scalar.activation(out=h_[:, kc, :n], in_=t_[:, :n], func=AF.Identity,
                                                       scale=g_col(k, l, which, kc, sel), bias=mod_col(k, l, 24 * which + kc, sel)),
              r=[t_, k.Gt, k.mods], w=[h_])


def load_w_group(k, wv, c0, ncol, wst, wbf):
    nc, P = k.nc, k.P
    P.dma("sp", wst[:, :, :ncol], wv[:, :, c0:c0 + ncol], w=[wst])
    P.pool(lambda: nc.gpsimd.tensor_copy(wbf[:, :, :ncol], wst[:, :, :ncol]), r=[wst], w=[wbf])


def phase_m1(k, l):
    nc, P = k.nc, k.P
    P.phase_begin()
    hl = P.sb("hl", [128, KC, NT], BF16)
    for kc in range(KC):
        P.dma("sp", hl[:, kc, :], k.hl_d.ap()[:, kc, :], w=[hl.b(kc)])
    hlr = hl.allb()
    wst = [P.sb("wst%d" % i, [128, KC, 512], F32) for i in range(2)]
    wbf = [P.sb("wbf%d" % i, [128, KC, 512], BF16) for i in range(2)]
    fst = [P.sb("fst%d" % i, [128, NT], BF16) for i in range(2)]
    tst = [P.sb("tst%d" % i, [128, 512], BF16) for i in range(3)]
    tstf = [P.sb("tstf%d" % i, [128, 16], F32) for i in range(2)]
    ropc = P.sb("ropc", [128, NL], F32)
    rops = P.sb("rops", [128, NL], F32)
    pswap = P.sb("pswap", [128, 128], F32)
    q32 = [P.sb("q32_%d" % i, [128, 512], F32) for i in range(2)]
    t1 = [P.sb("t1_%d" % i, [128, 512], F32) for i in range(2)]
    t2 = [P.sb("t2_%d" % i, [128, 512], F32) for i in range(2)]
    P.dma("act", ropc[:], k.rope_c_d.ap()[:, :], w=[ropc])
    P.dma("act", rops[:], k.rope_s_d.ap()[:, :], w=[rops])
    P.dma("act", pswap[:], k.pswap_d.ap()[:, :], w=[pswap])
    wv = k.w_in.ap()[l].rearrange("(k p) n -> p k n", p=128)
    st = dict(g=0, ps=0, f=0, t=0, ev=0, r=0)

    def nextps():
        st["ps"] = (st["ps"] + 1) % 6
        return k.ps[2 + st["ps"]]

    def feat_group(c0, dst, row0, kind, tok_blks=BLKS):
        w_, wb_ = wst[st["g"] % 2], wbf[st["g"] % 2]
        st["g"] += 1
        prefetch_next()
        for j in range(4):
            stg = fst[st["f"] % 2]
            st["f"] += 1
            for bi, (t0, n, sel) in enumerate(tok_blks):
                ps = nextps()
                sb_ = stg.b(bi)
                for kc in range(KC):
                    P.pe(lambda ps=ps, wb_=wb_, j=j, kc=kc, t0=t0, n=n: nc.tensor.matmul(
                        ps[:, :n], wb_[:, kc, j * 128:(j + 1) * 128], hl[:, kc, t0:t0 + n], start=(kc == 0), stop=(kc == KC - 1)),
                        r=[wb_, hlr], w=[ps])
                o = stg[:, t0:t0 + n]
                if kind == "rope" and sel == 0:
                    q_, a_, b_ = q32[st["r"] % 2], t1[st["r"] % 2], t2[st["r"] % 2]
                    st["r"] += 1
                    psr = k.ps[0 + st["r"] % 2]
                    P.act(lambda ps=ps, q_=q_, n=n: nc.scalar.copy(q_[:, :n], ps[:, :n]), r=[ps], w=[q_])
                    P.pe(lambda psr=psr, q_=q_, n=n: nc.tensor.matmul(psr[:, :n], pswap[:], q_[:, :n], start=True, stop=True),
                         r=[pswap, q_], w=[psr])
                    P.pool(lambda a_=a_, q_=q_, t0=t0, n=n: nc.gpsimd.tensor_tensor(out=a_[:, :n], in0=q_[:, :n], in1=ropc[:, t0:t0 + n], op=ALU.mult),
                           r=[q_, ropc], w=[a_])
                    P.dve(lambda b_=b_, psr=psr, t0=t0, n=n: nc.vector.tensor_tensor(out=b_[:, :n], in0=psr[:, :n], in1=rops[:, t0:t0 + n], op=ALU.mult),
                          r=[psr, rops], w=[b_])
                    P.dve(lambda o=o, a_=a_, b_=b_, n=n: nc.vector.tensor_tensor(out=o, in0=a_[:, :n], in1=b_[:, :n], op=ALU.add),
                          r=[a_, b_], w=[sb_])
                elif kind == "sigm":
                    P.act(lambda o=o, ps=ps, n=n: nc.scalar.activation(out=o, in_=ps[:, :n], func=AF.Sigmoid), r=[ps], w=[sb_])
                elif kind == "s16":
                    P.act(lambda o=o, ps=ps, n=n: nc.scalar.mul(o, ps[:, :n], 1.0 / 16.0), r=[ps], w=[sb_])
                else:
                    st["ev"] += 1
                    if st["ev"] % 2:
                        P.act(lambda o=o, ps=ps, n=n: nc.scalar.copy(o, ps[:, :n]), r=[ps], w=[sb_])
                    else:
                        P.dve(lambda o=o, ps=ps, n=n: nc.vector.tensor_copy(o, ps[:, :n]), r=[ps], w=[sb_])
            ta, tb = tok_blks[0][0], tok_blks[-1][0] + tok_blks[-1][1]
            P.dma("sp", dst.ap()[row0 + j * 128:row0 + (j + 1) * 128, ta:tb], stg[:, ta:tb], r=stg.allb(), w=[dst])

    def tok_group(c0, ncol, dst, dcol0, kind):
        w_, wb_ = wst[st["g"] % 2], wbf[st["g"] % 2]
        st["g"] += 1
        prefetch_next()
        for ti in range(NTILE):
            ps = nextps()
            for kc in range(KC):
                P.pe(lambda ps=ps, wb_=wb_, kc=kc, ti=ti: nc.tensor.matmul(
                    ps[:, :ncol], hl[:, kc, ti * 128:(ti + 1) * 128], wb_[:, kc, :ncol], start=(kc == 0), stop=(kc == KC - 1)),
                    r=[wb_, hlr], w=[ps])
            if kind == "f32":
                stg = tstf[st["t"] % 2]
            else:
                stg = tst[st["t"] % 3]
            st["t"] += 1
            o = stg[:, :ncol]
            if kind == "sigm":
                P.act(lambda o=o, ps=ps: nc.scalar.activation(out=o, in_=ps[:, :ncol], func=AF.Sigmoid), r=[ps], w=[stg])
            elif kind == "s16":
                P.act(lambda o=o, ps=ps: nc.scalar.mul(o, ps[:, :ncol], 1.0 / 16.0), r=[ps], w=[stg])
            else:
                st["ev"] += 1
                if st["ev"] % 2:
                    P.act(lambda o=o, ps=ps: nc.scalar.copy(o, ps[:, :ncol]), r=[ps], w=[stg])
                else:
                    P.dve(lambda o=o, ps=ps: nc.vector.tensor_copy(o, ps[:, :ncol]), r=[ps], w=[stg])
            P.dma("sp", dst.ap()[ti * 128:(ti + 1) * 128, dcol0:dcol0 + ncol], o, r=[stg], w=[dst])

    tasks = []
    for g in range(2):
        tasks.append(("f", C_MLQ + g * 512, 512, k.uF_d, F_MLQ + g * 512, "copy"))
        tasks.append(("f", C_MLK + g * 512, 512, k.uF_d, F_MLK + g * 512, "s16"))
        tasks.append(("f", C_DAQ + g * 512, 512, k.uF_d, F_DAQ + g * 512, "rope"))
        tasks.append(("f", C_DAK + g * 512, 512, k.uF_d, F_DAK + g * 512, "rope"))
        tasks.append(("f", C_FN + g * 512, 512, k.uF_d, F_FN + g * 512, "copy"))
        tasks.append(("t", C_MLK + g * 512, 512, k.uT_mlk, g * 512, "s16"))
        tasks.append(("t", C_MLV + g * 512, 512, k.uT_mlv, g * 512, "copy"))
        tasks.append(("t", C_MLO + g * 512, 512, k.uT_mlo, g * 512, "sigm"))
        tasks.append(("t", C_DAV + g * 512, 512, k.uT_dav, g * 512, "copy"))
    tasks.append(("t", C_MLG, 16, k.uT_g, 0, "f32"))
    for g in range(6):
        tasks.append(("f", C_GP + g * 512, 512, k.G_d, g * 512, "sigm"))
    pf = dict(i=0)

    def prefetch_next():
        i = pf["i"]
        if i < len(tasks):
            t = tasks[i]
            load_w_group(k, wv, t[1], t[2], wst[i % 2], wbf[i % 2])
            pf["i"] = i + 1

    prefetch_next()
    for t in tasks:
        if t[0] == "f":
            feat_group(t[1], t[3], t[4], t[5])
        else:
            tok_group(t[1], t[2], t[3], t[4], t[5])
    P.phase_end()


def fm(a):
    return np.ascontiguousarray(a.T.reshape(KC, 128, a.shape[0]).transpose(1, 0, 2))


def prep_core_inputs(inp, b, consts):
    m = {}
    xall = np.concatenate([inp["x"][b], inp["ctx"][b]], axis=0)
    m["xT"] = fm(xall).astype(np.float32)
    cc = np.stack([inp["c"][b], inp["c_ctx"]], axis=0)
    m["cT"] = np.ascontiguousarray(cc.T.reshape(KC, 128, 2).transpose(1, 0, 2)).astype(np.float32)
    m["w_ada"] = inp["w_ada"]
    m["b_adaT"] = np.ascontiguousarray(inp["b_ada"].reshape(DEPTH, 48, 128).transpose(0, 2, 1))
    m["norm_gT"] = np.ascontiguousarray(inp["norm_g"].reshape(DEPTH, 2, KC, 128).transpose(0, 3, 1, 2))
    m["w_in"] = inp["w_in"]
    m["da_lam"] = inp["da_lam"]
    m["ml_gate_bR"] = np.ascontiguousarray(np.tile(inp["ml_gate_b"][:, None, :], (1, NTILE, 1)).reshape(DEPTH, NTILE * 16))
    m["ml_head_g"] = inp["ml_head_g"]
    for nm in ("w_br_ml", "w_br_da", "w_br_fn", "w_out", "w_ffn_in", "w_ffn_out"):
        m[nm] = inp[nm]
    m["final_gT"] = np.ascontiguousarray(inp["final_g"].reshape(KC, 128).T)
    m["da_head_gT"] = np.ascontiguousarray(inp["da_head_g"].reshape(DEPTH, 8, 128).transpose(0, 2, 1))
    m.update(consts)
    return m


def lam_init_of(l):
    return 0.8 - 0.6 * math.exp(-0.3 * l)


def phase_da(k, l):
    nc, P = k.nc, k.P
    V_, A_, T_, G_ = nc.vector, nc.scalar, nc.tensor, nc.gpsimd
    P.phase_begin()
    li = lam_init_of(l)
    lq = P.sb("lq", [128, 256], F32)
    P.dma("sp", lq[:], k.da_lam.ap()[l:l + 1].rearrange("o a d -> o (a d)").partition_broadcast(128), w=[lq])
    pr = P.sb("pr", [128, 128], F32)
    sc = P.sb("sc", [128, 8], F32)
    eps = P.sb("eps", [128, 1], F32)
    P.dve(Dl(V_.memset, eps[:], EPS), w=[eps])
    P.dve(Dl(V_.tensor_tensor, out=pr[:, 0:64], in0=lq[:, 0:64], in1=lq[:, 64:128], op=ALU.mult), r=[lq], w=[pr])
    P.dve(Dl(V_.tensor_tensor, out=pr[:, 64:128], in0=lq[:, 128:192], in1=lq[:, 192:256], op=ALU.mult), r=[lq], w=[pr])
    P.dve(Dl(V_.reduce_sum, out=sc[:, 0:1], in_=pr[:, 0:64], axis=AX.X), r=[pr], w=[sc])
    P.dve(Dl(V_.reduce_sum, out=sc[:, 1:2], in_=pr[:, 64:128], axis=AX.X), r=[pr], w=[sc])
    P.act(Dl(A_.activation, out=sc[:, 2:4], in_=sc[:, 0:2], func=AF.Exp), r=[sc], w=[sc])
    P.dve(Dl(V_.scalar_tensor_tensor, out=sc[:, 4:5], in0=sc[:, 3:4], scalar=-li, in1=sc[:, 2:3], op0=ALU.add, op1=ALU.subtract),
          r=[sc], w=[sc])
    neglam = sc[:, 4:5]
    hg = P.sb("hg", [128, 8], F32)
    P.dma("sp", hg[:], k.da_head_gT.ap()[l], w=[hg])
    P.dve(Dl(V_.tensor_scalar, out=hg[:], in0=hg[:], scalar1=(1.0 - li), scalar2=None, op0=ALU.mult), r=[hg], w=[hg])

    qz = [[P.sb("qz%d%d" % (i, m), [128, NT], BF16) for m in range(2)] for i in range(2)]
    for i in range(2):
        for m in range(2):
            P.pool(Dl(G_.memset, qz[i][m][:], 0.0), w=[qz[i][m]])
    kT = [P.sb("kT%d" % i, [128, NT], BF16) for i in range(2)]
    V = [P.sb("V%d" % i, [128, NTILE, 128], BF16) for i in range(2)]
    pt = [P.sb("pt%d" % i, [128, 2, 512], BF16) for i in range(3)]
    ystg = [P.sb("ystg%d" % i, [128, NT], BF16) for i in range(2)]
    rd = [P.sb("rd%d" % i, [128, 512], F32) for i in range(2)]
    o_ = [P.sb("o%d" % i, [128, 512], F32) for i in range(2)]
    acc = [P.sb("acc%d" % i, [128, 2, 512], F32) for i in range(2)]
    ones_f = P.sb("ones_fd", [128, 128], F32)
    P.dve(Dl(V_.memset, ones_f[:], 1.0), w=[ones_f])
    ofs = [P.sb("of%d" % i, [128, 512], F32) for i in range(2)]
    sqs = [P.sb("sqd%d" % i, [128, 512], BF16) for i in range(2)]
    rs = P.sb("rsd", [128, 512], F32)
    deferred = []
    cO = [[P.sb("cO%d%d" % (i, m), [128, 512], F32) for m in range(2)] for i in range(2)]
    cD = [[P.sb("cD%d%d" % (i, m), [128, 512], F32) for m in range(2)] for i in range(2)]
    psO = [k.ps[4], k.ps[5]]
    psD = [k.ps[6], k.ps[7]]
    psE = k.ps[0]
    cnt = dict(s=0, p=0)

    for h in range(k.da_heads):
        qz_, k_, v_, y_ = qz[h % 2], kT[h % 2], V[h % 2], ystg[h % 2]
        for m in range(2):
            r0 = F_DAQ + h * 128 + 64 * m
            P.dma("sp", qz_[m][64 * m:64 * m + 64, :], k.uF_d.ap()[r0:r0 + 64, :], w=[qz_[m]])
        P.dma("sp", k_[:], k.uF_d.ap()[F_DAK + h * 128:F_DAK + (h + 1) * 128, :], w=[k_])
        P.dma("sp", v_[:], k.uT_dav.ap()[:, h * 128:(h + 1) * 128].rearrange("(n p) d -> p n d", p=128), w=[v_])
        for bi, (t0, n, sel) in enumerate(BLKS):
            ktiles = (LAT_TILES + CTX_TILES) if sel == 0 else CTX_TILES
            steps = [(m, ktiles[j], ktiles[j + 1], j) for m in range(2) for j in range(0, len(ktiles), 2)]
            nk = len(ktiles)

            def emit_s(i):
                m, ka, kb, j = steps[i]
                pi = cnt["s"] % 2
                cnt["s"] += 1
                banks = [k.ps[2 * pi], k.ps[2 * pi + 1]]
                for x, kt in enumerate((ka, kb)):
                    P.pe(Dl(T_.matmul, banks[x][:, :n], k_[:, kt * 128:(kt + 1) * 128], qz_[m][:, t0:t0 + n], start=True, stop=True),
                         r=[k_, qz_[m]], w=[banks[x]])
                p_ = pt[cnt["p"] % 3]
                cnt["p"] += 1
                P.act(Dl(A_.activation, out=p_[:, :, :n], in_=k.psall[:, 2 * pi:2 * pi + 2, :n], func=AF.Exp, scale=0.125), r=banks, w=[p_])
                return p_

            def emit_o(i, p_):
                m, ka, kb, j = steps[i]
                for x, kt in enumerate((ka, kb)):
                    jj = j + x
                    P.pe(Dl(T_.matmul, psO[m][:, :n], v_[:, kt, :], p_[:, x, :n], start=(jj == 0), stop=(jj == nk - 1)),
                         r=[v_, p_], w=[psO[m]])
                    P.pe(Dl(T_.matmul, psD[m][:, :n], k.ones_bf[:], p_[:, x, :n], start=(jj == 0), stop=(jj == nk - 1)),
                         r=[k.ones_bf, p_], w=[psD[m]])

            pend = []
            for i in range(len(steps)):
                pend.append((i, emit_s(i)))
                if len(pend) > 1:
                    emit_o(*pend.pop(0))
                if i == 8 or i == len(steps) - 1:
                    while deferred:
                        deferred.pop(0)()
            while pend:
                emit_o(*pend.pop(0))
            of_, sq_ = ofs[bi % 2], sqs[bi % 2]
            cO_, cD_ = cO[bi % 2], cD[bi % 2]
            for m in range(2):
                P.act(Dl(A_.copy, cD_[m][:, :n], psD[m][:, :n]), r=[psD[m]], w=[cD_[m]])
                P.act(Dl(A_.copy, cO_[m][:, :n], psO[m][:, :n]), r=[psO[m]], w=[cO_[m]])
            for m in range(2):
                P.dve(Dl(V_.reciprocal, out=rd[m][:, :n], in_=cD_[m][:, :n]), r=[cD_[m]], w=[rd[m]])
                P.dve(Dl(V_.tensor_tensor, out=o_[m][:, :n], in0=cO_[m][:, :n], in1=rd[m][:, :n], op=ALU.mult),
                      r=[cO_[m], rd[m]], w=[o_[m]])
            P.dve(Dl(V_.scalar_tensor_tensor, out=of_[:, :n], in0=o_[1][:, :n], scalar=neglam, in1=o_[0][:, :n], op0=ALU.mult, op1=ALU.add),
                  r=[o_[0], o_[1], sc], w=[of_])
            P.pool(Dl(G_.tensor_tensor, out=sq_[:, :n], in0=of_[:, :n], in1=of_[:, :n], op=ALU.mult), r=[of_], w=[sq_])
            def part_c(n=n, t0=t0, bi=bi, h=h, y_=y_, sq_=sq_, of_=of_):
                P.pe(Dl(T_.matmul, psE[:, :n], k.ones_bf[:], sq_[:, :n], start=True, stop=True), r=[k.ones_bf, sq_], w=[psE])
                P.act(Dl(A_.activation, out=rs[:, :n], in_=psE[:, :n], func=AF.Sqrt, scale=1.0 / 128, bias=eps[:, 0:1]), r=[psE, eps], w=[rs])
                P.dve(Dl(V_.reciprocal, out=rs[:, :n], in_=rs[:, :n]), r=[rs], w=[rs])
                P.dve(Dl(V_.tensor_tensor, out=of_[:, :n], in0=of_[:, :n], in1=rs[:, :n], op=ALU.mult), r=[of_, rs], w=[of_])
                P.act(Dl(A_.activation, out=y_[:, t0:t0 + n], in_=of_[:, :n], func=AF.Copy, scale=hg[:, h:h + 1]), r=[of_, hg], w=[y_.b(bi)])
            deferred.append(part_c)
        while deferred:
            deferred.pop(0)()
        P.dma("sp", k.br_d.ap()[1, h * 128:(h + 1) * 128, :], y_[:], r=y_.allb(), w=[k.br_d.b(("da", h))])
    P.phase_end()


def load_w_full(k, dst_bf, wv, ncols, kcn, wst, q="sp"):
    nc, P = k.nc, k.P
    for gi, c0 in enumerate(range(0, ncols, 512)):
        w_ = wst[gi % 2]
        P.dma(q, w_[:, :kcn, :], wv[:, :, c0:c0 + 512], w=[w_])
        P.pool(Dl(nc.gpsimd.tensor_copy, dst_bf[:, :, c0:c0 + 512], w_[:, :kcn, :]), r=[w_], w=[dst_bf.b(c0 // 512)])


def phase_t1(k, l, xsrc):
    nc, P = k.nc, k.P
    V_, A_, T_, G_ = nc.vector, nc.scalar, nc.tensor, nc.gpsimd
    P.phase_begin()
    wst = [P.sb("wst%d" % i, [128, KC, 512], F32) for i in range(2)]
    wbr = [P.sb("wbr%d" % i, [128, KC, 1024], BF16) for i in range(3)]
    wo = P.sb("wo", [128, KC, 1024], BF16)
    for x, wt in enumerate((k.w_br_ml, k.w_br_da, k.w_br_fn)):
        load_w_full(k, wbr[x], wt.ap()[l].rearrange("(k p) n -> p k n", p=128), 1024, KC, wst)
    load_w_full(k, wo, k.w_out.ap()[l].rearrange("(k p) n -> p k n", p=128), 1024, KC, wst)
    brb = [P.sb("brb%d" % i, [128, 24, TB1], BF16) for i in range(2)]
    gb = [P.sb("gb%d" % i, [128, 24, TB1], BF16) for i in range(2)]
    xb = [P.sb("xb%d" % i, [128, KC, TB1], F32) for i in range(2)]
    yb = P.sb("yb", [128, KC, TB1], BF16)
    ta = [P.sb("ta%d" % i, [128, TB1], F32) for i in range(2)]
    tb = [P.sb("tb%d" % i, [128, TB1], F32) for i in range(2)]
    tc = [P.sb("tc%d" % i, [128, TB1], F32) for i in range(2)]
    sq = P.sb("sq", [128, KC, TB1], BF16)
    rs = P.sb("rs", [128, TB1], F32)
    tmp = [P.sb("tmp%d" % i, [128, TB1], F32) for i in range(2)]
    hb = [P.sb("hb%d" % i, [128, KC, TB1], BF16) for i in range(2)]
    eps = P.sb("eps", [128, 1], F32)
    P.dve(Dl(V_.memset, eps[:], EPS), w=[eps])
    gv = k.G_d.ap().rearrange("(c p) t -> p c t", p=128)
    it = 0
    for bi, (t0, n, sel) in enumerate(BLKS1):
        b_, g_, x_, h_ = brb[bi % 2], gb[bi % 2], xb[bi % 2], hb[bi % 2]
        for x in range(3):
            P.dma("sp", b_[:, x * 8:(x + 1) * 8, :n], k.br_d.ap()[x].rearrange("(c p) t -> p c t", p=128)[:, :, t0:t0 + n], w=[b_.b(x)])
        P.dma("act", g_[:, :, :n], gv[:, :, t0:t0 + n], w=[g_])
        P.dma("act", x_[:, :, :n], xsrc.ap()[:, :, t0:t0 + n], w=[x_])
        for oc in range(KC):
            pss = [k.ps[(it % 2) * 3 + x] for x in range(3)]
            a_, b2_, c_ = ta[it % 2], tb[it % 2], tc[it % 2]
            it += 1
            for x in range(3):
                for kc in range(KC):
                    P.pe(Dl(T_.matmul, pss[x][:, :n], wbr[x][:, kc, oc * 128:(oc + 1) * 128], b_[:, x * 8 + kc, :n],
                            start=(kc == 0), stop=(kc == KC - 1)), r=[wbr[x].b(oc // 4), b_.b(x)], w=[pss[x]])
            P.dve(Dl(V_.tensor_tensor, out=a_[:, :n], in0=pss[0][:, :n], in1=g_[:, oc, :n], op=ALU.mult), r=[pss[0], g_], w=[a_])
            P.dve(Dl(V_.tensor_tensor, out=b2_[:, :n], in0=pss[1][:, :n], in1=g_[:, 8 + oc, :n], op=ALU.mult), r=[pss[1], g_], w=[b2_])
            P.dve(Dl(V_.tensor_tensor, out=c_[:, :n], in0=pss[2][:, :n], in1=g_[:, 16 + oc, :n], op=ALU.mult), r=[pss[2], g_], w=[c_])
            P.pool(Dl(G_.tensor_tensor, out=a_[:, :n], in0=a_[:, :n], in1=b2_[:, :n], op=ALU.add), r=[a_, b2_], w=[a_])
            P.pool(Dl(G_.tensor_tensor, out=yb[:, oc, :n], in0=a_[:, :n], in1=c_[:, :n], op=ALU.add), r=[a_, c_], w=[yb.b(oc)])
        for oc in range(KC):
            ps = k.ps[6]
            for kc in range(KC):
                P.pe(Dl(T_.matmul, ps[:, :n], wo[:, kc, oc * 128:(oc + 1) * 128], yb[:, kc, :n], start=(kc == 0), stop=(kc == KC - 1)),
                     r=[wo.b(oc // 4), yb.allb()], w=[ps])
            P.dve(Dl(V_.scalar_tensor_tensor, out=x_[:, oc, :n], in0=ps[:, :n], scalar=mod_col(k, l, 16 + oc, sel), in1=x_[:, oc, :n],
                     op0=ALU.mult, op1=ALU.add), r=[ps, x_, k.mods], w=[x_])
        P.dma("sp", k.x1_d.ap()[:, :, t0:t0 + n], x_[:, :, :n], r=[x_], w=[k.x1_d])
        norm_block(k, l, 1, sel, x_, n, sq, rs, tmp, h_, eps, k.ps[7])
        P.dma("sp", k.h2_d.ap()[:, :, t0:t0 + n], h_[:, :, :n], r=[h_], w=[k.h2_d])
    P.phase_end()


def phase_t2a(k, l):
    nc, P = k.nc, k.P
    V_, A_, T_, G_ = nc.vector, nc.scalar, nc.tensor, nc.gpsimd
    P.phase_begin()
    wst = [P.sb("wst%d" % i, [128, KC, 512], F32) for i in range(2)]
    wf = P.sb("wf", [128, KC, 2 * D_FF], BF16)
    load_w_full(k, wf, k.w_ffn_in.ap()[l].rearrange("(k p) n -> p k n", p=128), 2 * D_FF, KC, wst)
    hb = [P.sb("hb%d" % i, [128, KC, 512], BF16) for i in range(2)]
    gblk = [P.sb("gblk%d" % i, [128, FC, 512], BF16) for i in range(2)]
    sa = [P.sb("sa%d" % i, [128, 512], F32) for i in range(2)]
    it = 0
    for bi, (t0, n, sel) in enumerate(BLKS):
        h_, g_ = hb[bi % 2], gblk[bi % 2]
        P.dma("sp", h_[:, :, :n], k.h2_d.ap()[:, :, t0:t0 + n], w=[h_])
        for j in range(FC):
            pa, pb = k.ps[(it % 4) * 2], k.ps[(it % 4) * 2 + 1]
            s_ = sa[it % 2]
            it += 1
            for kc in range(KC):
                P.pe(Dl(T_.matmul, pa[:, :n], wf[:, kc, j * 128:(j + 1) * 128], h_[:, kc, :n], start=(kc == 0), stop=(kc == KC - 1)),
                     r=[wf.b((j * 128) // 512), h_], w=[pa])
            for kc in range(KC):
                P.pe(Dl(T_.matmul, pb[:, :n], wf[:, kc, D_FF + j * 128:D_FF + (j + 1) * 128], h_[:, kc, :n], start=(kc == 0), stop=(kc == KC - 1)),
                     r=[wf.b((D_FF + j * 128) // 512), h_], w=[pb])
            P.act(Dl(A_.activation, out=s_[:, :n], in_=pa[:, :n], func=AF.Silu), r=[pa], w=[s_])
            P.dve(Dl(V_.tensor_tensor, out=g_[:, j, :n], in0=pb[:, :n], in1=s_[:, :n], op=ALU.mult), r=[pb, s_], w=[g_])
        P.dma("sp", k.gT_d.ap()[:, :, t0:t0 + n], g_[:, :, :n], r=[g_], w=[k.gT_d])
    P.phase_end()


def phase_t2b(k, l, last):
    nc, P = k.nc, k.P
    V_, A_, T_, G_ = nc.vector, nc.scalar, nc.tensor, nc.gpsimd
    P.phase_begin()
    wst = [P.sb("wst%d" % i, [128, 11, 512], F32) for i in range(2)]
    wo = P.sb("wo", [128, FC, 1024], BF16)
    wv = k.w_ffn_out.ap()[l].rearrange("(k p) n -> p k n", p=128)
    gi = 0
    for c0 in (0, 512):
        for k0 in (0, 11):
            w_ = wst[gi % 2]
            gi += 1
            P.dma("sp", w_[:], wv[:, k0:k0 + 11, c0:c0 + 512], w=[w_])
            P.pool(Dl(G_.tensor_copy, wo[:, k0:k0 + 11, c0:c0 + 512], w_[:]), r=[w_], w=[wo.b((c0, k0))])
    gblk = [P.sb("gblk%d" % i, [128, FC, 512], BF16) for i in range(2)]
    xb = [P.sb("xb%d" % i, [128, KC, 512], F32) for i in range(2)]
    sq = P.sb("sq", [128, KC, 512], BF16)
    rs = P.sb("rs", [128, 512], F32)
    tmp = [P.sb("tmp%d" % i, [128, 512], F32) for i in range(2)]
    hb = [P.sb("hb%d" % i, [128, KC, 512], BF16) for i in range(2)] if not last else [None, None]
    eps = P.sb("eps", [128, 1], F32)
    P.dve(Dl(V_.memset, eps[:], EPS), w=[eps])
    if last:
        fg = P.sb("fg", [128, KC], F32)
        P.dma("sp", fg[:], k.final_gT.ap()[:, :], w=[fg])
        hf = P.sb("hf", [128, KC, 512], F32)
        ot = [P.sb("ot%d" % i, [128, 1024], F32) for i in range(2)]
    it = 0
    blks = BLKS[:-1] if last else BLKS
    for bi, (t0, n, sel) in enumerate(blks):
        g_, x_, h_ = gblk[bi % 2], xb[bi % 2], hb[bi % 2]
        P.dma("sp", g_[:, :, :n], k.gT_d.ap()[:, :, t0:t0 + n], w=[g_])
        P.dma("act", x_[:, :, :n], k.x1_d.ap()[:, :, t0:t0 + n], w=[x_])
        for oc in range(KC):
            ps = k.ps[it % 4]
            it += 1
            for j in range(FC):
                P.pe(Dl(T_.matmul, ps[:, :n], wo[:, j, oc * 128:(oc + 1) * 128], g_[:, j, :n], start=(j == 0), stop=(j == FC - 1)),
                     r=[wo.b(((oc // 4) * 512, (j // 11) * 11)), g_], w=[ps])
            P.dve(Dl(V_.scalar_tensor_tensor, out=x_[:, oc, :n], in0=ps[:, :n], scalar=mod_col(k, l, 40 + oc, sel), in1=x_[:, oc, :n],
                     op0=ALU.mult, op1=ALU.add), r=[ps, x_, k.mods], w=[x_])
        if not last:
            P.dma("sp", k.xT_d.ap()[:, :, t0:t0 + n], x_[:, :, :n], r=[x_], w=[k.xT_d])
            norm_block(k, l + 1, 0, sel, x_, n, sq, rs, tmp, h_, eps, k.ps[7])
            P.dma("sp", k.hl_d.ap()[:, :, t0:t0 + n], h_[:, :, :n], r=[h_], w=[k.hl_d])
        else:
            ps7 = k.ps[7]
            P.pool(Dl(G_.tensor_tensor, out=sq[:, :, :n], in0=x_[:, :, :n], in1=x_[:, :, :n], op=ALU.mult), r=[x_], w=[sq])
            for kc in range(KC):
                P.pe(Dl(T_.matmul, ps7[:, :n], k.ones_bf[:], sq[:, kc, :n], start=(kc == 0), stop=(kc == KC - 1)), r=[k.ones_bf, sq], w=[ps7])
            P.act(Dl(A_.activation, out=rs[:, :n], in_=ps7[:, :n], func=AF.Sqrt, scale=1.0 / D, bias=eps[:, 0:1]), r=[ps7, eps], w=[rs])
            P.dve(Dl(V_.reciprocal, out=rs[:, :n], in_=rs[:, :n]), r=[rs], w=[rs])
            for kc in range(KC):
                P.dve(Dl(V_.scalar_tensor_tensor, out=hf[:, kc, :n], in0=x_[:, kc, :n], scalar=fg[:, kc:kc + 1], in1=rs[:, :n],
                         op0=ALU.mult, op1=ALU.mult), r=[x_, fg, rs], w=[hf])
            for tt in range(n // 128):
                o_ = ot[tt % 2]
                for half in range(2):
                    pt_ = k.ps[4 + half]
                    for q4 in range(4):
                        kc = half * 4 + q4
                        P.pe(Dl(T_.transpose, pt_[:, q4 * 128:(q4 + 1) * 128], hf[:, kc, tt * 128:(tt + 1) * 128], k.ident_f[:]),
                             r=[hf, k.ident_f], w=[pt_])
                    if half == 0:
                        P.act(Dl(A_.copy, o_[:, 0:512], pt_[:, :]), r=[pt_], w=[o_])
                    else:
                        P.dve(Dl(V_.tensor_copy, o_[:, 512:1024], pt_[:, :]), r=[pt_], w=[o_])
                P.dma("sp", k.out.ap()[t0 + tt * 128:t0 + (tt + 1) * 128, :], o_[:], r=[o_], w=[k.out.b((t0, tt))])
    P.phase_end()


def host_consts_fn():
    import ml_dtypes
    bf = ml_dtypes.bfloat16
    c = {}
    cc = np.arange(256)
    ang = 2 * np.pi * np.outer(cc, cc) / 256.0
    cs = np.concatenate([np.cos(ang), np.sin(ang)], axis=1)
    c["fn_cs"] = np.ascontiguousarray(cs.reshape(2, 128, 512).transpose(1, 0, 2)).astype(bf)
    t = np.arange(64)
    a64 = 2 * np.pi * np.outer(t, t) / 64.0
    w1 = np.zeros((128, 128))
    w1[0:64, 0:64] = np.cos(a64)
    w1[0:64, 64:128] = -np.sin(a64)
    w1[64:128, 0:64] = -np.sin(a64)
    w1[64:128, 64:128] = -np.cos(a64)
    c["fn_w1"] = w1.astype(bf)
    t1 = np.repeat(np.arange(64), 2)
    atw = 2 * np.pi * np.outer(t1, np.arange(64)) / 4096.0
    c["fn_twr"] = np.ascontiguousarray(np.tile(np.cos(atw)[:, None, :], (1, 4, 1)).reshape(128, 256)).astype(np.float32)
    c["fn_twi"] = np.ascontiguousarray(np.tile(-np.sin(atw)[:, None, :], (1, 4, 1)).reshape(128, 256)).astype(np.float32)
    w3c = np.zeros((128, 128))
    w3s = np.zeros((128, 128))
    for gi in range(2):
        w3c[gi::2, gi * 64:(gi + 1) * 64] = np.cos(a64)
        w3s[gi::2, gi * 64:(gi + 1) * 64] = np.sin(a64)
    c["fn_w3c"] = w3c.astype(bf)
    c["fn_w3s"] = w3s.astype(bf)
    kk = np.arange(256)
    a256 = 2 * np.pi * np.outer(kk, kk) / 256.0
    csx = np.stack([np.cos(a256), -np.sin(a256)], axis=1)
    c["fn_csx"] = np.ascontiguousarray(csx.reshape(2, 128, 2, 256).transpose(1, 0, 2, 3)).astype(bf)
    return c


def phase_fn(k, l):
    nc, P = k.nc, k.P
    V_, A_, T_, G_ = nc.vector, nc.scalar, nc.tensor, nc.gpsimd
    P.phase_begin()
    cs = P.sb("cs", [128, 2, 512], BF16)
    w1 = P.sb("w1", [128, 128], BF16)
    twr = P.sb("twr", [128, 256], F32)
    twi = P.sb("twi", [128, 256], F32)
    w3c = P.sb("w3c", [128, 128], BF16)
    w3s = P.sb("w3s", [128, 128], BF16)
    csx = P.sb("csx", [128, 2, 2, 256], BF16)
    for t_, d_ in ((cs, k.fn_cs), (w1, k.fn_w1), (twr, k.fn_twr), (twi, k.fn_twi), (w3c, k.fn_w3c), (w3s, k.fn_w3s), (csx, k.fn_csx)):
        P.dma("act", t_[:], d_.ap(), w=[t_])
    zc = [P.sb("zc%d" % i, [128, 4, 1024], BF16) for i in range(2)]
    stg = [P.sb("stg%d" % i, [128, 2, 512], BF16) for i in range(3)]
    abc = [P.sb("abc%d" % i, [128, 2, 512], BF16) for i in range(2)]
    yf = [[P.sb("yf%d%d" % (gi, ch), [128, NT], BF16) for ch in range(2)] for gi in range(2)]
    D1 = P.sb("D1", [128, 64, 512], BF16)
    H = P.sb("H", [128, 2, 64, 256], BF16)
    wa = [P.sb("fwa%d" % i, [128, 256], F32) for i in range(2)]
    wb = [P.sb("fwb%d" % i, [128, 256], F32) for i in range(2)]
    wc = [P.sb("fwc%d" % i, [128, 256], F32) for i in range(2)]
    wd = [P.sb("fwd%d" % i, [128, 256], F32) for i in range(2)]
    ev = 0
    psi = 0
    for gp in range(2):
        chunks = [(i * 1024, 1024) for i in range(NL // 1024)] + [(NL, NCX)]
        si = 0
        for ci, (c0, cn) in enumerate(chunks):
            z_ = zc[ci % 2]
            for gi in range(2):
                for cc in range(2):
                    r0 = F_FN + (2 * gp + gi) * 256 + cc * 128
                    P.dma("sp", z_[:, gi * 2 + cc, :cn], k.uF_d.ap()[r0:r0 + 128, c0:c0 + cn], w=[z_.b(gi * 2 + cc)])
            for tl in range(cn // 128):
                tok0 = c0 + tl * 128
                isctx = tok0 >= NL
                s_ = abc[(tok0 - NL) // 128] if isctx else stg[si % 3]
                si += 1
                for gi in range(2):
                    ps = k.ps[psi % 4]
                    psi += 1
                    for cc in range(2):
                        P.pe(Dl(T_.matmul, ps[:, :], z_[:, gi * 2 + cc, tl * 128:(tl + 1) * 128], cs[:, cc, :], start=(cc == 0), stop=(cc == 1)),
                             r=[z_.b(gi * 2 + cc), cs], w=[ps])
                    ev += 1
                    if ev % 2:
                        P.act(Dl(A_.copy, s_[:, gi, :], ps[:, :]), r=[ps], w=[s_.b(gi)])
                    else:
                        P.dve(Dl(V_.tensor_copy, s_[:, gi, :], ps[:, :]), r=[ps], w=[s_.b(gi)])
                if not isctx:
                    for ab in range(2):
                        P.dma("sp", k.AB_d.ap()[ab, tok0:tok0 + 128, :].rearrange("t (g c) -> t g c", g=2), s_[:, :, ab * 256:(ab + 1) * 256],
                              r=s_.allb(), w=[k.AB_d.b((ab, tok0 // 2048))])
        if FN_STOP < 1:
            continue
        for ab in range(2):
            for hh in range(2):
                P.dma("sp", D1[ab * 64 + hh * 32:ab * 64 + hh * 32 + 32, :, :],
                      k.AB_d.ap()[ab, hh * 2048:(hh + 1) * 2048, :].rearrange("(t2 t1) c -> t2 t1 c", t1=64),
                      r=[k.AB_d.b((ab, hh))], w=[D1.b((ab, hh))])
        d1r = D1.allb()
        for cb in range(64):
            ps = k.ps[psi % 4]
            psi += 1
            for ci in range(4):
                cp = cb * 4 + ci
                P.pe(Dl(T_.matmul, ps[:, ci * 128:(ci + 1) * 128], D1[:, :, cp:512:256], w1[:, :], start=True, stop=True), r=[d1r, w1], w=[ps])
            psv = ps[:, :].rearrange("p (c r j) -> p c r j", c=4, r=2)
            gr, gim = psv[:, :, 0, :], psv[:, :, 1, :]
            a_, b_, c_, d_ = wa[cb % 2], wb[cb % 2], wc[cb % 2], wd[cb % 2]
            v4 = lambda t: t[:, :].rearrange("p (c j) -> p c j", c=4)
            P.dve(Dl(V_.tensor_tensor, out=v4(a_), in0=gr, in1=v4(twr), op=ALU.mult), r=[ps, twr], w=[a_])
            P.dve(Dl(V_.tensor_tensor, out=v4(b_), in0=gim, in1=v4(twi), op=ALU.mult), r=[ps, twi], w=[b_])
            P.dve(Dl(V_.tensor_tensor, out=v4(c_), in0=gr, in1=v4(twi), op=ALU.mult), r=[ps, twi], w=[c_])
            P.dve(Dl(V_.tensor_tensor, out=v4(d_), in0=gim, in1=v4(twr), op=ALU.mult), r=[ps, twr], w=[d_])
            P.pool(Dl(G_.tensor_tensor, out=H[:, 0, :, cb * 4:(cb + 1) * 4].rearrange("p j c -> p c j"), in0=v4(a_), in1=v4(b_), op=ALU.subtract), r=[a_, b_], w=[H.b(cb)])
            P.pool(Dl(G_.tensor_tensor, out=H[:, 1, :, cb * 4:(cb + 1) * 4].rearrange("p j c -> p c j"), in0=v4(c_), in1=v4(d_), op=ALU.add), r=[c_, d_], w=[H.b(cb)])
        if FN_STOP < 1.5:
            continue
        hr = H.allb()
        for ch in range(2):
            for jb in range(16):
                ps = k.ps[psi % 4]
                psi += 1
                for ji in range(4):
                    j2 = jb * 4 + ji
                    P.pe(Dl(T_.matmul, ps[:, ji * 128:(ji + 1) * 128], H[:, 0, j2, ch * 128:(ch + 1) * 128], w3c[:, :], start=True, stop=False),
                         r=[hr, w3c], w=[ps])
                    P.pe(Dl(T_.matmul, ps[:, ji * 128:(ji + 1) * 128], H[:, 1, j2, ch * 128:(ch + 1) * 128], w3s[:, :], start=False, stop=True),
                         r=[hr, w3s], w=[ps])
                for ji in range(4):
                    if FN_STOP == 1.5:
                        break
                    j2 = jb * 4 + ji
                    for gi in range(2):
                        o = yf[gi][ch][:, j2:NL:64]
                        i_ = ps[:, ji * 128 + gi * 64:ji * 128 + (gi + 1) * 64]
                        ev += 1
                        if (ev % 2 or FN_STOP == 2.1) and FN_STOP != 2.2:
                            P.act(Dl(A_.mul, o, i_, 1.0 / 1024.0), r=[ps], w=[yf[gi][ch].b(jb)])
                        else:
                            P.dve(Dl(V_.tensor_scalar, out=o, in0=i_, scalar1=1.0 / 1024.0, scalar2=None, op0=ALU.mult), r=[ps], w=[yf[gi][ch].b(jb)])
        if FN_STOP < 3:
            continue
        for gi in range(2):
            for ch in range(2):
                ps = k.ps[psi % 4]
                psi += 1
                idx = 0
                for kt in range(2):
                    for ab in range(2):
                        P.pe(Dl(T_.matmul, ps[:, 0:256], abc[kt][:, gi, ab * 256 + ch * 128:ab * 256 + (ch + 1) * 128], csx[:, kt, ab, :],
                                start=(idx == 0), stop=(idx == 3)), r=[abc[kt].allb(), csx], w=[ps])
                        idx += 1
                P.act(Dl(A_.mul, yf[gi][ch][:, NL:NT], ps[:, 0:256], 1.0 / 256.0), r=[ps], w=[yf[gi][ch].b("ctx")])
                r0 = (2 * gp + gi) * 256 + ch * 128
                P.dma("sp", k.br_d.ap()[2, r0:r0 + 128, :], yf[gi][ch][:], r=yf[gi][ch].allb(), w=[k.br_d.b(("fn", r0))])
    P.phase_end()


NEG = -30000.0


def host_consts_ml():
    c = {}
    r = np.arange(128)
    c["ml_triF"] = (r[:, None] <= r[None, :]).astype(np.float32)
    c["ml_triB"] = (r[:, None] >= r[None, :]).astype(np.float32)
    c["ml_maskF"] = np.where(r[None, :] <= r[:, None], 0.0, NEG).astype(np.float32)
    c["ml_maskB"] = np.where(r[None, :] >= r[:, None], 0.0, NEG).astype(np.float32)
    return c


def phase_ml(k, l):
    nc, P = k.nc, k.P
    V_, A_, T_, G_ = nc.vector, nc.scalar, nc.tensor, nc.gpsimd
    P.phase_begin()
    ones_f = P.sb("ones_f", [128, 128], F32)
    tri = [P.sb("triF", [128, 128], F32), P.sb("triB", [128, 128], F32)]
    msk = [P.sb("maskF", [128, 128], F32), P.sb("maskB", [128, 128], F32)]
    P.dve(Dl(V_.memset, ones_f[:], 1.0), w=[ones_f])
    for t_, d_ in ((tri[0], k.ml_triF), (tri[1], k.ml_triB), (msk[0], k.ml_maskF), (msk[1], k.ml_maskB)):
        P.dma("act", t_[:], d_.ap(), w=[t_])
    GL = P.sb("GL", [128, NTILE, 16], F32)
    gbias = P.sb("gbias", [128, NTILE, 16], F32)
    P.dma("sp", GL[:], k.uT_g.ap().rearrange("(n p) c -> p n c", p=128), w=[GL])
    P.dma("sp", gbias[:].rearrange("p n c -> p (n c)"), k.ml_gate_bR.ap()[l:l + 1, :].partition_broadcast(128), w=[gbias])
    P.dve(Dl(V_.tensor_tensor, out=GL[:], in0=GL[:], in1=gbias[:], op=ALU.add), r=[GL, gbias], w=[GL])
    gax = P.sb("gax", [128, NTILE, 2, 4], F32)
    gmn = P.sb("gmn", [128, NTILE, 2, 4], F32)
    GLf = GL[:].rearrange("p n (a c) -> p n a c", a=2)[:, :, :, 4:8]
    P.dve(Dl(V_.scalar_tensor_tensor, out=gax[:], in0=GLf, scalar=-1.0, in1=GLf, op0=ALU.mult, op1=ALU.max), r=[GL], w=[gax])
    P.act(Dl(A_.activation, out=gax[:], in_=gax[:], func=AF.Exp, scale=-1.0), r=[gax], w=[gax])
    P.act(Dl(A_.activation, out=gax[:], in_=gax[:], func=AF.Ln, bias=ones_f[:, 0:1], scale=1.0), r=[gax, ones_f], w=[gax])
    P.dve(Dl(V_.tensor_single_scalar, out=gmn[:], in_=GLf, scalar=0.0, op=ALU.min), r=[GL], w=[gmn])
    P.dve(Dl(V_.tensor_tensor, out=GLf, in0=gmn[:], in1=gax[:], op=ALU.subtract), r=[gmn, gax], w=[GL])

    NH = 2
    qT = [[P.sb("mq%d%d" % (i, c), [128, NT], BF16) for c in range(2)] for i in range(NH)]
    kT = [[P.sb("mk%d%d" % (i, c), [128, NT], BF16) for c in range(2)] for i in range(NH)]
    ktok = [P.sb("mkt%d" % i, [128, NTILE, 256], BF16) for i in range(NH)]
    vaug = [P.sb("mv%d" % i, [128, NTILE, 260], BF16) for i in range(NH)]
    C32 = [[P.sb("C32_%d%d" % (i, c), [128, 257], F32) for c in range(2)] for i in range(2 * NH)]
    Cb = [[P.sb("Cb_%d%d" % (i, c), [128, 260], BF16) for c in range(2)] for i in range(2 * NH)]
    mst = P.sb("mst", [128, 2, 2 * NH], F32)
    NB = 4
    IB = [P.sb("IB%d" % i, [128, 128], F32) for i in range(NB)]
    NLB = [P.sb("NLB%d" % i, [128, 128], F32) for i in range(NB)]
    wm = [P.sb("wm%d" % i, [128, 128], F32) for i in range(NB)]
    Dm = [P.sb("Dm%d" % i, [128, 128], F32) for i in range(NB)]
    st = [P.sb("st%d" % i, [128, 16], F32) for i in range(NB)]
    ex = [P.sb("ex%d" % i, [128, 4], F32) for i in range(NB)]
    a_ = [P.sb("a%d" % i, [128, 128], BF16) for i in range(NB)]
    aTs = [P.sb("aTs%d" % i, [128, 128], BF16) for i in range(NB)]
    Xs = [P.sb("Xs%d" % i, [128, 257], F32) for i in range(NB)]
    Z = [P.sb("Z%d" % i, [128, 257], F32) for i in range(NB)]
    dmr = [P.sb("dmr%d" % i, [128, 2], F32) for i in range(NB)]
    ho = [P.sb("ho%d" % i, [128, 256], F32) for i in range(NB)]
    kw = [P.sb("kw%d" % i, [128, 256], BF16) for i in range(NB)]
    psG, psS, psA, psY, psX, psC = k.ps[0], k.ps[1], k.ps[2], k.ps[3], k.ps[4], [k.ps[5], k.ps[6]]
    psA_bf = psA[:, :].bitcast(BF16)
    for i in range(NH):
        P.dve(Dl(V_.memset, vaug[i][:, :, 256:260], 1.0), w=[vaug[i].b("ones")])

    fwd_tiles = CTX_TILES + LAT_TILES
    bwd_tiles = CTX_TILES[::-1] + LAT_TILES[::-1]
    for hp in range(4 // NH):
        for i in range(NH):
            h = hp * NH + i
            for c in range(2):
                P.dma("sp", qT[i][c][:], k.uF_d.ap()[F_MLQ + h * 256 + c * 128:F_MLQ + h * 256 + (c + 1) * 128, :], w=[qT[i][c]])
                P.dma("sp", kT[i][c][:], k.uF_d.ap()[F_MLK + h * 256 + c * 128:F_MLK + h * 256 + (c + 1) * 128, :], w=[kT[i][c]])
            P.dma("sp", ktok[i][:], k.uT_mlk.ap()[:, h * 256:(h + 1) * 256].rearrange("(n p) d -> p n d", p=128), w=[ktok[i]])
            P.dma("sp", vaug[i][:, :, 0:256], k.uT_mlv.ap()[:, h * 256:(h + 1) * 256].rearrange("(n p) d -> p n d", p=128), w=[vaug[i].b("v")])
        for ch in range(2 * NH):
            for c in range(2):
                P.dve(Dl(V_.memset, C32[ch][c][:], 0.0), w=[C32[ch][c]])
                P.pool(Dl(G_.memset, Cb[ch][c][:], 0.0), w=[Cb[ch][c]])
        P.dve(Dl(V_.memset, mst[:], 0.0), w=[mst])
        def ctxv(idx):
            step, ch = idx // (2 * NH), idx % (2 * NH)
            i, d = ch // 2, ch % 2
            h = hp * NH + i
            ti = (fwd_tiles if d == 0 else bwd_tiles)[step]
            tsl = slice(ti * 128, (ti + 1) * 128)
            b = idx % NB
            icol = GL[:, ti, 8 * d + h:8 * d + h + 1]
            fcol = GL[:, ti, 8 * d + 4 + h:8 * d + 4 + h + 1]
            mcur = mst[:, step % 2, ch:ch + 1]
            mnxt = mst[:, (step + 1) % 2, ch:ch + 1]
            s_, e_ = st[b], ex[b]
            vr = [vaug[i].b("v"), vaug[i].b("ones")]
            return step, ch, i, d, h, ti, tsl, b, icol, fcol, mcur, mnxt, s_, e_, vr

        def stage_a(idx):
            step, ch, i, d, h, ti, tsl, b, icol, fcol, mcur, mnxt, s_, e_, vr = ctxv(idx)
            P.dve(Dl(V_.tensor_scalar, out=IB[b][:], in0=ones_f[:], scalar1=icol, scalar2=None, op0=ALU.mult), r=[ones_f, GL], w=[IB[b]])
            P.dve(Dl(V_.tensor_scalar, out=NLB[b][:], in0=ones_f[:], scalar1=fcol, scalar2=-1.0, op0=ALU.mult, op1=ALU.mult), r=[ones_f, GL], w=[NLB[b]])
            P.pe(Dl(T_.matmul, psG[:, 0:128], IB[b][:], k.ident_f[:], start=True, stop=False), r=[IB[b], k.ident_f], w=[psG])
            P.pe(Dl(T_.matmul, psG[:, 0:128], NLB[b][:], tri[d][:], start=False, stop=True), r=[NLB[b], tri[d]], w=[psG])
            P.pe(Dl(T_.matmul, psG[:, 128:129], tri[d][:], fcol, start=True, stop=True), r=[tri[d], GL], w=[psG])
            P.pe(Dl(T_.matmul, psG[:, 136:137], ones_f[:], fcol, start=True, stop=True), r=[ones_f, GL], w=[psG])
            P.dve(Dl(V_.tensor_tensor, out=wm[b][:], in0=psG[:, 0:128], in1=msk[d][:], op=ALU.add), r=[psG, msk[d]], w=[wm[b]])
            P.dve(Dl(V_.reduce_max, out=s_[:, 1:2], in_=psG[:, 0:128], axis=AX.X), r=[psG], w=[s_])
            P.dve(Dl(V_.tensor_copy, s_[:, 2:4], psG[:, 128:137:8]), r=[psG], w=[s_])
            P.dve(Dl(V_.reduce_max, out=s_[:, 0:1], in_=wm[b][:], axis=AX.X), r=[wm[b]], w=[s_])
            P.dve(Dl(V_.tensor_scalar, out=s_[:, 4:6], in0=s_[:, 0:2], scalar1=mcur, scalar2=None, op0=ALU.max), r=[s_, mst], w=[s_])
            P.dve(Dl(V_.tensor_scalar, out=s_[:, 6:8], in0=s_[:, 4:6], scalar1=-1.0, scalar2=None, op0=ALU.mult), r=[s_], w=[s_])
            P.dve(Dl(V_.tensor_tensor, out=mnxt, in0=s_[:, 3:4], in1=s_[:, 5:6], op=ALU.add), r=[s_], w=[mst])
            P.dve(Dl(V_.tensor_tensor, out=s_[:, 8:9], in0=icol, in1=s_[:, 2:3], op=ALU.subtract), r=[s_, GL], w=[s_])
            P.act(Dl(A_.activation, out=Dm[b][:], in_=wm[b][:], func=AF.Exp, bias=s_[:, 6:7], scale=1.0), r=[wm[b], s_], w=[Dm[b]])
            P.act(Dl(A_.activation, out=e_[:, 0:1], in_=mcur, func=AF.Exp, bias=s_[:, 6:7], scale=1.0), r=[mst, s_], w=[e_])
            P.act(Dl(A_.activation, out=e_[:, 1:2], in_=mcur, func=AF.Exp, bias=s_[:, 7:8], scale=1.0), r=[mst, s_], w=[e_])
            P.act(Dl(A_.activation, out=e_[:, 2:3], in_=s_[:, 8:9], func=AF.Exp, bias=s_[:, 7:8], scale=1.0), r=[s_], w=[e_])
            P.act(Dl(A_.activation, out=e_[:, 3:4], in_=s_[:, 2:3], func=AF.Exp, bias=s_[:, 6:7], scale=-1.0), r=[s_], w=[e_])

        def stage_b(idx):
            step, ch, i, d, h, ti, tsl, b, icol, fcol, mcur, mnxt, s_, e_, vr = ctxv(idx)
            for c in range(2):
                P.pe(Dl(T_.matmul, psS[:, 0:128], qT[i][c][:, tsl], kT[i][c][:, tsl], start=(c == 0), stop=(c == 1)),
                     r=[qT[i][c], kT[i][c]], w=[psS])
            P.dve(Dl(V_.tensor_tensor, out=a_[b][:], in0=psS[:, 0:128], in1=Dm[b][:], op=ALU.mult), r=[psS, Dm[b]], w=[a_[b]])
            P.pe(Dl(T_.transpose, psA_bf[:, 0:128], a_[b][:], k.ident_bf[:]), r=[a_[b], k.ident_bf], w=[psA])
            P.act(Dl(A_.copy, aTs[b][:], psA_bf[:, 0:128]), r=[psA], w=[aTs[b]])
            P.pe(Dl(T_.matmul, psY[:, 0:257], aTs[b][:], vaug[i][:, ti, 0:257], start=True, stop=True), r=[aTs[b], vr], w=[psY])
            for c in range(2):
                P.pe(Dl(T_.matmul, psX[:, 0:257], qT[i][c][:, tsl], Cb[ch][c][:, 0:257], start=(c == 0), stop=(c == 1)),
                     r=[qT[i][c], Cb[ch][c]], w=[psX])
            P.act(Dl(A_.activation, out=Xs[b][:], in_=psX[:, 0:257], func=AF.Copy, scale=e_[:, 0:1]), r=[psX, e_], w=[Xs[b]])
            P.dve(Dl(V_.tensor_tensor, out=Z[b][:], in0=psY[:, 0:257], in1=Xs[b][:], op=ALU.add), r=[psY, Xs[b]], w=[Z[b]])
            P.dve(Dl(V_.scalar_tensor_tensor, out=dmr[b][:, 0:1], in0=Z[b][:, 256:257], scalar=-1.0, in1=Z[b][:, 256:257], op0=ALU.mult, op1=ALU.max),
                  r=[Z[b]], w=[dmr[b]])
            P.dve(Dl(V_.tensor_tensor, out=dmr[b][:, 0:1], in0=dmr[b][:, 0:1], in1=e_[:, 3:4], op=ALU.max), r=[dmr[b], e_], w=[dmr[b]])
            P.dve(Dl(V_.reciprocal, out=dmr[b][:, 1:2], in_=dmr[b][:, 0:1]), r=[dmr[b]], w=[dmr[b]])
            P.act(Dl(A_.activation, out=ho[b][:], in_=Z[b][:, 0:256], func=AF.Copy, scale=dmr[b][:, 1:2]), r=[Z[b], dmr[b]], w=[ho[b]])
            P.dma("sp", k.h_d.ap()[d, ti * 128:(ti + 1) * 128, h * 256:(h + 1) * 256], ho[b][:], r=[ho[b]], w=[k.h_d.b((d, ti, h))])

        def stage_c(idx):
            step, ch, i, d, h, ti, tsl, b, icol, fcol, mcur, mnxt, s_, e_, vr = ctxv(idx)
            P.act(Dl(A_.activation, out=kw[b][:], in_=ktok[i][:, ti, :], func=AF.Copy, scale=e_[:, 2:3]), r=[ktok[i], e_], w=[kw[b]])
            for c in range(2):
                P.pe(Dl(T_.matmul, psC[c][:, 0:257], kw[b][:, c * 128:(c + 1) * 128], vaug[i][:, ti, 0:257], start=True, stop=True),
                     r=[kw[b], vr], w=[psC[c]])
                P.dve(Dl(V_.scalar_tensor_tensor, out=C32[ch][c][:], in0=C32[ch][c][:], scalar=e_[:, 1:2], in1=psC[c][:, 0:257],
                         op0=ALU.mult, op1=ALU.add), r=[C32[ch][c], e_, psC[c]], w=[C32[ch][c]])
                P.act(Dl(A_.copy, Cb[ch][c][:, 0:257], C32[ch][c][:]), r=[C32[ch][c]], w=[Cb[ch][c]])

        NI = NTILE * 2 * NH
        SK = 2
        for idx in range(NI + SK):
            if idx < NI:
                stage_a(idx)
            if idx >= SK:
                stage_b(idx - SK)
                stage_c(idx - SK)
    P.phase_end()

    P.phase_begin()
    hgb = P.sb("hgb", [128, 1024], F32)
    P.dma("sp", hgb[:], k.ml_head_g.ap()[l:l + 1, :].partition_broadcast(128), w=[hgb])
    eps = P.sb("eps", [128, 1], F32)
    P.dve(Dl(V_.memset, eps[:], EPS), w=[eps])
    hf = [P.sb("hf%d" % i, [128, 1024], F32) for i in range(2)]
    hb_ = [P.sb("hbk%d" % i, [128, 1024], F32) for i in range(2)]
    og = [P.sb("og%d" % i, [128, 1024], BF16) for i in range(2)]
    sqm = P.sb("sqm", [128, 1024], F32)
    ssm = [P.sb("ssm%d" % i, [128, 4], F32) for i in range(2)]
    ym = [P.sb("ym%d" % i, [128, 1024], BF16) for i in range(2)]
    fst = [P.sb("fstm%d" % i, [128, KC, 512], BF16) for i in range(2)]
    pst = [k.ps[0][:, :].bitcast(BF16), k.ps[1][:, :].bitcast(BF16)]
    for ti in range(NTILE):
        b = ti % 2
        f_, b_, o_, y_, s_ = hf[b], hb_[b], og[b], ym[b], ssm[b]
        P.dma("sp", f_[:], k.h_d.ap()[0, ti * 128:(ti + 1) * 128, :], r=[k.h_d.b((0, ti, h)) for h in range(4)], w=[f_])
        P.dma("sp", b_[:], k.h_d.ap()[1, ti * 128:(ti + 1) * 128, :], r=[k.h_d.b((1, ti, h)) for h in range(4)], w=[b_])
        P.dma("act", o_[:], k.uT_mlo.ap()[ti * 128:(ti + 1) * 128, :], w=[o_])
        P.pool(Dl(G_.tensor_tensor, out=f_[:], in0=f_[:], in1=b_[:], op=ALU.add), r=[f_, b_], w=[f_])
        P.pool(Dl(G_.tensor_tensor, out=sqm[:], in0=f_[:], in1=f_[:], op=ALU.mult), r=[f_], w=[sqm])
        P.dve(Dl(V_.reduce_sum, out=s_[:], in_=sqm[:].rearrange("p (h d) -> p h d", h=4), axis=AX.X), r=[sqm], w=[s_])
        P.act(Dl(A_.activation, out=s_[:], in_=s_[:], func=AF.Sqrt, scale=1.0 / 256, bias=eps[:, 0:1]), r=[s_, eps], w=[s_])
        P.dve(Dl(V_.reciprocal, out=s_[:], in_=s_[:]), r=[s_], w=[s_])
        for h in range(4):
            P.act(Dl(A_.activation, out=f_[:, h * 256:(h + 1) * 256], in_=f_[:, h * 256:(h + 1) * 256], func=AF.Copy, scale=s_[:, h:h + 1]),
                  r=[f_, s_], w=[f_])
        P.dve(Dl(V_.tensor_tensor, out=f_[:], in0=f_[:], in1=hgb[:], op=ALU.mult), r=[f_, hgb], w=[f_])
        P.dve(Dl(V_.tensor_tensor, out=y_[:], in0=f_[:], in1=o_[:], op=ALU.mult), r=[f_, o_], w=[y_])
        ps = k.ps[b]
        for c in range(KC):
            P.pe(Dl(T_.transpose, pst[b][:, c * 128:(c + 1) * 128], y_[:, c * 128:(c + 1) * 128], k.ident_bf[:]), r=[y_, k.ident_bf], w=[ps])
        g4, t4 = ti // 4, ti % 4
        fs = fst[g4 % 2]
        o_ap = fs[:, :, t4 * 128:(t4 + 1) * 128]
        i_ap = pst[b][:, :].rearrange("p (c t) -> p c t", c=KC)
        if ti % 2:
            P.act(Dl(A_.copy, o_ap, i_ap), r=[ps], w=[fs.b(t4)])
        else:
            P.dve(Dl(V_.tensor_copy, o_ap, i_ap), r=[ps], w=[fs.b(t4)])
        if t4 == 3 or ti == NTILE - 1:
            nt_ = (t4 + 1) * 128
            P.dma("sp", k.br_d.ap()[0].rearrange("(c p) t -> p c t", p=128)[:, :, g4 * 512:g4 * 512 + nt_], fs[:, :, :nt_], r=fs.allb(),
                  w=[k.br_d.b(("ml", g4))])
    P.phase_end()


_CACHE = {}


def kernel(**inputs):
    inp = {k_: np.asarray(v) for k_, v in inputs.items()}
    if "nc" not in _CACHE:
        _CACHE["nc"] = build()[0]
        _CACHE["consts"] = host_consts()
    nc = _CACHE["nc"]
    consts = _CACHE["consts"]
    B = inp["x"].shape[0]
    in_maps = [prep_core_inputs(inp, c % B, consts) for c in range(8)]
    res = run_bass_kernel_spmd(nc, in_maps, core_ids=list(range(8)))
    out = np.stack([np.asarray(res.results[b]["out"]) for b in range(B)], axis=0)
    return out.astype(np.float32)
```

```python
import math
import numpy as np
import concourse.bass as bass
import concourse.mybir as mybir
from concourse.bass_utils import run_bass_kernel_spmd

F32 = mybir.dt.float32
BF16 = mybir.dt.bfloat16
AF = mybir.ActivationFunctionType
ALU = mybir.AluOpType
AX = mybir.AxisListType

D = 1024
KC = 8
DEPTH = 4
NL = 4096
NCX = 256
NT = NL + NCX
NTILE = NT // 128
D_IN = 11280
D_FF = 2816
FC = D_FF // 128
FN_STOP = 99
EPS = 1e-6
SBUF_BASE = 16640
SBUF_BYTES = 229376


class Buf:
    __slots__ = ("name", "w", "rs", "rd", "excl")

    def __init__(self, name=""):
        self.name = name
        self.excl = False
        self.w = None
        self.rs = {}
        self.rd = []


class Ins:
    __slots__ = ("eng", "fn", "deps", "sig", "sigval", "dma", "dsem", "dval", "dprev")

    def __init__(self, eng, fn, dma):
        self.eng = eng
        self.fn = fn
        self.dma = dma
        self.deps = set()
        self.sig = False
        self.sigval = 0
        self.dsem = None
        self.dval = 0
        self.dprev = 0


class Tl:
    def __init__(self, t, name):
        self.t = t
        self.name = name
        self.buf = Buf(name)
        self.subs = {}

    def b(self, i):
        s = self.subs.get(i)
        if s is None:
            s = Buf("%s.%s" % (self.name, i))
            self.subs[i] = s
        return s

    def allb(self):
        return list(self.subs.values())

    def ap(self):
        return self.t.ap()

    def __getitem__(self, k):
        return self.t[k]


class PsBank(Tl):
    def __init__(self, t, i):
        Tl.__init__(self, t, "ps%d" % i)
        self.i = i
        self.buf.excl = True

    def __getitem__(self, key):
        if isinstance(key, tuple):
            return self.t[key[0], self.i, key[1]]
        return self.t[key, self.i, :]


def _bufs(lst):
    out = []
    for x in lst:
        if x is None:
            continue
        if isinstance(x, Tl):
            out.append(x.buf)
        elif isinstance(x, Buf):
            out.append(x)
        else:
            out.extend(_bufs(x))
    return out


class Prog:
    ENGS = ("pe", "act", "dve", "pool", "sp")
    NDMA = {"sp": 40, "act": 16, "pool": 16}

    def __init__(self, nc):
        self.nc = nc
        self.ins = []
        self.eng = {"pe": nc.tensor, "act": nc.scalar, "dve": nc.vector, "pool": nc.gpsimd, "sp": nc.sync}
        self.last = {e: None for e in self.ENGS}
        self.dmas_open = []
        self.sb_off = SBUF_BASE
        self.sb_mark = 0
        self.nid = 0

    def sb(self, name, shape, dtype, persist=False):
        nbytes = int(np.prod(shape[1:])) * (4 if dtype == F32 else 2)
        nbytes = (nbytes + 63) // 64 * 64
        off = self.sb_off
        assert off + nbytes <= SBUF_BYTES, "SBUF overflow %s %d" % (name, off + nbytes)
        self.nid += 1
        t = self.nc.alloc_sbuf_tensor_at("%s_%d" % (name, self.nid), list(shape), dtype, offset=off)
        self.sb_off = off + nbytes
        return Tl(t, name)

    def phase_begin(self):
        self.sb_mark_stack = getattr(self, "sb_mark_stack", [])
        self.sb_mark_stack.append(self.sb_off)

    def phase_end(self):
        self.barrier()
        self.sb_off = self.sb_mark_stack.pop()

    def dram(self, name, shape, dtype, kind="Internal"):
        t = self.nc.dram_tensor(name, list(shape), dtype, kind=kind)
        return Tl(t, name)

    def op(self, eng, fn, r=(), w=(), dma=False):
        if getattr(self, "capture", None) is not None:
            self.capture.append((eng, fn, r, w, dma))
            return None
        i = Ins(eng, fn, dma)
        rb = _bufs(r)
        wb = _bufs(w)
        ex = [b for b in rb if b.excl]
        if ex:
            rb = [b for b in rb if not b.excl]
            wb = wb + [b for b in ex if b not in wb]
        deps = i.deps
        for b in rb:
            if b.w is not None:
                if not (eng == "pe" and b.w.eng == "pe" and not b.w.dma):
                    deps.add(b.w)
        for b in wb:
            x = b.w
            if x is not None and (x.eng != eng or x.dma or dma):
                deps.add(x)
            for x in b.rs.values():
                if x.eng != eng or dma:
                    deps.add(x)
            for x in b.rd:
                deps.add(x)
        for b in rb:
            if dma:
                b.rd.append(i)
            else:
                b.rs[eng] = i
        for b in wb:
            b.w = i
            b.rs = {}
            b.rd = []
        self.ins.append(i)
        if dma:
            self.dmas_open.append(i)
        else:
            self.last[eng] = i
        return i

    def pe(self, fn, r=(), w=()):
        return self.op("pe", fn, r, w)

    def act(self, fn, r=(), w=()):
        return self.op("act", fn, r, w)

    def dve(self, fn, r=(), w=()):
        return self.op("dve", fn, r, w)

    def pool(self, fn, r=(), w=()):
        return self.op("pool", fn, r, w)

    def dma(self, q, out, in_, r=(), w=()):
        e = self.eng[q]
        return self.op(q, lambda: e.dma_start(out=out, in_=in_), r, w, dma=True)

    def barrier(self):
        prev = [x for x in self.last.values() if x is not None] + list(self.dmas_open)
        self.dmas_open = []
        for e in self.ENGS:
            eh = self.eng[e]
            i = Ins(e, (lambda eh=eh: eh.nop()), False)
            i.deps = set(prev)
            self.ins.append(i)
            self.last[e] = i

    def finalize(self):
        nc = self.nc
        self.barrier()
        esem = {e: nc.alloc_semaphore("es_" + e) for e in self.ENGS}
        dsems = {q: [nc.alloc_semaphore("ds_%s%d" % (q, k)) for k in range(n)] for q, n in self.NDMA.items()}
        duse = {q: [0] * n for q, n in self.NDMA.items()}
        drr = {q: 0 for q in self.NDMA}
        for i in self.ins:
            for d in i.deps:
                if not d.dma:
                    d.sig = True
        cnt = {e: 0 for e in self.ENGS}
        for i in self.ins:
            if i.dma:
                q = i.eng
                k = drr[q]
                drr[q] = (k + 1) % self.NDMA[q]
                i.dsem = dsems[q][k]
                i.dprev = duse[q][k]
                duse[q][k] += 16
                i.dval = duse[q][k]
            elif i.sig:
                cnt[i.eng] += 1
                i.sigval = cnt[i.eng]
        seen = {e: {} for e in self.ENGS}
        nwait = 0
        self.trace = {e: [] for e in self.ENGS}
        for i in self.ins:
            e = i.eng
            eh = self.eng[e]
            need = {}
            for d in i.deps:
                if d.dma:
                    s, v = d.dsem, d.dval
                else:
                    s, v = esem[d.eng], d.sigval
                if need.get(s, 0) < v:
                    need[s] = v
            if i.dma and i.dprev > 0:
                if need.get(i.dsem, 0) < i.dprev:
                    need[i.dsem] = i.dprev
            sn = seen[e]
            wl = []
            for s, v in need.items():
                if sn.get(s, 0) < v:
                    eh.wait_ge(s, v)
                    sn[s] = v
                    nwait += 1
                    wl.append((id(s), v))
            ins = i.fn()
            if i.dma:
                ins.then_inc(i.dsem, 16)
                self.trace[e].append((wl, (id(i.dsem), 16)))
            elif i.sig:
                ins.then_inc(esem[e], 1)
                self.trace[e].append((wl, (id(esem[e]), 1)))
            else:
                self.trace[e].append((wl, None))
        self.simulate()
        return dict(n_ins=len(self.ins), n_wait=nwait, sig=dict(cnt))

    def simulate(self):
        sem = {}
        pc = {e: 0 for e in self.ENGS}
        tr = self.trace
        while True:
            prog = False
            done = True
            for e in self.ENGS:
                t = tr[e]
                while pc[e] < len(t):
                    wl, inc = t[pc[e]]
                    if any(sem.get(s, 0) < v for s, v in wl):
                        break
                    if inc is not None:
                        sem[inc[0]] = sem.get(inc[0], 0) + inc[1]
                    pc[e] += 1
                    prog = True
                if pc[e] < len(t):
                    done = False
            if done:
                return
            if not prog:
                raise RuntimeError("sync deadlock at %s" % {e: (pc[e], len(tr[e])) for e in self.ENGS})


BLKS = [(i * 512, 512, 0) for i in range(NL // 512)] + [(NL, NCX, 1)]
TB1 = 256
BLKS1 = [(i * TB1, TB1, 0) for i in range(NL // TB1)] + [(NL, NCX, 1)]
LAT_TILES = list(range(NL // 128))
CTX_TILES = [NL // 128 + i for i in range(NCX // 128)]

C_MLQ, C_MLK, C_MLV, C_MLO, C_MLG, C_DAQ, C_DAK, C_DAV, C_FN, C_GP = 0, 1024, 2048, 3072, 4096, 4112, 5136, 6160, 7184, 8208
F_MLQ, F_MLK, F_DAQ, F_DAK, F_FN = 0, 1024, 2048, 3072, 4096


def host_consts():
    c = {}
    c["ident_f"] = np.eye(128, dtype=np.float32)
    perm = np.array([(m // 64) * 64 + ((m % 64) + 32) % 64 for m in range(128)])
    pw = np.zeros((128, 128), np.float32)
    pw[perm, np.arange(128)] = 1.0
    c["pswap"] = pw
    n_freq = 16
    inv = (10000.0 ** (-np.arange(n_freq, dtype=np.float32) / n_freq)).astype(np.float32)
    rows = NL // 64
    r = np.repeat(np.arange(rows, dtype=np.float32), 64)
    col = np.tile(np.arange(64, dtype=np.float32), rows)
    ang = np.concatenate([r[:, None] * inv, col[:, None] * inv], axis=-1).astype(np.float32)
    cos = np.cos(ang).astype(np.float32).T
    sin = np.sin(ang).astype(np.float32).T
    ct = np.zeros((128, NL), np.float32)
    st = np.zeros((128, NL), np.float32)
    for p in range(128):
        d = p % 64
        f = d % 32
        ct[p] = cos[f]
        st[p] = -sin[f] if d < 32 else sin[f]
    c["rope_c"] = ct
    c["rope_s"] = st
    c.update(host_consts_fn())
    c.update(host_consts_ml())
    return c


class K:
    pass


def Dl(fn, *a, **kw):
    return lambda: fn(*a, **kw)


def build(debug=None, nlayers=DEPTH, stop_after=None, skip=(), br_input=False, uf_input=False, fn_stop=99):
    global FN_STOP
    FN_STOP = fn_stop
    nc = bass.Bass("TRN2", target_bir_lowering=False)
    P = Prog(nc)
    k = K()
    k.nc, k.P = nc, P
    k.da_heads = 8

    def ein(name, shape, dt=F32):
        return Tl(nc.dram_tensor(name, list(shape), dt, kind="ExternalInput"), name)

    dbg_kind = {}

    def scratch(name, shape, dt):
        kind = "ExternalOutput" if (debug and name in debug) else "Internal"
        return Tl(nc.dram_tensor(name, list(shape), dt, kind=kind), name)

    k.xT_in = ein("xT", [128, KC, NT])
    k.cT = ein("cT", [128, KC, 2])
    k.w_ada = ein("w_ada", [DEPTH, D, 6 * D])
    k.b_adaT = ein("b_adaT", [DEPTH, 128, 48])
    k.norm_gT = ein("norm_gT", [DEPTH, 128, 2, KC])
    k.w_in = ein("w_in", [DEPTH, D, D_IN])
    k.ident_f_d = ein("ident_f", [128, 128])
    k.pswap_d = ein("pswap", [128, 128])
    k.rope_c_d = ein("rope_c", [128, NL])
    k.rope_s_d = ein("rope_s", [128, NL])
    k.da_lam = ein("da_lam", [DEPTH, 4, 64])
    k.da_head_gT = ein("da_head_gT", [DEPTH, 128, 8])
    k.w_br_ml = ein("w_br_ml", [DEPTH, D, D])
    k.w_br_da = ein("w_br_da", [DEPTH, D, D])
    k.w_br_fn = ein("w_br_fn", [DEPTH, D, D])
    k.w_out = ein("w_out", [DEPTH, D, D])
    k.w_ffn_in = ein("w_ffn_in", [DEPTH, D, 2 * D_FF])
    k.w_ffn_out = ein("w_ffn_out", [DEPTH, D_FF, D])
    k.final_gT = ein("final_gT", [128, KC])
    k.ml_triF = ein("ml_triF", [128, 128])
    k.ml_triB = ein("ml_triB", [128, 128])
    k.ml_maskF = ein("ml_maskF", [128, 128])
    k.ml_maskB = ein("ml_maskB", [128, 128])
    k.ml_gate_bR = ein("ml_gate_bR", [DEPTH, NTILE * 16])
    k.ml_head_g = ein("ml_head_g", [DEPTH, 1024])
    k.fn_cs = ein("fn_cs", [128, 2, 512], BF16)
    k.fn_w1 = ein("fn_w1", [128, 128], BF16)
    k.fn_twr = ein("fn_twr", [128, 256])
    k.fn_twi = ein("fn_twi", [128, 256])
    k.fn_w3c = ein("fn_w3c", [128, 128], BF16)
    k.fn_w3s = ein("fn_w3s", [128, 128], BF16)
    k.fn_csx = ein("fn_csx", [128, 2, 2, 256], BF16)

    k.hl_d = scratch("hl_d", [128, KC, NT], BF16)
    k.uF_d = ein("uF_d", [5120, NT], BF16) if uf_input else scratch("uF_d", [5120, NT], BF16)
    k.uT_mlk = scratch("uT_mlk", [NT, 1024], BF16)
    k.uT_mlv = scratch("uT_mlv", [NT, 1024], BF16)
    k.uT_mlo = scratch("uT_mlo", [NT, 1024], BF16)
    k.uT_g = scratch("uT_g", [NT, 16], F32)
    k.uT_dav = scratch("uT_dav", [NT, 1024], BF16)
    k.G_d = scratch("G_d", [3072, NT], BF16)
    k.mods_d = scratch("mods_d", [128, DEPTH * 96], F32)
    k.br_d = ein("br_d", [3, 1024, NT], BF16) if br_input else scratch("br_d", [3, 1024, NT], BF16)
    k.x1_d = scratch("x1_d", [128, KC, NT], F32)
    k.AB_d = scratch("AB_d", [2, NL, 512], BF16)
    k.h_d = scratch("h_d", [2, NT, 1024], F32)
    k.h2_d = scratch("h2_d", [128, KC, NT], BF16)
    k.gT_d = scratch("gT_d", [128, FC, NT], BF16)
    k.xT_d = scratch("xT_d", [128, KC, NT], F32)
    k.out = Tl(nc.dram_tensor("out", [NL, D], F32, kind="ExternalOutput"), "out")
    k.dbg = scratch("dbg", [6, 128, 512], F32) if (debug and "dbg" in debug) else None

    k.ident_f = P.sb("ident_f", [128, 128], F32)
    k.ident_bf = P.sb("ident_bf", [128, 128], BF16)
    k.ones_bf = P.sb("ones_bf", [128, 128], BF16)
    k.mods = P.sb("mods", [128, DEPTH * 96], F32)
    k.Gt = P.sb("Gt", [128, DEPTH * 2 * KC * 2], F32)
    k.psall = nc.alloc_psum_tensor("psall", [128, 8, 512], F32)
    k.ps = [PsBank(k.psall, i) for i in range(8)]

    P.dma("sp", k.ident_f[:], k.ident_f_d.ap()[:, :], w=[k.ident_f])
    P.dve(lambda: nc.vector.tensor_copy(k.ident_bf[:], k.ident_f[:]), r=[k.ident_f], w=[k.ident_bf])
    P.dve(lambda: nc.vector.memset(k.ones_bf[:], 1.0), w=[k.ones_bf])

    if "adaln" not in skip:
        phase_adaln(k)
    if debug and "mods_d" in debug:
        P.dma("sp", k.mods_d.ap()[:, :], k.mods[:], r=[k.mods], w=[k.mods_d])
    for l in range(nlayers):
        if l == 0 and "m1" not in skip:
            phase_norm(k, l, 0, k.xT_in, k.hl_d)
        if stop_after == "norm":
            break
        if "m1" not in skip:
            phase_m1(k, l)
        if stop_after == "m1":
            break
        if "da" not in skip:
            phase_da(k, l)
        if stop_after == "da":
            break
        if "fn" not in skip:
            phase_fn(k, l)
        if stop_after == "fn":
            break
        if "ml" not in skip:
            phase_ml(k, l)
        if stop_after == "ml":
            break
        phase_t1(k, l, k.xT_in if l == 0 else k.xT_d)
        phase_t2a(k, l)
        phase_t2b(k, l, last=(l == nlayers - 1))
    info = P.finalize()
    return nc, info


def mods_view(k, l, chunk0, sel):
    base = l * 96 + chunk0 * 2 + sel
    return k.mods[:, base:base + 15:2]


def mod_col(k, l, chunk, sel):
    base = l * 96 + chunk * 2 + sel
    return k.mods[:, base:base + 1]


def g_col(k, l, which, kc, sel):
    base = ((l * 2 + which) * KC + kc) * 2 + sel
    return k.Gt[:, base:base + 1]


def phase_adaln(k):
    nc, P = k.nc, k.P
    P.phase_begin()
    s_c = P.sb("s_c", [128, KC * 2], F32)
    P.dma("sp", s_c[:], k.cT.ap().rearrange("p k s -> p (k s)"), w=[s_c])
    P.act(lambda: nc.scalar.activation(out=s_c[:], in_=s_c[:], func=AF.Silu), r=[s_c], w=[s_c])
    wst = [P.sb("wa%d" % i, [128, KC, 512], F32) for i in range(2)]
    bad = P.sb("bad", [128, DEPTH * 48], F32)
    ng = P.sb("ng", [128, DEPTH * 2 * KC], F32)
    P.dma("sp", bad[:].rearrange("p (l c) -> p l c", l=DEPTH), k.b_adaT.ap().rearrange("l p c -> p l c"), w=[bad])
    P.dma("sp", ng[:].rearrange("p (l c) -> p l c", l=DEPTH), k.norm_gT.ap().rearrange("l p w c -> p l (w c)"), w=[ng])
    ps0 = k.ps[0]
    gi = 0
    for l in range(DEPTH):
        wv = k.w_ada.ap()[l].rearrange("(k p) n -> p k n", p=128)
        for g in range(12):
            w_ = wst[gi % 2]
            gi += 1
            P.dma("sp", w_[:], wv[:, :, g * 512:(g + 1) * 512], w=[w_])
            for j in range(4):
                col = g * 4 + j
                for kc in range(KC):
                    P.pe(lambda w_=w_, j=j, kc=kc, col=col: nc.tensor.matmul(
                        ps0[:, 2 * col:2 * col + 2], w_[:, kc, j * 128:(j + 1) * 128], s_c[:, 2 * kc:2 * kc + 2],
                        start=(kc == 0), stop=(kc == KC - 1)), r=[w_, s_c], w=[ps0])
        for s in range(2):
            P.dve(lambda l=l, s=s: nc.vector.tensor_tensor(
                out=k.mods[:, l * 96 + s:(l + 1) * 96:2], in0=ps0[:, s:96:2], in1=bad[:, l * 48:(l + 1) * 48], op=ALU.add),
                r=[ps0, bad], w=[k.mods])
        for which in range(2):
            for s in range(2):
                sc = mods_view(k, l, 8 + 24 * which, s)
                gb = ((l * 2 + which) * KC) * 2 + s
                P.dve(lambda sc=sc, gb=gb, l=l, which=which: nc.vector.scalar_tensor_tensor(
                    out=k.Gt[:, gb:gb + 15:2], in0=sc, scalar=1.0, in1=ng[:, (l * 2 + which) * KC:(l * 2 + which + 1) * KC],
                    op0=ALU.add, op1=ALU.mult), r=[k.mods, ng], w=[k.Gt])
    P.phase_end()


def phase_norm(k, l, which, xsrc, hdst):
    nc, P = k.nc, k.P
    P.phase_begin()
    xb = [P.sb("xb%d" % i, [128, KC, 512], F32) for i in range(2)]
    sq = P.sb("sq", [128, KC, 512], BF16)
    rs = P.sb("rs", [128, 512], F32)
    tmp = [P.sb("tmp%d" % i, [128, 512], F32) for i in range(2)]
    hb = [P.sb("hb%d" % i, [128, KC, 512], BF16) for i in range(2)]
    eps = P.sb("eps", [128, 1], F32)
    P.dve(lambda: nc.vector.memset(eps[:], EPS), w=[eps])
    for bi, (t0, n, sel) in enumerate(BLKS):
        x_, h_ = xb[bi % 2], hb[bi % 2]
        P.dma("sp", x_[:, :, :n], xsrc.ap()[:, :, t0:t0 + n], w=[x_])
        norm_block(k, l, which, sel, x_, n, sq, rs, tmp, h_, eps, k.ps[1])
        P.dma("sp", hdst.ap()[:, :, t0:t0 + n], h_[:, :, :n], r=[h_], w=[hdst])
    P.phase_end()


def norm_block(k, l, which, sel, x_, n, sq, rs, tmp, h_, eps, ps):
    nc, P = k.nc, k.P
    P.pool(lambda: nc.gpsimd.tensor_tensor(out=sq[:, :, :n], in0=x_[:, :, :n], in1=x_[:, :, :n], op=ALU.mult), r=[x_], w=[sq])
    for kc in range(KC):
        P.pe(lambda kc=kc: nc.tensor.matmul(ps[:, :n], k.ones_bf[:], sq[:, kc, :n], start=(kc == 0), stop=(kc == KC - 1)),
             r=[k.ones_bf, sq], w=[ps])
    P.act(lambda: nc.scalar.activation(out=rs[:, :n], in_=ps[:, :n], func=AF.Sqrt, scale=1.0 / D, bias=eps[:, 0:1]), r=[ps, eps], w=[rs])
    P.dve(lambda: nc.vector.reciprocal(out=rs[:, :n], in_=rs[:, :n]), r=[rs], w=[rs])
    for kc in range(KC):
        t_ = tmp[kc % 2]
        P.dve(lambda kc=kc, t_=t_: nc.vector.tensor_tensor(out=t_[:, :n], in0=x_[:, kc, :n], in1=rs[:, :n], op=ALU.mult), r=[x_, rs], w=[t_])
        P.act(lambda kc=kc, t_=t_: nc.scalar.activation(out=h_[:, kc, :n], in_=t_[:, :n], func=AF.Identity,
                                                       scale=g_col(k, l, which, kc, sel), bias=mod_col(k, l, 24 * which + kc, sel)),
              r=[t_, k.Gt, k.mods], w=[h_])


def load_w_group(k, wv, c0, ncol, wst, wbf):
    nc, P = k.nc, k.P
    P.dma("sp", wst[:, :, :ncol], wv[:, :, c0:c0 + ncol], w=[wst])
    P.pool(lambda: nc.gpsimd.tensor_copy(wbf[:, :, :ncol], wst[:, :, :ncol]), r=[wst], w=[wbf])


def phase_m1(k, l):
    nc, P = k.nc, k.P
    P.phase_begin()
    hl = P.sb("hl", [128, KC, NT], BF16)
    for kc in range(KC):
        P.dma("sp", hl[:, kc, :], k.hl_d.ap()[:, kc, :], w=[hl.b(kc)])
    hlr = hl.allb()
    wst = [P.sb("wst%d" % i, [128, KC, 512], F32) for i in range(2)]
    wbf = [P.sb("wbf%d" % i, [128, KC, 512], BF16) for i in range(2)]
    fst = [P.sb("fst%d" % i, [128, NT], BF16) for i in range(2)]
    tst = [P.sb("tst%d" % i, [128, 512], BF16) for i in range(3)]
    tstf = [P.sb("tstf%d" % i, [128, 16], F32) for i in range(2)]
    ropc = P.sb("ropc", [128, NL], F32)
    rops = P.sb("rops", [128, NL], F32)
    pswap = P.sb("pswap", [128, 128], F32)
    q32 = [P.sb("q32_%d" % i, [128, 512], F32) for i in range(2)]
    t1 = [P.sb("t1_%d" % i, [128, 512], F32) for i in range(2)]
    t2 = [P.sb("t2_%d" % i, [128, 512], F32) for i in range(2)]
    P.dma("act", ropc[:], k.rope_c_d.ap()[:, :], w=[ropc])
    P.dma("act", rops[:], k.rope_s_d.ap()[:, :], w=[rops])
    P.dma("act", pswap[:], k.pswap_d.ap()[:, :], w=[pswap])
    wv = k.w_in.ap()[l].rearrange("(k p) n -> p k n", p=128)
    st = dict(g=0, ps=0, f=0, t=0, ev=0, r=0)

    def nextps():
        st["ps"] = (st["ps"] + 1) % 6
        return k.ps[2 + st["ps"]]

    def feat_group(c0, dst, row0, kind, tok_blks=BLKS):
        w_, wb_ = wst[st["g"] % 2], wbf[st["g"] % 2]
        st["g"] += 1
        prefetch_next()
        for j in range(4):
            stg = fst[st["f"] % 2]
            st["f"] += 1
            for bi, (t0, n, sel) in enumerate(tok_blks):
                ps = nextps()
                sb_ = stg.b(bi)
                for kc in range(KC):
                    P.pe(lambda ps=ps, wb_=wb_, j=j, kc=kc, t0=t0, n=n: nc.tensor.matmul(
                        ps[:, :n], wb_[:, kc, j * 128:(j + 1) * 128], hl[:, kc, t0:t0 + n], start=(kc == 0), stop=(kc == KC - 1)),
                        r=[wb_, hlr], w=[ps])
                o = stg[:, t0:t0 + n]
                if kind == "rope" and sel == 0:
                    q_, a_, b_ = q32[st["r"] % 2], t1[st["r"] % 2], t2[st["r"] % 2]
                    st["r"] += 1
                    psr = k.ps[0 + st["r"] % 2]
                    P.act(lambda ps=ps, q_=q_, n=n: nc.scalar.copy(q_[:, :n], ps[:, :n]), r=[ps], w=[q_])
                    P.pe(lambda psr=psr, q_=q_, n=n: nc.tensor.matmul(psr[:, :n], pswap[:], q_[:, :n], start=True, stop=True),
                         r=[pswap, q_], w=[psr])
                    P.pool(lambda a_=a_, q_=q_, t0=t0, n=n: nc.gpsimd.tensor_tensor(out=a_[:, :n], in0=q_[:, :n], in1=ropc[:, t0:t0 + n], op=ALU.mult),
                           r=[q_, ropc], w=[a_])
                    P.dve(lambda b_=b_, psr=psr, t0=t0, n=n: nc.vector.tensor_tensor(out=b_[:, :n], in0=psr[:, :n], in1=rops[:, t0:t0 + n], op=ALU.mult),
                          r=[psr, rops], w=[b_])
                    P.dve(lambda o=o, a_=a_, b_=b_, n=n: nc.vector.tensor_tensor(out=o, in0=a_[:, :n], in1=b_[:, :n], op=ALU.add),
                          r=[a_, b_], w=[sb_])
                elif kind == "sigm":
                    P.act(lambda o=o, ps=ps, n=n: nc.scalar.activation(out=o, in_=ps[:, :n], func=AF.Sigmoid), r=[ps], w=[sb_])
                elif kind == "s16":
                    P.act(lambda o=o, ps=ps, n=n: nc.scalar.mul(o, ps[:, :n], 1.0 / 16.0), r=[ps], w=[sb_])
                else:
                    st["ev"] += 1
                    if st["ev"] % 2:
                        P.act(lambda o=o, ps=ps, n=n: nc.scalar.copy(o, ps[:, :n]), r=[ps], w=[sb_])
                    else:
                        P.dve(lambda o=o, ps=ps, n=n: nc.vector.tensor_copy(o, ps[:, :n]), r=[ps], w=[sb_])
            ta, tb = tok_blks[0][0], tok_blks[-1][0] + tok_blks[-1][1]
            P.dma("sp", dst.ap()[row0 + j * 128:row0 + (j + 1) * 128, ta:tb], stg[:, ta:tb], r=stg.allb(), w=[dst])

    def tok_group(c0, ncol, dst, dcol0, kind):
        w_, wb_ = wst[st["g"] % 2], wbf[st["g"] % 2]
        st["g"] += 1
        prefetch_next()
        for ti in range(NTILE):
            ps = nextps()
            for kc in range(KC):
                P.pe(lambda ps=ps, wb_=wb_, kc=kc, ti=ti: nc.tensor.matmul(
                    ps[:, :ncol], hl[:, kc, ti * 128:(ti + 1) * 128], wb_[:, kc, :ncol], start=(kc == 0), stop=(kc == KC - 1)),
                    r=[wb_, hlr], w=[ps])
            if kind == "f32":
                stg = tstf[st["t"] % 2]
            else:
                stg = tst[st["t"] % 3]
            st["t"] += 1
            o = stg[:, :ncol]
            if kind == "sigm":
                P.act(lambda o=o, ps=ps: nc.scalar.activation(out=o, in_=ps[:, :ncol], func=AF.Sigmoid), r=[ps], w=[stg])
            elif kind == "s16":
                P.act(lambda o=o, ps=ps: nc.scalar.mul(o, ps[:, :ncol], 1.0 / 16.0), r=[ps], w=[stg])
            else:
                st["ev"] += 1
                if st["ev"] % 2:
                    P.act(lambda o=o, ps=ps: nc.scalar.copy(o, ps[:, :ncol]), r=[ps], w=[stg])
                else:
                    P.dve(lambda o=o, ps=ps: nc.vector.tensor_copy(o, ps[:, :ncol]), r=[ps], w=[stg])
            P.dma("sp", dst.ap()[ti * 128:(ti + 1) * 128, dcol0:dcol0 + ncol], o, r=[stg], w=[dst])

    tasks = []
    for g in range(2):
        tasks.append(("f", C_MLQ + g * 512, 512, k.uF_d, F_MLQ + g * 512, "copy"))
        tasks.append(("f", C_MLK + g * 512, 512, k.uF_d, F_MLK + g * 512, "s16"))
        tasks.append(("f", C_DAQ + g * 512, 512, k.uF_d, F_DAQ + g * 512, "rope"))
        tasks.append(("f", C_DAK + g * 512, 512, k.uF_d, F_DAK + g * 512, "rope"))
        tasks.append(("f", C_FN + g * 512, 512, k.uF_d, F_FN + g * 512, "copy"))
        tasks.append(("t", C_MLK + g * 512, 512, k.uT_mlk, g * 512, "s16"))
        tasks.append(("t", C_MLV + g * 512, 512, k.uT_mlv, g * 512, "copy"))
        tasks.append(("t", C_MLO + g * 512, 512, k.uT_mlo, g * 512, "sigm"))
        tasks.append(("t", C_DAV + g * 512, 512, k.uT_dav, g * 512, "copy"))
    tasks.append(("t", C_MLG, 16, k.uT_g, 0, "f32"))
    for g in range(6):
        tasks.append(("f", C_GP + g * 512, 512, k.G_d, g * 512, "sigm"))
    pf = dict(i=0)

    def prefetch_next():
        i = pf["i"]
        if i < len(tasks):
            t = tasks[i]
            load_w_group(k, wv, t[1], t[2], wst[i % 2], wbf[i % 2])
            pf["i"] = i + 1

    prefetch_next()
    for t in tasks:
        if t[0] == "f":
            feat_group(t[1], t[3], t[4], t[5])
        else:
            tok_group(t[1], t[2], t[3], t[4], t[5])
    P.phase_end()


def fm(a):
    return np.ascontiguousarray(a.T.reshape(KC, 128, a.shape[0]).transpose(1, 0, 2))


def prep_core_inputs(inp, b, consts):
    m = {}
    xall = np.concatenate([inp["x"][b], inp["ctx"][b]], axis=0)
    m["xT"] = fm(xall).astype(np.float32)
    cc = np.stack([inp["c"][b], inp["c_ctx"]], axis=0)
    m["cT"] = np.ascontiguousarray(cc.T.reshape(KC, 128, 2).transpose(1, 0, 2)).astype(np.float32)
    m["w_ada"] = inp["w_ada"]
    m["b_adaT"] = np.ascontiguousarray(inp["b_ada"].reshape(DEPTH, 48, 128).transpose(0, 2, 1))
    m["norm_gT"] = np.ascontiguousarray(inp["norm_g"].reshape(DEPTH, 2, KC, 128).transpose(0, 3, 1, 2))
    m["w_in"] = inp["w_in"]
    m["da_lam"] = inp["da_lam"]
    m["ml_gate_bR"] = np.ascontiguousarray(np.tile(inp["ml_gate_b"][:, None, :], (1, NTILE, 1)).reshape(DEPTH, NTILE * 16))
    m["ml_head_g"] = inp["ml_head_g"]
    for nm in ("w_br_ml", "w_br_da", "w_br_fn", "w_out", "w_ffn_in", "w_ffn_out"):
        m[nm] = inp[nm]
    m["final_gT"] = np.ascontiguousarray(inp["final_g"].reshape(KC, 128).T)
    m["da_head_gT"] = np.ascontiguousarray(inp["da_head_g"].reshape(DEPTH, 8, 128).transpose(0, 2, 1))
    m.update(consts)
    return m


def lam_init_of(l):
    return 0.8 - 0.6 * math.exp(-0.3 * l)


def phase_da(k, l):
    nc, P = k.nc, k.P
    V_, A_, T_, G_ = nc.vector, nc.scalar, nc.tensor, nc.gpsimd
    P.phase_begin()
    li = lam_init_of(l)
    lq = P.sb("lq", [128, 256], F32)
    P.dma("sp", lq[:], k.da_lam.ap()[l:l + 1].rearrange("o a d -> o (a d)").partition_broadcast(128), w=[lq])
    pr = P.sb("pr", [128, 128], F32)
    sc = P.sb("sc", [128, 8], F32)
    eps = P.sb("eps", [128, 1], F32)
    P.dve(Dl(V_.memset, eps[:], EPS), w=[eps])
    P.dve(Dl(V_.tensor_tensor, out=pr[:, 0:64], in0=lq[:, 0:64], in1=lq[:, 64:128], op=ALU.mult), r=[lq], w=[pr])
    P.dve(Dl(V_.tensor_tensor, out=pr[:, 64:128], in0=lq[:, 128:192], in1=lq[:, 192:256], op=ALU.mult), r=[lq], w=[pr])
    P.dve(Dl(V_.reduce_sum, out=sc[:, 0:1], in_=pr[:, 0:64], axis=AX.X), r=[pr], w=[sc])
    P.dve(Dl(V_.reduce_sum, out=sc[:, 1:2], in_=pr[:, 64:128], axis=AX.X), r=[pr], w=[sc])
    P.act(Dl(A_.activation, out=sc[:, 2:4], in_=sc[:, 0:2], func=AF.Exp), r=[sc], w=[sc])
    P.dve(Dl(V_.scalar_tensor_tensor, out=sc[:, 4:5], in0=sc[:, 3:4], scalar=-li, in1=sc[:, 2:3], op0=ALU.add, op1=ALU.subtract),
          r=[sc], w=[sc])
    neglam = sc[:, 4:5]
    hg = P.sb("hg", [128, 8], F32)
    P.dma("sp", hg[:], k.da_head_gT.ap()[l], w=[hg])
    P.dve(Dl(V_.tensor_scalar, out=hg[:], in0=hg[:], scalar1=(1.0 - li), scalar2=None, op0=ALU.mult), r=[hg], w=[hg])

    qz = [[P.sb("qz%d%d" % (i, m), [128, NT], BF16) for m in range(2)] for i in range(2)]
    for i in range(2):
        for m in range(2):
            P.pool(Dl(G_.memset, qz[i][m][:], 0.0), w=[qz[i][m]])
    kT = [P.sb("kT%d" % i, [128, NT], BF16) for i in range(2)]
    V = [P.sb("V%d" % i, [128, NTILE, 128], BF16) for i in range(2)]
    pt = [P.sb("pt%d" % i, [128, 2, 512], BF16) for i in range(3)]
    ystg = [P.sb("ystg%d" % i, [128, NT], BF16) for i in range(2)]
    rd = [P.sb("rd%d" % i, [128, 512], F32) for i in range(2)]
    o_ = [P.sb("o%d" % i, [128, 512], F32) for i in range(2)]
    acc = [P.sb("acc%d" % i, [128, 2, 512], F32) for i in range(2)]
    ones_f = P.sb("ones_fd", [128, 128], F32)
    P.dve(Dl(V_.memset, ones_f[:], 1.0), w=[ones_f])
    ofs = [P.sb("of%d" % i, [128, 512], F32) for i in range(2)]
    sqs = [P.sb("sqd%d" % i, [128, 512], BF16) for i in range(2)]
    rs = P.sb("rsd", [128, 512], F32)
    deferred = []
    cO = [[P.sb("cO%d%d" % (i, m), [128, 512], F32) for m in range(2)] for i in range(2)]
    cD = [[P.sb("cD%d%d" % (i, m), [128, 512], F32) for m in range(2)] for i in range(2)]
    psO = [k.ps[4], k.ps[5]]
    psD = [k.ps[6], k.ps[7]]
    psE = k.ps[0]
    cnt = dict(s=0, p=0)

    for h in range(k.da_heads):
        qz_, k_, v_, y_ = qz[h % 2], kT[h % 2], V[h % 2], ystg[h % 2]
        for m in range(2):
            r0 = F_DAQ + h * 128 + 64 * m
            P.dma("sp", qz_[m][64 * m:64 * m + 64, :], k.uF_d.ap()[r0:r0 + 64, :], w=[qz_[m]])
        P.dma("sp", k_[:], k.uF_d.ap()[F_DAK + h * 128:F_DAK + (h + 1) * 128, :], w=[k_])
        P.dma("sp", v_[:], k.uT_dav.ap()[:, h * 128:(h + 1) * 128].rearrange("(n p) d -> p n d", p=128), w=[v_])
        for bi, (t0, n, sel) in enumerate(BLKS):
            ktiles = (LAT_TILES + CTX_TILES) if sel == 0 else CTX_TILES
            steps = [(m, ktiles[j], ktiles[j + 1], j) for m in range(2) for j in range(0, len(ktiles), 2)]
            nk = len(ktiles)

            def emit_s(i):
                m, ka, kb, j = steps[i]
                pi = cnt["s"] % 2
                cnt["s"] += 1
                banks = [k.ps[2 * pi], k.ps[2 * pi + 1]]
                for x, kt in enumerate((ka, kb)):
                    P.pe(Dl(T_.matmul, banks[x][:, :n], k_[:, kt * 128:(kt + 1) * 128], qz_[m][:, t0:t0 + n], start=True, stop=True),
                         r=[k_, qz_[m]], w=[banks[x]])
                p_ = pt[cnt["p"] % 3]
                cnt["p"] += 1
                P.act(Dl(A_.activation, out=p_[:, :, :n], in_=k.psall[:, 2 * pi:2 * pi + 2, :n], func=AF.Exp, scale=0.125), r=banks, w=[p_])
                return p_

            def emit_o(i, p_):
                m, ka, kb, j = steps[i]
                for x, kt in enumerate((ka, kb)):
                    jj = j + x
                    P.pe(Dl(T_.matmul, psO[m][:, :n], v_[:, kt, :], p_[:, x, :n], start=(jj == 0), stop=(jj == nk - 1)),
                         r=[v_, p_], w=[psO[m]])
                    P.pe(Dl(T_.matmul, psD[m][:, :n], k.ones_bf[:], p_[:, x, :n], start=(jj == 0), stop=(jj == nk - 1)),
                         r=[k.ones_bf, p_], w=[psD[m]])

            pend = []
            for i in range(len(steps)):
                pend.append((i, emit_s(i)))
                if len(pend) > 1:
                    emit_o(*pend.pop(0))
                if i == 8 or i == len(steps) - 1:
                    while deferred:
                        deferred.pop(0)()
            while pend:
                emit_o(*pend.pop(0))
            of_, sq_ = ofs[bi % 2], sqs[bi % 2]
            cO_, cD_ = cO[bi % 2], cD[bi % 2]
            for m in range(2):
                P.act(Dl(A_.copy, cD_[m][:, :n], psD[m][:, :n]), r=[psD[m]], w=[cD_[m]])
                P.act(Dl(A_.copy, cO_[m][:, :n], psO[m][:, :n]), r=[psO[m]], w=[cO_[m]])
            for m in range(2):
                P.dve(Dl(V_.reciprocal, out=rd[m][:, :n], in_=cD_[m][:, :n]), r=[cD_[m]], w=[rd[m]])
                P.dve(Dl(V_.tensor_tensor, out=o_[m][:, :n], in0=cO_[m][:, :n], in1=rd[m][:, :n], op=ALU.mult),
                      r=[cO_[m], rd[m]], w=[o_[m]])
            P.dve(Dl(V_.scalar_tensor_tensor, out=of_[:, :n], in0=o_[1][:, :n], scalar=neglam, in1=o_[0][:, :n], op0=ALU.mult, op1=ALU.add),
                  r=[o_[0], o_[1], sc], w=[of_])
            P.pool(Dl(G_.tensor_tensor, out=sq_[:, :n], in0=of_[:, :n], in1=of_[:, :n], op=ALU.mult), r=[of_], w=[sq_])
            def part_c(n=n, t0=t0, bi=bi, h=h, y_=y_, sq_=sq_, of_=of_):
                P.pe(Dl(T_.matmul, psE[:, :n], k.ones_bf[:], sq_[:, :n], start=True, stop=True), r=[k.ones_bf, sq_], w=[psE])
                P.act(Dl(A_.activation, out=rs[:, :n], in_=psE[:, :n], func=AF.Sqrt, scale=1.0 / 128, bias=eps[:, 0:1]), r=[psE, eps], w=[rs])
                P.dve(Dl(V_.reciprocal, out=rs[:, :n], in_=rs[:, :n]), r=[rs], w=[rs])
                P.dve(Dl(V_.tensor_tensor, out=of_[:, :n], in0=of_[:, :n], in1=rs[:, :n], op=ALU.mult), r=[of_, rs], w=[of_])
                P.act(Dl(A_.activation, out=y_[:, t0:t0 + n], in_=of_[:, :n], func=AF.Copy, scale=hg[:, h:h + 1]), r=[of_, hg], w=[y_.b(bi)])
            deferred.append(part_c)
        while deferred:
            deferred.pop(0)()
        P.dma("sp", k.br_d.ap()[1, h * 128:(h + 1) * 128, :], y_[:], r=y_.allb(), w=[k.br_d.b(("da", h))])
    P.phase_end()


def load_w_full(k, dst_bf, wv, ncols, kcn, wst, q="sp"):
    nc, P = k.nc, k.P
    for gi, c0 in enumerate(range(0, ncols, 512)):
        w_ = wst[gi % 2]
        P.dma(q, w_[:, :kcn, :], wv[:, :, c0:c0 + 512], w=[w_])
        P.pool(Dl(nc.gpsimd.tensor_copy, dst_bf[:, :, c0:c0 + 512], w_[:, :kcn, :]), r=[w_], w=[dst_bf.b(c0 // 512)])


def phase_t1(k, l, xsrc):
    nc, P = k.nc, k.P
    V_, A_, T_, G_ = nc.vector, nc.scalar, nc.tensor, nc.gpsimd
    P.phase_begin()
    wst = [P.sb("wst%d" % i, [128, KC, 512], F32) for i in range(2)]
    wbr = [P.sb("wbr%d" % i, [128, KC, 1024], BF16) for i in range(3)]
    wo = P.sb("wo", [128, KC, 1024], BF16)
    for x, wt in enumerate((k.w_br_ml, k.w_br_da, k.w_br_fn)):
        load_w_full(k, wbr[x], wt.ap()[l].rearrange("(k p) n -> p k n", p=128), 1024, KC, wst)
    load_w_full(k, wo, k.w_out.ap()[l].rearrange("(k p) n -> p k n", p=128), 1024, KC, wst)
    brb = [P.sb("brb%d" % i, [128, 24, TB1], BF16) for i in range(2)]
    gb = [P.sb("gb%d" % i, [128, 24, TB1], BF16) for i in range(2)]
    xb = [P.sb("xb%d" % i, [128, KC, TB1], F32) for i in range(2)]
    yb = P.sb("yb", [128, KC, TB1], BF16)
    ta = [P.sb("ta%d" % i, [128, TB1], F32) for i in range(2)]
    tb = [P.sb("tb%d" % i, [128, TB1], F32) for i in range(2)]
    tc = [P.sb("tc%d" % i, [128, TB1], F32) for i in range(2)]
    sq = P.sb("sq", [128, KC, TB1], BF16)
    rs = P.sb("rs", [128, TB1], F32)
    tmp = [P.sb("tmp%d" % i, [128, TB1], F32) for i in range(2)]
    hb = [P.sb("hb%d" % i, [128, KC, TB1], BF16) for i in range(2)]
    eps = P.sb("eps", [128, 1], F32)
    P.dve(Dl(V_.memset, eps[:], EPS), w=[eps])
    gv = k.G_d.ap().rearrange("(c p) t -> p c t", p=128)
    it = 0
    for bi, (t0, n, sel) in enumerate(BLKS1):
        b_, g_, x_, h_ = brb[bi % 2], gb[bi % 2], xb[bi % 2], hb[bi % 2]
        for x in range(3):
            P.dma("sp", b_[:, x * 8:(x + 1) * 8, :n], k.br_d.ap()[x].rearrange("(c p) t -> p c t", p=128)[:, :, t0:t0 + n], w=[b_.b(x)])
        P.dma("act", g_[:, :, :n], gv[:, :, t0:t0 + n], w=[g_])
        P.dma("act", x_[:, :, :n], xsrc.ap()[:, :, t0:t0 + n], w=[x_])
        for oc in range(KC):
            pss = [k.ps[(it % 2) * 3 + x] for x in range(3)]
            a_, b2_, c_ = ta[it % 2], tb[it % 2], tc[it % 2]
            it += 1
            for x in range(3):
                for kc in range(KC):
                    P.pe(Dl(T_.matmul, pss[x][:, :n], wbr[x][:, kc, oc * 128:(oc + 1) * 128], b_[:, x * 8 + kc, :n],
                            start=(kc == 0), stop=(kc == KC - 1)), r=[wbr[x].b(oc // 4), b_.b(x)], w=[pss[x]])
            P.dve(Dl(V_.tensor_tensor, out=a_[:, :n], in0=pss[0][:, :n], in1=g_[:, oc, :n], op=ALU.mult), r=[pss[0], g_], w=[a_])
            P.dve(Dl(V_.tensor_tensor, out=b2_[:, :n], in0=pss[1][:, :n], in1=g_[:, 8 + oc, :n], op=ALU.mult), r=[pss[1], g_], w=[b2_])
            P.dve(Dl(V_.tensor_tensor, out=c_[:, :n], in0=pss[2][:, :n], in1=g_[:, 16 + oc, :n], op=ALU.mult), r=[pss[2], g_], w=[c_])
            P.pool(Dl(G_.tensor_tensor, out=a_[:, :n], in0=a_[:, :n], in1=b2_[:, :n], op=ALU.add), r=[a_, b2_], w=[a_])
            P.pool(Dl(G_.tensor_tensor, out=yb[:, oc, :n], in0=a_[:, :n], in1=c_[:, :n], op=ALU.add), r=[a_, c_], w=[yb.b(oc)])
        for oc in range(KC):
            ps = k.ps[6]
            for kc in range(KC):
                P.pe(Dl(T_.matmul, ps[:, :n], wo[:, kc, oc * 128:(oc + 1) * 128], yb[:, kc, :n], start=(kc == 0), stop=(kc == KC - 1)),
                     r=[wo.b(oc // 4), yb.allb()], w=[ps])
            P.dve(Dl(V_.scalar_tensor_tensor, out=x_[:, oc, :n], in0=ps[:, :n], scalar=mod_col(k, l, 16 + oc, sel), in1=x_[:, oc, :n],
                     op0=ALU.mult, op1=ALU.add), r=[ps, x_, k.mods], w=[x_])
        P.dma("sp", k.x1_d.ap()[:, :, t0:t0 + n], x_[:, :, :n], r=[x_], w=[k.x1_d])
        norm_block(k, l, 1, sel, x_, n, sq, rs, tmp, h_, eps, k.ps[7])
        P.dma("sp", k.h2_d.ap()[:, :, t0:t0 + n], h_[:, :, :n], r=[h_], w=[k.h2_d])
    P.phase_end()


def phase_t2a(k, l):
    nc, P = k.nc, k.P
    V_, A_, T_, G_ = nc.vector, nc.scalar, nc.tensor, nc.gpsimd
    P.phase_begin()
    wst = [P.sb("wst%d" % i, [128, KC, 512], F32) for i in range(2)]
    wf = P.sb("wf", [128, KC, 2 * D_FF], BF16)
    load_w_full(k, wf, k.w_ffn_in.ap()[l].rearrange("(k p) n -> p k n", p=128), 2 * D_FF, KC, wst)
    hb = [P.sb("hb%d" % i, [128, KC, 512], BF16) for i in range(2)]
    gblk = [P.sb("gblk%d" % i, [128, FC, 512], BF16) for i in range(2)]
    sa = [P.sb("sa%d" % i, [128, 512], F32) for i in range(2)]
    it = 0
    for bi, (t0, n, sel) in enumerate(BLKS):
        h_, g_ = hb[bi % 2], gblk[bi % 2]
        P.dma("sp", h_[:, :, :n], k.h2_d.ap()[:, :, t0:t0 + n], w=[h_])
        for j in range(FC):
            pa, pb = k.ps[(it % 4) * 2], k.ps[(it % 4) * 2 + 1]
            s_ = sa[it % 2]
            it += 1
            for kc in range(KC):
                P.pe(Dl(T_.matmul, pa[:, :n], wf[:, kc, j * 128:(j + 1) * 128], h_[:, kc, :n], start=(kc == 0), stop=(kc == KC - 1)),
                     r=[wf.b((j * 128) // 512), h_], w=[pa])
            for kc in range(KC):
                P.pe(Dl(T_.matmul, pb[:, :n], wf[:, kc, D_FF + j * 128:D_FF + (j + 1) * 128], h_[:, kc, :n], start=(kc == 0), stop=(kc == KC - 1)),
                     r=[wf.b((D_FF + j * 128) // 512), h_], w=[pb])
            P.act(Dl(A_.activation, out=s_[:, :n], in_=pa[:, :n], func=AF.Silu), r=[pa], w=[s_])
            P.dve(Dl(V_.tensor_tensor, out=g_[:, j, :n], in0=pb[:, :n], in1=s_[:, :n], op=ALU.mult), r=[pb, s_], w=[g_])
        P.dma("sp", k.gT_d.ap()[:, :, t0:t0 + n], g_[:, :, :n], r=[g_], w=[k.gT_d])
    P.phase_end()


def phase_t2b(k, l, last):
    nc, P = k.nc, k.P
    V_, A_, T_, G_ = nc.vector, nc.scalar, nc.tensor, nc.gpsimd
    P.phase_begin()
    wst = [P.sb("wst%d" % i, [128, 11, 512], F32) for i in range(2)]
    wo = P.sb("wo", [128, FC, 1024], BF16)
    wv = k.w_ffn_out.ap()[l].rearrange("(k p) n -> p k n", p=128)
    gi = 0
    for c0 in (0, 512):
        for k0 in (0, 11):
            w_ = wst[gi % 2]
            gi += 1
            P.dma("sp", w_[:], wv[:, k0:k0 + 11, c0:c0 + 512], w=[w_])
            P.pool(Dl(G_.tensor_copy, wo[:, k0:k0 + 11, c0:c0 + 512], w_[:]), r=[w_], w=[wo.b((c0, k0))])
    gblk = [P.sb("gblk%d" % i, [128, FC, 512], BF16) for i in range(2)]
    xb = [P.sb("xb%d" % i, [128, KC, 512], F32) for i in range(2)]
    sq = P.sb("sq", [128, KC, 512], BF16)
    rs = P.sb("rs", [128, 512], F32)
    tmp = [P.sb("tmp%d" % i, [128, 512], F32) for i in range(2)]
    hb = [P.sb("hb%d" % i, [128, KC, 512], BF16) for i in range(2)] if not last else [None, None]
    eps = P.sb("eps", [128, 1], F32)
    P.dve(Dl(V_.memset, eps[:], EPS), w=[eps])
    if last:
        fg = P.sb("fg", [128, KC], F32)
        P.dma("sp", fg[:], k.final_gT.ap()[:, :], w=[fg])
        hf = P.sb("hf", [128, KC, 512], F32)
        ot = [P.sb("ot%d" % i, [128, 1024], F32) for i in range(2)]
    it = 0
    blks = BLKS[:-1] if last else BLKS
    for bi, (t0, n, sel) in enumerate(blks):
        g_, x_, h_ = gblk[bi % 2], xb[bi % 2], hb[bi % 2]
        P.dma("sp", g_[:, :, :n], k.gT_d.ap()[:, :, t0:t0 + n], w=[g_])
        P.dma("act", x_[:, :, :n], k.x1_d.ap()[:, :, t0:t0 + n], w=[x_])
        for oc in range(KC):
            ps = k.ps[it % 4]
            it += 1
            for j in range(FC):
                P.pe(Dl(T_.matmul, ps[:, :n], wo[:, j, oc * 128:(oc + 1) * 128], g_[:, j, :n], start=(j == 0), stop=(j == FC - 1)),
                     r=[wo.b(((oc // 4) * 512, (j // 11) * 11)), g_], w=[ps])
            P.dve(Dl(V_.scalar_tensor_tensor, out=x_[:, oc, :n], in0=ps[:, :n], scalar=mod_col(k, l, 40 + oc, sel), in1=x_[:, oc, :n],
                     op0=ALU.mult, op1=ALU.add), r=[ps, x_, k.mods], w=[x_])
        if not last:
            P.dma("sp", k.xT_d.ap()[:, :, t0:t0 + n], x_[:, :, :n], r=[x_], w=[k.xT_d])
            norm_block(k, l + 1, 0, sel, x_, n, sq, rs, tmp, h_, eps, k.ps[7])
            P.dma("sp", k.hl_d.ap()[:, :, t0:t0 + n], h_[:, :, :n], r=[h_], w=[k.hl_d])
        else:
            ps7 = k.ps[7]
            P.pool(Dl(G_.tensor_tensor, out=sq[:, :, :n], in0=x_[:, :, :n], in1=x_[:, :, :n], op=ALU.mult), r=[x_], w=[sq])
            for kc in range(KC):
                P.pe(Dl(T_.matmul, ps7[:, :n], k.ones_bf[:], sq[:, kc, :n], start=(kc == 0), stop=(kc == KC - 1)), r=[k.ones_bf, sq], w=[ps7])
            P.act(Dl(A_.activation, out=rs[:, :n], in_=ps7[:, :n], func=AF.Sqrt, scale=1.0 / D, bias=eps[:, 0:1]), r=[ps7, eps], w=[rs])
            P.dve(Dl(V_.reciprocal, out=rs[:, :n], in_=rs[:, :n]), r=[rs], w=[rs])
            for kc in range(KC):
                P.dve(Dl(V_.scalar_tensor_tensor, out=hf[:, kc, :n], in0=x_[:, kc, :n], scalar=fg[:, kc:kc + 1], in1=rs[:, :n],
                         op0=ALU.mult, op1=ALU.mult), r=[x_, fg, rs], w=[hf])
            for tt in range(n // 128):
                o_ = ot[tt % 2]
                for half in range(2):
                    pt_ = k.ps[4 + half]
                    for q4 in range(4):
                        kc = half * 4 + q4
                        P.pe(Dl(T_.transpose, pt_[:, q4 * 128:(q4 + 1) * 128], hf[:, kc, tt * 128:(tt + 1) * 128], k.ident_f[:]),
                             r=[hf, k.ident_f], w=[pt_])
                    if half == 0:
                        P.act(Dl(A_.copy, o_[:, 0:512], pt_[:, :]), r=[pt_], w=[o_])
                    else:
                        P.dve(Dl(V_.tensor_copy, o_[:, 512:1024], pt_[:, :]), r=[pt_], w=[o_])
                P.dma("sp", k.out.ap()[t0 + tt * 128:t0 + (tt + 1) * 128, :], o_[:], r=[o_], w=[k.out.b((t0, tt))])
    P.phase_end()


def host_consts_fn():
    import ml_dtypes
    bf = ml_dtypes.bfloat16
    c = {}
    cc = np.arange(256)
    ang = 2 * np.pi * np.outer(cc, cc) / 256.0
    cs = np.concatenate([np.cos(ang), np.sin(ang)], axis=1)
    c["fn_cs"] = np.ascontiguousarray(cs.reshape(2, 128, 512).transpose(1, 0, 2)).astype(bf)
    t = np.arange(64)
    a64 = 2 * np.pi * np.outer(t, t) / 64.0
    w1 = np.zeros((128, 128))
    w1[0:64, 0:64] = np.cos(a64)
    w1[0:64, 64:128] = -np.sin(a64)
    w1[64:128, 0:64] = -np.sin(a64)
    w1[64:128, 64:128] = -np.cos(a64)
    c["fn_w1"] = w1.astype(bf)
    t1 = np.repeat(np.arange(64), 2)
    atw = 2 * np.pi * np.outer(t1, np.arange(64)) / 4096.0
    c["fn_twr"] = np.ascontiguousarray(np.tile(np.cos(atw)[:, None, :], (1, 4, 1)).reshape(128, 256)).astype(np.float32)
    c["fn_twi"] = np.ascontiguousarray(np.tile(-np.sin(atw)[:, None, :], (1, 4, 1)).reshape(128, 256)).astype(np.float32)
    w3c = np.zeros((128, 128))
    w3s = np.zeros((128, 128))
    for gi in range(2):
        w3c[gi::2, gi * 64:(gi + 1) * 64] = np.cos(a64)
        w3s[gi::2, gi * 64:(gi + 1) * 64] = np.sin(a64)
    c["fn_w3c"] = w3c.astype(bf)
    c["fn_w3s"] = w3s.astype(bf)
    kk = np.arange(256)
    a256 = 2 * np.pi * np.outer(kk, kk) / 256.0
    csx = np.stack([np.cos(a256), -np.sin(a256)], axis=1)
    c["fn_csx"] = np.ascontiguousarray(csx.reshape(2, 128, 2, 256).transpose(1, 0, 2, 3)).astype(bf)
    return c


def phase_fn(k, l):
    nc, P = k.nc, k.P
    V_, A_, T_, G_ = nc.vector, nc.scalar, nc.tensor, nc.gpsimd
    P.phase_begin()
    cs = P.sb("cs", [128, 2, 512], BF16)
    w1 = P.sb("w1", [128, 128], BF16)
    twr = P.sb("twr", [128, 256], F32)
    twi = P.sb("twi", [128, 256], F32)
    w3c = P.sb("w3c", [128, 128], BF16)
    w3s = P.sb("w3s", [128, 128], BF16)
    csx = P.sb("csx", [128, 2, 2, 256], BF16)
    for t_, d_ in ((cs, k.fn_cs), (w1, k.fn_w1), (twr, k.fn_twr), (twi, k.fn_twi), (w3c, k.fn_w3c), (w3s, k.fn_w3s), (csx, k.fn_csx)):
        P.dma("act", t_[:], d_.ap(), w=[t_])
    zc = [P.sb("zc%d" % i, [128, 4, 1024], BF16) for i in range(2)]
    stg = [P.sb("stg%d" % i, [128, 2, 512], BF16) for i in range(3)]
    abc = [P.sb("abc%d" % i, [128, 2, 512], BF16) for i in range(2)]
    yf = [[P.sb("yf%d%d" % (gi, ch), [128, NT], BF16) for ch in range(2)] for gi in range(2)]
    D1 = P.sb("D1", [128, 64, 512], BF16)
    H = P.sb("H", [128, 2, 64, 256], BF16)
    wa = [P.sb("fwa%d" % i, [128, 256], F32) for i in range(2)]
    wb = [P.sb("fwb%d" % i, [128, 256], F32) for i in range(2)]
    wc = [P.sb("fwc%d" % i, [128, 256], F32) for i in range(2)]
    wd = [P.sb("fwd%d" % i, [128, 256], F32) for i in range(2)]
    ev = 0
    psi = 0
    for gp in range(2):
        chunks = [(i * 1024, 1024) for i in range(NL // 1024)] + [(NL, NCX)]
        si = 0
        for ci, (c0, cn) in enumerate(chunks):
            z_ = zc[ci % 2]
            for gi in range(2):
                for cc in range(2):
                    r0 = F_FN + (2 * gp + gi) * 256 + cc * 128
                    P.dma("sp", z_[:, gi * 2 + cc, :cn], k.uF_d.ap()[r0:r0 + 128, c0:c0 + cn], w=[z_.b(gi * 2 + cc)])
            for tl in range(cn // 128):
                tok0 = c0 + tl * 128
                isctx = tok0 >= NL
                s_ = abc[(tok0 - NL) // 128] if isctx else stg[si % 3]
                si += 1
                for gi in range(2):
                    ps = k.ps[psi % 4]
                    psi += 1
                    for cc in range(2):
                        P.pe(Dl(T_.matmul, ps[:, :], z_[:, gi * 2 + cc, tl * 128:(tl + 1) * 128], cs[:, cc, :], start=(cc == 0), stop=(cc == 1)),
                             r=[z_.b(gi * 2 + cc), cs], w=[ps])
                    ev += 1
                    if ev % 2:
                        P.act(Dl(A_.copy, s_[:, gi, :], ps[:, :]), r=[ps], w=[s_.b(gi)])
                    else:
                        P.dve(Dl(V_.tensor_copy, s_[:, gi, :], ps[:, :]), r=[ps], w=[s_.b(gi)])
                if not isctx:
                    for ab in range(2):
                        P.dma("sp", k.AB_d.ap()[ab, tok0:tok0 + 128, :].rearrange("t (g c) -> t g c", g=2), s_[:, :, ab * 256:(ab + 1) * 256],
                              r=s_.allb(), w=[k.AB_d.b((ab, tok0 // 2048))])
        if FN_STOP < 1:
            continue
        for ab in range(2):
            for hh in range(2):
                P.dma("sp", D1[ab * 64 + hh * 32:ab * 64 + hh * 32 + 32, :, :],
                      k.AB_d.ap()[ab, hh * 2048:(hh + 1) * 2048, :].rearrange("(t2 t1) c -> t2 t1 c", t1=64),
                      r=[k.AB_d.b((ab, hh))], w=[D1.b((ab, hh))])
        d1r = D1.allb()
        for cb in range(64):
            ps = k.ps[psi % 4]
            psi += 1
            for ci in range(4):
                cp = cb * 4 + ci
                P.pe(Dl(T_.matmul, ps[:, ci * 128:(ci + 1) * 128], D1[:, :, cp:512:256], w1[:, :], start=True, stop=True), r=[d1r, w1], w=[ps])
            psv = ps[:, :].rearrange("p (c r j) -> p c r j", c=4, r=2)
            gr, gim = psv[:, :, 0, :], psv[:, :, 1, :]
            a_, b_, c_, d_ = wa[cb % 2], wb[cb % 2], wc[cb % 2], wd[cb % 2]
            v4 = lambda t: t[:, :].rearrange("p (c j) -> p c j", c=4)
            P.dve(Dl(V_.tensor_tensor, out=v4(a_), in0=gr, in1=v4(twr), op=ALU.mult), r=[ps, twr], w=[a_])
            P.dve(Dl(V_.tensor_tensor, out=v4(b_), in0=gim, in1=v4(twi), op=ALU.mult), r=[ps, twi], w=[b_])
            P.dve(Dl(V_.tensor_tensor, out=v4(c_), in0=gr, in1=v4(twi), op=ALU.mult), r=[ps, twi], w=[c_])
            P.dve(Dl(V_.tensor_tensor, out=v4(d_), in0=gim, in1=v4(twr), op=ALU.mult), r=[ps, twr], w=[d_])
            P.pool(Dl(G_.tensor_tensor, out=H[:, 0, :, cb * 4:(cb + 1) * 4].rearrange("p j c -> p c j"), in0=v4(a_), in1=v4(b_), op=ALU.subtract), r=[a_, b_], w=[H.b(cb)])
            P.pool(Dl(G_.tensor_tensor, out=H[:, 1, :, cb * 4:(cb + 1) * 4].rearrange("p j c -> p c j"), in0=v4(c_), in1=v4(d_), op=ALU.add), r=[c_, d_], w=[H.b(cb)])
        if FN_STOP < 1.5:
            continue
        hr = H.allb()
        for ch in range(2):
            for jb in range(16):
                ps = k.ps[psi % 4]
                psi += 1
                for ji in range(4):
                    j2 = jb * 4 + ji
                    P.pe(Dl(T_.matmul, ps[:, ji * 128:(ji + 1) * 128], H[:, 0, j2, ch * 128:(ch + 1) * 128], w3c[:, :], start=True, stop=False),
                         r=[hr, w3c], w=[ps])
                    P.pe(Dl(T_.matmul, ps[:, ji * 128:(ji + 1) * 128], H[:, 1, j2, ch * 128:(ch + 1) * 128], w3s[:, :], start=False, stop=True),
                         r=[hr, w3s], w=[ps])
                for ji in range(4):
                    if FN_STOP == 1.5:
                        break
                    j2 = jb * 4 + ji
                    for gi in range(2):
                        o = yf[gi][ch][:, j2:NL:64]
                        i_ = ps[:, ji * 128 + gi * 64:ji * 128 + (gi + 1) * 64]
                        ev += 1
                        if (ev % 2 or FN_STOP == 2.1) and FN_STOP != 2.2:
                            P.act(Dl(A_.mul, o, i_, 1.0 / 1024.0), r=[ps], w=[yf[gi][ch].b(jb)])
                        else:
                            P.dve(Dl(V_.tensor_scalar, out=o, in0=i_, scalar1=1.0 / 1024.0, scalar2=None, op0=ALU.mult), r=[ps], w=[yf[gi][ch].b(jb)])
        if FN_STOP < 3:
            continue
        for gi in range(2):
            for ch in range(2):
                ps = k.ps[psi % 4]
                psi += 1
                idx = 0
                for kt in range(2):
                    for ab in range(2):
                        P.pe(Dl(T_.matmul, ps[:, 0:256], abc[kt][:, gi, ab * 256 + ch * 128:ab * 256 + (ch + 1) * 128], csx[:, kt, ab, :],
                                start=(idx == 0), stop=(idx == 3)), r=[abc[kt].allb(), csx], w=[ps])
                        idx += 1
                P.act(Dl(A_.mul, yf[gi][ch][:, NL:NT], ps[:, 0:256], 1.0 / 256.0), r=[ps], w=[yf[gi][ch].b("ctx")])
                r0 = (2 * gp + gi) * 256 + ch * 128
                P.dma("sp", k.br_d.ap()[2, r0:r0 + 128, :], yf[gi][ch][:], r=yf[gi][ch].allb(), w=[k.br_d.b(("fn", r0))])
    P.phase_end()


NEG = -30000.0


def host_consts_ml():
    c = {}
    r = np.arange(128)
    c["ml_triF"] = (r[:, None] <= r[None, :]).astype(np.float32)
    c["ml_triB"] = (r[:, None] >= r[None, :]).astype(np.float32)
    c["ml_maskF"] = np.where(r[None, :] <= r[:, None], 0.0, NEG).astype(np.float32)
    c["ml_maskB"] = np.where(r[None, :] >= r[:, None], 0.0, NEG).astype(np.float32)
    return c


def phase_ml(k, l):
    nc, P = k.nc, k.P
    V_, A_, T_, G_ = nc.vector, nc.scalar, nc.tensor, nc.gpsimd
    P.phase_begin()
    ones_f = P.sb("ones_f", [128, 128], F32)
    tri = [P.sb("triF", [128, 128], F32), P.sb("triB", [128, 128], F32)]
    msk = [P.sb("maskF", [128, 128], F32), P.sb("maskB", [128, 128], F32)]
    P.dve(Dl(V_.memset, ones_f[:], 1.0), w=[ones_f])
    for t_, d_ in ((tri[0], k.ml_triF), (tri[1], k.ml_triB), (msk[0], k.ml_maskF), (msk[1], k.ml_maskB)):
        P.dma("act", t_[:], d_.ap(), w=[t_])
    GL = P.sb("GL", [128, NTILE, 16], F32)
    gbias = P.sb("gbias", [128, NTILE, 16], F32)
    P.dma("sp", GL[:], k.uT_g.ap().rearrange("(n p) c -> p n c", p=128), w=[GL])
    P.dma("sp", gbias[:].rearrange("p n c -> p (n c)"), k.ml_gate_bR.ap()[l:l + 1, :].partition_broadcast(128), w=[gbias])
    P.dve(Dl(V_.tensor_tensor, out=GL[:], in0=GL[:], in1=gbias[:], op=ALU.add), r=[GL, gbias], w=[GL])
    gax = P.sb("gax", [128, NTILE, 2, 4], F32)
    gmn = P.sb("gmn", [128, NTILE, 2, 4], F32)
    GLf = GL[:].rearrange("p n (a c) -> p n a c", a=2)[:, :, :, 4:8]
    P.dve(Dl(V_.scalar_tensor_tensor, out=gax[:], in0=GLf, scalar=-1.0, in1=GLf, op0=ALU.mult, op1=ALU.max), r=[GL], w=[gax])
    P.act(Dl(A_.activation, out=gax[:], in_=gax[:], func=AF.Exp, scale=-1.0), r=[gax], w=[gax])
    P.act(Dl(A_.activation, out=gax[:], in_=gax[:], func=AF.Ln, bias=ones_f[:, 0:1], scale=1.0), r=[gax, ones_f], w=[gax])
    P.dve(Dl(V_.tensor_single_scalar, out=gmn[:], in_=GLf, scalar=0.0, op=ALU.min), r=[GL], w=[gmn])
    P.dve(Dl(V_.tensor_tensor, out=GLf, in0=gmn[:], in1=gax[:], op=ALU.subtract), r=[gmn, gax], w=[GL])

    NH = 2
    qT = [[P.sb("mq%d%d" % (i, c), [128, NT], BF16) for c in range(2)] for i in range(NH)]
    kT = [[P.sb("mk%d%d" % (i, c), [128, NT], BF16) for c in range(2)] for i in range(NH)]
    ktok = [P.sb("mkt%d" % i, [128, NTILE, 256], BF16) for i in range(NH)]
    vaug = [P.sb("mv%d" % i, [128, NTILE, 260], BF16) for i in range(NH)]
    C32 = [[P.sb("C32_%d%d" % (i, c), [128, 257], F32) for c in range(2)] for i in range(2 * NH)]
    Cb = [[P.sb("Cb_%d%d" % (i, c), [128, 260], BF16) for c in range(2)] for i in range(2 * NH)]
    mst = P.sb("mst", [128, 2, 2 * NH], F32)
    NB = 4
    IB = [P.sb("IB%d" % i, [128, 128], F32) for i in range(NB)]
    NLB = [P.sb("NLB%d" % i, [128, 128], F32) for i in range(NB)]
    wm = [P.sb("wm%d" % i, [128, 128], F32) for i in range(NB)]
    Dm = [P.sb("Dm%d" % i, [128, 128], F32) for i in range(NB)]
    st = [P.sb("st%d" % i, [128, 16], F32) for i in range(NB)]
    ex = [P.sb("ex%d" % i, [128, 4], F32) for i in range(NB)]
    a_ = [P.sb("a%d" % i, [128, 128], BF16) for i in range(NB)]
    aTs = [P.sb("aTs%d" % i, [128, 128], BF16) for i in range(NB)]
    Xs = [P.sb("Xs%d" % i, [128, 257], F32) for i in range(NB)]
    Z = [P.sb("Z%d" % i, [128, 257], F32) for i in range(NB)]
    dmr = [P.sb("dmr%d" % i, [128, 2], F32) for i in range(NB)]
    ho = [P.sb("ho%d" % i, [128, 256], F32) for i in range(NB)]
    kw = [P.sb("kw%d" % i, [128, 256], BF16) for i in range(NB)]
    bank_sets = [(k.ps[0], k.ps[1], k.ps[2], k.ps[3]), (k.ps[4], k.ps[5], k.ps[6], k.ps[7])]
    for i in range(NH):
        P.dve(Dl(V_.memset, vaug[i][:, :, 256:260], 1.0), w=[vaug[i].b("ones")])

    fwd_tiles = CTX_TILES + LAT_TILES
    bwd_tiles = CTX_TILES[::-1] + LAT_TILES[::-1]
    for hp in range(4 // NH):
        for i in range(NH):
            h = hp * NH + i
            for c in range(2):
                P.dma("sp", qT[i][c][:], k.uF_d.ap()[F_MLQ + h * 256 + c * 128:F_MLQ + h * 256 + (c + 1) * 128, :], w=[qT[i][c]])
                P.dma("sp", kT[i][c][:], k.uF_d.ap()[F_MLK + h * 256 + c * 128:F_MLK + h * 256 + (c + 1) * 128, :], w=[kT[i][c]])
            P.dma("sp", ktok[i][:], k.uT_mlk.ap()[:, h * 256:(h + 1) * 256].rearrange("(n p) d -> p n d", p=128), w=[ktok[i]])
            P.dma("sp", vaug[i][:, :, 0:256], k.uT_mlv.ap()[:, h * 256:(h + 1) * 256].rearrange("(n p) d -> p n d", p=128), w=[vaug[i].b("v")])
        for ch in range(2 * NH):
            for c in range(2):
                P.dve(Dl(V_.memset, C32[ch][c][:], 0.0), w=[C32[ch][c]])
                P.pool(Dl(G_.memset, Cb[ch][c][:], 0.0), w=[Cb[ch][c]])
        P.dve(Dl(V_.memset, mst[:], 0.0), w=[mst])
        def ctxv(idx):
            step, ch = idx // (2 * NH), idx % (2 * NH)
            i, d = ch // 2, ch % 2
            h = hp * NH + i
            ti = (fwd_tiles if d == 0 else bwd_tiles)[step]
            tsl = slice(ti * 128, (ti + 1) * 128)
            b = idx % NB
            icol = GL[:, ti, 8 * d + h:8 * d + h + 1]
            fcol = GL[:, ti, 8 * d + 4 + h:8 * d + 4 + h + 1]
            mcur = mst[:, step % 2, ch:ch + 1]
            mnxt = mst[:, (step + 1) % 2, ch:ch + 1]
            s_, e_ = st[b], ex[b]
            vr = [vaug[i].b("v"), vaug[i].b("ones")]
            return step, ch, i, d, h, ti, tsl, b, icol, fcol, mcur, mnxt, s_, e_, vr

        def banks(ch):
            psP, psQ, psY, psCb = bank_sets[ch % 2]
            return psP, psQ, psY, psCb, psQ[:, :].bitcast(BF16)

        def stage_a(idx):
            step, ch, i, d, h, ti, tsl, b, icol, fcol, mcur, mnxt, s_, e_, vr = ctxv(idx)
            psP, psQ, psY, psCb, psQ_bf = banks(ch)
            P.dve(Dl(V_.tensor_scalar, out=IB[b][:], in0=ones_f[:], scalar1=icol, scalar2=None, op0=ALU.mult), r=[ones_f, GL], w=[IB[b]])
            P.dve(Dl(V_.tensor_scalar, out=NLB[b][:], in0=ones_f[:], scalar1=fcol, scalar2=-1.0, op0=ALU.mult, op1=ALU.mult), r=[ones_f, GL], w=[NLB[b]])
            P.pe(Dl(T_.matmul, psP[:, 0:128], IB[b][:], k.ident_f[:], start=True, stop=False), r=[IB[b], k.ident_f], w=[psP])
            P.pe(Dl(T_.matmul, psP[:, 0:128], NLB[b][:], tri[d][:], start=False, stop=True), r=[NLB[b], tri[d]], w=[psP])
            P.pe(Dl(T_.matmul, psP[:, 128:129], tri[d][:], fcol, start=True, stop=True), r=[tri[d], GL], w=[psP])
            P.pe(Dl(T_.matmul, psP[:, 136:137], ones_f[:], fcol, start=True, stop=True), r=[ones_f, GL], w=[psP])
            P.dve(Dl(V_.tensor_tensor, out=wm[b][:], in0=psP[:, 0:128], in1=msk[d][:], op=ALU.add), r=[psP, msk[d]], w=[wm[b]])
            P.dve(Dl(V_.reduce_max, out=s_[:, 1:2], in_=psP[:, 0:128], axis=AX.X), r=[psP], w=[s_])
            P.dve(Dl(V_.tensor_copy, s_[:, 2:4], psP[:, 128:137:8]), r=[psP], w=[s_])
            P.dve(Dl(V_.reduce_max, out=s_[:, 0:1], in_=wm[b][:], axis=AX.X), r=[wm[b]], w=[s_])
            P.dve(Dl(V_.tensor_scalar, out=s_[:, 4:6], in0=s_[:, 0:2], scalar1=mcur, scalar2=None, op0=ALU.max), r=[s_, mst], w=[s_])
            P.dve(Dl(V_.tensor_scalar, out=s_[:, 6:8], in0=s_[:, 4:6], scalar1=-1.0, scalar2=None, op0=ALU.mult), r=[s_], w=[s_])
            P.dve(Dl(V_.tensor_tensor, out=mnxt, in0=s_[:, 3:4], in1=s_[:, 5:6], op=ALU.add), r=[s_], w=[mst])
            P.dve(Dl(V_.tensor_tensor, out=s_[:, 8:9], in0=icol, in1=s_[:, 2:3], op=ALU.subtract), r=[s_, GL], w=[s_])
            P.act(Dl(A_.activation, out=Dm[b][:], in_=wm[b][:], func=AF.Exp, bias=s_[:, 6:7], scale=1.0), r=[wm[b], s_], w=[Dm[b]])
            P.act(Dl(A_.activation, out=e_[:, 0:1], in_=mcur, func=AF.Exp, bias=s_[:, 6:7], scale=1.0), r=[mst, s_], w=[e_])
            P.act(Dl(A_.activation, out=e_[:, 1:2], in_=mcur, func=AF.Exp, bias=s_[:, 7:8], scale=1.0), r=[mst, s_], w=[e_])
            P.act(Dl(A_.activation, out=e_[:, 2:3], in_=s_[:, 8:9], func=AF.Exp, bias=s_[:, 7:8], scale=1.0), r=[s_], w=[e_])
            P.act(Dl(A_.activation, out=e_[:, 3:4], in_=s_[:, 2:3], func=AF.Exp, bias=s_[:, 6:7], scale=-1.0), r=[s_], w=[e_])

        def stage_b(idx):
            step, ch, i, d, h, ti, tsl, b, icol, fcol, mcur, mnxt, s_, e_, vr = ctxv(idx)
            psP, psQ, psY, psCb, psQ_bf = banks(ch)
            for c in range(2):
                P.pe(Dl(T_.matmul, psP[:, 256:384], qT[i][c][:, tsl], kT[i][c][:, tsl], start=(c == 0), stop=(c == 1)),
                     r=[qT[i][c], kT[i][c]], w=[psP])
            P.dve(Dl(V_.tensor_tensor, out=a_[b][:], in0=psP[:, 256:384], in1=Dm[b][:], op=ALU.mult), r=[psP, Dm[b]], w=[a_[b]])
            P.pe(Dl(T_.transpose, psQ_bf[:, 0:128], a_[b][:], k.ident_bf[:]), r=[a_[b], k.ident_bf], w=[psQ])
            P.act(Dl(A_.copy, aTs[b][:], psQ_bf[:, 0:128]), r=[psQ], w=[aTs[b]])
            P.pe(Dl(T_.matmul, psY[:, 0:257], aTs[b][:], vaug[i][:, ti, 0:257], start=True, stop=True), r=[aTs[b], vr], w=[psY])
            for c in range(2):
                P.pe(Dl(T_.matmul, psQ[:, 128:385], qT[i][c][:, tsl], Cb[ch][c][:, 0:257], start=(c == 0), stop=(c == 1)),
                     r=[qT[i][c], Cb[ch][c]], w=[psQ])
            P.act(Dl(A_.activation, out=Xs[b][:], in_=psQ[:, 128:385], func=AF.Copy, scale=e_[:, 0:1]), r=[psQ, e_], w=[Xs[b]])
            P.dve(Dl(V_.tensor_tensor, out=Z[b][:], in0=psY[:, 0:257], in1=Xs[b][:], op=ALU.add), r=[psY, Xs[b]], w=[Z[b]])
            P.dve(Dl(V_.scalar_tensor_tensor, out=dmr[b][:, 0:1], in0=Z[b][:, 256:257], scalar=-1.0, in1=Z[b][:, 256:257], op0=ALU.mult, op1=ALU.max),
                  r=[Z[b]], w=[dmr[b]])
            P.dve(Dl(V_.tensor_tensor, out=dmr[b][:, 0:1], in0=dmr[b][:, 0:1], in1=e_[:, 3:4], op=ALU.max), r=[dmr[b], e_], w=[dmr[b]])
            P.dve(Dl(V_.reciprocal, out=dmr[b][:, 1:2], in_=dmr[b][:, 0:1]), r=[dmr[b]], w=[dmr[b]])
            P.act(Dl(A_.activation, out=ho[b][:], in_=Z[b][:, 0:256], func=AF.Copy, scale=dmr[b][:, 1:2]), r=[Z[b], dmr[b]], w=[ho[b]])
            P.dma("sp", k.h_d.ap()[d, ti * 128:(ti + 1) * 128, h * 256:(h + 1) * 256], ho[b][:], r=[ho[b]], w=[k.h_d.b((d, ti, h))])

        def stage_c(idx):
            step, ch, i, d, h, ti, tsl, b, icol, fcol, mcur, mnxt, s_, e_, vr = ctxv(idx)
            psP, psQ, psY, psCb, psQ_bf = banks(ch)
            P.act(Dl(A_.activation, out=kw[b][:], in_=ktok[i][:, ti, :], func=AF.Copy, scale=e_[:, 2:3]), r=[ktok[i], e_], w=[kw[b]])
            for c in range(2):
                P.pe(Dl(T_.matmul, psCb[:, 0:257], kw[b][:, c * 128:(c + 1) * 128], vaug[i][:, ti, 0:257], start=True, stop=True),
                     r=[kw[b], vr], w=[psCb])
                P.dve(Dl(V_.scalar_tensor_tensor, out=C32[ch][c][:], in0=C32[ch][c][:], scalar=e_[:, 1:2], in1=psCb[:, 0:257],
                         op0=ALU.mult, op1=ALU.add), r=[C32[ch][c], e_, psCb], w=[C32[ch][c]])
                P.act(Dl(A_.copy, Cb[ch][c][:, 0:257], C32[ch][c][:]), r=[C32[ch][c]], w=[Cb[ch][c]])

        def capture_chain_step(idx):
            P.capture = []
            stage_a(idx)
            stage_b(idx)
            stage_c(idx)
            ops = P.capture
            P.capture = None
            return ops

        for step in range(NTILE):
            for pair in range(NH):
                la = capture_chain_step(step * 2 * NH + 2 * pair)
                lb = capture_chain_step(step * 2 * NH + 2 * pair + 1)
                for j in range(max(len(la), len(lb))):
                    for lst in (la, lb):
                        if j < len(lst):
                            eng, fn, r, w, dma = lst[j]
                            P.op(eng, fn, r, w, dma)
    P.phase_end()

    P.phase_begin()
    hgb = P.sb("hgb", [128, 1024], F32)
    P.dma("sp", hgb[:], k.ml_head_g.ap()[l:l + 1, :].partition_broadcast(128), w=[hgb])
    eps = P.sb("eps", [128, 1], F32)
    P.dve(Dl(V_.memset, eps[:], EPS), w=[eps])
    hf = [P.sb("hf%d" % i, [128, 1024], F32) for i in range(2)]
    hb_ = [P.sb("hbk%d" % i, [128, 1024], F32) for i in range(2)]
    og = [P.sb("og%d" % i, [128, 1024], BF16) for i in range(2)]
    sqm = P.sb("sqm", [128, 1024], F32)
    ssm = [P.sb("ssm%d" % i, [128, 4], F32) for i in range(2)]
    ym = [P.sb("ym%d" % i, [128, 1024], BF16) for i in range(2)]
    fst = [P.sb("fstm%d" % i, [128, KC, 512], BF16) for i in range(2)]
    pst = [k.ps[0][:, :].bitcast(BF16), k.ps[1][:, :].bitcast(BF16)]
    for ti in range(NTILE):
        b = ti % 2
        f_, b_, o_, y_, s_ = hf[b], hb_[b], og[b], ym[b], ssm[b]
        P.dma("sp", f_[:], k.h_d.ap()[0, ti * 128:(ti + 1) * 128, :], r=[k.h_d.b((0, ti, h)) for h in range(4)], w=[f_])
        P.dma("sp", b_[:], k.h_d.ap()[1, ti * 128:(ti + 1) * 128, :], r=[k.h_d.b((1, ti, h)) for h in range(4)], w=[b_])
        P.dma("act", o_[:], k.uT_mlo.ap()[ti * 128:(ti + 1) * 128, :], w=[o_])
        P.pool(Dl(G_.tensor_tensor, out=f_[:], in0=f_[:], in1=b_[:], op=ALU.add), r=[f_, b_], w=[f_])
        P.pool(Dl(G_.tensor_tensor, out=sqm[:], in0=f_[:], in1=f_[:], op=ALU.mult), r=[f_], w=[sqm])
        P.dve(Dl(V_.reduce_sum, out=s_[:], in_=sqm[:].rearrange("p (h d) -> p h d", h=4), axis=AX.X), r=[sqm], w=[s_])
        P.act(Dl(A_.activation, out=s_[:], in_=s_[:], func=AF.Sqrt, scale=1.0 / 256, bias=eps[:, 0:1]), r=[s_, eps], w=[s_])
        P.dve(Dl(V_.reciprocal, out=s_[:], in_=s_[:]), r=[s_], w=[s_])
        for h in range(4):
            P.act(Dl(A_.activation, out=f_[:, h * 256:(h + 1) * 256], in_=f_[:, h * 256:(h + 1) * 256], func=AF.Copy, scale=s_[:, h:h + 1]),
                  r=[f_, s_], w=[f_])
        P.dve(Dl(V_.tensor_tensor, out=f_[:], in0=f_[:], in1=hgb[:], op=ALU.mult), r=[f_, hgb], w=[f_])
        P.dve(Dl(V_.tensor_tensor, out=y_[:], in0=f_[:], in1=o_[:], op=ALU.mult), r=[f_, o_], w=[y_])
        ps = k.ps[b]
        for c in range(KC):
            P.pe(Dl(T_.transpose, pst[b][:, c * 128:(c + 1) * 128], y_[:, c * 128:(c + 1) * 128], k.ident_bf[:]), r=[y_, k.ident_bf], w=[ps])
        g4, t4 = ti // 4, ti % 4
        fs = fst[g4 % 2]
        o_ap = fs[:, :, t4 * 128:(t4 + 1) * 128]
        i_ap = pst[b][:, :].rearrange("p (c t) -> p c t", c=KC)
        if ti % 2:
            P.act(Dl(A_.copy, o_ap, i_ap), r=[ps], w=[fs.b(t4)])
        else:
            P.dve(Dl(V_.tensor_copy, o_ap, i_ap), r=[ps], w=[fs.b(t4)])
        if t4 == 3 or ti == NTILE - 1:
            nt_ = (t4 + 1) * 128
            P.dma("sp", k.br_d.ap()[0].rearrange("(c p) t -> p c t", p=128)[:, :, g4 * 512:g4 * 512 + nt_], fs[:, :, :nt_], r=fs.allb(),
                  w=[k.br_d.b(("ml", g4))])
    P.phase_end()


_CACHE = {}


def kernel(**inputs):
    inp = {k_: np.asarray(v) for k_, v in inputs.items()}
    if "nc" not in _CACHE:
        _CACHE["nc"] = build()[0]
        _CACHE["consts"] = host_consts()
    nc = _CACHE["nc"]
    consts = _CACHE["consts"]
    B = inp["x"].shape[0]
    in_maps = [prep_core_inputs(inp, c % B, consts) for c in range(8)]
    res = run_bass_kernel_spmd(nc, in_maps, core_ids=list(range(8)))
    out = np.stack([np.asarray(res.results[b]["out"]) for b in range(B)], axis=0)
    return out.astype(np.float32)
```

```python
import math
import numpy as np
import concourse.bass as bass
import concourse.mybir as mybir
from concourse.bass_utils import run_bass_kernel_spmd

F32 = mybir.dt.float32
BF16 = mybir.dt.bfloat16
AF = mybir.ActivationFunctionType
ALU = mybir.AluOpType
AX = mybir.AxisListType

D = 1024
KC = 8
DEPTH = 4
NL = 4096
NCX = 256
NT = NL + NCX
NTILE = NT // 128
D_IN = 11280
D_FF = 2816
FC = D_FF // 128
FN_STOP = 99
EPS = 1e-6
SBUF_BASE = 16640
SBUF_BYTES = 229376


class Buf:
    __slots__ = ("name", "w", "rs", "rd", "excl")

    def __init__(self, name=""):
        self.name = name
        self.excl = False
        self.w = None
        self.rs = {}
        self.rd = []


class Ins:
    __slots__ = ("eng", "fn", "deps", "sig", "sigval", "dma", "dsem", "dval", "dprev")

    def __init__(self, eng, fn, dma):
        self.eng = eng
        self.fn = fn
        self.dma = dma
        self.deps = set()
        self.sig = False
        self.sigval = 0
        self.dsem = None
        self.dval = 0
        self.dprev = 0


class Tl:
    def __init__(self, t, name):
        self.t = t
        self.name = name
        self.buf = Buf(name)
        self.subs = {}

    def b(self, i):
        s = self.subs.get(i)
        if s is None:
            s = Buf("%s.%s" % (self.name, i))
            self.subs[i] = s
        return s

    def allb(self):
        return list(self.subs.values())

    def ap(self):
        return self.t.ap()

    def __getitem__(self, k):
        return self.t[k]


class PsBank(Tl):
    def __init__(self, t, i):
        Tl.__init__(self, t, "ps%d" % i)
        self.i = i
        self.buf.excl = True

    def __getitem__(self, key):
        if isinstance(key, tuple):
            return self.t[key[0], self.i, key[1]]
        return self.t[key, self.i, :]


def _bufs(lst):
    out = []
    for x in lst:
        if x is None:
            continue
        if isinstance(x, Tl):
            out.append(x.buf)
        elif isinstance(x, Buf):
            out.append(x)
        else:
            out.extend(_bufs(x))
    return out


class Prog:
    ENGS = ("pe", "act", "dve", "pool", "sp")
    NDMA = {"sp": 40, "act": 16, "pool": 16}

    def __init__(self, nc):
        self.nc = nc
        self.ins = []
        self.eng = {"pe": nc.tensor, "act": nc.scalar, "dve": nc.vector, "pool": nc.gpsimd, "sp": nc.sync}
        self.last = {e: None for e in self.ENGS}
        self.dmas_open = []
        self.sb_off = SBUF_BASE
        self.sb_mark = 0
        self.nid = 0

    def sb(self, name, shape, dtype, persist=False):
        nbytes = int(np.prod(shape[1:])) * (4 if dtype == F32 else 2)
        nbytes = (nbytes + 63) // 64 * 64
        off = self.sb_off
        assert off + nbytes <= SBUF_BYTES, "SBUF overflow %s %d" % (name, off + nbytes)
        self.nid += 1
        t = self.nc.alloc_sbuf_tensor_at("%s_%d" % (name, self.nid), list(shape), dtype, offset=off)
        self.sb_off = off + nbytes
        return Tl(t, name)

    def phase_begin(self):
        self.sb_mark_stack = getattr(self, "sb_mark_stack", [])
        self.sb_mark_stack.append(self.sb_off)

    def phase_end(self):
        self.barrier()
        self.sb_off = self.sb_mark_stack.pop()

    def dram(self, name, shape, dtype, kind="Internal"):
        t = self.nc.dram_tensor(name, list(shape), dtype, kind=kind)
        return Tl(t, name)

    def op(self, eng, fn, r=(), w=(), dma=False):
        if getattr(self, "capture", None) is not None:
            self.capture.append((eng, fn, r, w, dma))
            return None
        i = Ins(eng, fn, dma)
        rb = _bufs(r)
        wb = _bufs(w)
        ex = [b for b in rb if b.excl]
        if ex:
            rb = [b for b in rb if not b.excl]
            wb = wb + [b for b in ex if b not in wb]
        deps = i.deps
        for b in rb:
            if b.w is not None:
                if not (eng == "pe" and b.w.eng == "pe" and not b.w.dma):
                    deps.add(b.w)
        for b in wb:
            x = b.w
            if x is not None and (x.eng != eng or x.dma or dma):
                deps.add(x)
            for x in b.rs.values():
                if x.eng != eng or dma:
                    deps.add(x)
            for x in b.rd:
                deps.add(x)
        for b in rb:
            if dma:
                b.rd.append(i)
            else:
                b.rs[eng] = i
        for b in wb:
            b.w = i
            b.rs = {}
            b.rd = []
        self.ins.append(i)
        if dma:
            self.dmas_open.append(i)
        else:
            self.last[eng] = i
        return i

    def pe(self, fn, r=(), w=()):
        return self.op("pe", fn, r, w)

    def act(self, fn, r=(), w=()):
        return self.op("act", fn, r, w)

    def dve(self, fn, r=(), w=()):
        return self.op("dve", fn, r, w)

    def pool(self, fn, r=(), w=()):
        return self.op("pool", fn, r, w)

    def dma(self, q, out, in_, r=(), w=()):
        e = self.eng[q]
        return self.op(q, lambda: e.dma_start(out=out, in_=in_), r, w, dma=True)

    def barrier(self):
        prev = [x for x in self.last.values() if x is not None] + list(self.dmas_open)
        self.dmas_open = []
        for e in self.ENGS:
            eh = self.eng[e]
            i = Ins(e, (lambda eh=eh: eh.nop()), False)
            i.deps = set(prev)
            self.ins.append(i)
            self.last[e] = i

    def finalize(self):
        nc = self.nc
        self.barrier()
        esem = {e: nc.alloc_semaphore("es_" + e) for e in self.ENGS}
        dsems = {q: [nc.alloc_semaphore("ds_%s%d" % (q, k)) for k in range(n)] for q, n in self.NDMA.items()}
        duse = {q: [0] * n for q, n in self.NDMA.items()}
        drr = {q: 0 for q in self.NDMA}
        for i in self.ins:
            for d in i.deps:
                if not d.dma:
                    d.sig = True
        cnt = {e: 0 for e in self.ENGS}
        for i in self.ins:
            if i.dma:
                q = i.eng
                k = drr[q]
                drr[q] = (k + 1) % self.NDMA[q]
                i.dsem = dsems[q][k]
                i.dprev = duse[q][k]
                duse[q][k] += 16
                i.dval = duse[q][k]
            elif i.sig:
                cnt[i.eng] += 1
                i.sigval = cnt[i.eng]
        seen = {e: {} for e in self.ENGS}
        nwait = 0
        self.trace = {e: [] for e in self.ENGS}
        for i in self.ins:
            e = i.eng
            eh = self.eng[e]
            need = {}
            for d in i.deps:
                if d.dma:
                    s, v = d.dsem, d.dval
                else:
                    s, v = esem[d.eng], d.sigval
                if need.get(s, 0) < v:
                    need[s] = v
            if i.dma and i.dprev > 0:
                if need.get(i.dsem, 0) < i.dprev:
                    need[i.dsem] = i.dprev
            sn = seen[e]
            wl = []
            for s, v in need.items():
                if sn.get(s, 0) < v:
                    eh.wait_ge(s, v)
                    sn[s] = v
                    nwait += 1
                    wl.append((id(s), v))
            ins = i.fn()
            if i.dma:
                ins.then_inc(i.dsem, 16)
                self.trace[e].append((wl, (id(i.dsem), 16)))
            elif i.sig:
                ins.then_inc(esem[e], 1)
                self.trace[e].append((wl, (id(esem[e]), 1)))
            else:
                self.trace[e].append((wl, None))
        self.simulate()
        return dict(n_ins=len(self.ins), n_wait=nwait, sig=dict(cnt))

    def simulate(self):
        sem = {}
        pc = {e: 0 for e in self.ENGS}
        tr = self.trace
        while True:
            prog = False
            done = True
            for e in self.ENGS:
                t = tr[e]
                while pc[e] < len(t):
                    wl, inc = t[pc[e]]
                    if any(sem.get(s, 0) < v for s, v in wl):
                        break
                    if inc is not None:
                        sem[inc[0]] = sem.get(inc[0], 0) + inc[1]
                    pc[e] += 1
                    prog = True
                if pc[e] < len(t):
                    done = False
            if done:
                return
            if not prog:
                raise RuntimeError("sync deadlock at %s" % {e: (pc[e], len(tr[e])) for e in self.ENGS})


BLKS = [(i * 512, 512, 0) for i in range(NL // 512)] + [(NL, NCX, 1)]
TB1 = 256
BLKS1 = [(i * TB1, TB1, 0) for i in range(NL // TB1)] + [(NL, NCX, 1)]
LAT_TILES = list(range(NL // 128))
CTX_TILES = [NL // 128 + i for i in range(NCX // 128)]

C_MLQ, C_MLK, C_MLV, C_MLO, C_MLG, C_DAQ, C_DAK, C_DAV, C_FN, C_GP = 0, 1024, 2048, 3072, 4096, 4112, 5136, 6160, 7184, 8208
F_MLQ, F_MLK, F_DAQ, F_DAK, F_FN = 0, 1024, 2048, 3072, 4096


def host_consts():
    c = {}
    c["ident_f"] = np.eye(128, dtype=np.float32)
    perm = np.array([(m // 64) * 64 + ((m % 64) + 32) % 64 for m in range(128)])
    pw = np.zeros((128, 128), np.float32)
    pw[perm, np.arange(128)] = 1.0
    c["pswap"] = pw
    n_freq = 16
    inv = (10000.0 ** (-np.arange(n_freq, dtype=np.float32) / n_freq)).astype(np.float32)
    rows = NL // 64
    r = np.repeat(np.arange(rows, dtype=np.float32), 64)
    col = np.tile(np.arange(64, dtype=np.float32), rows)
    ang = np.concatenate([r[:, None] * inv, col[:, None] * inv], axis=-1).astype(np.float32)
    cos = np.cos(ang).astype(np.float32).T
    sin = np.sin(ang).astype(np.float32).T
    ct = np.zeros((128, NL), np.float32)
    st = np.zeros((128, NL), np.float32)
    for p in range(128):
        d = p % 64
        f = d % 32
        ct[p] = cos[f]
        st[p] = -sin[f] if d < 32 else sin[f]
    c["rope_c"] = ct
    c["rope_s"] = st
    c.update(host_consts_fn())
    c.update(host_consts_ml())
    return c


class K:
    pass


def Dl(fn, *a, **kw):
    return lambda: fn(*a, **kw)


def build(debug=None, nlayers=DEPTH, stop_after=None, skip=(), br_input=False, uf_input=False, fn_stop=99):
    global FN_STOP
    FN_STOP = fn_stop
    nc = bass.Bass("TRN2", target_bir_lowering=False)
    P = Prog(nc)
    k = K()
    k.nc, k.P = nc, P
    k.da_heads = 8

    def ein(name, shape, dt=F32):
        return Tl(nc.dram_tensor(name, list(shape), dt, kind="ExternalInput"), name)

    dbg_kind = {}

    def scratch(name, shape, dt):
        kind = "ExternalOutput" if (debug and name in debug) else "Internal"
        return Tl(nc.dram_tensor(name, list(shape), dt, kind=kind), name)

    k.xT_in = ein("xT", [128, KC, NT])
    k.cT = ein("cT", [128, KC, 2])
    k.w_ada = ein("w_ada", [DEPTH, D, 6 * D])
    k.b_adaT = ein("b_adaT", [DEPTH, 128, 48])
    k.norm_gT = ein("norm_gT", [DEPTH, 128, 2, KC])
    k.w_in = ein("w_in", [DEPTH, D, D_IN])
    k.ident_f_d = ein("ident_f", [128, 128])
    k.pswap_d = ein("pswap", [128, 128])
    k.rope_c_d = ein("rope_c", [128, NL])
    k.rope_s_d = ein("rope_s", [128, NL])
    k.da_lam = ein("da_lam", [DEPTH, 4, 64])
    k.da_head_gT = ein("da_head_gT", [DEPTH, 128, 8])
    k.w_br_ml = ein("w_br_ml", [DEPTH, D, D])
    k.w_br_da = ein("w_br_da", [DEPTH, D, D])
    k.w_br_fn = ein("w_br_fn", [DEPTH, D, D])
    k.w_out = ein("w_out", [DEPTH, D, D])
    k.w_ffn_in = ein("w_ffn_in", [DEPTH, D, 2 * D_FF])
    k.w_ffn_out = ein("w_ffn_out", [DEPTH, D_FF, D])
    k.final_gT = ein("final_gT", [128, KC])
    k.ml_triF = ein("ml_triF", [128, 128])
    k.ml_triB = ein("ml_triB", [128, 128])
    k.ml_maskF = ein("ml_maskF", [128, 128])
    k.ml_maskB = ein("ml_maskB", [128, 128])
    k.ml_gate_bR = ein("ml_gate_bR", [DEPTH, NTILE * 16])
    k.ml_head_g = ein("ml_head_g", [DEPTH, 1024])
    k.fn_cs = ein("fn_cs", [128, 2, 512], BF16)
    k.fn_w1 = ein("fn_w1", [128, 128], BF16)
    k.fn_twr = ein("fn_twr", [128, 256])
    k.fn_twi = ein("fn_twi", [128, 256])
    k.fn_w3c = ein("fn_w3c", [128, 128], BF16)
    k.fn_w3s = ein("fn_w3s", [128, 128], BF16)
    k.fn_csx = ein("fn_csx", [128, 2, 2, 256], BF16)

    k.hl_d = scratch("hl_d", [128, KC, NT], BF16)
    k.uF_d = ein("uF_d", [5120, NT], BF16) if uf_input else scratch("uF_d", [5120, NT], BF16)
    k.uT_mlk = scratch("uT_mlk", [NT, 1024], BF16)
    k.uT_mlv = scratch("uT_mlv", [NT, 1024], BF16)
    k.uT_mlo = scratch("uT_mlo", [NT, 1024], BF16)
    k.uT_g = scratch("uT_g", [NT, 16], F32)
    k.uT_dav = scratch("uT_dav", [NT, 1024], BF16)
    k.G_d = scratch("G_d", [3072, NT], BF16)
    k.mods_d = scratch("mods_d", [128, DEPTH * 96], F32)
    k.br_d = ein("br_d", [3, 1024, NT], BF16) if br_input else scratch("br_d", [3, 1024, NT], BF16)
    k.x1_d = scratch("x1_d", [128, KC, NT], F32)
    k.AB_d = scratch("AB_d", [2, NL, 512], BF16)
    k.h_d = scratch("h_d", [2, NT, 1024], F32)
    k.h2_d = scratch("h2_d", [128, KC, NT], BF16)
    k.gT_d = scratch("gT_d", [128, FC, NT], BF16)
    k.xT_d = scratch("xT_d", [128, KC, NT], F32)
    k.out = Tl(nc.dram_tensor("out", [NL, D], F32, kind="ExternalOutput"), "out")
    k.dbg = scratch("dbg", [6, 128, 512], F32) if (debug and "dbg" in debug) else None

    k.ident_f = P.sb("ident_f", [128, 128], F32)
    k.ident_bf = P.sb("ident_bf", [128, 128], BF16)
    k.ones_bf = P.sb("ones_bf", [128, 128], BF16)
    k.mods = P.sb("mods", [128, DEPTH * 96], F32)
    k.Gt = P.sb("Gt", [128, DEPTH * 2 * KC * 2], F32)
    k.psall = nc.alloc_psum_tensor("psall", [128, 8, 512], F32)
    k.ps = [PsBank(k.psall, i) for i in range(8)]

    P.dma("sp", k.ident_f[:], k.ident_f_d.ap()[:, :], w=[k.ident_f])
    P.dve(lambda: nc.vector.tensor_copy(k.ident_bf[:], k.ident_f[:]), r=[k.ident_f], w=[k.ident_bf])
    P.dve(lambda: nc.vector.memset(k.ones_bf[:], 1.0), w=[k.ones_bf])

    if "adaln" not in skip:
        phase_adaln(k)
    if debug and "mods_d" in debug:
        P.dma("sp", k.mods_d.ap()[:, :], k.mods[:], r=[k.mods], w=[k.mods_d])
    for l in range(nlayers):
        if l == 0 and "m1" not in skip:
            phase_norm(k, l, 0, k.xT_in, k.hl_d)
        if stop_after == "norm":
            break
        if "m1" not in skip:
            phase_m1(k, l)
        if stop_after == "m1":
            break
        if "da" not in skip:
            phase_da(k, l)
        if stop_after == "da":
            break
        if "fn" not in skip:
            phase_fn(k, l)
        if stop_after == "fn":
            break
        if "ml" not in skip:
            phase_ml(k, l)
        if stop_after == "ml":
            break
        phase_t1(k, l, k.xT_in if l == 0 else k.xT_d)
        phase_t2a(k, l)
        phase_t2b(k, l, last=(l == nlayers - 1))
    info = P.finalize()
    return nc, info


def mods_view(k, l, chunk0, sel):
    base = l * 96 + chunk0 * 2 + sel
    return k.mods[:, base:base + 15:2]


def mod_col(k, l, chunk, sel):
    base = l * 96 + chunk * 2 + sel
    return k.mods[:, base:base + 1]


def g_col(k, l, which, kc, sel):
    base = ((l * 2 + which) * KC + kc) * 2 + sel
    return k.Gt[:, base:base + 1]


def phase_adaln(k):
    nc, P = k.nc, k.P
    P.phase_begin()
    s_c = P.sb("s_c", [128, KC * 2], F32)
    P.dma("sp", s_c[:], k.cT.ap().rearrange("p k s -> p (k s)"), w=[s_c])
    P.act(lambda: nc.scalar.activation(out=s_c[:], in_=s_c[:], func=AF.Silu), r=[s_c], w=[s_c])
    wst = [P.sb("wa%d" % i, [128, KC, 512], F32) for i in range(2)]
    bad = P.sb("bad", [128, DEPTH * 48], F32)
    ng = P.sb("ng", [128, DEPTH * 2 * KC], F32)
    P.dma("sp", bad[:].rearrange("p (l c) -> p l c", l=DEPTH), k.b_adaT.ap().rearrange("l p c -> p l c"), w=[bad])
    P.dma("sp", ng[:].rearrange("p (l c) -> p l c", l=DEPTH), k.norm_gT.ap().rearrange("l p w c -> p l (w c)"), w=[ng])
    ps0 = k.ps[0]
    gi = 0
    for l in range(DEPTH):
        wv = k.w_ada.ap()[l].rearrange("(k p) n -> p k n", p=128)
        for g in range(12):
            w_ = wst[gi % 2]
            gi += 1
            P.dma("sp", w_[:], wv[:, :, g * 512:(g + 1) * 512], w=[w_])
            for j in range(4):
                col = g * 4 + j
                for kc in range(KC):
                    P.pe(lambda w_=w_, j=j, kc=kc, col=col: nc.tensor.matmul(
                        ps0[:, 2 * col:2 * col + 2], w_[:, kc, j * 128:(j + 1) * 128], s_c[:, 2 * kc:2 * kc + 2],
                        start=(kc == 0), stop=(kc == KC - 1)), r=[w_, s_c], w=[ps0])
        for s in range(2):
            P.dve(lambda l=l, s=s: nc.vector.tensor_tensor(
                out=k.mods[:, l * 96 + s:(l + 1) * 96:2], in0=ps0[:, s:96:2], in1=bad[:, l * 48:(l + 1) * 48], op=ALU.add),
                r=[ps0, bad], w=[k.mods])
        for which in range(2):
            for s in range(2):
                sc = mods_view(k, l, 8 + 24 * which, s)
                gb = ((l * 2 + which) * KC) * 2 + s
                P.dve(lambda sc=sc, gb=gb, l=l, which=which: nc.vector.scalar_tensor_tensor(
                    out=k.Gt[:, gb:gb + 15:2], in0=sc, scalar=1.0, in1=ng[:, (l * 2 + which) * KC:(l * 2 + which + 1) * KC],
                    op0=ALU.add, op1=ALU.mult), r=[k.mods, ng], w=[k.Gt])
    P.phase_end()


def phase_norm(k, l, which, xsrc, hdst):
    nc, P = k.nc, k.P
    P.phase_begin()
    xb = [P.sb("xb%d" % i, [128, KC, 512], F32) for i in range(2)]
    sq = P.sb("sq", [128, KC, 512], BF16)
    rs = P.sb("rs", [128, 512], F32)
    tmp = [P.sb("tmp%d" % i, [128, 512], F32) for i in range(2)]
    hb = [P.sb("hb%d" % i, [128, KC, 512], BF16) for i in range(2)]
    eps = P.sb("eps", [128, 1], F32)
    P.dve(lambda: nc.vector.memset(eps[:], EPS), w=[eps])
    for bi, (t0, n, sel) in enumerate(BLKS):
        x_, h_ = xb[bi % 2], hb[bi % 2]
        P.dma("sp", x_[:, :, :n], xsrc.ap()[:, :, t0:t0 + n], w=[x_])
        norm_block(k, l, which, sel, x_, n, sq, rs, tmp, h_, eps, k.ps[1])
        P.dma("sp", hdst.ap()[:, :, t0:t0 + n], h_[:, :, :n], r=[h_], w=[hdst])
    P.phase_end()


def norm_block(k, l, which, sel, x_, n, sq, rs, tmp, h_, eps, ps):
    nc, P = k.nc, k.P
    P.pool(lambda: nc.gpsimd.tensor_tensor(out=sq[:, :, :n], in0=x_[:, :, :n], in1=x_[:, :, :n], op=ALU.mult), r=[x_], w=[sq])
    for kc in range(KC):
        P.pe(lambda kc=kc: nc.tensor.matmul(ps[:, :n], k.ones_bf[:], sq[:, kc, :n], start=(kc == 0), stop=(kc == KC - 1)),
             r=[k.ones_bf, sq], w=[ps])
    P.act(lambda: nc.scalar.activation(out=rs[:, :n], in_=ps[:, :n], func=AF.Sqrt, scale=1.0 / D, bias=eps[:, 0:1]), r=[ps, eps], w=[rs])
    P.dve(lambda: nc.vector.reciprocal(out=rs[:, :n], in_=rs[:, :n]), r=[rs], w=[rs])
    for kc in range(KC):
        t_ = tmp[kc % 2]
        P.dve(lambda kc=kc, t_=t_: nc.vector.tensor_tensor(out=t_[:, :n], in0=x_[:, kc, :n], in1=rs[:, :n], op=ALU.mult), r=[x_, rs], w=[t_])
        P.act(lambda kc=kc, t_=t_: nc.scalar.activation(out=h_[:, kc, :n], in_=t_[:, :n], func=AF.Identity,
                                                       scale=g_col(k, l, which, kc, sel), bias=mod_col(k, l, 24 * which + kc, sel)),
              r=[t_, k.Gt, k.mods], w=[h_])


def load_w_group(k, wv, c0, ncol, wst, wbf):
    nc, P = k.nc, k.P
    P.dma("sp", wst[:, :, :ncol], wv[:, :, c0:c0 + ncol], w=[wst])
    P.pool(lambda: nc.gpsimd.tensor_copy(wbf[:, :, :ncol], wst[:, :, :ncol]), r=[wst], w=[wbf])


def phase_m1(k, l):
    nc, P = k.nc, k.P
    P.phase_begin()
    hl = P.sb("hl", [128, KC, NT], BF16)
    for kc in range(KC):
        P.dma("sp", hl[:, kc, :], k.hl_d.ap()[:, kc, :], w=[hl.b(kc)])
    hlr = hl.allb()
    wst = [P.sb("wst%d" % i, [128, KC, 512], F32) for i in range(2)]
    wbf = [P.sb("wbf%d" % i, [128, KC, 512], BF16) for i in range(2)]
    fst = [P.sb("fst%d" % i, [128, NT], BF16) for i in range(2)]
    tst = [P.sb("tst%d" % i, [128, 512], BF16) for i in range(3)]
    tstf = [P.sb("tstf%d" % i, [128, 16], F32) for i in range(2)]
    ropc = P.sb("ropc", [128, NL], F32)
    rops = P.sb("rops", [128, NL], F32)
    pswap = P.sb("pswap", [128, 128], F32)
    q32 = [P.sb("q32_%d" % i, [128, 512], F32) for i in range(2)]
    t1 = [P.sb("t1_%d" % i, [128, 512], F32) for i in range(2)]
    t2 = [P.sb("t2_%d" % i, [128, 512], F32) for i in range(2)]
    P.dma("act", ropc[:], k.rope_c_d.ap()[:, :], w=[ropc])
    P.dma("act", rops[:], k.rope_s_d.ap()[:, :], w=[rops])
    P.dma("act", pswap[:], k.pswap_d.ap()[:, :], w=[pswap])
    wv = k.w_in.ap()[l].rearrange("(k p) n -> p k n", p=128)
    st = dict(g=0, ps=0, f=0, t=0, ev=0, r=0)

    def nextps():
        st["ps"] = (st["ps"] + 1) % 6
        return k.ps[2 + st["ps"]]

    def feat_group(c0, dst, row0, kind, tok_blks=BLKS):
        w_, wb_ = wst[st["g"] % 2], wbf[st["g"] % 2]
        st["g"] += 1
        prefetch_next()
        for j in range(4):
            stg = fst[st["f"] % 2]
            st["f"] += 1
            for bi, (t0, n, sel) in enumerate(tok_blks):
                ps = nextps()
                sb_ = stg.b(bi)
                for kc in range(KC):
                    P.pe(lambda ps=ps, wb_=wb_, j=j, kc=kc, t0=t0, n=n: nc.tensor.matmul(
                        ps[:, :n], wb_[:, kc, j * 128:(j + 1) * 128], hl[:, kc, t0:t0 + n], start=(kc == 0), stop=(kc == KC - 1)),
                        r=[wb_, hlr], w=[ps])
                o = stg[:, t0:t0 + n]
                if kind == "rope" and sel == 0:
                    q_, a_, b_ = q32[st["r"] % 2], t1[st["r"] % 2], t2[st["r"] % 2]
                    st["r"] += 1
                    psr = k.ps[0 + st["r"] % 2]
                    P.act(lambda ps=ps, q_=q_, n=n: nc.scalar.copy(q_[:, :n], ps[:, :n]), r=[ps], w=[q_])
                    P.pe(lambda psr=psr, q_=q_, n=n: nc.tensor.matmul(psr[:, :n], pswap[:], q_[:, :n], start=True, stop=True),
                         r=[pswap, q_], w=[psr])
                    P.pool(lambda a_=a_, q_=q_, t0=t0, n=n: nc.gpsimd.tensor_tensor(out=a_[:, :n], in0=q_[:, :n], in1=ropc[:, t0:t0 + n], op=ALU.mult),
                           r=[q_, ropc], w=[a_])
                    P.dve(lambda b_=b_, psr=psr, t0=t0, n=n: nc.vector.tensor_tensor(out=b_[:, :n], in0=psr[:, :n], in1=rops[:, t0:t0 + n], op=ALU.mult),
                          r=[psr, rops], w=[b_])
                    P.dve(lambda o=o, a_=a_, b_=b_, n=n: nc.vector.tensor_tensor(out=o, in0=a_[:, :n], in1=b_[:, :n], op=ALU.add),
                          r=[a_, b_], w=[sb_])
                elif kind == "sigm":
                    P.act(lambda o=o, ps=ps, n=n: nc.scalar.activation(out=o, in_=ps[:, :n], func=AF.Sigmoid), r=[ps], w=[sb_])
                elif kind == "s16":
                    P.act(lambda o=o, ps=ps, n=n: nc.scalar.mul(o, ps[:, :n], 1.0 / 16.0), r=[ps], w=[sb_])
                else:
                    st["ev"] += 1
                    if st["ev"] % 2:
                        P.act(lambda o=o, ps=ps, n=n: nc.scalar.copy(o, ps[:, :n]), r=[ps], w=[sb_])
                    else:
                        P.dve(lambda o=o, ps=ps, n=n: nc.vector.tensor_copy(o, ps[:, :n]), r=[ps], w=[sb_])
            ta, tb = tok_blks[0][0], tok_blks[-1][0] + tok_blks[-1][1]
            P.dma("sp", dst.ap()[row0 + j * 128:row0 + (j + 1) * 128, ta:tb], stg[:, ta:tb], r=stg.allb(), w=[dst])

    def tok_group(c0, ncol, dst, dcol0, kind):
        w_, wb_ = wst[st["g"] % 2], wbf[st["g"] % 2]
        st["g"] += 1
        prefetch_next()
        for ti in range(NTILE):
            ps = nextps()
            for kc in range(KC):
                P.pe(lambda ps=ps, wb_=wb_, kc=kc, ti=ti: nc.tensor.matmul(
                    ps[:, :ncol], hl[:, kc, ti * 128:(ti + 1) * 128], wb_[:, kc, :ncol], start=(kc == 0), stop=(kc == KC - 1)),
                    r=[wb_, hlr], w=[ps])
            if kind == "f32":
                stg = tstf[st["t"] % 2]
            else:
                stg = tst[st["t"] % 3]
            st["t"] += 1
            o = stg[:, :ncol]
            if kind == "sigm":
                P.act(lambda o=o, ps=ps: nc.scalar.activation(out=o, in_=ps[:, :ncol], func=AF.Sigmoid), r=[ps], w=[stg])
            elif kind == "s16":
                P.act(lambda o=o, ps=ps: nc.scalar.mul(o, ps[:, :ncol], 1.0 / 16.0), r=[ps], w=[stg])
            else:
                st["ev"] += 1
                if st["ev"] % 2:
                    P.act(lambda o=o, ps=ps: nc.scalar.copy(o, ps[:, :ncol]), r=[ps], w=[stg])
                else:
                    P.dve(lambda o=o, ps=ps: nc.vector.tensor_copy(o, ps[:, :ncol]), r=[ps], w=[stg])
            P.dma("sp", dst.ap()[ti * 128:(ti + 1) * 128, dcol0:dcol0 + ncol], o, r=[stg], w=[dst])

    tasks = []
    for g in range(2):
        tasks.append(("f", C_MLQ + g * 512, 512, k.uF_d, F_MLQ + g * 512, "copy"))
        tasks.append(("f", C_MLK + g * 512, 512, k.uF_d, F_MLK + g * 512, "s16"))
        tasks.append(("f", C_DAQ + g * 512, 512, k.uF_d, F_DAQ + g * 512, "rope"))
        tasks.append(("f", C_DAK + g * 512, 512, k.uF_d, F_DAK + g * 512, "rope"))
        tasks.append(("f", C_FN + g * 512, 512, k.uF_d, F_FN + g * 512, "copy"))
        tasks.append(("t", C_MLK + g * 512, 512, k.uT_mlk, g * 512, "s16"))
        tasks.append(("t", C_MLV + g * 512, 512, k.uT_mlv, g * 512, "copy"))
        tasks.append(("t", C_MLO + g * 512, 512, k.uT_mlo, g * 512, "sigm"))
        tasks.append(("t", C_DAV + g * 512, 512, k.uT_dav, g * 512, "copy"))
    tasks.append(("t", C_MLG, 16, k.uT_g, 0, "f32"))
    for g in range(6):
        tasks.append(("f", C_GP + g * 512, 512, k.G_d, g * 512, "sigm"))
    pf = dict(i=0)

    def prefetch_next():
        i = pf["i"]
        if i < len(tasks):
            t = tasks[i]
            load_w_group(k, wv, t[1], t[2], wst[i % 2], wbf[i % 2])
            pf["i"] = i + 1

    prefetch_next()
    for t in tasks:
        if t[0] == "f":
            feat_group(t[1], t[3], t[4], t[5])
        else:
            tok_group(t[1], t[2], t[3], t[4], t[5])
    P.phase_end()


def fm(a):
    return np.ascontiguousarray(a.T.reshape(KC, 128, a.shape[0]).transpose(1, 0, 2))


def prep_core_inputs(inp, b, consts):
    m = {}
    xall = np.concatenate([inp["x"][b], inp["ctx"][b]], axis=0)
    m["xT"] = fm(xall).astype(np.float32)
    cc = np.stack([inp["c"][b], inp["c_ctx"]], axis=0)
    m["cT"] = np.ascontiguousarray(cc.T.reshape(KC, 128, 2).transpose(1, 0, 2)).astype(np.float32)
    m["w_ada"] = inp["w_ada"]
    m["b_adaT"] = np.ascontiguousarray(inp["b_ada"].reshape(DEPTH, 48, 128).transpose(0, 2, 1))
    m["norm_gT"] = np.ascontiguousarray(inp["norm_g"].reshape(DEPTH, 2, KC, 128).transpose(0, 3, 1, 2))
    m["w_in"] = inp["w_in"]
    m["da_lam"] = inp["da_lam"]
    m["ml_gate_bR"] = np.ascontiguousarray(np.tile(inp["ml_gate_b"][:, None, :], (1, NTILE, 1)).reshape(DEPTH, NTILE * 16))
    m["ml_head_g"] = inp["ml_head_g"]
    for nm in ("w_br_ml", "w_br_da", "w_br_fn", "w_out", "w_ffn_in", "w_ffn_out"):
        m[nm] = inp[nm]
    m["final_gT"] = np.ascontiguousarray(inp["final_g"].reshape(KC, 128).T)
    m["da_head_gT"] = np.ascontiguousarray(inp["da_head_g"].reshape(DEPTH, 8, 128).transpose(0, 2, 1))
    m.update(consts)
    return m


def lam_init_of(l):
    return 0.8 - 0.6 * math.exp(-0.3 * l)


def phase_da(k, l):
    nc, P = k.nc, k.P
    V_, A_, T_, G_ = nc.vector, nc.scalar, nc.tensor, nc.gpsimd
    P.phase_begin()
    li = lam_init_of(l)
    lq = P.sb("lq", [128, 256], F32)
    P.dma("sp", lq[:], k.da_lam.ap()[l:l + 1].rearrange("o a d -> o (a d)").partition_broadcast(128), w=[lq])
    pr = P.sb("pr", [128, 128], F32)
    sc = P.sb("sc", [128, 8], F32)
    eps = P.sb("eps", [128, 1], F32)
    P.dve(Dl(V_.memset, eps[:], EPS), w=[eps])
    P.dve(Dl(V_.tensor_tensor, out=pr[:, 0:64], in0=lq[:, 0:64], in1=lq[:, 64:128], op=ALU.mult), r=[lq], w=[pr])
    P.dve(Dl(V_.tensor_tensor, out=pr[:, 64:128], in0=lq[:, 128:192], in1=lq[:, 192:256], op=ALU.mult), r=[lq], w=[pr])
    P.dve(Dl(V_.reduce_sum, out=sc[:, 0:1], in_=pr[:, 0:64], axis=AX.X), r=[pr], w=[sc])
    P.dve(Dl(V_.reduce_sum, out=sc[:, 1:2], in_=pr[:, 64:128], axis=AX.X), r=[pr], w=[sc])
    P.act(Dl(A_.activation, out=sc[:, 2:4], in_=sc[:, 0:2], func=AF.Exp), r=[sc], w=[sc])
    P.dve(Dl(V_.scalar_tensor_tensor, out=sc[:, 4:5], in0=sc[:, 3:4], scalar=-li, in1=sc[:, 2:3], op0=ALU.add, op1=ALU.subtract),
          r=[sc], w=[sc])
    neglam = sc[:, 4:5]
    hg = P.sb("hg", [128, 8], F32)
    P.dma("sp", hg[:], k.da_head_gT.ap()[l], w=[hg])
    P.dve(Dl(V_.tensor_scalar, out=hg[:], in0=hg[:], scalar1=(1.0 - li), scalar2=None, op0=ALU.mult), r=[hg], w=[hg])

    qz = [[P.sb("qz%d%d" % (i, m), [128, NT], BF16) for m in range(2)] for i in range(2)]
    for i in range(2):
        for m in range(2):
            P.pool(Dl(G_.memset, qz[i][m][:], 0.0), w=[qz[i][m]])
    kT = [P.sb("kT%d" % i, [128, NT], BF16) for i in range(2)]
    V = [P.sb("V%d" % i, [128, NTILE, 128], BF16) for i in range(2)]
    pt = [P.sb("pt%d" % i, [128, 2, 512], BF16) for i in range(3)]
    ystg = [P.sb("ystg%d" % i, [128, NT], BF16) for i in range(2)]
    rd = [P.sb("rd%d" % i, [128, 512], F32) for i in range(2)]
    o_ = [P.sb("o%d" % i, [128, 512], F32) for i in range(2)]
    acc = [P.sb("acc%d" % i, [128, 2, 512], F32) for i in range(2)]
    ones_f = P.sb("ones_fd", [128, 128], F32)
    P.dve(Dl(V_.memset, ones_f[:], 1.0), w=[ones_f])
    ofs = [P.sb("of%d" % i, [128, 512], F32) for i in range(2)]
    sqs = [P.sb("sqd%d" % i, [128, 512], BF16) for i in range(2)]
    rs = P.sb("rsd", [128, 512], F32)
    deferred = []
    cO = [[P.sb("cO%d%d" % (i, m), [128, 512], F32) for m in range(2)] for i in range(2)]
    cD = [[P.sb("cD%d%d" % (i, m), [128, 512], F32) for m in range(2)] for i in range(2)]
    psO = [k.ps[4], k.ps[5]]
    psD = [k.ps[6], k.ps[7]]
    psE = k.ps[0]
    cnt = dict(s=0, p=0)

    for h in range(k.da_heads):
        qz_, k_, v_, y_ = qz[h % 2], kT[h % 2], V[h % 2], ystg[h % 2]
        for m in range(2):
            r0 = F_DAQ + h * 128 + 64 * m
            P.dma("sp", qz_[m][64 * m:64 * m + 64, :], k.uF_d.ap()[r0:r0 + 64, :], w=[qz_[m]])
        P.dma("sp", k_[:], k.uF_d.ap()[F_DAK + h * 128:F_DAK + (h + 1) * 128, :], w=[k_])
        P.dma("sp", v_[:], k.uT_dav.ap()[:, h * 128:(h + 1) * 128].rearrange("(n p) d -> p n d", p=128), w=[v_])
        for bi, (t0, n, sel) in enumerate(BLKS):
            ktiles = (LAT_TILES + CTX_TILES) if sel == 0 else CTX_TILES
            steps = [(m, ktiles[j], ktiles[j + 1], j) for m in range(2) for j in range(0, len(ktiles), 2)]
            nk = len(ktiles)

            def emit_s(i):
                m, ka, kb, j = steps[i]
                pi = cnt["s"] % 2
                cnt["s"] += 1
                banks = [k.ps[2 * pi], k.ps[2 * pi + 1]]
                for x, kt in enumerate((ka, kb)):
                    P.pe(Dl(T_.matmul, banks[x][:, :n], k_[:, kt * 128:(kt + 1) * 128], qz_[m][:, t0:t0 + n], start=True, stop=True),
                         r=[k_, qz_[m]], w=[banks[x]])
                p_ = pt[cnt["p"] % 3]
                cnt["p"] += 1
                P.act(Dl(A_.activation, out=p_[:, :, :n], in_=k.psall[:, 2 * pi:2 * pi + 2, :n], func=AF.Exp, scale=0.125), r=banks, w=[p_])
                return p_

            def emit_o(i, p_):
                m, ka, kb, j = steps[i]
                for x, kt in enumerate((ka, kb)):
                    jj = j + x
                    P.pe(Dl(T_.matmul, psO[m][:, :n], v_[:, kt, :], p_[:, x, :n], start=(jj == 0), stop=(jj == nk - 1)),
                         r=[v_, p_], w=[psO[m]])
                    P.pe(Dl(T_.matmul, psD[m][:, :n], k.ones_bf[:], p_[:, x, :n], start=(jj == 0), stop=(jj == nk - 1)),
                         r=[k.ones_bf, p_], w=[psD[m]])

            pend = []
            for i in range(len(steps)):
                pend.append((i, emit_s(i)))
                if len(pend) > 1:
                    emit_o(*pend.pop(0))
                if i == 8 or i == len(steps) - 1:
                    while deferred:
                        deferred.pop(0)()
            while pend:
                emit_o(*pend.pop(0))
            of_, sq_ = ofs[bi % 2], sqs[bi % 2]
            cO_, cD_ = cO[bi % 2], cD[bi % 2]
            for m in range(2):
                P.act(Dl(A_.copy, cD_[m][:, :n], psD[m][:, :n]), r=[psD[m]], w=[cD_[m]])
                P.act(Dl(A_.copy, cO_[m][:, :n], psO[m][:, :n]), r=[psO[m]], w=[cO_[m]])
            for m in range(2):
                P.dve(Dl(V_.reciprocal, out=rd[m][:, :n], in_=cD_[m][:, :n]), r=[cD_[m]], w=[rd[m]])
                P.dve(Dl(V_.tensor_tensor, out=o_[m][:, :n], in0=cO_[m][:, :n], in1=rd[m][:, :n], op=ALU.mult),
                      r=[cO_[m], rd[m]], w=[o_[m]])
            P.dve(Dl(V_.scalar_tensor_tensor, out=of_[:, :n], in0=o_[1][:, :n], scalar=neglam, in1=o_[0][:, :n], op0=ALU.mult, op1=ALU.add),
                  r=[o_[0], o_[1], sc], w=[of_])
            P.pool(Dl(G_.tensor_tensor, out=sq_[:, :n], in0=of_[:, :n], in1=of_[:, :n], op=ALU.mult), r=[of_], w=[sq_])
            def part_c(n=n, t0=t0, bi=bi, h=h, y_=y_, sq_=sq_, of_=of_):
                P.pe(Dl(T_.matmul, psE[:, :n], k.ones_bf[:], sq_[:, :n], start=True, stop=True), r=[k.ones_bf, sq_], w=[psE])
                P.act(Dl(A_.activation, out=rs[:, :n], in_=psE[:, :n], func=AF.Sqrt, scale=1.0 / 128, bias=eps[:, 0:1]), r=[psE, eps], w=[rs])
                P.dve(Dl(V_.reciprocal, out=rs[:, :n], in_=rs[:, :n]), r=[rs], w=[rs])
                P.dve(Dl(V_.tensor_tensor, out=of_[:, :n], in0=of_[:, :n], in1=rs[:, :n], op=ALU.mult), r=[of_, rs], w=[of_])
                P.act(Dl(A_.activation, out=y_[:, t0:t0 + n], in_=of_[:, :n], func=AF.Copy, scale=hg[:, h:h + 1]), r=[of_, hg], w=[y_.b(bi)])
            deferred.append(part_c)
        while deferred:
            deferred.pop(0)()
        P.dma("sp", k.br_d.ap()[1, h * 128:(h + 1) * 128, :], y_[:], r=y_.allb(), w=[k.br_d.b(("da", h))])
    P.phase_end()


def load_w_full(k, dst_bf, wv, ncols, kcn, wst, q="sp"):
    nc, P = k.nc, k.P
    for gi, c0 in enumerate(range(0, ncols, 512)):
        w_ = wst[gi % 2]
        P.dma(q, w_[:, :kcn, :], wv[:, :, c0:c0 + 512], w=[w_])
        P.pool(Dl(nc.gpsimd.tensor_copy, dst_bf[:, :, c0:c0 + 512], w_[:, :kcn, :]), r=[w_], w=[dst_bf.b(c0 // 512)])


def phase_t1(k, l, xsrc):
    nc, P = k.nc, k.P
    V_, A_, T_, G_ = nc.vector, nc.scalar, nc.tensor, nc.gpsimd
    P.phase_begin()
    wst = [P.sb("wst%d" % i, [128, KC, 512], F32) for i in range(2)]
    wbr = [P.sb("wbr%d" % i, [128, KC, 1024], BF16) for i in range(3)]
    wo = P.sb("wo", [128, KC, 1024], BF16)
    for x, wt in enumerate((k.w_br_ml, k.w_br_da, k.w_br_fn)):
        load_w_full(k, wbr[x], wt.ap()[l].rearrange("(k p) n -> p k n", p=128), 1024, KC, wst)
    load_w_full(k, wo, k.w_out.ap()[l].rearrange("(k p) n -> p k n", p=128), 1024, KC, wst)
    brb = [P.sb("brb%d" % i, [128, 24, TB1], BF16) for i in range(2)]
    gb = [P.sb("gb%d" % i, [128, 24, TB1], BF16) for i in range(2)]
    xb = [P.sb("xb%d" % i, [128, KC, TB1], F32) for i in range(2)]
    yb = P.sb("yb", [128, KC, TB1], BF16)
    ta = [P.sb("ta%d" % i, [128, TB1], F32) for i in range(2)]
    tb = [P.sb("tb%d" % i, [128, TB1], F32) for i in range(2)]
    tc = [P.sb("tc%d" % i, [128, TB1], F32) for i in range(2)]
    sq = P.sb("sq", [128, KC, TB1], BF16)
    rs = P.sb("rs", [128, TB1], F32)
    tmp = [P.sb("tmp%d" % i, [128, TB1], F32) for i in range(2)]
    hb = [P.sb("hb%d" % i, [128, KC, TB1], BF16) for i in range(2)]
    eps = P.sb("eps", [128, 1], F32)
    P.dve(Dl(V_.memset, eps[:], EPS), w=[eps])
    gv = k.G_d.ap().rearrange("(c p) t -> p c t", p=128)
    it = 0
    for bi, (t0, n, sel) in enumerate(BLKS1):
        b_, g_, x_, h_ = brb[bi % 2], gb[bi % 2], xb[bi % 2], hb[bi % 2]
        for x in range(3):
            P.dma("sp", b_[:, x * 8:(x + 1) * 8, :n], k.br_d.ap()[x].rearrange("(c p) t -> p c t", p=128)[:, :, t0:t0 + n], w=[b_.b(x)])
        P.dma("act", g_[:, :, :n], gv[:, :, t0:t0 + n], w=[g_])
        P.dma("act", x_[:, :, :n], xsrc.ap()[:, :, t0:t0 + n], w=[x_])
        for oc in range(KC):
            pss = [k.ps[(it % 2) * 3 + x] for x in range(3)]
            a_, b2_, c_ = ta[it % 2], tb[it % 2], tc[it % 2]
            it += 1
            for x in range(3):
                for kc in range(KC):
                    P.pe(Dl(T_.matmul, pss[x][:, :n], wbr[x][:, kc, oc * 128:(oc + 1) * 128], b_[:, x * 8 + kc, :n],
                            start=(kc == 0), stop=(kc == KC - 1)), r=[wbr[x].b(oc // 4), b_.b(x)], w=[pss[x]])
            P.dve(Dl(V_.tensor_tensor, out=a_[:, :n], in0=pss[0][:, :n], in1=g_[:, oc, :n], op=ALU.mult), r=[pss[0], g_], w=[a_])
            P.dve(Dl(V_.tensor_tensor, out=b2_[:, :n], in0=pss[1][:, :n], in1=g_[:, 8 + oc, :n], op=ALU.mult), r=[pss[1], g_], w=[b2_])
            P.dve(Dl(V_.tensor_tensor, out=c_[:, :n], in0=pss[2][:, :n], in1=g_[:, 16 + oc, :n], op=ALU.mult), r=[pss[2], g_], w=[c_])
            P.pool(Dl(G_.tensor_tensor, out=a_[:, :n], in0=a_[:, :n], in1=b2_[:, :n], op=ALU.add), r=[a_, b2_], w=[a_])
            P.pool(Dl(G_.tensor_tensor, out=yb[:, oc, :n], in0=a_[:, :n], in1=c_[:, :n], op=ALU.add), r=[a_, c_], w=[yb.b(oc)])
        for oc in range(KC):
            ps = k.ps[6]
            for kc in range(KC):
                P.pe(Dl(T_.matmul, ps[:, :n], wo[:, kc, oc * 128:(oc + 1) * 128], yb[:, kc, :n], start=(kc == 0), stop=(kc == KC - 1)),
                     r=[wo.b(oc // 4), yb.allb()], w=[ps])
            P.dve(Dl(V_.scalar_tensor_tensor, out=x_[:, oc, :n], in0=ps[:, :n], scalar=mod_col(k, l, 16 + oc, sel), in1=x_[:, oc, :n],
                     op0=ALU.mult, op1=ALU.add), r=[ps, x_, k.mods], w=[x_])
        P.dma("sp", k.x1_d.ap()[:, :, t0:t0 + n], x_[:, :, :n], r=[x_], w=[k.x1_d])
        norm_block(k, l, 1, sel, x_, n, sq, rs, tmp, h_, eps, k.ps[7])
        P.dma("sp", k.h2_d.ap()[:, :, t0:t0 + n], h_[:, :, :n], r=[h_], w=[k.h2_d])
    P.phase_end()


def phase_t2a(k, l):
    nc, P = k.nc, k.P
    V_, A_, T_, G_ = nc.vector, nc.scalar, nc.tensor, nc.gpsimd
    P.phase_begin()
    wst = [P.sb("wst%d" % i, [128, KC, 512], F32) for i in range(2)]
    wf = P.sb("wf", [128, KC, 2 * D_FF], BF16)
    load_w_full(k, wf, k.w_ffn_in.ap()[l].rearrange("(k p) n -> p k n", p=128), 2 * D_FF, KC, wst)
    hb = [P.sb("hb%d" % i, [128, KC, 512], BF16) for i in range(2)]
    gblk = [P.sb("gblk%d" % i, [128, FC, 512], BF16) for i in range(2)]
    sa = [P.sb("sa%d" % i, [128, 512], F32) for i in range(2)]
    it = 0
    for bi, (t0, n, sel) in enumerate(BLKS):
        h_, g_ = hb[bi % 2], gblk[bi % 2]
        P.dma("sp", h_[:, :, :n], k.h2_d.ap()[:, :, t0:t0 + n], w=[h_])
        for j in range(FC):
            pa, pb = k.ps[(it % 4) * 2], k.ps[(it % 4) * 2 + 1]
            s_ = sa[it % 2]
            it += 1
            for kc in range(KC):
                P.pe(Dl(T_.matmul, pa[:, :n], wf[:, kc, j * 128:(j + 1) * 128], h_[:, kc, :n], start=(kc == 0), stop=(kc == KC - 1)),
                     r=[wf.b((j * 128) // 512), h_], w=[pa])
            for kc in range(KC):
                P.pe(Dl(T_.matmul, pb[:, :n], wf[:, kc, D_FF + j * 128:D_FF + (j + 1) * 128], h_[:, kc, :n], start=(kc == 0), stop=(kc == KC - 1)),
                     r=[wf.b((D_FF + j * 128) // 512), h_], w=[pb])
            P.act(Dl(A_.activation, out=s_[:, :n], in_=pa[:, :n], func=AF.Silu), r=[pa], w=[s_])
            P.dve(Dl(V_.tensor_tensor, out=g_[:, j, :n], in0=pb[:, :n], in1=s_[:, :n], op=ALU.mult), r=[pb, s_], w=[g_])
        P.dma("sp", k.gT_d.ap()[:, :, t0:t0 + n], g_[:, :, :n], r=[g_], w=[k.gT_d])
    P.phase_end()


def phase_t2b(k, l, last):
    nc, P = k.nc, k.P
    V_, A_, T_, G_ = nc.vector, nc.scalar, nc.tensor, nc.gpsimd
    P.phase_begin()
    wst = [P.sb("wst%d" % i, [128, 11, 512], F32) for i in range(2)]
    wo = P.sb("wo", [128, FC, 1024], BF16)
    wv = k.w_ffn_out.ap()[l].rearrange("(k p) n -> p k n", p=128)
    gi = 0
    for c0 in (0, 512):
        for k0 in (0, 11):
            w_ = wst[gi % 2]
            gi += 1
            P.dma("sp", w_[:], wv[:, k0:k0 + 11, c0:c0 + 512], w=[w_])
            P.pool(Dl(G_.tensor_copy, wo[:, k0:k0 + 11, c0:c0 + 512], w_[:]), r=[w_], w=[wo.b((c0, k0))])
    gblk = [P.sb("gblk%d" % i, [128, FC, 512], BF16) for i in range(2)]
    xb = [P.sb("xb%d" % i, [128, KC, 512], F32) for i in range(2)]
    sq = P.sb("sq", [128, KC, 512], BF16)
    rs = P.sb("rs", [128, 512], F32)
    tmp = [P.sb("tmp%d" % i, [128, 512], F32) for i in range(2)]
    hb = [P.sb("hb%d" % i, [128, KC, 512], BF16) for i in range(2)] if not last else [None, None]
    eps = P.sb("eps", [128, 1], F32)
    P.dve(Dl(V_.memset, eps[:], EPS), w=[eps])
    if last:
        fg = P.sb("fg", [128, KC], F32)
        P.dma("sp", fg[:], k.final_gT.ap()[:, :], w=[fg])
        hf = P.sb("hf", [128, KC, 512], F32)
        ot = [P.sb("ot%d" % i, [128, 1024], F32) for i in range(2)]
    it = 0
    blks = BLKS[:-1] if last else BLKS
    for bi, (t0, n, sel) in enumerate(blks):
        g_, x_, h_ = gblk[bi % 2], xb[bi % 2], hb[bi % 2]
        P.dma("sp", g_[:, :, :n], k.gT_d.ap()[:, :, t0:t0 + n], w=[g_])
        P.dma("act", x_[:, :, :n], k.x1_d.ap()[:, :, t0:t0 + n], w=[x_])
        for oc in range(KC):
            ps = k.ps[it % 4]
            it += 1
            for j in range(FC):
                P.pe(Dl(T_.matmul, ps[:, :n], wo[:, j, oc * 128:(oc + 1) * 128], g_[:, j, :n], start=(j == 0), stop=(j == FC - 1)),
                     r=[wo.b(((oc // 4) * 512, (j // 11) * 11)), g_], w=[ps])
            P.dve(Dl(V_.scalar_tensor_tensor, out=x_[:, oc, :n], in0=ps[:, :n], scalar=mod_col(k, l, 40 + oc, sel), in1=x_[:, oc, :n],
                     op0=ALU.mult, op1=ALU.add), r=[ps, x_, k.mods], w=[x_])
        if not last:
            P.dma("sp", k.xT_d.ap()[:, :, t0:t0 + n], x_[:, :, :n], r=[x_], w=[k.xT_d])
            norm_block(k, l + 1, 0, sel, x_, n, sq, rs, tmp, h_, eps, k.ps[7])
            P.dma("sp", k.hl_d.ap()[:, :, t0:t0 + n], h_[:, :, :n], r=[h_], w=[k.hl_d])
        else:
            ps7 = k.ps[7]
            P.pool(Dl(G_.tensor_tensor, out=sq[:, :, :n], in0=x_[:, :, :n], in1=x_[:, :, :n], op=ALU.mult), r=[x_], w=[sq])
            for kc in range(KC):
                P.pe(Dl(T_.matmul, ps7[:, :n], k.ones_bf[:], sq[:, kc, :n], start=(kc == 0), stop=(kc == KC - 1)), r=[k.ones_bf, sq], w=[ps7])
            P.act(Dl(A_.activation, out=rs[:, :n], in_=ps7[:, :n], func=AF.Sqrt, scale=1.0 / D, bias=eps[:, 0:1]), r=[ps7, eps], w=[rs])
            P.dve(Dl(V_.reciprocal, out=rs[:, :n], in_=rs[:, :n]), r=[rs], w=[rs])
            for kc in range(KC):
                P.dve(Dl(V_.scalar_tensor_tensor, out=hf[:, kc, :n], in0=x_[:, kc, :n], scalar=fg[:, kc:kc + 1], in1=rs[:, :n],
                         op0=ALU.mult, op1=ALU.mult), r=[x_, fg, rs], w=[hf])
            for tt in range(n // 128):
                o_ = ot[tt % 2]
                for half in range(2):
                    pt_ = k.ps[4 + half]
                    for q4 in range(4):
                        kc = half * 4 + q4
                        P.pe(Dl(T_.transpose, pt_[:, q4 * 128:(q4 + 1) * 128], hf[:, kc, tt * 128:(tt + 1) * 128], k.ident_f[:]),
                             r=[hf, k.ident_f], w=[pt_])
                    if half == 0:
                        P.act(Dl(A_.copy, o_[:, 0:512], pt_[:, :]), r=[pt_], w=[o_])
                    else:
                        P.dve(Dl(V_.tensor_copy, o_[:, 512:1024], pt_[:, :]), r=[pt_], w=[o_])
                P.dma("sp", k.out.ap()[t0 + tt * 128:t0 + (tt + 1) * 128, :], o_[:], r=[o_], w=[k.out.b((t0, tt))])
    P.phase_end()


def host_consts_fn():
    import ml_dtypes
    bf = ml_dtypes.bfloat16
    c = {}
    cc = np.arange(256)
    ang = 2 * np.pi * np.outer(cc, cc) / 256.0
    cs = np.concatenate([np.cos(ang), np.sin(ang)], axis=1)
    c["fn_cs"] = np.ascontiguousarray(cs.reshape(2, 128, 512).transpose(1, 0, 2)).astype(bf)
    t = np.arange(64)
    a64 = 2 * np.pi * np.outer(t, t) / 64.0
    w1 = np.zeros((128, 128))
    w1[0:64, 0:64] = np.cos(a64)
    w1[0:64, 64:128] = -np.sin(a64)
    w1[64:128, 0:64] = -np.sin(a64)
    w1[64:128, 64:128] = -np.cos(a64)
    c["fn_w1"] = w1.astype(bf)
    t1 = np.repeat(np.arange(64), 2)
    atw = 2 * np.pi * np.outer(t1, np.arange(64)) / 4096.0
    c["fn_twr"] = np.ascontiguousarray(np.tile(np.cos(atw)[:, None, :], (1, 4, 1)).reshape(128, 256)).astype(np.float32)
    c["fn_twi"] = np.ascontiguousarray(np.tile(-np.sin(atw)[:, None, :], (1, 4, 1)).reshape(128, 256)).astype(np.float32)
    w3c = np.zeros((128, 128))
    w3s = np.zeros((128, 128))
    for gi in range(2):
        w3c[gi::2, gi * 64:(gi + 1) * 64] = np.cos(a64)
        w3s[gi::2, gi * 64:(gi + 1) * 64] = np.sin(a64)
    c["fn_w3c"] = w3c.astype(bf)
    c["fn_w3s"] = w3s.astype(bf)
    kk = np.arange(256)
    a256 = 2 * np.pi * np.outer(kk, kk) / 256.0
    csx = np.stack([np.cos(a256), -np.sin(a256)], axis=1)
    c["fn_csx"] = np.ascontiguousarray(csx.reshape(2, 128, 2, 256).transpose(1, 0, 2, 3)).astype(bf)
    return c


def phase_fn(k, l):
    nc, P = k.nc, k.P
    V_, A_, T_, G_ = nc.vector, nc.scalar, nc.tensor, nc.gpsimd
    P.phase_begin()
    cs = P.sb("cs", [128, 2, 512], BF16)
    w1 = P.sb("w1", [128, 128], BF16)
    twr = P.sb("twr", [128, 256], F32)
    twi = P.sb("twi", [128, 256], F32)
    w3c = P.sb("w3c", [128, 128], BF16)
    w3s = P.sb("w3s", [128, 128], BF16)
    csx = P.sb("csx", [128, 2, 2, 256], BF16)
    for t_, d_ in ((cs, k.fn_cs), (w1, k.fn_w1), (twr, k.fn_twr), (twi, k.fn_twi), (w3c, k.fn_w3c), (w3s, k.fn_w3s), (csx, k.fn_csx)):
        P.dma("act", t_[:], d_.ap(), w=[t_])
    zc = [P.sb("zc%d" % i, [128, 4, 1024], BF16) for i in range(2)]
    stg = [P.sb("stg%d" % i, [128, 2, 512], BF16) for i in range(3)]
    abc = [P.sb("abc%d" % i, [128, 2, 512], BF16) for i in range(2)]
    yf = [[P.sb("yf%d%d" % (gi, ch), [128, NT], BF16) for ch in range(2)] for gi in range(2)]
    D1 = P.sb("D1", [128, 64, 512], BF16)
    H = P.sb("H", [128, 2, 64, 256], BF16)
    wa = [P.sb("fwa%d" % i, [128, 256], F32) for i in range(2)]
    wb = [P.sb("fwb%d" % i, [128, 256], F32) for i in range(2)]
    wc = [P.sb("fwc%d" % i, [128, 256], F32) for i in range(2)]
    wd = [P.sb("fwd%d" % i, [128, 256], F32) for i in range(2)]
    ev = 0
    psi = 0
    for gp in range(2):
        chunks = [(i * 1024, 1024) for i in range(NL // 1024)] + [(NL, NCX)]
        si = 0
        for ci, (c0, cn) in enumerate(chunks):
            z_ = zc[ci % 2]
            for gi in range(2):
                for cc in range(2):
                    r0 = F_FN + (2 * gp + gi) * 256 + cc * 128
                    P.dma("sp", z_[:, gi * 2 + cc, :cn], k.uF_d.ap()[r0:r0 + 128, c0:c0 + cn], w=[z_.b(gi * 2 + cc)])
            for tl in range(cn // 128):
                tok0 = c0 + tl * 128
                isctx = tok0 >= NL
                s_ = abc[(tok0 - NL) // 128] if isctx else stg[si % 3]
                si += 1
                for gi in range(2):
                    ps = k.ps[psi % 4]
                    psi += 1
                    for cc in range(2):
                        P.pe(Dl(T_.matmul, ps[:, :], z_[:, gi * 2 + cc, tl * 128:(tl + 1) * 128], cs[:, cc, :], start=(cc == 0), stop=(cc == 1)),
                             r=[z_.b(gi * 2 + cc), cs], w=[ps])
                    ev += 1
                    if ev % 2:
                        P.act(Dl(A_.copy, s_[:, gi, :], ps[:, :]), r=[ps], w=[s_.b(gi)])
                    else:
                        P.dve(Dl(V_.tensor_copy, s_[:, gi, :], ps[:, :]), r=[ps], w=[s_.b(gi)])
                if not isctx:
                    for ab in range(2):
                        P.dma("sp", k.AB_d.ap()[ab, tok0:tok0 + 128, :].rearrange("t (g c) -> t g c", g=2), s_[:, :, ab * 256:(ab + 1) * 256],
                              r=s_.allb(), w=[k.AB_d.b((ab, tok0 // 2048))])
        if FN_STOP < 1:
            continue
        for ab in range(2):
            for hh in range(2):
                P.dma("sp", D1[ab * 64 + hh * 32:ab * 64 + hh * 32 + 32, :, :],
                      k.AB_d.ap()[ab, hh * 2048:(hh + 1) * 2048, :].rearrange("(t2 t1) c -> t2 t1 c", t1=64),
                      r=[k.AB_d.b((ab, hh))], w=[D1.b((ab, hh))])
        d1r = D1.allb()
        for cb in range(64):
            ps = k.ps[psi % 4]
            psi += 1
            for ci in range(4):
                cp = cb * 4 + ci
                P.pe(Dl(T_.matmul, ps[:, ci * 128:(ci + 1) * 128], D1[:, :, cp:512:256], w1[:, :], start=True, stop=True), r=[d1r, w1], w=[ps])
            psv = ps[:, :].rearrange("p (c r j) -> p c r j", c=4, r=2)
            gr, gim = psv[:, :, 0, :], psv[:, :, 1, :]
            a_, b_, c_, d_ = wa[cb % 2], wb[cb % 2], wc[cb % 2], wd[cb % 2]
            v4 = lambda t: t[:, :].rearrange("p (c j) -> p c j", c=4)
            P.dve(Dl(V_.tensor_tensor, out=v4(a_), in0=gr, in1=v4(twr), op=ALU.mult), r=[ps, twr], w=[a_])
            P.dve(Dl(V_.tensor_tensor, out=v4(b_), in0=gim, in1=v4(twi), op=ALU.mult), r=[ps, twi], w=[b_])
            P.dve(Dl(V_.tensor_tensor, out=v4(c_), in0=gr, in1=v4(twi), op=ALU.mult), r=[ps, twi], w=[c_])
            P.dve(Dl(V_.tensor_tensor, out=v4(d_), in0=gim, in1=v4(twr), op=ALU.mult), r=[ps, twr], w=[d_])
            P.pool(Dl(G_.tensor_tensor, out=H[:, 0, :, cb * 4:(cb + 1) * 4].rearrange("p j c -> p c j"), in0=v4(a_), in1=v4(b_), op=ALU.subtract), r=[a_, b_], w=[H.b(cb)])
            P.pool(Dl(G_.tensor_tensor, out=H[:, 1, :, cb * 4:(cb + 1) * 4].rearrange("p j c -> p c j"), in0=v4(c_), in1=v4(d_), op=ALU.add), r=[c_, d_], w=[H.b(cb)])
        if FN_STOP < 1.5:
            continue
        hr = H.allb()
        for ch in range(2):
            for jb in range(16):
                ps = k.ps[psi % 4]
                psi += 1
                for ji in range(4):
                    j2 = jb * 4 + ji
                    P.pe(Dl(T_.matmul, ps[:, ji * 128:(ji + 1) * 128], H[:, 0, j2, ch * 128:(ch + 1) * 128], w3c[:, :], start=True, stop=False),
                         r=[hr, w3c], w=[ps])
                    P.pe(Dl(T_.matmul, ps[:, ji * 128:(ji + 1) * 128], H[:, 1, j2, ch * 128:(ch + 1) * 128], w3s[:, :], start=False, stop=True),
                         r=[hr, w3s], w=[ps])
                for ji in range(4):
                    if FN_STOP == 1.5:
                        break
                    j2 = jb * 4 + ji
                    for gi in range(2):
                        o = yf[gi][ch][:, j2:NL:64]
                        i_ = ps[:, ji * 128 + gi * 64:ji * 128 + (gi + 1) * 64]
                        ev += 1
                        if (ev % 2 or FN_STOP == 2.1) and FN_STOP != 2.2:
                            P.act(Dl(A_.mul, o, i_, 1.0 / 1024.0), r=[ps], w=[yf[gi][ch].b(jb)])
                        else:
                            P.dve(Dl(V_.tensor_scalar, out=o, in0=i_, scalar1=1.0 / 1024.0, scalar2=None, op0=ALU.mult), r=[ps], w=[yf[gi][ch].b(jb)])
        if FN_STOP < 3:
            continue
        for gi in range(2):
            for ch in range(2):
                ps = k.ps[psi % 4]
                psi += 1
                idx = 0
                for kt in range(2):
                    for ab in range(2):
                        P.pe(Dl(T_.matmul, ps[:, 0:256], abc[kt][:, gi, ab * 256 + ch * 128:ab * 256 + (ch + 1) * 128], csx[:, kt, ab, :],
                                start=(idx == 0), stop=(idx == 3)), r=[abc[kt].allb(), csx], w=[ps])
                        idx += 1
                P.act(Dl(A_.mul, yf[gi][ch][:, NL:NT], ps[:, 0:256], 1.0 / 256.0), r=[ps], w=[yf[gi][ch].b("ctx")])
                r0 = (2 * gp + gi) * 256 + ch * 128
                P.dma("sp", k.br_d.ap()[2, r0:r0 + 128, :], yf[gi][ch][:], r=yf[gi][ch].allb(), w=[k.br_d.b(("fn", r0))])
    P.phase_end()


NEG = -30000.0


def host_consts_ml():
    c = {}
    r = np.arange(128)
    c["ml_triF"] = (r[:, None] <= r[None, :]).astype(np.float32)
    c["ml_triB"] = (r[:, None] >= r[None, :]).astype(np.float32)
    c["ml_maskF"] = np.where(r[None, :] <= r[:, None], 0.0, NEG).astype(np.float32)
    c["ml_maskB"] = np.where(r[None, :] >= r[:, None], 0.0, NEG).astype(np.float32)
    return c


def phase_ml(k, l):
    nc, P = k.nc, k.P
    V_, A_, T_, G_ = nc.vector, nc.scalar, nc.tensor, nc.gpsimd
    P.phase_begin()
    ones_f = P.sb("ones_f", [128, 128], F32)
    tri = [P.sb("triF", [128, 128], F32), P.sb("triB", [128, 128], F32)]
    msk = [P.sb("maskF", [128, 128], F32), P.sb("maskB", [128, 128], F32)]
    P.dve(Dl(V_.memset, ones_f[:], 1.0), w=[ones_f])
    for t_, d_ in ((tri[0], k.ml_triF), (tri[1], k.ml_triB), (msk[0], k.ml_maskF), (msk[1], k.ml_maskB)):
        P.dma("act", t_[:], d_.ap(), w=[t_])
    GL = P.sb("GL", [128, NTILE, 16], F32)
    gbias = P.sb("gbias", [128, NTILE, 16], F32)
    P.dma("sp", GL[:], k.uT_g.ap().rearrange("(n p) c -> p n c", p=128), w=[GL])
    P.dma("sp", gbias[:].rearrange("p n c -> p (n c)"), k.ml_gate_bR.ap()[l:l + 1, :].partition_broadcast(128), w=[gbias])
    P.dve(Dl(V_.tensor_tensor, out=GL[:], in0=GL[:], in1=gbias[:], op=ALU.add), r=[GL, gbias], w=[GL])
    gax = P.sb("gax", [128, NTILE, 2, 4], F32)
    gmn = P.sb("gmn", [128, NTILE, 2, 4], F32)
    GLf = GL[:].rearrange("p n (a c) -> p n a c", a=2)[:, :, :, 4:8]
    P.dve(Dl(V_.scalar_tensor_tensor, out=gax[:], in0=GLf, scalar=-1.0, in1=GLf, op0=ALU.mult, op1=ALU.max), r=[GL], w=[gax])
    P.act(Dl(A_.activation, out=gax[:], in_=gax[:], func=AF.Exp, scale=-1.0), r=[gax], w=[gax])
    P.act(Dl(A_.activation, out=gax[:], in_=gax[:], func=AF.Ln, bias=ones_f[:, 0:1], scale=1.0), r=[gax, ones_f], w=[gax])
    P.dve(Dl(V_.tensor_single_scalar, out=gmn[:], in_=GLf, scalar=0.0, op=ALU.min), r=[GL], w=[gmn])
    P.dve(Dl(V_.tensor_tensor, out=GLf, in0=gmn[:], in1=gax[:], op=ALU.subtract), r=[gmn, gax], w=[GL])

    NH = 2
    qT = [[P.sb("mq%d%d" % (i, c), [128, NT], BF16) for c in range(2)] for i in range(NH)]
    kT = [[P.sb("mk%d%d" % (i, c), [128, NT], BF16) for c in range(2)] for i in range(NH)]
    ktok = [P.sb("mkt%d" % i, [128, NTILE, 256], BF16) for i in range(NH)]
    vaug = [P.sb("mv%d" % i, [128, NTILE, 260], BF16) for i in range(NH)]
    C32 = [[P.sb("C32_%d%d" % (i, c), [128, 257], F32) for c in range(2)] for i in range(2 * NH)]
    Cb = [[P.sb("Cb_%d%d" % (i, c), [128, 260], BF16) for c in range(2)] for i in range(2 * NH)]
    mst = P.sb("mst", [128, 2, 2 * NH], F32)
    NB = 8
    IB = [P.sb("IB%d" % i, [128, 128], F32) for i in range(NB)]
    NLB = [P.sb("NLB%d" % i, [128, 128], F32) for i in range(NB)]
    wm = [P.sb("wm%d" % i, [128, 128], F32) for i in range(NB)]
    Dm = [P.sb("Dm%d" % i, [128, 128], F32) for i in range(NB)]
    st = [P.sb("st%d" % i, [128, 16], F32) for i in range(NB)]
    ex = [P.sb("ex%d" % i, [128, 4], F32) for i in range(NB)]
    a_ = [P.sb("a%d" % i, [128, 128], BF16) for i in range(4)] * 2
    aTs = [P.sb("aTs%d" % i, [128, 128], BF16) for i in range(4)] * 2
    Xs = [P.sb("Xs%d" % i, [128, 257], F32) for i in range(4)] * 2
    Z = [P.sb("Z%d" % i, [128, 257], F32) for i in range(4)] * 2
    dmr = [P.sb("dmr%d" % i, [128, 2], F32) for i in range(4)] * 2
    ho = [P.sb("ho%d" % i, [128, 256], F32) for i in range(4)] * 2
    kw = [P.sb("kw%d" % i, [128, 256], BF16) for i in range(4)] * 2
    bank_sets = [(k.ps[0], k.ps[1], k.ps[2], k.ps[3]), (k.ps[4], k.ps[5], k.ps[6], k.ps[7])]
    for i in range(NH):
        P.dve(Dl(V_.memset, vaug[i][:, :, 256:260], 1.0), w=[vaug[i].b("ones")])

    fwd_tiles = CTX_TILES + LAT_TILES
    bwd_tiles = CTX_TILES[::-1] + LAT_TILES[::-1]
    for hp in range(4 // NH):
        for i in range(NH):
            h = hp * NH + i
            for c in range(2):
                P.dma("sp", qT[i][c][:], k.uF_d.ap()[F_MLQ + h * 256 + c * 128:F_MLQ + h * 256 + (c + 1) * 128, :], w=[qT[i][c]])
                P.dma("sp", kT[i][c][:], k.uF_d.ap()[F_MLK + h * 256 + c * 128:F_MLK + h * 256 + (c + 1) * 128, :], w=[kT[i][c]])
            P.dma("sp", ktok[i][:], k.uT_mlk.ap()[:, h * 256:(h + 1) * 256].rearrange("(n p) d -> p n d", p=128), w=[ktok[i]])
            P.dma("sp", vaug[i][:, :, 0:256], k.uT_mlv.ap()[:, h * 256:(h + 1) * 256].rearrange("(n p) d -> p n d", p=128), w=[vaug[i].b("v")])
        for ch in range(2 * NH):
            for c in range(2):
                P.dve(Dl(V_.memset, C32[ch][c][:], 0.0), w=[C32[ch][c]])
                P.pool(Dl(G_.memset, Cb[ch][c][:], 0.0), w=[Cb[ch][c]])
        P.dve(Dl(V_.memset, mst[:], 0.0), w=[mst])
        def ctxv(idx):
            step, ch = idx // (2 * NH), idx % (2 * NH)
            i, d = ch // 2, ch % 2
            h = hp * NH + i
            ti = (fwd_tiles if d == 0 else bwd_tiles)[step]
            tsl = slice(ti * 128, (ti + 1) * 128)
            b = idx % NB
            icol = GL[:, ti, 8 * d + h:8 * d + h + 1]
            fcol = GL[:, ti, 8 * d + 4 + h:8 * d + 4 + h + 1]
            mcur = mst[:, step % 2, ch:ch + 1]
            mnxt = mst[:, (step + 1) % 2, ch:ch + 1]
            s_, e_ = st[b], ex[b]
            vr = [vaug[i].b("v"), vaug[i].b("ones")]
            return step, ch, i, d, h, ti, tsl, b, icol, fcol, mcur, mnxt, s_, e_, vr

        def banks(ch):
            psP, psQ, psY, psCb = bank_sets[ch % 2]
            return psP, psQ, psY, psCb, psQ[:, :].bitcast(BF16)

        def stage_a(idx):
            step, ch, i, d, h, ti, tsl, b, icol, fcol, mcur, mnxt, s_, e_, vr = ctxv(idx)
            psP, psQ, psY, psCb, psQ_bf = banks(ch)
            P.dve(Dl(V_.tensor_scalar, out=IB[b][:], in0=ones_f[:], scalar1=icol, scalar2=None, op0=ALU.mult), r=[ones_f, GL], w=[IB[b]])
            P.dve(Dl(V_.tensor_scalar, out=NLB[b][:], in0=ones_f[:], scalar1=fcol, scalar2=-1.0, op0=ALU.mult, op1=ALU.mult), r=[ones_f, GL], w=[NLB[b]])
            P.pe(Dl(T_.matmul, psP[:, 0:128], IB[b][:], k.ident_f[:], start=True, stop=False), r=[IB[b], k.ident_f], w=[psP])
            P.pe(Dl(T_.matmul, psP[:, 0:128], NLB[b][:], tri[d][:], start=False, stop=True), r=[NLB[b], tri[d]], w=[psP])
            P.pe(Dl(T_.matmul, psP[:, 128:129], tri[d][:], fcol, start=True, stop=True), r=[tri[d], GL], w=[psP])
            P.pe(Dl(T_.matmul, psP[:, 136:137], ones_f[:], fcol, start=True, stop=True), r=[ones_f, GL], w=[psP])
            P.dve(Dl(V_.tensor_tensor, out=wm[b][:], in0=psP[:, 0:128], in1=msk[d][:], op=ALU.add), r=[psP, msk[d]], w=[wm[b]])
            P.dve(Dl(V_.reduce_max, out=s_[:, 1:2], in_=psP[:, 0:128], axis=AX.X), r=[psP], w=[s_])
            P.dve(Dl(V_.tensor_copy, s_[:, 2:4], psP[:, 128:137:8]), r=[psP], w=[s_])
            P.dve(Dl(V_.reduce_max, out=s_[:, 0:1], in_=wm[b][:], axis=AX.X), r=[wm[b]], w=[s_])
            P.dve(Dl(V_.tensor_scalar, out=s_[:, 4:6], in0=s_[:, 0:2], scalar1=mcur, scalar2=None, op0=ALU.max), r=[s_, mst], w=[s_])
            P.dve(Dl(V_.tensor_scalar, out=s_[:, 6:8], in0=s_[:, 4:6], scalar1=-1.0, scalar2=None, op0=ALU.mult), r=[s_], w=[s_])
            P.dve(Dl(V_.tensor_tensor, out=mnxt, in0=s_[:, 3:4], in1=s_[:, 5:6], op=ALU.add), r=[s_], w=[mst])
            P.dve(Dl(V_.tensor_tensor, out=s_[:, 8:9], in0=icol, in1=s_[:, 2:3], op=ALU.subtract), r=[s_, GL], w=[s_])
            P.act(Dl(A_.activation, out=Dm[b][:], in_=wm[b][:], func=AF.Exp, bias=s_[:, 6:7], scale=1.0), r=[wm[b], s_], w=[Dm[b]])
            P.act(Dl(A_.activation, out=e_[:, 0:1], in_=mcur, func=AF.Exp, bias=s_[:, 6:7], scale=1.0), r=[mst, s_], w=[e_])
            P.act(Dl(A_.activation, out=e_[:, 1:2], in_=mcur, func=AF.Exp, bias=s_[:, 7:8], scale=1.0), r=[mst, s_], w=[e_])
            P.act(Dl(A_.activation, out=e_[:, 2:3], in_=s_[:, 8:9], func=AF.Exp, bias=s_[:, 7:8], scale=1.0), r=[s_], w=[e_])
            P.act(Dl(A_.activation, out=e_[:, 3:4], in_=s_[:, 2:3], func=AF.Exp, bias=s_[:, 6:7], scale=-1.0), r=[s_], w=[e_])

        def stage_b(idx):
            step, ch, i, d, h, ti, tsl, b, icol, fcol, mcur, mnxt, s_, e_, vr = ctxv(idx)
            psP, psQ, psY, psCb, psQ_bf = banks(ch)
            for c in range(2):
                P.pe(Dl(T_.matmul, psY[:, 384:512], qT[i][c][:, tsl], kT[i][c][:, tsl], start=(c == 0), stop=(c == 1)),
                     r=[qT[i][c], kT[i][c]], w=[psY])
            P.dve(Dl(V_.tensor_tensor, out=a_[b][:], in0=psY[:, 384:512], in1=Dm[b][:], op=ALU.mult), r=[psY, Dm[b]], w=[a_[b]])
            P.pe(Dl(T_.transpose, psQ_bf[:, 0:128], a_[b][:], k.ident_bf[:]), r=[a_[b], k.ident_bf], w=[psQ])
            P.act(Dl(A_.copy, aTs[b][:], psQ_bf[:, 0:128]), r=[psQ], w=[aTs[b]])
            P.pe(Dl(T_.matmul, psY[:, 0:257], aTs[b][:], vaug[i][:, ti, 0:257], start=True, stop=True), r=[aTs[b], vr], w=[psY])
            for c in range(2):
                P.pe(Dl(T_.matmul, psQ[:, 128:385], qT[i][c][:, tsl], Cb[ch][c][:, 0:257], start=(c == 0), stop=(c == 1)),
                     r=[qT[i][c], Cb[ch][c]], w=[psQ])
            P.act(Dl(A_.activation, out=Xs[b][:], in_=psQ[:, 128:385], func=AF.Copy, scale=e_[:, 0:1]), r=[psQ, e_], w=[Xs[b]])
            P.dve(Dl(V_.tensor_tensor, out=Z[b][:], in0=psY[:, 0:257], in1=Xs[b][:], op=ALU.add), r=[psY, Xs[b]], w=[Z[b]])
            P.dve(Dl(V_.scalar_tensor_tensor, out=dmr[b][:, 0:1], in0=Z[b][:, 256:257], scalar=-1.0, in1=Z[b][:, 256:257], op0=ALU.mult, op1=ALU.max),
                  r=[Z[b]], w=[dmr[b]])
            P.dve(Dl(V_.tensor_tensor, out=dmr[b][:, 0:1], in0=dmr[b][:, 0:1], in1=e_[:, 3:4], op=ALU.max), r=[dmr[b], e_], w=[dmr[b]])
            P.dve(Dl(V_.reciprocal, out=dmr[b][:, 1:2], in_=dmr[b][:, 0:1]), r=[dmr[b]], w=[dmr[b]])
            P.act(Dl(A_.activation, out=ho[b][:], in_=Z[b][:, 0:256], func=AF.Copy, scale=dmr[b][:, 1:2]), r=[Z[b], dmr[b]], w=[ho[b]])
            P.dma("sp", k.h_d.ap()[d, ti * 128:(ti + 1) * 128, h * 256:(h + 1) * 256], ho[b][:], r=[ho[b]], w=[k.h_d.b((d, ti, h))])

        def stage_c(idx):
            step, ch, i, d, h, ti, tsl, b, icol, fcol, mcur, mnxt, s_, e_, vr = ctxv(idx)
            psP, psQ, psY, psCb, psQ_bf = banks(ch)
            P.act(Dl(A_.activation, out=kw[b][:], in_=ktok[i][:, ti, :], func=AF.Copy, scale=e_[:, 2:3]), r=[ktok[i], e_], w=[kw[b]])
            for c in range(2):
                P.pe(Dl(T_.matmul, psCb[:, 0:257], kw[b][:, c * 128:(c + 1) * 128], vaug[i][:, ti, 0:257], start=True, stop=True),
                     r=[kw[b], vr], w=[psCb])
                P.dve(Dl(V_.scalar_tensor_tensor, out=C32[ch][c][:], in0=C32[ch][c][:], scalar=e_[:, 1:2], in1=psCb[:, 0:257],
                         op0=ALU.mult, op1=ALU.add), r=[C32[ch][c], e_, psCb], w=[C32[ch][c]])
                P.act(Dl(A_.copy, Cb[ch][c][:, 0:257], C32[ch][c][:]), r=[C32[ch][c]], w=[Cb[ch][c]])

        def capture(fns):
            P.capture = []
            for f, idx in fns:
                f(idx)
            ops = P.capture
            P.capture = None
            return ops

        NCH = 2 * NH
        for pair in range(NH):
            P.capture = None
        for ch in range(NCH):
            stage_a(ch)
        for step in range(NTILE):
            for pair in range(NH):
                lists = []
                for ch in (2 * pair, 2 * pair + 1):
                    idx = step * NCH + ch
                    if step + 1 < NTILE:
                        lists.append(capture([(stage_a, idx + NCH)]))
                    lists.append(capture([(stage_b, idx), (stage_c, idx)]))
                for j in range(max(len(x) for x in lists)):
                    for lst in lists:
                        if j < len(lst):
                            eng, fn, r, w, dma = lst[j]
                            P.op(eng, fn, r, w, dma)
    P.phase_end()

    P.phase_begin()
    hgb = P.sb("hgb", [128, 1024], F32)
    P.dma("sp", hgb[:], k.ml_head_g.ap()[l:l + 1, :].partition_broadcast(128), w=[hgb])
    eps = P.sb("eps", [128, 1], F32)
    P.dve(Dl(V_.memset, eps[:], EPS), w=[eps])
    hf = [P.sb("hf%d" % i, [128, 1024], F32) for i in range(2)]
    hb_ = [P.sb("hbk%d" % i, [128, 1024], F32) for i in range(2)]
    og = [P.sb("og%d" % i, [128, 1024], BF16) for i in range(2)]
    sqm = P.sb("sqm", [128, 1024], F32)
    ssm = [P.sb("ssm%d" % i, [128, 4], F32) for i in range(2)]
    ym = [P.sb("ym%d" % i, [128, 1024], BF16) for i in range(2)]
    fst = [P.sb("fstm%d" % i, [128, KC, 512], BF16) for i in range(2)]
    pst = [k.ps[0][:, :].bitcast(BF16), k.ps[1][:, :].bitcast(BF16)]
    for ti in range(NTILE):
        b = ti % 2
        f_, b_, o_, y_, s_ = hf[b], hb_[b], og[b], ym[b], ssm[b]
        P.dma("sp", f_[:], k.h_d.ap()[0, ti * 128:(ti + 1) * 128, :], r=[k.h_d.b((0, ti, h)) for h in range(4)], w=[f_])
        P.dma("sp", b_[:], k.h_d.ap()[1, ti * 128:(ti + 1) * 128, :], r=[k.h_d.b((1, ti, h)) for h in range(4)], w=[b_])
        P.dma("act", o_[:], k.uT_mlo.ap()[ti * 128:(ti + 1) * 128, :], w=[o_])
        P.pool(Dl(G_.tensor_tensor, out=f_[:], in0=f_[:], in1=b_[:], op=ALU.add), r=[f_, b_], w=[f_])
        P.pool(Dl(G_.tensor_tensor, out=sqm[:], in0=f_[:], in1=f_[:], op=ALU.mult), r=[f_], w=[sqm])
        P.dve(Dl(V_.reduce_sum, out=s_[:], in_=sqm[:].rearrange("p (h d) -> p h d", h=4), axis=AX.X), r=[sqm], w=[s_])
        P.act(Dl(A_.activation, out=s_[:], in_=s_[:], func=AF.Sqrt, scale=1.0 / 256, bias=eps[:, 0:1]), r=[s_, eps], w=[s_])
        P.dve(Dl(V_.reciprocal, out=s_[:], in_=s_[:]), r=[s_], w=[s_])
        for h in range(4):
            P.act(Dl(A_.activation, out=f_[:, h * 256:(h + 1) * 256], in_=f_[:, h * 256:(h + 1) * 256], func=AF.Copy, scale=s_[:, h:h + 1]),
                  r=[f_, s_], w=[f_])
        P.dve(Dl(V_.tensor_tensor, out=f_[:], in0=f_[:], in1=hgb[:], op=ALU.mult), r=[f_, hgb], w=[f_])
        P.dve(Dl(V_.tensor_tensor, out=y_[:], in0=f_[:], in1=o_[:], op=ALU.mult), r=[f_, o_], w=[y_])
        ps = k.ps[b]
        for c in range(KC):
            P.pe(Dl(T_.transpose, pst[b][:, c * 128:(c + 1) * 128], y_[:, c * 128:(c + 1) * 128], k.ident_bf[:]), r=[y_, k.ident_bf], w=[ps])
        g4, t4 = ti // 4, ti % 4
        fs = fst[g4 % 2]
        o_ap = fs[:, :, t4 * 128:(t4 + 1) * 128]
        i_ap = pst[b][:, :].rearrange("p (c t) -> p c t", c=KC)
        if ti % 2:
            P.act(Dl(A_.copy, o_ap, i_ap), r=[ps], w=[fs.b(t4)])
        else:
            P.dve(Dl(V_.tensor_copy, o_ap, i_ap), r=[ps], w=[fs.b(t4)])
        if t4 == 3 or ti == NTILE - 1:
            nt_ = (t4 + 1) * 128
            P.dma("sp", k.br_d.ap()[0].rearrange("(c p) t -> p c t", p=128)[:, :, g4 * 512:g4 * 512 + nt_], fs[:, :, :nt_], r=fs.allb(),
                  w=[k.br_d.b(("ml", g4))])
    P.phase_end()


_CACHE = {}


def kernel(**inputs):
    inp = {k_: np.asarray(v) for k_, v in inputs.items()}
    if "nc" not in _CACHE:
        _CACHE["nc"] = build()[0]
        _CACHE["consts"] = host_consts()
    nc = _CACHE["nc"]
    consts = _CACHE["consts"]
    B = inp["x"].shape[0]
    in_maps = [prep_core_inputs(inp, c % B, consts) for c in range(8)]
    res = run_bass_kernel_spmd(nc, in_maps, core_ids=list(range(8)))
    out = np.stack([np.asarray(res.results[b]["out"]) for b in range(B)], axis=0)
    return out.astype(np.float32)
```

```python
import math
import numpy as np
import concourse.bass as bass
import concourse.mybir as mybir
from concourse.bass_utils import run_bass_kernel_spmd

F32 = mybir.dt.float32
BF16 = mybir.dt.bfloat16
AF = mybir.ActivationFunctionType
ALU = mybir.AluOpType
AX = mybir.AxisListType

D = 1024
KC = 8
DEPTH = 4
NL = 4096
NCX = 256
NT = NL + NCX
NTILE = NT // 128
D_IN = 11280
D_FF = 2816
FC = D_FF // 128
FN_STOP = 99
EPS = 1e-6
SBUF_BASE = 16640
SBUF_BYTES = 229376


class Buf:
    __slots__ = ("name", "w", "rs", "rd", "excl")

    def __init__(self, name=""):
        self.name = name
        self.excl = False
        self.w = None
        self.rs = {}
        self.rd = []


class Ins:
    __slots__ = ("eng", "fn", "deps", "sig", "sigval", "dma", "dsem", "dval", "dprev")

    def __init__(self, eng, fn, dma):
        self.eng = eng
        self.fn = fn
        self.dma = dma
        self.deps = set()
        self.sig = False
        self.sigval = 0
        self.dsem = None
        self.dval = 0
        self.dprev = 0


class Tl:
    def __init__(self, t, name):
        self.t = t
        self.name = name
        self.buf = Buf(name)
        self.subs = {}

    def b(self, i):
        s = self.subs.get(i)
        if s is None:
            s = Buf("%s.%s" % (self.name, i))
            self.subs[i] = s
        return s

    def allb(self):
        return list(self.subs.values())

    def ap(self):
        return self.t.ap()

    def __getitem__(self, k):
        return self.t[k]


class PsBank(Tl):
    def __init__(self, t, i):
        Tl.__init__(self, t, "ps%d" % i)
        self.i = i
        self.buf.excl = True

    def __getitem__(self, key):
        if isinstance(key, tuple):
            return self.t[key[0], self.i, key[1]]
        return self.t[key, self.i, :]


def _bufs(lst):
    out = []
    for x in lst:
        if x is None:
            continue
        if isinstance(x, Tl):
            out.append(x.buf)
        elif isinstance(x, Buf):
            out.append(x)
        else:
            out.extend(_bufs(x))
    return out


class Prog:
    ENGS = ("pe", "act", "dve", "pool", "sp")
    NDMA = {"sp": 40, "act": 16, "pool": 16}

    def __init__(self, nc):
        self.nc = nc
        self.ins = []
        self.eng = {"pe": nc.tensor, "act": nc.scalar, "dve": nc.vector, "pool": nc.gpsimd, "sp": nc.sync}
        self.last = {e: None for e in self.ENGS}
        self.dmas_open = []
        self.sb_off = SBUF_BASE
        self.sb_mark = 0
        self.nid = 0

    def sb(self, name, shape, dtype, persist=False):
        nbytes = int(np.prod(shape[1:])) * (4 if dtype == F32 else 2)
        nbytes = (nbytes + 63) // 64 * 64
        off = self.sb_off
        assert off + nbytes <= SBUF_BYTES, "SBUF overflow %s %d" % (name, off + nbytes)
        self.nid += 1
        t = self.nc.alloc_sbuf_tensor_at("%s_%d" % (name, self.nid), list(shape), dtype, offset=off)
        self.sb_off = off + nbytes
        return Tl(t, name)

    def phase_begin(self):
        self.sb_mark_stack = getattr(self, "sb_mark_stack", [])
        self.sb_mark_stack.append(self.sb_off)

    def phase_end(self):
        self.barrier()
        self.sb_off = self.sb_mark_stack.pop()

    def dram(self, name, shape, dtype, kind="Internal"):
        t = self.nc.dram_tensor(name, list(shape), dtype, kind=kind)
        return Tl(t, name)

    def op(self, eng, fn, r=(), w=(), dma=False):
        if getattr(self, "capture", None) is not None:
            self.capture.append((eng, fn, r, w, dma))
            return None
        i = Ins(eng, fn, dma)
        rb = _bufs(r)
        wb = _bufs(w)
        ex = [b for b in rb if b.excl]
        if ex:
            rb = [b for b in rb if not b.excl]
            wb = wb + [b for b in ex if b not in wb]
        deps = i.deps
        for b in rb:
            if b.w is not None:
                if not (eng == "pe" and b.w.eng == "pe" and not b.w.dma):
                    deps.add(b.w)
        for b in wb:
            x = b.w
            if x is not None and (x.eng != eng or x.dma or dma):
                deps.add(x)
            for x in b.rs.values():
                if x.eng != eng or dma:
                    deps.add(x)
            for x in b.rd:
                deps.add(x)
        for b in rb:
            if dma:
                b.rd.append(i)
            else:
                b.rs[eng] = i
        for b in wb:
            b.w = i
            b.rs = {}
            b.rd = []
        self.ins.append(i)
        if dma:
            self.dmas_open.append(i)
        else:
            self.last[eng] = i
        return i

    def pe(self, fn, r=(), w=()):
        return self.op("pe", fn, r, w)

    def act(self, fn, r=(), w=()):
        return self.op("act", fn, r, w)

    def dve(self, fn, r=(), w=()):
        return self.op("dve", fn, r, w)

    def pool(self, fn, r=(), w=()):
        return self.op("pool", fn, r, w)

    def dma(self, q, out, in_, r=(), w=()):
        e = self.eng[q]
        return self.op(q, lambda: e.dma_start(out=out, in_=in_), r, w, dma=True)

    def barrier(self):
        prev = [x for x in self.last.values() if x is not None] + list(self.dmas_open)
        self.dmas_open = []
        for e in self.ENGS:
            eh = self.eng[e]
            i = Ins(e, (lambda eh=eh: eh.nop()), False)
            i.deps = set(prev)
            self.ins.append(i)
            self.last[e] = i

    def finalize(self):
        nc = self.nc
        self.barrier()
        esem = {e: nc.alloc_semaphore("es_" + e) for e in self.ENGS}
        dsems = {q: [nc.alloc_semaphore("ds_%s%d" % (q, k)) for k in range(n)] for q, n in self.NDMA.items()}
        duse = {q: [0] * n for q, n in self.NDMA.items()}
        drr = {q: 0 for q in self.NDMA}
        for i in self.ins:
            for d in i.deps:
                if not d.dma:
                    d.sig = True
        cnt = {e: 0 for e in self.ENGS}
        for i in self.ins:
            if i.dma:
                q = i.eng
                k = drr[q]
                drr[q] = (k + 1) % self.NDMA[q]
                i.dsem = dsems[q][k]
                i.dprev = duse[q][k]
                duse[q][k] += 16
                i.dval = duse[q][k]
            elif i.sig:
                cnt[i.eng] += 1
                i.sigval = cnt[i.eng]
        seen = {e: {} for e in self.ENGS}
        nwait = 0
        self.trace = {e: [] for e in self.ENGS}
        for i in self.ins:
            e = i.eng
            eh = self.eng[e]
            need = {}
            for d in i.deps:
                if d.dma:
                    s, v = d.dsem, d.dval
                else:
                    s, v = esem[d.eng], d.sigval
                if need.get(s, 0) < v:
                    need[s] = v
            if i.dma and i.dprev > 0:
                if need.get(i.dsem, 0) < i.dprev:
                    need[i.dsem] = i.dprev
            sn = seen[e]
            wl = []
            for s, v in need.items():
                if sn.get(s, 0) < v:
                    eh.wait_ge(s, v)
                    sn[s] = v
                    nwait += 1
                    wl.append((id(s), v))
            ins = i.fn()
            if i.dma:
                ins.then_inc(i.dsem, 16)
                self.trace[e].append((wl, (id(i.dsem), 16)))
            elif i.sig:
                ins.then_inc(esem[e], 1)
                self.trace[e].append((wl, (id(esem[e]), 1)))
            else:
                self.trace[e].append((wl, None))
        self.simulate()
        return dict(n_ins=len(self.ins), n_wait=nwait, sig=dict(cnt))

    def simulate(self):
        sem = {}
        pc = {e: 0 for e in self.ENGS}
        tr = self.trace
        while True:
            prog = False
            done = True
            for e in self.ENGS:
                t = tr[e]
                while pc[e] < len(t):
                    wl, inc = t[pc[e]]
                    if any(sem.get(s, 0) < v for s, v in wl):
                        break
                    if inc is not None:
                        sem[inc[0]] = sem.get(inc[0], 0) + inc[1]
                    pc[e] += 1
                    prog = True
                if pc[e] < len(t):
                    done = False
            if done:
                return
            if not prog:
                raise RuntimeError("sync deadlock at %s" % {e: (pc[e], len(tr[e])) for e in self.ENGS})


BLKS = [(i * 512, 512, 0) for i in range(NL // 512)] + [(NL, NCX, 1)]
TB1 = 256
BLKS1 = [(i * TB1, TB1, 0) for i in range(NL // TB1)] + [(NL, NCX, 1)]
LAT_TILES = list(range(NL // 128))
CTX_TILES = [NL // 128 + i for i in range(NCX // 128)]

C_MLQ, C_MLK, C_MLV, C_MLO, C_MLG, C_DAQ, C_DAK, C_DAV, C_FN, C_GP = 0, 1024, 2048, 3072, 4096, 4112, 5136, 6160, 7184, 8208
F_MLQ, F_MLK, F_DAQ, F_DAK, F_FN = 0, 1024, 2048, 3072, 4096


def host_consts():
    c = {}
    c["ident_f"] = np.eye(128, dtype=np.float32)
    perm = np.array([(m // 64) * 64 + ((m % 64) + 32) % 64 for m in range(128)])
    pw = np.zeros((128, 128), np.float32)
    pw[perm, np.arange(128)] = 1.0
    c["pswap"] = pw
    n_freq = 16
    inv = (10000.0 ** (-np.arange(n_freq, dtype=np.float32) / n_freq)).astype(np.float32)
    rows = NL // 64
    r = np.repeat(np.arange(rows, dtype=np.float32), 64)
    col = np.tile(np.arange(64, dtype=np.float32), rows)
    ang = np.concatenate([r[:, None] * inv, col[:, None] * inv], axis=-1).astype(np.float32)
    cos = np.cos(ang).astype(np.float32).T
    sin = np.sin(ang).astype(np.float32).T
    ct = np.zeros((128, NL), np.float32)
    st = np.zeros((128, NL), np.float32)
    for p in range(128):
        d = p % 64
        f = d % 32
        ct[p] = cos[f]
        st[p] = -sin[f] if d < 32 else sin[f]
    c["rope_c"] = ct
    c["rope_s"] = st
    c.update(host_consts_fn())
    c.update(host_consts_ml())
    return c


class K:
    pass


def Dl(fn, *a, **kw):
    return lambda: fn(*a, **kw)


def build(debug=None, nlayers=DEPTH, stop_after=None, skip=(), br_input=False, uf_input=False, fn_stop=99):
    global FN_STOP
    FN_STOP = fn_stop
    nc = bass.Bass("TRN2", target_bir_lowering=False)
    P = Prog(nc)
    k = K()
    k.nc, k.P = nc, P
    k.da_heads = 8

    def ein(name, shape, dt=F32):
        return Tl(nc.dram_tensor(name, list(shape), dt, kind="ExternalInput"), name)

    dbg_kind = {}

    def scratch(name, shape, dt):
        kind = "ExternalOutput" if (debug and name in debug) else "Internal"
        return Tl(nc.dram_tensor(name, list(shape), dt, kind=kind), name)

    k.xT_in = ein("xT", [128, KC, NT])
    k.cT = ein("cT", [128, KC, 2])
    k.w_ada = ein("w_ada", [DEPTH, D, 6 * D])
    k.b_adaT = ein("b_adaT", [DEPTH, 128, 48])
    k.norm_gT = ein("norm_gT", [DEPTH, 128, 2, KC])
    k.w_in = ein("w_in", [DEPTH, D, D_IN])
    k.ident_f_d = ein("ident_f", [128, 128])
    k.pswap_d = ein("pswap", [128, 128])
    k.rope_c_d = ein("rope_c", [128, NL])
    k.rope_s_d = ein("rope_s", [128, NL])
    k.da_lam = ein("da_lam", [DEPTH, 4, 64])
    k.da_head_gT = ein("da_head_gT", [DEPTH, 128, 8])
    k.w_br_ml = ein("w_br_ml", [DEPTH, D, D])
    k.w_br_da = ein("w_br_da", [DEPTH, D, D])
    k.w_br_fn = ein("w_br_fn", [DEPTH, D, D])
    k.w_out = ein("w_out", [DEPTH, D, D])
    k.w_ffn_in = ein("w_ffn_in", [DEPTH, D, 2 * D_FF])
    k.w_ffn_out = ein("w_ffn_out", [DEPTH, D_FF, D])
    k.final_gT = ein("final_gT", [128, KC])
    k.ml_triF = ein("ml_triF", [128, 128])
    k.ml_triB = ein("ml_triB", [128, 128])
    k.ml_maskF = ein("ml_maskF", [128, 128])
    k.ml_maskB = ein("ml_maskB", [128, 128])
    k.ml_gate_bR = ein("ml_gate_bR", [DEPTH, NTILE * 16])
    k.ml_head_g = ein("ml_head_g", [DEPTH, 1024])
    k.fn_cs = ein("fn_cs", [128, 2, 512], BF16)
    k.fn_w1 = ein("fn_w1", [128, 128], BF16)
    k.fn_twr = ein("fn_twr", [128, 256])
    k.fn_twi = ein("fn_twi", [128, 256])
    k.fn_w3c = ein("fn_w3c", [128, 128], BF16)
    k.fn_w3s = ein("fn_w3s", [128, 128], BF16)
    k.fn_csx = ein("fn_csx", [128, 2, 2, 256], BF16)

    k.hl_d = scratch("hl_d", [128, KC, NT], BF16)
    k.uF_d = ein("uF_d", [5120, NT], BF16) if uf_input else scratch("uF_d", [5120, NT], BF16)
    k.uT_mlk = scratch("uT_mlk", [NT, 1024], BF16)
    k.uT_mlv = scratch("uT_mlv", [NT, 1024], BF16)
    k.uT_mlo = scratch("uT_mlo", [NT, 1024], BF16)
    k.uT_g = scratch("uT_g", [NT, 16], F32)
    k.uT_dav = scratch("uT_dav", [NT, 1024], BF16)
    k.G_d = scratch("G_d", [3072, NT], BF16)
    k.mods_d = scratch("mods_d", [128, DEPTH * 96], F32)
    k.br_d = ein("br_d", [3, 1024, NT], BF16) if br_input else scratch("br_d", [3, 1024, NT], BF16)
    k.x1_d = scratch("x1_d", [128, KC, NT], F32)
    k.AB_d = scratch("AB_d", [2, NL, 512], BF16)
    k.h_d = scratch("h_d", [2, NT, 1024], F32)
    k.h2_d = scratch("h2_d", [128, KC, NT], BF16)
    k.gT_d = scratch("gT_d", [128, FC, NT], BF16)
    k.xT_d = scratch("xT_d", [128, KC, NT], F32)
    k.out = Tl(nc.dram_tensor("out", [NL, D], F32, kind="ExternalOutput"), "out")
    k.dbg = scratch("dbg", [6, 128, 512], F32) if (debug and "dbg" in debug) else None

    k.ident_f = P.sb("ident_f", [128, 128], F32)
    k.ident_bf = P.sb("ident_bf", [128, 128], BF16)
    k.ones_bf = P.sb("ones_bf", [128, 128], BF16)
    k.mods = P.sb("mods", [128, DEPTH * 96], F32)
    k.Gt = P.sb("Gt", [128, DEPTH * 2 * KC * 2], F32)
    k.psall = nc.alloc_psum_tensor("psall", [128, 8, 512], F32)
    k.ps = [PsBank(k.psall, i) for i in range(8)]

    P.dma("sp", k.ident_f[:], k.ident_f_d.ap()[:, :], w=[k.ident_f])
    P.dve(lambda: nc.vector.tensor_copy(k.ident_bf[:], k.ident_f[:]), r=[k.ident_f], w=[k.ident_bf])
    P.dve(lambda: nc.vector.memset(k.ones_bf[:], 1.0), w=[k.ones_bf])

    if "adaln" not in skip:
        phase_adaln(k)
    if debug and "mods_d" in debug:
        P.dma("sp", k.mods_d.ap()[:, :], k.mods[:], r=[k.mods], w=[k.mods_d])
    for l in range(nlayers):
        if l == 0 and "m1" not in skip:
            phase_norm(k, l, 0, k.xT_in, k.hl_d)
        if stop_after == "norm":
            break
        if "m1" not in skip:
            phase_m1(k, l)
        if stop_after == "m1":
            break
        if "da" not in skip:
            phase_da(k, l)
        if stop_after == "da":
            break
        if "fn" not in skip:
            phase_fn(k, l)
        if stop_after == "fn":
            break
        if "ml" not in skip:
            phase_ml(k, l)
        if stop_after == "ml":
            break
        phase_t1(k, l, k.xT_in if l == 0 else k.xT_d)
        phase_t2a(k, l)
        phase_t2b(k, l, last=(l == nlayers - 1))
    info = P.finalize()
    return nc, info


def mods_view(k, l, chunk0, sel):
    base = l * 96 + chunk0 * 2 + sel
    return k.mods[:, base:base + 15:2]


def mod_col(k, l, chunk, sel):
    base = l * 96 + chunk * 2 + sel
    return k.mods[:, base:base + 1]


def g_col(k, l, which, kc, sel):
    base = ((l * 2 + which) * KC + kc) * 2 + sel
    return k.Gt[:, base:base + 1]


def phase_adaln(k):
    nc, P = k.nc, k.P
    P.phase_begin()
    s_c = P.sb("s_c", [128, KC * 2], F32)
    P.dma("sp", s_c[:], k.cT.ap().rearrange("p k s -> p (k s)"), w=[s_c])
    P.act(lambda: nc.scalar.activation(out=s_c[:], in_=s_c[:], func=AF.Silu), r=[s_c], w=[s_c])
    wst = [P.sb("wa%d" % i, [128, KC, 512], F32) for i in range(2)]
    bad = P.sb("bad", [128, DEPTH * 48], F32)
    ng = P.sb("ng", [128, DEPTH * 2 * KC], F32)
    P.dma("sp", bad[:].rearrange("p (l c) -> p l c", l=DEPTH), k.b_adaT.ap().rearrange("l p c -> p l c"), w=[bad])
    P.dma("sp", ng[:].rearrange("p (l c) -> p l c", l=DEPTH), k.norm_gT.ap().rearrange("l p w c -> p l (w c)"), w=[ng])
    ps0 = k.ps[0]
    gi = 0
    for l in range(DEPTH):
        wv = k.w_ada.ap()[l].rearrange("(k p) n -> p k n", p=128)
        for g in range(12):
            w_ = wst[gi % 2]
            gi += 1
            P.dma("sp", w_[:], wv[:, :, g * 512:(g + 1) * 512], w=[w_])
            for j in range(4):
                col = g * 4 + j
                for kc in range(KC):
                    P.pe(lambda w_=w_, j=j, kc=kc, col=col: nc.tensor.matmul(
                        ps0[:, 2 * col:2 * col + 2], w_[:, kc, j * 128:(j + 1) * 128], s_c[:, 2 * kc:2 * kc + 2],
                        start=(kc == 0), stop=(kc == KC - 1)), r=[w_, s_c], w=[ps0])
        for s in range(2):
            P.dve(lambda l=l, s=s: nc.vector.tensor_tensor(
                out=k.mods[:, l * 96 + s:(l + 1) * 96:2], in0=ps0[:, s:96:2], in1=bad[:, l * 48:(l + 1) * 48], op=ALU.add),
                r=[ps0, bad], w=[k.mods])
        for which in range(2):
            for s in range(2):
                sc = mods_view(k, l, 8 + 24 * which, s)
                gb = ((l * 2 + which) * KC) * 2 + s
                P.dve(lambda sc=sc, gb=gb, l=l, which=which: nc.vector.scalar_tensor_tensor(
                    out=k.Gt[:, gb:gb + 15:2], in0=sc, scalar=1.0, in1=ng[:, (l * 2 + which) * KC:(l * 2 + which + 1) * KC],
                    op0=ALU.add, op1=ALU.mult), r=[k.mods, ng], w=[k.Gt])
    P.phase_end()


def phase_norm(k, l, which, xsrc, hdst):
    nc, P = k.nc, k.P
    P.phase_begin()
    xb = [P.sb("xb%d" % i, [128, KC, 512], F32) for i in range(2)]
    sq = P.sb("sq", [128, KC, 512], BF16)
    rs = P.sb("rs", [128, 512], F32)
    tmp = [P.sb("tmp%d" % i, [128, 512], F32) for i in range(2)]
    hb = [P.sb("hb%d" % i, [128, KC, 512], BF16) for i in range(2)]
    eps = P.sb("eps", [128, 1], F32)
    P.dve(lambda: nc.vector.memset(eps[:], EPS), w=[eps])
    for bi, (t0, n, sel) in enumerate(BLKS):
        x_, h_ = xb[bi % 2], hb[bi % 2]
        P.dma("sp", x_[:, :, :n], xsrc.ap()[:, :, t0:t0 + n], w=[x_])
        norm_block(k, l, which, sel, x_, n, sq, rs, tmp, h_, eps, k.ps[1])
        P.dma("sp", hdst.ap()[:, :, t0:t0 + n], h_[:, :, :n], r=[h_], w=[hdst])
    P.phase_end()


def norm_block(k, l, which, sel, x_, n, sq, rs, tmp, h_, eps, ps):
    nc, P = k.nc, k.P
    P.pool(lambda: nc.gpsimd.tensor_tensor(out=sq[:, :, :n], in0=x_[:, :, :n], in1=x_[:, :, :n], op=ALU.mult), r=[x_], w=[sq])
    for kc in range(KC):
        P.pe(lambda kc=kc: nc.tensor.matmul(ps[:, :n], k.ones_bf[:], sq[:, kc, :n], start=(kc == 0), stop=(kc == KC - 1)),
             r=[k.ones_bf, sq], w=[ps])
    P.act(lambda: nc.scalar.activation(out=rs[:, :n], in_=ps[:, :n], func=AF.Sqrt, scale=1.0 / D, bias=eps[:, 0:1]), r=[ps, eps], w=[rs])
    P.dve(lambda: nc.vector.reciprocal(out=rs[:, :n], in_=rs[:, :n]), r=[rs], w=[rs])
    for kc in range(KC):
        t_ = tmp[kc % 2]
        P.dve(lambda kc=kc, t_=t_: nc.vector.tensor_tensor(out=t_[:, :n], in0=x_[:, kc, :n], in1=rs[:, :n], op=ALU.mult), r=[x_, rs], w=[t_])
        P.act(lambda kc=kc, t_=t_: nc.scalar.activation(out=h_[:, kc, :n], in_=t_[:, :n], func=AF.Identity,
                                                       scale=g_col(k, l, which, kc, sel), bias=mod_col(k, l, 24 * which + kc, sel)),
              r=[t_, k.Gt, k.mods], w=[h_])


def load_w_group(k, wv, c0, ncol, wst, wbf):
    nc, P = k.nc, k.P
    P.dma("sp", wst[:, :, :ncol], wv[:, :, c0:c0 + ncol], w=[wst])
    P.pool(lambda: nc.gpsimd.tensor_copy(wbf[:, :, :ncol], wst[:, :, :ncol]), r=[wst], w=[wbf])


def phase_m1(k, l):
    nc, P = k.nc, k.P
    P.phase_begin()
    hl = P.sb("hl", [128, KC, NT], BF16)
    for kc in range(KC):
        P.dma("sp", hl[:, kc, :], k.hl_d.ap()[:, kc, :], w=[hl.b(kc)])
    hlr = hl.allb()
    wst = [P.sb("wst%d" % i, [128, KC, 512], F32) for i in range(2)]
    wbf = [P.sb("wbf%d" % i, [128, KC, 512], BF16) for i in range(2)]
    fst = [P.sb("fst%d" % i, [128, NT], BF16) for i in range(2)]
    tst = [P.sb("tst%d" % i, [128, 512], BF16) for i in range(3)]
    tstf = [P.sb("tstf%d" % i, [128, 16], F32) for i in range(2)]
    ropc = P.sb("ropc", [128, NL], F32)
    rops = P.sb("rops", [128, NL], F32)
    pswap = P.sb("pswap", [128, 128], F32)
    q32 = [P.sb("q32_%d" % i, [128, 512], F32) for i in range(2)]
    t1 = [P.sb("t1_%d" % i, [128, 512], F32) for i in range(2)]
    t2 = [P.sb("t2_%d" % i, [128, 512], F32) for i in range(2)]
    P.dma("act", ropc[:], k.rope_c_d.ap()[:, :], w=[ropc])
    P.dma("act", rops[:], k.rope_s_d.ap()[:, :], w=[rops])
    P.dma("act", pswap[:], k.pswap_d.ap()[:, :], w=[pswap])
    wv = k.w_in.ap()[l].rearrange("(k p) n -> p k n", p=128)
    st = dict(g=0, ps=0, f=0, t=0, ev=0, r=0)

    def nextps():
        st["ps"] = (st["ps"] + 1) % 6
        return k.ps[2 + st["ps"]]

    def feat_group(c0, dst, row0, kind, tok_blks=BLKS):
        w_, wb_ = wst[st["g"] % 2], wbf[st["g"] % 2]
        st["g"] += 1
        prefetch_next()
        for j in range(4):
            stg = fst[st["f"] % 2]
            st["f"] += 1
            for bi, (t0, n, sel) in enumerate(tok_blks):
                ps = nextps()
                sb_ = stg.b(bi)
                for kc in range(KC):
                    P.pe(lambda ps=ps, wb_=wb_, j=j, kc=kc, t0=t0, n=n: nc.tensor.matmul(
                        ps[:, :n], wb_[:, kc, j * 128:(j + 1) * 128], hl[:, kc, t0:t0 + n], start=(kc == 0), stop=(kc == KC - 1)),
                        r=[wb_, hlr], w=[ps])
                o = stg[:, t0:t0 + n]
                if kind == "rope" and sel == 0:
                    q_, a_, b_ = q32[st["r"] % 2], t1[st["r"] % 2], t2[st["r"] % 2]
                    st["r"] += 1
                    psr = k.ps[0 + st["r"] % 2]
                    P.act(lambda ps=ps, q_=q_, n=n: nc.scalar.copy(q_[:, :n], ps[:, :n]), r=[ps], w=[q_])
                    P.pe(lambda psr=psr, q_=q_, n=n: nc.tensor.matmul(psr[:, :n], pswap[:], q_[:, :n], start=True, stop=True),
                         r=[pswap, q_], w=[psr])
                    P.pool(lambda a_=a_, q_=q_, t0=t0, n=n: nc.gpsimd.tensor_tensor(out=a_[:, :n], in0=q_[:, :n], in1=ropc[:, t0:t0 + n], op=ALU.mult),
                           r=[q_, ropc], w=[a_])
                    P.dve(lambda b_=b_, psr=psr, t0=t0, n=n: nc.vector.tensor_tensor(out=b_[:, :n], in0=psr[:, :n], in1=rops[:, t0:t0 + n], op=ALU.mult),
                          r=[psr, rops], w=[b_])
                    P.dve(lambda o=o, a_=a_, b_=b_, n=n: nc.vector.tensor_tensor(out=o, in0=a_[:, :n], in1=b_[:, :n], op=ALU.add),
                          r=[a_, b_], w=[sb_])
                elif kind == "sigm":
                    P.act(lambda o=o, ps=ps, n=n: nc.scalar.activation(out=o, in_=ps[:, :n], func=AF.Sigmoid), r=[ps], w=[sb_])
                elif kind == "s16":
                    P.act(lambda o=o, ps=ps, n=n: nc.scalar.mul(o, ps[:, :n], 1.0 / 16.0), r=[ps], w=[sb_])
                else:
                    st["ev"] += 1
                    if st["ev"] % 2:
                        P.act(lambda o=o, ps=ps, n=n: nc.scalar.copy(o, ps[:, :n]), r=[ps], w=[sb_])
                    else:
                        P.dve(lambda o=o, ps=ps, n=n: nc.vector.tensor_copy(o, ps[:, :n]), r=[ps], w=[sb_])
            ta, tb = tok_blks[0][0], tok_blks[-1][0] + tok_blks[-1][1]
            P.dma("sp", dst.ap()[row0 + j * 128:row0 + (j + 1) * 128, ta:tb], stg[:, ta:tb], r=stg.allb(), w=[dst])

    def tok_group(c0, ncol, dst, dcol0, kind):
        w_, wb_ = wst[st["g"] % 2], wbf[st["g"] % 2]
        st["g"] += 1
        prefetch_next()
        for ti in range(NTILE):
            ps = nextps()
            for kc in range(KC):
                P.pe(lambda ps=ps, wb_=wb_, kc=kc, ti=ti: nc.tensor.matmul(
                    ps[:, :ncol], hl[:, kc, ti * 128:(ti + 1) * 128], wb_[:, kc, :ncol], start=(kc == 0), stop=(kc == KC - 1)),
                    r=[wb_, hlr], w=[ps])
            if kind == "f32":
                stg = tstf[st["t"] % 2]
            else:
                stg = tst[st["t"] % 3]
            st["t"] += 1
            o = stg[:, :ncol]
            if kind == "sigm":
                P.act(lambda o=o, ps=ps: nc.scalar.activation(out=o, in_=ps[:, :ncol], func=AF.Sigmoid), r=[ps], w=[stg])
            elif kind == "s16":
                P.act(lambda o=o, ps=ps: nc.scalar.mul(o, ps[:, :ncol], 1.0 / 16.0), r=[ps], w=[stg])
            else:
                st["ev"] += 1
                if st["ev"] % 2:
                    P.act(lambda o=o, ps=ps: nc.scalar.copy(o, ps[:, :ncol]), r=[ps], w=[stg])
                else:
                    P.dve(lambda o=o, ps=ps: nc.vector.tensor_copy(o, ps[:, :ncol]), r=[ps], w=[stg])
            P.dma("sp", dst.ap()[ti * 128:(ti + 1) * 128, dcol0:dcol0 + ncol], o, r=[stg], w=[dst])

    tasks = []
    for g in range(2):
        tasks.append(("f", C_MLQ + g * 512, 512, k.uF_d, F_MLQ + g * 512, "copy"))
        tasks.append(("f", C_MLK + g * 512, 512, k.uF_d, F_MLK + g * 512, "s16"))
        tasks.append(("f", C_DAQ + g * 512, 512, k.uF_d, F_DAQ + g * 512, "rope"))
        tasks.append(("f", C_DAK + g * 512, 512, k.uF_d, F_DAK + g * 512, "rope"))
        tasks.append(("f", C_FN + g * 512, 512, k.uF_d, F_FN + g * 512, "copy"))
        tasks.append(("t", C_MLK + g * 512, 512, k.uT_mlk, g * 512, "s16"))
        tasks.append(("t", C_MLV + g * 512, 512, k.uT_mlv, g * 512, "copy"))
        tasks.append(("t", C_MLO + g * 512, 512, k.uT_mlo, g * 512, "sigm"))
        tasks.append(("t", C_DAV + g * 512, 512, k.uT_dav, g * 512, "copy"))
    tasks.append(("t", C_MLG, 16, k.uT_g, 0, "f32"))
    for g in range(6):
        tasks.append(("f", C_GP + g * 512, 512, k.G_d, g * 512, "sigm"))
    pf = dict(i=0)

    def prefetch_next():
        i = pf["i"]
        if i < len(tasks):
            t = tasks[i]
            load_w_group(k, wv, t[1], t[2], wst[i % 2], wbf[i % 2])
            pf["i"] = i + 1

    prefetch_next()
    for t in tasks:
        if t[0] == "f":
            feat_group(t[1], t[3], t[4], t[5])
        else:
            tok_group(t[1], t[2], t[3], t[4], t[5])
    P.phase_end()


def fm(a):
    return np.ascontiguousarray(a.T.reshape(KC, 128, a.shape[0]).transpose(1, 0, 2))


def prep_core_inputs(inp, b, consts):
    m = {}
    xall = np.concatenate([inp["x"][b], inp["ctx"][b]], axis=0)
    m["xT"] = fm(xall).astype(np.float32)
    cc = np.stack([inp["c"][b], inp["c_ctx"]], axis=0)
    m["cT"] = np.ascontiguousarray(cc.T.reshape(KC, 128, 2).transpose(1, 0, 2)).astype(np.float32)
    m["w_ada"] = inp["w_ada"]
    m["b_adaT"] = np.ascontiguousarray(inp["b_ada"].reshape(DEPTH, 48, 128).transpose(0, 2, 1))
    m["norm_gT"] = np.ascontiguousarray(inp["norm_g"].reshape(DEPTH, 2, KC, 128).transpose(0, 3, 1, 2))
    m["w_in"] = inp["w_in"]
    m["da_lam"] = inp["da_lam"]
    m["ml_gate_bR"] = np.ascontiguousarray(np.tile(inp["ml_gate_b"][:, None, :], (1, NTILE, 1)).reshape(DEPTH, NTILE * 16))
    m["ml_head_g"] = inp["ml_head_g"]
    for nm in ("w_br_ml", "w_br_da", "w_br_fn", "w_out", "w_ffn_in", "w_ffn_out"):
        m[nm] = inp[nm]
    m["final_gT"] = np.ascontiguousarray(inp["final_g"].reshape(KC, 128).T)
    m["da_head_gT"] = np.ascontiguousarray(inp["da_head_g"].reshape(DEPTH, 8, 128).transpose(0, 2, 1))
    m.update(consts)
    return m


def lam_init_of(l):
    return 0.8 - 0.6 * math.exp(-0.3 * l)


def phase_da(k, l):
    nc, P = k.nc, k.P
    V_, A_, T_, G_ = nc.vector, nc.scalar, nc.tensor, nc.gpsimd
    P.phase_begin()
    li = lam_init_of(l)
    lq = P.sb("lq", [128, 256], F32)
    P.dma("sp", lq[:], k.da_lam.ap()[l:l + 1].rearrange("o a d -> o (a d)").partition_broadcast(128), w=[lq])
    pr = P.sb("pr", [128, 128], F32)
    sc = P.sb("sc", [128, 8], F32)
    eps = P.sb("eps", [128, 1], F32)
    P.dve(Dl(V_.memset, eps[:], EPS), w=[eps])
    P.dve(Dl(V_.tensor_tensor, out=pr[:, 0:64], in0=lq[:, 0:64], in1=lq[:, 64:128], op=ALU.mult), r=[lq], w=[pr])
    P.dve(Dl(V_.tensor_tensor, out=pr[:, 64:128], in0=lq[:, 128:192], in1=lq[:, 192:256], op=ALU.mult), r=[lq], w=[pr])
    P.dve(Dl(V_.reduce_sum, out=sc[:, 0:1], in_=pr[:, 0:64], axis=AX.X), r=[pr], w=[sc])
    P.dve(Dl(V_.reduce_sum, out=sc[:, 1:2], in_=pr[:, 64:128], axis=AX.X), r=[pr], w=[sc])
    P.act(Dl(A_.activation, out=sc[:, 2:4], in_=sc[:, 0:2], func=AF.Exp), r=[sc], w=[sc])
    P.dve(Dl(V_.scalar_tensor_tensor, out=sc[:, 4:5], in0=sc[:, 3:4], scalar=-li, in1=sc[:, 2:3], op0=ALU.add, op1=ALU.subtract),
          r=[sc], w=[sc])
    neglam = sc[:, 4:5]
    hg = P.sb("hg", [128, 8], F32)
    P.dma("sp", hg[:], k.da_head_gT.ap()[l], w=[hg])
    P.dve(Dl(V_.tensor_scalar, out=hg[:], in0=hg[:], scalar1=(1.0 - li), scalar2=None, op0=ALU.mult), r=[hg], w=[hg])

    qz = [[P.sb("qz%d%d" % (i, m), [128, NT], BF16) for m in range(2)] for i in range(2)]
    for i in range(2):
        for m in range(2):
            P.pool(Dl(G_.memset, qz[i][m][:], 0.0), w=[qz[i][m]])
    kT = [P.sb("kT%d" % i, [128, NT], BF16) for i in range(2)]
    V = [P.sb("V%d" % i, [128, NTILE, 128], BF16) for i in range(2)]
    pt = [P.sb("pt%d" % i, [128, 2, 512], BF16) for i in range(3)]
    ystg = [P.sb("ystg%d" % i, [128, NT], BF16) for i in range(2)]
    rd = [P.sb("rd%d" % i, [128, 512], F32) for i in range(2)]
    o_ = [P.sb("o%d" % i, [128, 512], F32) for i in range(2)]
    acc = [P.sb("acc%d" % i, [128, 2, 512], F32) for i in range(2)]
    ones_f = P.sb("ones_fd", [128, 128], F32)
    P.dve(Dl(V_.memset, ones_f[:], 1.0), w=[ones_f])
    ofs = [P.sb("of%d" % i, [128, 512], F32) for i in range(2)]
    sqs = [P.sb("sqd%d" % i, [128, 512], BF16) for i in range(2)]
    rs = P.sb("rsd", [128, 512], F32)
    deferred = []
    cO = [[P.sb("cO%d%d" % (i, m), [128, 512], F32) for m in range(2)] for i in range(2)]
    cD = [[P.sb("cD%d%d" % (i, m), [128, 512], F32) for m in range(2)] for i in range(2)]
    psO = [k.ps[4], k.ps[5]]
    psD = [k.ps[6], k.ps[7]]
    psE = k.ps[0]
    cnt = dict(s=0, p=0)

    def load_head(h):
        for m in range(2):
            r0 = F_DAQ + h * 128 + 64 * m
            P.dma("sp", qz[h % 2][m][64 * m:64 * m + 64, :], k.uF_d.ap()[r0:r0 + 64, :], w=[qz[h % 2][m]])
        P.dma("sp", kT[h % 2][:], k.uF_d.ap()[F_DAK + h * 128:F_DAK + (h + 1) * 128, :], w=[kT[h % 2]])
        P.dma("sp", V[h % 2][:], k.uT_dav.ap()[:, h * 128:(h + 1) * 128].rearrange("(n p) d -> p n d", p=128), w=[V[h % 2]])

    load_head(0)
    for h in range(k.da_heads):
        qz_, k_, v_, y_ = qz[h % 2], kT[h % 2], V[h % 2], ystg[h % 2]
        for bi, (t0, n, sel) in enumerate(BLKS):
            if bi == 1 and h + 1 < k.da_heads:
                load_head(h + 1)
            ktiles = (LAT_TILES + CTX_TILES) if sel == 0 else CTX_TILES
            steps = [(m, ktiles[j], ktiles[j + 1], j) for m in range(2) for j in range(0, len(ktiles), 2)]
            nk = len(ktiles)

            def emit_s(i):
                m, ka, kb, j = steps[i]
                pi = cnt["s"] % 2
                cnt["s"] += 1
                banks = [k.ps[2 * pi], k.ps[2 * pi + 1]]
                for x, kt in enumerate((ka, kb)):
                    P.pe(Dl(T_.matmul, banks[x][:, :n], k_[:, kt * 128:(kt + 1) * 128], qz_[m][:, t0:t0 + n], start=True, stop=True),
                         r=[k_, qz_[m]], w=[banks[x]])
                p_ = pt[cnt["p"] % 3]
                cnt["p"] += 1
                P.act(Dl(A_.activation, out=p_[:, :, :n], in_=k.psall[:, 2 * pi:2 * pi + 2, :n], func=AF.Exp, scale=0.125), r=banks, w=[p_])
                return p_

            def emit_o(i, p_):
                m, ka, kb, j = steps[i]
                for x, kt in enumerate((ka, kb)):
                    jj = j + x
                    P.pe(Dl(T_.matmul, psO[m][:, :n], v_[:, kt, :], p_[:, x, :n], start=(jj == 0), stop=(jj == nk - 1)),
                         r=[v_, p_], w=[psO[m]])
                    P.pe(Dl(T_.matmul, psD[m][:, :n], k.ones_bf[:], p_[:, x, :n], start=(jj == 0), stop=(jj == nk - 1)),
                         r=[k.ones_bf, p_], w=[psD[m]])

            pend = []
            for i in range(len(steps)):
                pend.append((i, emit_s(i)))
                if len(pend) > 1:
                    emit_o(*pend.pop(0))
                if i == 8 or i == len(steps) - 1:
                    while deferred:
                        deferred.pop(0)()
            while pend:
                emit_o(*pend.pop(0))
            of_, sq_ = ofs[bi % 2], sqs[bi % 2]
            cO_, cD_ = cO[bi % 2], cD[bi % 2]
            for m in range(2):
                P.act(Dl(A_.copy, cD_[m][:, :n], psD[m][:, :n]), r=[psD[m]], w=[cD_[m]])
                P.act(Dl(A_.copy, cO_[m][:, :n], psO[m][:, :n]), r=[psO[m]], w=[cO_[m]])
            for m in range(2):
                P.dve(Dl(V_.reciprocal, out=rd[m][:, :n], in_=cD_[m][:, :n]), r=[cD_[m]], w=[rd[m]])
                P.dve(Dl(V_.tensor_tensor, out=o_[m][:, :n], in0=cO_[m][:, :n], in1=rd[m][:, :n], op=ALU.mult),
                      r=[cO_[m], rd[m]], w=[o_[m]])
            P.dve(Dl(V_.scalar_tensor_tensor, out=of_[:, :n], in0=o_[1][:, :n], scalar=neglam, in1=o_[0][:, :n], op0=ALU.mult, op1=ALU.add),
                  r=[o_[0], o_[1], sc], w=[of_])
            P.pool(Dl(G_.tensor_tensor, out=sq_[:, :n], in0=of_[:, :n], in1=of_[:, :n], op=ALU.mult), r=[of_], w=[sq_])
            def part_c(n=n, t0=t0, bi=bi, h=h, y_=y_, sq_=sq_, of_=of_):
                P.pe(Dl(T_.matmul, psE[:, :n], k.ones_bf[:], sq_[:, :n], start=True, stop=True), r=[k.ones_bf, sq_], w=[psE])
                P.act(Dl(A_.activation, out=rs[:, :n], in_=psE[:, :n], func=AF.Sqrt, scale=1.0 / 128, bias=eps[:, 0:1]), r=[psE, eps], w=[rs])
                P.dve(Dl(V_.reciprocal, out=rs[:, :n], in_=rs[:, :n]), r=[rs], w=[rs])
                P.dve(Dl(V_.tensor_tensor, out=of_[:, :n], in0=of_[:, :n], in1=rs[:, :n], op=ALU.mult), r=[of_, rs], w=[of_])
                P.act(Dl(A_.activation, out=y_[:, t0:t0 + n], in_=of_[:, :n], func=AF.Copy, scale=hg[:, h:h + 1]), r=[of_, hg], w=[y_.b(bi)])
            deferred.append(part_c)
        while deferred:
            deferred.pop(0)()
        P.dma("sp", k.br_d.ap()[1, h * 128:(h + 1) * 128, :], y_[:], r=y_.allb(), w=[k.br_d.b(("da", h))])
    P.phase_end()


def load_w_full(k, dst_bf, wv, ncols, kcn, wst, q="sp"):
    nc, P = k.nc, k.P
    for gi, c0 in enumerate(range(0, ncols, 512)):
        w_ = wst[gi % 2]
        P.dma(q, w_[:, :kcn, :], wv[:, :, c0:c0 + 512], w=[w_])
        P.pool(Dl(nc.gpsimd.tensor_copy, dst_bf[:, :, c0:c0 + 512], w_[:, :kcn, :]), r=[w_], w=[dst_bf.b(c0 // 512)])


def phase_t1(k, l, xsrc):
    nc, P = k.nc, k.P
    V_, A_, T_, G_ = nc.vector, nc.scalar, nc.tensor, nc.gpsimd
    P.phase_begin()
    wst = [P.sb("wst%d" % i, [128, KC, 512], F32) for i in range(2)]
    wbr = [P.sb("wbr%d" % i, [128, KC, 1024], BF16) for i in range(3)]
    wo = P.sb("wo", [128, KC, 1024], BF16)
    for x, wt in enumerate((k.w_br_ml, k.w_br_da, k.w_br_fn)):
        load_w_full(k, wbr[x], wt.ap()[l].rearrange("(k p) n -> p k n", p=128), 1024, KC, wst)
    load_w_full(k, wo, k.w_out.ap()[l].rearrange("(k p) n -> p k n", p=128), 1024, KC, wst)
    brb = [P.sb("brb%d" % i, [128, 24, TB1], BF16) for i in range(2)]
    gb = [P.sb("gb%d" % i, [128, 24, TB1], BF16) for i in range(2)]
    xb = [P.sb("xb%d" % i, [128, KC, TB1], F32) for i in range(2)]
    yb = P.sb("yb", [128, KC, TB1], BF16)
    ta = [P.sb("ta%d" % i, [128, TB1], F32) for i in range(2)]
    tb = [P.sb("tb%d" % i, [128, TB1], F32) for i in range(2)]
    tc = [P.sb("tc%d" % i, [128, TB1], F32) for i in range(2)]
    sq = P.sb("sq", [128, KC, TB1], BF16)
    rs = P.sb("rs", [128, TB1], F32)
    tmp = [P.sb("tmp%d" % i, [128, TB1], F32) for i in range(2)]
    hb = [P.sb("hb%d" % i, [128, KC, TB1], BF16) for i in range(2)]
    eps = P.sb("eps", [128, 1], F32)
    P.dve(Dl(V_.memset, eps[:], EPS), w=[eps])
    gv = k.G_d.ap().rearrange("(c p) t -> p c t", p=128)
    it = 0
    for bi, (t0, n, sel) in enumerate(BLKS1):
        b_, g_, x_, h_ = brb[bi % 2], gb[bi % 2], xb[bi % 2], hb[bi % 2]
        for x in range(3):
            P.dma("sp", b_[:, x * 8:(x + 1) * 8, :n], k.br_d.ap()[x].rearrange("(c p) t -> p c t", p=128)[:, :, t0:t0 + n], w=[b_.b(x)])
        P.dma("act", g_[:, :, :n], gv[:, :, t0:t0 + n], w=[g_])
        P.dma("act", x_[:, :, :n], xsrc.ap()[:, :, t0:t0 + n], w=[x_])
        for oc in range(KC):
            pss = [k.ps[(it % 2) * 3 + x] for x in range(3)]
            a_, b2_, c_ = ta[it % 2], tb[it % 2], tc[it % 2]
            it += 1
            for x in range(3):
                for kc in range(KC):
                    P.pe(Dl(T_.matmul, pss[x][:, :n], wbr[x][:, kc, oc * 128:(oc + 1) * 128], b_[:, x * 8 + kc, :n],
                            start=(kc == 0), stop=(kc == KC - 1)), r=[wbr[x].b(oc // 4), b_.b(x)], w=[pss[x]])
            P.dve(Dl(V_.tensor_tensor, out=a_[:, :n], in0=pss[0][:, :n], in1=g_[:, oc, :n], op=ALU.mult), r=[pss[0], g_], w=[a_])
            P.dve(Dl(V_.tensor_tensor, out=b2_[:, :n], in0=pss[1][:, :n], in1=g_[:, 8 + oc, :n], op=ALU.mult), r=[pss[1], g_], w=[b2_])
            P.dve(Dl(V_.tensor_tensor, out=c_[:, :n], in0=pss[2][:, :n], in1=g_[:, 16 + oc, :n], op=ALU.mult), r=[pss[2], g_], w=[c_])
            P.pool(Dl(G_.tensor_tensor, out=a_[:, :n], in0=a_[:, :n], in1=b2_[:, :n], op=ALU.add), r=[a_, b2_], w=[a_])
            P.pool(Dl(G_.tensor_tensor, out=yb[:, oc, :n], in0=a_[:, :n], in1=c_[:, :n], op=ALU.add), r=[a_, c_], w=[yb.b(oc)])
        for oc in range(KC):
            ps = k.ps[6]
            for kc in range(KC):
                P.pe(Dl(T_.matmul, ps[:, :n], wo[:, kc, oc * 128:(oc + 1) * 128], yb[:, kc, :n], start=(kc == 0), stop=(kc == KC - 1)),
                     r=[wo.b(oc // 4), yb.allb()], w=[ps])
            P.dve(Dl(V_.scalar_tensor_tensor, out=x_[:, oc, :n], in0=ps[:, :n], scalar=mod_col(k, l, 16 + oc, sel), in1=x_[:, oc, :n],
                     op0=ALU.mult, op1=ALU.add), r=[ps, x_, k.mods], w=[x_])
        P.dma("sp", k.x1_d.ap()[:, :, t0:t0 + n], x_[:, :, :n], r=[x_], w=[k.x1_d])
        norm_block(k, l, 1, sel, x_, n, sq, rs, tmp, h_, eps, k.ps[7])
        P.dma("sp", k.h2_d.ap()[:, :, t0:t0 + n], h_[:, :, :n], r=[h_], w=[k.h2_d])
    P.phase_end()


def phase_t2a(k, l):
    nc, P = k.nc, k.P
    V_, A_, T_, G_ = nc.vector, nc.scalar, nc.tensor, nc.gpsimd
    P.phase_begin()
    wst = [P.sb("wst%d" % i, [128, KC, 512], F32) for i in range(2)]
    wf = P.sb("wf", [128, KC, 2 * D_FF], BF16)
    load_w_full(k, wf, k.w_ffn_in.ap()[l].rearrange("(k p) n -> p k n", p=128), 2 * D_FF, KC, wst)
    hb = [P.sb("hb%d" % i, [128, KC, 512], BF16) for i in range(2)]
    gblk = [P.sb("gblk%d" % i, [128, FC, 512], BF16) for i in range(2)]
    sa = [P.sb("sa%d" % i, [128, 512], F32) for i in range(2)]
    it = 0
    for bi, (t0, n, sel) in enumerate(BLKS):
        h_, g_ = hb[bi % 2], gblk[bi % 2]
        P.dma("sp", h_[:, :, :n], k.h2_d.ap()[:, :, t0:t0 + n], w=[h_])
        for j in range(FC):
            pa, pb = k.ps[(it % 4) * 2], k.ps[(it % 4) * 2 + 1]
            s_ = sa[it % 2]
            it += 1
            for kc in range(KC):
                P.pe(Dl(T_.matmul, pa[:, :n], wf[:, kc, j * 128:(j + 1) * 128], h_[:, kc, :n], start=(kc == 0), stop=(kc == KC - 1)),
                     r=[wf.b((j * 128) // 512), h_], w=[pa])
            for kc in range(KC):
                P.pe(Dl(T_.matmul, pb[:, :n], wf[:, kc, D_FF + j * 128:D_FF + (j + 1) * 128], h_[:, kc, :n], start=(kc == 0), stop=(kc == KC - 1)),
                     r=[wf.b((D_FF + j * 128) // 512), h_], w=[pb])
            P.act(Dl(A_.activation, out=s_[:, :n], in_=pa[:, :n], func=AF.Silu), r=[pa], w=[s_])
            P.dve(Dl(V_.tensor_tensor, out=g_[:, j, :n], in0=pb[:, :n], in1=s_[:, :n], op=ALU.mult), r=[pb, s_], w=[g_])
        P.dma("sp", k.gT_d.ap()[:, :, t0:t0 + n], g_[:, :, :n], r=[g_], w=[k.gT_d])
    P.phase_end()


def phase_t2b(k, l, last):
    nc, P = k.nc, k.P
    V_, A_, T_, G_ = nc.vector, nc.scalar, nc.tensor, nc.gpsimd
    P.phase_begin()
    wst = [P.sb("wst%d" % i, [128, 11, 512], F32) for i in range(2)]
    wo = P.sb("wo", [128, FC, 1024], BF16)
    wv = k.w_ffn_out.ap()[l].rearrange("(k p) n -> p k n", p=128)
    gi = 0
    for c0 in (0, 512):
        for k0 in (0, 11):
            w_ = wst[gi % 2]
            gi += 1
            P.dma("sp", w_[:], wv[:, k0:k0 + 11, c0:c0 + 512], w=[w_])
            P.pool(Dl(G_.tensor_copy, wo[:, k0:k0 + 11, c0:c0 + 512], w_[:]), r=[w_], w=[wo.b((c0, k0))])
    gblk = [P.sb("gblk%d" % i, [128, FC, 512], BF16) for i in range(2)]
    xb = [P.sb("xb%d" % i, [128, KC, 512], F32) for i in range(2)]
    sq = P.sb("sq", [128, KC, 512], BF16)
    rs = P.sb("rs", [128, 512], F32)
    tmp = [P.sb("tmp%d" % i, [128, 512], F32) for i in range(2)]
    hb = [P.sb("hb%d" % i, [128, KC, 512], BF16) for i in range(2)] if not last else [None, None]
    eps = P.sb("eps", [128, 1], F32)
    P.dve(Dl(V_.memset, eps[:], EPS), w=[eps])
    if last:
        fg = P.sb("fg", [128, KC], F32)
        P.dma("sp", fg[:], k.final_gT.ap()[:, :], w=[fg])
        hf = P.sb("hf", [128, KC, 512], F32)
        ot = [P.sb("ot%d" % i, [128, 1024], F32) for i in range(2)]
    it = 0
    blks = BLKS[:-1] if last else BLKS
    for bi, (t0, n, sel) in enumerate(blks):
        g_, x_, h_ = gblk[bi % 2], xb[bi % 2], hb[bi % 2]
        P.dma("sp", g_[:, :, :n], k.gT_d.ap()[:, :, t0:t0 + n], w=[g_])
        P.dma("act", x_[:, :, :n], k.x1_d.ap()[:, :, t0:t0 + n], w=[x_])
        for oc in range(KC):
            ps = k.ps[it % 4]
            it += 1
            for j in range(FC):
                P.pe(Dl(T_.matmul, ps[:, :n], wo[:, j, oc * 128:(oc + 1) * 128], g_[:, j, :n], start=(j == 0), stop=(j == FC - 1)),
                     r=[wo.b(((oc // 4) * 512, (j // 11) * 11)), g_], w=[ps])
            P.dve(Dl(V_.scalar_tensor_tensor, out=x_[:, oc, :n], in0=ps[:, :n], scalar=mod_col(k, l, 40 + oc, sel), in1=x_[:, oc, :n],
                     op0=ALU.mult, op1=ALU.add), r=[ps, x_, k.mods], w=[x_])
        if not last:
            P.dma("sp", k.xT_d.ap()[:, :, t0:t0 + n], x_[:, :, :n], r=[x_], w=[k.xT_d])
            norm_block(k, l + 1, 0, sel, x_, n, sq, rs, tmp, h_, eps, k.ps[7])
            P.dma("sp", k.hl_d.ap()[:, :, t0:t0 + n], h_[:, :, :n], r=[h_], w=[k.hl_d])
        else:
            ps7 = k.ps[7]
            P.pool(Dl(G_.tensor_tensor, out=sq[:, :, :n], in0=x_[:, :, :n], in1=x_[:, :, :n], op=ALU.mult), r=[x_], w=[sq])
            for kc in range(KC):
                P.pe(Dl(T_.matmul, ps7[:, :n], k.ones_bf[:], sq[:, kc, :n], start=(kc == 0), stop=(kc == KC - 1)), r=[k.ones_bf, sq], w=[ps7])
            P.act(Dl(A_.activation, out=rs[:, :n], in_=ps7[:, :n], func=AF.Sqrt, scale=1.0 / D, bias=eps[:, 0:1]), r=[ps7, eps], w=[rs])
            P.dve(Dl(V_.reciprocal, out=rs[:, :n], in_=rs[:, :n]), r=[rs], w=[rs])
            for kc in range(KC):
                P.dve(Dl(V_.scalar_tensor_tensor, out=hf[:, kc, :n], in0=x_[:, kc, :n], scalar=fg[:, kc:kc + 1], in1=rs[:, :n],
                         op0=ALU.mult, op1=ALU.mult), r=[x_, fg, rs], w=[hf])
            for tt in range(n // 128):
                o_ = ot[tt % 2]
                for half in range(2):
                    pt_ = k.ps[4 + half]
                    for q4 in range(4):
                        kc = half * 4 + q4
                        P.pe(Dl(T_.transpose, pt_[:, q4 * 128:(q4 + 1) * 128], hf[:, kc, tt * 128:(tt + 1) * 128], k.ident_f[:]),
                             r=[hf, k.ident_f], w=[pt_])
                    if half == 0:
                        P.act(Dl(A_.copy, o_[:, 0:512], pt_[:, :]), r=[pt_], w=[o_])
                    else:
                        P.dve(Dl(V_.tensor_copy, o_[:, 512:1024], pt_[:, :]), r=[pt_], w=[o_])
                P.dma("sp", k.out.ap()[t0 + tt * 128:t0 + (tt + 1) * 128, :], o_[:], r=[o_], w=[k.out.b((t0, tt))])
    P.phase_end()


def host_consts_fn():
    import ml_dtypes
    bf = ml_dtypes.bfloat16
    c = {}
    cc = np.arange(256)
    ang = 2 * np.pi * np.outer(cc, cc) / 256.0
    cs = np.concatenate([np.cos(ang), np.sin(ang)], axis=1)
    c["fn_cs"] = np.ascontiguousarray(cs.reshape(2, 128, 512).transpose(1, 0, 2)).astype(bf)
    t = np.arange(64)
    a64 = 2 * np.pi * np.outer(t, t) / 64.0
    w1 = np.zeros((128, 128))
    w1[0:64, 0:64] = np.cos(a64)
    w1[0:64, 64:128] = -np.sin(a64)
    w1[64:128, 0:64] = -np.sin(a64)
    w1[64:128, 64:128] = -np.cos(a64)
    c["fn_w1"] = w1.astype(bf)
    t1 = np.repeat(np.arange(64), 2)
    atw = 2 * np.pi * np.outer(t1, np.arange(64)) / 4096.0
    c["fn_twr"] = np.ascontiguousarray(np.tile(np.cos(atw)[:, None, :], (1, 4, 1)).reshape(128, 256)).astype(np.float32)
    c["fn_twi"] = np.ascontiguousarray(np.tile(-np.sin(atw)[:, None, :], (1, 4, 1)).reshape(128, 256)).astype(np.float32)
    w3c = np.zeros((128, 128))
    w3s = np.zeros((128, 128))
    for gi in range(2):
        w3c[gi::2, gi * 64:(gi + 1) * 64] = np.cos(a64)
        w3s[gi::2, gi * 64:(gi + 1) * 64] = np.sin(a64)
    c["fn_w3c"] = w3c.astype(bf)
    c["fn_w3s"] = w3s.astype(bf)
    kk = np.arange(256)
    a256 = 2 * np.pi * np.outer(kk, kk) / 256.0
    csx = np.stack([np.cos(a256), -np.sin(a256)], axis=1)
    c["fn_csx"] = np.ascontiguousarray(csx.reshape(2, 128, 2, 256).transpose(1, 0, 2, 3)).astype(bf)
    return c


def phase_fn(k, l):
    nc, P = k.nc, k.P
    V_, A_, T_, G_ = nc.vector, nc.scalar, nc.tensor, nc.gpsimd
    P.phase_begin()
    cs = P.sb("cs", [128, 2, 512], BF16)
    w1 = P.sb("w1", [128, 128], BF16)
    twr = P.sb("twr", [128, 256], F32)
    twi = P.sb("twi", [128, 256], F32)
    w3c = P.sb("w3c", [128, 128], BF16)
    w3s = P.sb("w3s", [128, 128], BF16)
    csx = P.sb("csx", [128, 2, 2, 256], BF16)
    for t_, d_ in ((cs, k.fn_cs), (w1, k.fn_w1), (twr, k.fn_twr), (twi, k.fn_twi), (w3c, k.fn_w3c), (w3s, k.fn_w3s), (csx, k.fn_csx)):
        P.dma("act", t_[:], d_.ap(), w=[t_])
    zc = [P.sb("zc%d" % i, [128, 4, 1024], BF16) for i in range(2)]
    stg = [P.sb("stg%d" % i, [128, 2, 512], BF16) for i in range(3)]
    abc = [P.sb("abc%d" % i, [128, 2, 512], BF16) for i in range(2)]
    yf = [[P.sb("yf%d%d" % (gi, ch), [128, NT], BF16) for ch in range(2)] for gi in range(2)]
    D1 = P.sb("D1", [128, 64, 512], BF16)
    H = P.sb("H", [128, 2, 64, 256], BF16)
    wa = [P.sb("fwa%d" % i, [128, 256], F32) for i in range(2)]
    wb = [P.sb("fwb%d" % i, [128, 256], F32) for i in range(2)]
    wc = [P.sb("fwc%d" % i, [128, 256], F32) for i in range(2)]
    wd = [P.sb("fwd%d" % i, [128, 256], F32) for i in range(2)]
    ev = 0
    psi = 0
    for gp in range(2):
        chunks = [(i * 1024, 1024) for i in range(NL // 1024)] + [(NL, NCX)]
        si = 0
        for ci, (c0, cn) in enumerate(chunks):
            z_ = zc[ci % 2]
            for gi in range(2):
                for cc in range(2):
                    r0 = F_FN + (2 * gp + gi) * 256 + cc * 128
                    P.dma("sp", z_[:, gi * 2 + cc, :cn], k.uF_d.ap()[r0:r0 + 128, c0:c0 + cn], w=[z_.b(gi * 2 + cc)])
            for tl in range(cn // 128):
                tok0 = c0 + tl * 128
                isctx = tok0 >= NL
                s_ = abc[(tok0 - NL) // 128] if isctx else stg[si % 3]
                si += 1
                for gi in range(2):
                    ps = k.ps[psi % 4]
                    psi += 1
                    for cc in range(2):
                        P.pe(Dl(T_.matmul, ps[:, :], z_[:, gi * 2 + cc, tl * 128:(tl + 1) * 128], cs[:, cc, :], start=(cc == 0), stop=(cc == 1)),
                             r=[z_.b(gi * 2 + cc), cs], w=[ps])
                    ev += 1
                    if ev % 2:
                        P.act(Dl(A_.copy, s_[:, gi, :], ps[:, :]), r=[ps], w=[s_.b(gi)])
                    else:
                        P.dve(Dl(V_.tensor_copy, s_[:, gi, :], ps[:, :]), r=[ps], w=[s_.b(gi)])
                if not isctx:
                    for ab in range(2):
                        P.dma("sp", k.AB_d.ap()[ab, tok0:tok0 + 128, :].rearrange("t (g c) -> t g c", g=2), s_[:, :, ab * 256:(ab + 1) * 256],
                              r=s_.allb(), w=[k.AB_d.b((ab, tok0 // 2048))])
        if FN_STOP < 1:
            continue
        for ab in range(2):
            for hh in range(2):
                P.dma("sp", D1[ab * 64 + hh * 32:ab * 64 + hh * 32 + 32, :, :],
                      k.AB_d.ap()[ab, hh * 2048:(hh + 1) * 2048, :].rearrange("(t2 t1) c -> t2 t1 c", t1=64),
                      r=[k.AB_d.b((ab, hh))], w=[D1.b((ab, hh))])
        d1r = D1.allb()
        for cb in range(64):
            ps = k.ps[psi % 4]
            psi += 1
            for ci in range(4):
                cp = cb * 4 + ci
                P.pe(Dl(T_.matmul, ps[:, ci * 128:(ci + 1) * 128], D1[:, :, cp:512:256], w1[:, :], start=True, stop=True), r=[d1r, w1], w=[ps])
            psv = ps[:, :].rearrange("p (c r j) -> p c r j", c=4, r=2)
            gr, gim = psv[:, :, 0, :], psv[:, :, 1, :]
            a_, b_, c_, d_ = wa[cb % 2], wb[cb % 2], wc[cb % 2], wd[cb % 2]
            v4 = lambda t: t[:, :].rearrange("p (c j) -> p c j", c=4)
            P.dve(Dl(V_.tensor_tensor, out=v4(a_), in0=gr, in1=v4(twr), op=ALU.mult), r=[ps, twr], w=[a_])
            P.dve(Dl(V_.tensor_tensor, out=v4(b_), in0=gim, in1=v4(twi), op=ALU.mult), r=[ps, twi], w=[b_])
            P.dve(Dl(V_.tensor_tensor, out=v4(c_), in0=gr, in1=v4(twi), op=ALU.mult), r=[ps, twi], w=[c_])
            P.dve(Dl(V_.tensor_tensor, out=v4(d_), in0=gim, in1=v4(twr), op=ALU.mult), r=[ps, twr], w=[d_])
            P.pool(Dl(G_.tensor_tensor, out=H[:, 0, :, cb * 4:(cb + 1) * 4].rearrange("p j c -> p c j"), in0=v4(a_), in1=v4(b_), op=ALU.subtract), r=[a_, b_], w=[H.b(cb)])
            P.pool(Dl(G_.tensor_tensor, out=H[:, 1, :, cb * 4:(cb + 1) * 4].rearrange("p j c -> p c j"), in0=v4(c_), in1=v4(d_), op=ALU.add), r=[c_, d_], w=[H.b(cb)])
        if FN_STOP < 1.5:
            continue
        hr = H.allb()
        for ch in range(2):
            for jb in range(16):
                ps = k.ps[psi % 4]
                psi += 1
                for ji in range(4):
                    j2 = jb * 4 + ji
                    P.pe(Dl(T_.matmul, ps[:, ji * 128:(ji + 1) * 128], H[:, 0, j2, ch * 128:(ch + 1) * 128], w3c[:, :], start=True, stop=False),
                         r=[hr, w3c], w=[ps])
                    P.pe(Dl(T_.matmul, ps[:, ji * 128:(ji + 1) * 128], H[:, 1, j2, ch * 128:(ch + 1) * 128], w3s[:, :], start=False, stop=True),
                         r=[hr, w3s], w=[ps])
                for ji in range(4):
                    if FN_STOP == 1.5:
                        break
                    j2 = jb * 4 + ji
                    for gi in range(2):
                        o = yf[gi][ch][:, j2:NL:64]
                        i_ = ps[:, ji * 128 + gi * 64:ji * 128 + (gi + 1) * 64]
                        ev += 1
                        if (ev % 2 or FN_STOP == 2.1) and FN_STOP != 2.2:
                            P.act(Dl(A_.mul, o, i_, 1.0 / 1024.0), r=[ps], w=[yf[gi][ch].b(jb)])
                        else:
                            P.dve(Dl(V_.tensor_scalar, out=o, in0=i_, scalar1=1.0 / 1024.0, scalar2=None, op0=ALU.mult), r=[ps], w=[yf[gi][ch].b(jb)])
        if FN_STOP < 3:
            continue
        for gi in range(2):
            for ch in range(2):
                ps = k.ps[psi % 4]
                psi += 1
                idx = 0
                for kt in range(2):
                    for ab in range(2):
                        P.pe(Dl(T_.matmul, ps[:, 0:256], abc[kt][:, gi, ab * 256 + ch * 128:ab * 256 + (ch + 1) * 128], csx[:, kt, ab, :],
                                start=(idx == 0), stop=(idx == 3)), r=[abc[kt].allb(), csx], w=[ps])
                        idx += 1
                P.act(Dl(A_.mul, yf[gi][ch][:, NL:NT], ps[:, 0:256], 1.0 / 256.0), r=[ps], w=[yf[gi][ch].b("ctx")])
                r0 = (2 * gp + gi) * 256 + ch * 128
                P.dma("sp", k.br_d.ap()[2, r0:r0 + 128, :], yf[gi][ch][:], r=yf[gi][ch].allb(), w=[k.br_d.b(("fn", r0))])
    P.phase_end()


NEG = -30000.0


def host_consts_ml():
    c = {}
    r = np.arange(128)
    c["ml_triF"] = (r[:, None] <= r[None, :]).astype(np.float32)
    c["ml_triB"] = (r[:, None] >= r[None, :]).astype(np.float32)
    c["ml_maskF"] = np.where(r[None, :] <= r[:, None], 0.0, NEG).astype(np.float32)
    c["ml_maskB"] = np.where(r[None, :] >= r[:, None], 0.0, NEG).astype(np.float32)
    return c


def phase_ml(k, l):
    nc, P = k.nc, k.P
    V_, A_, T_, G_ = nc.vector, nc.scalar, nc.tensor, nc.gpsimd
    P.phase_begin()
    ones_f = P.sb("ones_f", [128, 128], F32)
    tri = [P.sb("triF", [128, 128], F32), P.sb("triB", [128, 128], F32)]
    msk = [P.sb("maskF", [128, 128], F32), P.sb("maskB", [128, 128], F32)]
    P.dve(Dl(V_.memset, ones_f[:], 1.0), w=[ones_f])
    for t_, d_ in ((tri[0], k.ml_triF), (tri[1], k.ml_triB), (msk[0], k.ml_maskF), (msk[1], k.ml_maskB)):
        P.dma("act", t_[:], d_.ap(), w=[t_])
    GL = P.sb("GL", [128, NTILE, 16], F32)
    gbias = P.sb("gbias", [128, NTILE, 16], F32)
    P.dma("sp", GL[:], k.uT_g.ap().rearrange("(n p) c -> p n c", p=128), w=[GL])
    P.dma("sp", gbias[:].rearrange("p n c -> p (n c)"), k.ml_gate_bR.ap()[l:l + 1, :].partition_broadcast(128), w=[gbias])
    P.dve(Dl(V_.tensor_tensor, out=GL[:], in0=GL[:], in1=gbias[:], op=ALU.add), r=[GL, gbias], w=[GL])
    gax = P.sb("gax", [128, NTILE, 2, 4], F32)
    gmn = P.sb("gmn", [128, NTILE, 2, 4], F32)
    GLf = GL[:].rearrange("p n (a c) -> p n a c", a=2)[:, :, :, 4:8]
    P.dve(Dl(V_.scalar_tensor_tensor, out=gax[:], in0=GLf, scalar=-1.0, in1=GLf, op0=ALU.mult, op1=ALU.max), r=[GL], w=[gax])
    P.act(Dl(A_.activation, out=gax[:], in_=gax[:], func=AF.Exp, scale=-1.0), r=[gax], w=[gax])
    P.act(Dl(A_.activation, out=gax[:], in_=gax[:], func=AF.Ln, bias=ones_f[:, 0:1], scale=1.0), r=[gax, ones_f], w=[gax])
    P.dve(Dl(V_.tensor_single_scalar, out=gmn[:], in_=GLf, scalar=0.0, op=ALU.min), r=[GL], w=[gmn])
    P.dve(Dl(V_.tensor_tensor, out=GLf, in0=gmn[:], in1=gax[:], op=ALU.subtract), r=[gmn, gax], w=[GL])

    NH = 2
    qT = [[P.sb("mq%d%d" % (i, c), [128, NT], BF16) for c in range(2)] for i in range(NH)]
    kT = [[P.sb("mk%d%d" % (i, c), [128, NT], BF16) for c in range(2)] for i in range(NH)]
    ktok = [P.sb("mkt%d" % i, [128, NTILE, 256], BF16) for i in range(NH)]
    vaug = [P.sb("mv%d" % i, [128, NTILE, 260], BF16) for i in range(NH)]
    C32 = [[P.sb("C32_%d%d" % (i, c), [128, 257], F32) for c in range(2)] for i in range(2 * NH)]
    Cb = [[P.sb("Cb_%d%d" % (i, c), [128, 260], BF16) for c in range(2)] for i in range(2 * NH)]
    mst = P.sb("mst", [128, 2, 2 * NH], F32)
    NB = 8
    IB = [P.sb("IB%d" % i, [128, 128], F32) for i in range(NB)]
    NLB = [P.sb("NLB%d" % i, [128, 128], F32) for i in range(NB)]
    wm = [P.sb("wm%d" % i, [128, 128], F32) for i in range(NB)]
    Dm = [P.sb("Dm%d" % i, [128, 128], F32) for i in range(NB)]
    st = [P.sb("st%d" % i, [128, 16], F32) for i in range(NB)]
    ex = [P.sb("ex%d" % i, [128, 4], F32) for i in range(NB)]
    a_ = [P.sb("a%d" % i, [128, 128], BF16) for i in range(4)] * 2
    aTs = [P.sb("aTs%d" % i, [128, 128], BF16) for i in range(4)] * 2
    Xs = [P.sb("Xs%d" % i, [128, 257], F32) for i in range(4)] * 2
    Z = [P.sb("Z%d" % i, [128, 257], F32) for i in range(4)] * 2
    dmr = [P.sb("dmr%d" % i, [128, 2], F32) for i in range(4)] * 2
    ho = [P.sb("ho%d" % i, [128, 256], F32) for i in range(4)] * 2
    kw = [P.sb("kw%d" % i, [128, 256], BF16) for i in range(4)] * 2
    bank_sets = [(k.ps[0], k.ps[1], k.ps[2], k.ps[3]), (k.ps[4], k.ps[5], k.ps[6], k.ps[7])]
    for i in range(NH):
        P.dve(Dl(V_.memset, vaug[i][:, :, 256:260], 1.0), w=[vaug[i].b("ones")])

    fwd_tiles = CTX_TILES + LAT_TILES
    bwd_tiles = CTX_TILES[::-1] + LAT_TILES[::-1]
    for hp in range(4 // NH):
        for i in range(NH):
            h = hp * NH + i
            for c in range(2):
                P.dma("sp", qT[i][c][:], k.uF_d.ap()[F_MLQ + h * 256 + c * 128:F_MLQ + h * 256 + (c + 1) * 128, :], w=[qT[i][c]])
                P.dma("sp", kT[i][c][:], k.uF_d.ap()[F_MLK + h * 256 + c * 128:F_MLK + h * 256 + (c + 1) * 128, :], w=[kT[i][c]])
            P.dma("sp", ktok[i][:], k.uT_mlk.ap()[:, h * 256:(h + 1) * 256].rearrange("(n p) d -> p n d", p=128), w=[ktok[i]])
            P.dma("sp", vaug[i][:, :, 0:256], k.uT_mlv.ap()[:, h * 256:(h + 1) * 256].rearrange("(n p) d -> p n d", p=128), w=[vaug[i].b("v")])
        for ch in range(2 * NH):
            for c in range(2):
                P.dve(Dl(V_.memset, C32[ch][c][:], 0.0), w=[C32[ch][c]])
                P.pool(Dl(G_.memset, Cb[ch][c][:], 0.0), w=[Cb[ch][c]])
        P.dve(Dl(V_.memset, mst[:], 0.0), w=[mst])
        def ctxv(idx):
            step, ch = idx // (2 * NH), idx % (2 * NH)
            i, d = ch // 2, ch % 2
            h = hp * NH + i
            ti = (fwd_tiles if d == 0 else bwd_tiles)[step]
            tsl = slice(ti * 128, (ti + 1) * 128)
            b = idx % NB
            icol = GL[:, ti, 8 * d + h:8 * d + h + 1]
            fcol = GL[:, ti, 8 * d + 4 + h:8 * d + 4 + h + 1]
            mcur = mst[:, step % 2, ch:ch + 1]
            mnxt = mst[:, (step + 1) % 2, ch:ch + 1]
            s_, e_ = st[b], ex[b]
            vr = [vaug[i].b("v"), vaug[i].b("ones")]
            return step, ch, i, d, h, ti, tsl, b, icol, fcol, mcur, mnxt, s_, e_, vr

        def banks(ch):
            psP, psQ, psY, psCb = bank_sets[ch % 2]
            return psP, psQ, psY, psCb, psQ[:, :].bitcast(BF16)

        def stage_a(idx):
            step, ch, i, d, h, ti, tsl, b, icol, fcol, mcur, mnxt, s_, e_, vr = ctxv(idx)
            psP, psQ, psY, psCb, psQ_bf = banks(ch)
            P.dve(Dl(V_.tensor_scalar, out=IB[b][:], in0=ones_f[:], scalar1=icol, scalar2=None, op0=ALU.mult), r=[ones_f, GL], w=[IB[b]])
            P.dve(Dl(V_.tensor_scalar, out=NLB[b][:], in0=ones_f[:], scalar1=fcol, scalar2=-1.0, op0=ALU.mult, op1=ALU.mult), r=[ones_f, GL], w=[NLB[b]])
            P.pe(Dl(T_.matmul, psP[:, 0:128], IB[b][:], k.ident_f[:], start=True, stop=False), r=[IB[b], k.ident_f], w=[psP])
            P.pe(Dl(T_.matmul, psP[:, 0:128], NLB[b][:], tri[d][:], start=False, stop=True), r=[NLB[b], tri[d]], w=[psP])
            P.pe(Dl(T_.matmul, psP[:, 128:129], tri[d][:], fcol, start=True, stop=True), r=[tri[d], GL], w=[psP])
            P.pe(Dl(T_.matmul, psP[:, 136:137], ones_f[:], fcol, start=True, stop=True), r=[ones_f, GL], w=[psP])
            P.dve(Dl(V_.tensor_tensor, out=wm[b][:], in0=psP[:, 0:128], in1=msk[d][:], op=ALU.add), r=[psP, msk[d]], w=[wm[b]])
            P.dve(Dl(V_.reduce_max, out=s_[:, 1:2], in_=psP[:, 0:128], axis=AX.X), r=[psP], w=[s_])
            P.dve(Dl(V_.tensor_copy, s_[:, 2:4], psP[:, 128:137:8]), r=[psP], w=[s_])
            P.dve(Dl(V_.reduce_max, out=s_[:, 0:1], in_=wm[b][:], axis=AX.X), r=[wm[b]], w=[s_])
            P.dve(Dl(V_.tensor_scalar, out=s_[:, 4:6], in0=s_[:, 0:2], scalar1=mcur, scalar2=None, op0=ALU.max), r=[s_, mst], w=[s_])
            P.dve(Dl(V_.tensor_scalar, out=s_[:, 6:8], in0=s_[:, 4:6], scalar1=-1.0, scalar2=None, op0=ALU.mult), r=[s_], w=[s_])
            P.dve(Dl(V_.tensor_tensor, out=mnxt, in0=s_[:, 3:4], in1=s_[:, 5:6], op=ALU.add), r=[s_], w=[mst])
            P.dve(Dl(V_.tensor_tensor, out=s_[:, 8:9], in0=icol, in1=s_[:, 2:3], op=ALU.subtract), r=[s_, GL], w=[s_])
            P.act(Dl(A_.activation, out=Dm[b][:], in_=wm[b][:], func=AF.Exp, bias=s_[:, 6:7], scale=1.0), r=[wm[b], s_], w=[Dm[b]])
            P.act(Dl(A_.activation, out=e_[:, 0:1], in_=mcur, func=AF.Exp, bias=s_[:, 6:7], scale=1.0), r=[mst, s_], w=[e_])
            P.act(Dl(A_.activation, out=e_[:, 1:2], in_=mcur, func=AF.Exp, bias=s_[:, 7:8], scale=1.0), r=[mst, s_], w=[e_])
            P.act(Dl(A_.activation, out=e_[:, 2:3], in_=s_[:, 8:9], func=AF.Exp, bias=s_[:, 7:8], scale=1.0), r=[s_], w=[e_])
            P.act(Dl(A_.activation, out=e_[:, 3:4], in_=s_[:, 2:3], func=AF.Exp, bias=s_[:, 6:7], scale=-1.0), r=[s_], w=[e_])

        def stage_b(idx):
            step, ch, i, d, h, ti, tsl, b, icol, fcol, mcur, mnxt, s_, e_, vr = ctxv(idx)
            psP, psQ, psY, psCb, psQ_bf = banks(ch)
            for c in range(2):
                P.pe(Dl(T_.matmul, psY[:, 384:512], qT[i][c][:, tsl], kT[i][c][:, tsl], start=(c == 0), stop=(c == 1)),
                     r=[qT[i][c], kT[i][c]], w=[psY])
            P.dve(Dl(V_.tensor_tensor, out=a_[b][:], in0=psY[:, 384:512], in1=Dm[b][:], op=ALU.mult), r=[psY, Dm[b]], w=[a_[b]])
            P.pe(Dl(T_.transpose, psQ_bf[:, 0:128], a_[b][:], k.ident_bf[:]), r=[a_[b], k.ident_bf], w=[psQ])
            P.act(Dl(A_.copy, aTs[b][:], psQ_bf[:, 0:128]), r=[psQ], w=[aTs[b]])
            P.pe(Dl(T_.matmul, psY[:, 0:257], aTs[b][:], vaug[i][:, ti, 0:257], start=True, stop=True), r=[aTs[b], vr], w=[psY])
            for c in range(2):
                P.pe(Dl(T_.matmul, psQ[:, 128:385], qT[i][c][:, tsl], Cb[ch][c][:, 0:257], start=(c == 0), stop=(c == 1)),
                     r=[qT[i][c], Cb[ch][c]], w=[psQ])
            P.act(Dl(A_.activation, out=Xs[b][:], in_=psQ[:, 128:385], func=AF.Copy, scale=e_[:, 0:1]), r=[psQ, e_], w=[Xs[b]])
            P.dve(Dl(V_.tensor_tensor, out=Z[b][:], in0=psY[:, 0:257], in1=Xs[b][:], op=ALU.add), r=[psY, Xs[b]], w=[Z[b]])
            P.dve(Dl(V_.scalar_tensor_tensor, out=dmr[b][:, 0:1], in0=Z[b][:, 256:257], scalar=-1.0, in1=Z[b][:, 256:257], op0=ALU.mult, op1=ALU.max),
                  r=[Z[b]], w=[dmr[b]])
            P.dve(Dl(V_.tensor_tensor, out=dmr[b][:, 0:1], in0=dmr[b][:, 0:1], in1=e_[:, 3:4], op=ALU.max), r=[dmr[b], e_], w=[dmr[b]])
            P.dve(Dl(V_.reciprocal, out=dmr[b][:, 1:2], in_=dmr[b][:, 0:1]), r=[dmr[b]], w=[dmr[b]])
            P.act(Dl(A_.activation, out=ho[b][:], in_=Z[b][:, 0:256], func=AF.Copy, scale=dmr[b][:, 1:2]), r=[Z[b], dmr[b]], w=[ho[b]])
            P.dma("sp", k.h_d.ap()[d, ti * 128:(ti + 1) * 128, h * 256:(h + 1) * 256], ho[b][:], r=[ho[b]], w=[k.h_d.b((d, ti, h))])

        def stage_c(idx):
            step, ch, i, d, h, ti, tsl, b, icol, fcol, mcur, mnxt, s_, e_, vr = ctxv(idx)
            psP, psQ, psY, psCb, psQ_bf = banks(ch)
            P.act(Dl(A_.activation, out=kw[b][:], in_=ktok[i][:, ti, :], func=AF.Copy, scale=e_[:, 2:3]), r=[ktok[i], e_], w=[kw[b]])
            for c in range(2):
                P.pe(Dl(T_.matmul, psCb[:, 0:257], kw[b][:, c * 128:(c + 1) * 128], vaug[i][:, ti, 0:257], start=True, stop=True),
                     r=[kw[b], vr], w=[psCb])
                P.dve(Dl(V_.scalar_tensor_tensor, out=C32[ch][c][:], in0=C32[ch][c][:], scalar=e_[:, 1:2], in1=psCb[:, 0:257],
                         op0=ALU.mult, op1=ALU.add), r=[C32[ch][c], e_, psCb], w=[C32[ch][c]])
                P.act(Dl(A_.copy, Cb[ch][c][:, 0:257], C32[ch][c][:]), r=[C32[ch][c]], w=[Cb[ch][c]])

        def capture(fns):
            P.capture = []
            for f, idx in fns:
                f(idx)
            ops = P.capture
            P.capture = None
            return ops

        NCH = 2 * NH
        for pair in range(NH):
            P.capture = None
        for ch in range(NCH):
            stage_a(ch)
        for step in range(NTILE):
            for pair in range(NH):
                lists = []
                for ch in (2 * pair, 2 * pair + 1):
                    idx = step * NCH + ch
                    if step + 1 < NTILE:
                        lists.append(capture([(stage_a, idx + NCH)]))
                    lists.append(capture([(stage_b, idx), (stage_c, idx)]))
                for j in range(max(len(x) for x in lists)):
                    for lst in lists:
                        if j < len(lst):
                            eng, fn, r, w, dma = lst[j]
                            P.op(eng, fn, r, w, dma)
    P.phase_end()

    P.phase_begin()
    hgb = P.sb("hgb", [128, 1024], F32)
    P.dma("sp", hgb[:], k.ml_head_g.ap()[l:l + 1, :].partition_broadcast(128), w=[hgb])
    eps = P.sb("eps", [128, 1], F32)
    P.dve(Dl(V_.memset, eps[:], EPS), w=[eps])
    hf = [P.sb("hf%d" % i, [128, 1024], F32) for i in range(2)]
    hb_ = [P.sb("hbk%d" % i, [128, 1024], F32) for i in range(2)]
    og = [P.sb("og%d" % i, [128, 1024], BF16) for i in range(2)]
    sqm = P.sb("sqm", [128, 1024], F32)
    ssm = [P.sb("ssm%d" % i, [128, 4], F32) for i in range(2)]
    ym = [P.sb("ym%d" % i, [128, 1024], BF16) for i in range(2)]
    fst = [P.sb("fstm%d" % i, [128, KC, 512], BF16) for i in range(2)]
    pst = [k.ps[0][:, :].bitcast(BF16), k.ps[1][:, :].bitcast(BF16)]
    for ti in range(NTILE):
        b = ti % 2
        f_, b_, o_, y_, s_ = hf[b], hb_[b], og[b], ym[b], ssm[b]
        P.dma("sp", f_[:], k.h_d.ap()[0, ti * 128:(ti + 1) * 128, :], r=[k.h_d.b((0, ti, h)) for h in range(4)], w=[f_])
        P.dma("sp", b_[:], k.h_d.ap()[1, ti * 128:(ti + 1) * 128, :], r=[k.h_d.b((1, ti, h)) for h in range(4)], w=[b_])
        P.dma("act", o_[:], k.uT_mlo.ap()[ti * 128:(ti + 1) * 128, :], w=[o_])
        P.pool(Dl(G_.tensor_tensor, out=f_[:], in0=f_[:], in1=b_[:], op=ALU.add), r=[f_, b_], w=[f_])
        P.pool(Dl(G_.tensor_tensor, out=sqm[:], in0=f_[:], in1=f_[:], op=ALU.mult), r=[f_], w=[sqm])
        P.dve(Dl(V_.reduce_sum, out=s_[:], in_=sqm[:].rearrange("p (h d) -> p h d", h=4), axis=AX.X), r=[sqm], w=[s_])
        P.act(Dl(A_.activation, out=s_[:], in_=s_[:], func=AF.Sqrt, scale=1.0 / 256, bias=eps[:, 0:1]), r=[s_, eps], w=[s_])
        P.dve(Dl(V_.reciprocal, out=s_[:], in_=s_[:]), r=[s_], w=[s_])
        for h in range(4):
            P.act(Dl(A_.activation, out=f_[:, h * 256:(h + 1) * 256], in_=f_[:, h * 256:(h + 1) * 256], func=AF.Copy, scale=s_[:, h:h + 1]),
                  r=[f_, s_], w=[f_])
        P.dve(Dl(V_.tensor_tensor, out=f_[:], in0=f_[:], in1=hgb[:], op=ALU.mult), r=[f_, hgb], w=[f_])
        P.dve(Dl(V_.tensor_tensor, out=y_[:], in0=f_[:], in1=o_[:], op=ALU.mult), r=[f_, o_], w=[y_])
        ps = k.ps[b]
        for c in range(KC):
            P.pe(Dl(T_.transpose, pst[b][:, c * 128:(c + 1) * 128], y_[:, c * 128:(c + 1) * 128], k.ident_bf[:]), r=[y_, k.ident_bf], w=[ps])
        g4, t4 = ti // 4, ti % 4
        fs = fst[g4 % 2]
        o_ap = fs[:, :, t4 * 128:(t4 + 1) * 128]
        i_ap = pst[b][:, :].rearrange("p (c t) -> p c t", c=KC)
        if ti % 2:
            P.act(Dl(A_.copy, o_ap, i_ap), r=[ps], w=[fs.b(t4)])
        else:
            P.dve(Dl(V_.tensor_copy, o_ap, i_ap), r=[ps], w=[fs.b(t4)])
        if t4 == 3 or ti == NTILE - 1:
            nt_ = (t4 + 1) * 128
            P.dma("sp", k.br_d.ap()[0].rearrange("(c p) t -> p c t", p=128)[:, :, g4 * 512:g4 * 512 + nt_], fs[:, :, :nt_], r=fs.allb(),
                  w=[k.br_d.b(("ml", g4))])
    P.phase_end()


_CACHE = {}


def kernel(**inputs):
    inp = {k_: np.asarray(v) for k_, v in inputs.items()}
    if "nc" not in _CACHE:
        _CACHE["nc"] = build()[0]
        _CACHE["consts"] = host_consts()
    nc = _CACHE["nc"]
    consts = _CACHE["consts"]
    B = inp["x"].shape[0]
    in_maps = [prep_core_inputs(inp, c % B, consts) for c in range(8)]
    res = run_bass_kernel_spmd(nc, in_maps, core_ids=list(range(8)))
    out = np.stack([np.asarray(res.results[b]["out"]) for b in range(B)], axis=0)
    return out.astype(np.float32)
```
